# Optimizing a Trainium2 kernel written in Bass

```python
import jax, jax.numpy as jnp
from jax import lax
import numpy as np

D_MODEL = 2048
BATCH = 1
SEQ = 8192
DEPTH = 2
DEC_BATCH = 8
DEC_SEQ = 4096
PAST_LEN = 128

GDN_HEADS = 8
GDN_DK = 128
GDN_DV = 128
GDN_WIDTH = GDN_HEADS * GDN_DV
GDN_QKV = 2 * GDN_HEADS * GDN_DK + GDN_WIDTH
CONV_K = 5
CHUNK = 64
MLA_HEADS = 8
MLA_NOPE = 128
MLA_ROPE = 64
MLA_DV = 128
MLA_WIDTH = MLA_HEADS * MLA_DV
Q_LORA = 512
KV_LORA = 256
ROPE_BASE = 10000.0
Q_BLOCK = 128
D_MIX = GDN_WIDTH + MLA_WIDTH
EPS = 1e-6
MLA_SCALE = (MLA_NOPE + MLA_ROPE) ** -0.5
GDN_SCALE = GDN_DK ** -0.5

SPLIT_SIZES = (GDN_QKV, GDN_WIDTH, 2 * GDN_HEADS, 2 * GDN_HEADS, Q_LORA, KV_LORA, MLA_ROPE, MLA_WIDTH)
D_IN = sum(SPLIT_SIZES)
SPLIT_IDX = tuple(int(i) for i in np.cumsum(SPLIT_SIZES)[:-1])

kernel_name = "hybrid_gdn_mla_parallel_encoder"


def rmsnorm(x, g):
    xf = x.astype(jnp.float32)
    y = xf * lax.rsqrt(jnp.mean(xf * xf, axis=-1, keepdims=True) + EPS)
    return (y * g.astype(jnp.float32)).astype(x.dtype)


def l2norm(x):
    return x * lax.rsqrt(jnp.sum(x * x, axis=-1, keepdims=True) + EPS)


def centred_depthwise_conv(x, w):
    pad = (CONV_K - 1) // 2
    S = x.shape[1]
    xp = jnp.pad(x, ((0, 0), (pad, pad), (0, 0)))
    out = xp[:, 0:S] * w[0]
    for t in range(1, CONV_K):
        out = out + xp[:, t:t + S] * w[t]
    return out


def gated_delta_rule_chunked(q, k, v, g, beta):
    B, S, H, DK = q.shape
    DV = v.shape[-1]
    N = S // CHUNK

    def chunks(t):
        t = jnp.moveaxis(t, 2, 1)
        return t.reshape((B, H, N, CHUNK) + t.shape[3:])

    q, k, v, g, beta = (chunks(t) for t in (q * GDN_SCALE, k, v, g, beta))
    gc = jnp.cumsum(g, axis=-1)
    causal = jnp.tril(jnp.ones((CHUNK, CHUNK), bool))
    strict = jnp.tril(jnp.ones((CHUNK, CHUNK), bool), -1)
    diff = gc[..., :, None] - gc[..., None, :]
    decay = jnp.where(causal, jnp.exp(jnp.where(causal, diff, 0.0)), 0.0)
    kb = k * beta[..., None]
    L = jnp.where(strict, jnp.einsum('bhnid,bhnjd->bhnij', kb, k) * decay, 0.0)
    A = L + jnp.eye(CHUNK, dtype=L.dtype)
    u = lax.linalg.triangular_solve(A, v * beta[..., None], left_side=True, lower=True, unit_diagonal=True)
    w = lax.linalg.triangular_solve(A, kb * jnp.exp(gc)[..., None], left_side=True, lower=True, unit_diagonal=True)
    attn = jnp.where(causal, jnp.einsum('bhnid,bhnjd->bhnij', q, k) * decay, 0.0)
    q_dec = q * jnp.exp(gc)[..., None]
    g_last = gc[..., -1]
    k_dec = k * jnp.exp(g_last[..., None] - gc)[..., None]

    def step(state, xs):
        q_i, k_i, u_i, w_i, a_i, gl_i = xs
        v_new = u_i - jnp.einsum('bhcd,bhde->bhce', w_i, state)
        o_i = jnp.einsum('bhcd,bhde->bhce', q_i, state) + jnp.einsum('bhcj,bhje->bhce', a_i, v_new)
        state = state * jnp.exp(gl_i)[..., None, None] + jnp.einsum('bhcd,bhce->bhde', k_i, v_new)
        return state, o_i

    xs = tuple(jnp.moveaxis(t, 2, 0) for t in (q_dec, k_dec, u, w, attn, g_last))
    state0 = jnp.zeros((B, H, DK, DV), jnp.float32)
    _, o = lax.scan(step, state0, xs)
    return jnp.transpose(o, (1, 0, 3, 2, 4)).reshape(B, S, H, DV)


def gdn_branch(qkv, b, a, a_log, dt_bias, norm_g):
    B, S, _ = qkv.shape
    qkv = qkv.astype(jnp.float32)
    qd = GDN_HEADS * GDN_DK
    q = l2norm(qkv[..., :qd].reshape(B, S, GDN_HEADS, GDN_DK))
    k = l2norm(qkv[..., qd:2 * qd].reshape(B, S, GDN_HEADS, GDN_DK))
    v = qkv[..., 2 * qd:].reshape(B, S, GDN_HEADS, GDN_DV)
    beta = jax.nn.sigmoid(b.astype(jnp.float32).reshape(B, S, 2, GDN_HEADS))
    g = -jnp.exp(a_log.astype(jnp.float32)) * jax.nn.softplus(
        a.astype(jnp.float32).reshape(B, S, 2, GDN_HEADS) + dt_bias.astype(jnp.float32))
    o_fwd = gated_delta_rule_chunked(q, k, v, g[:, :, 0], beta[:, :, 0])
    flip = lambda t: jnp.flip(t, axis=1)
    o_bwd = flip(gated_delta_rule_chunked(flip(q), flip(k), flip(v), flip(g[:, :, 1]), flip(beta[:, :, 1])))
    o = rmsnorm(o_fwd + o_bwd, norm_g)
    return o.reshape(B, S, GDN_WIDTH)


def rope_tables(S):
    pos = jnp.arange(S, dtype=jnp.float32)
    inv = ROPE_BASE ** (-jnp.arange(0, MLA_ROPE, 2, dtype=jnp.float32) / MLA_ROPE)
    ang = pos[:, None] * inv[None, :]
    return jnp.cos(ang), jnp.sin(ang)


def apply_rope(x, cos, sin):
    xf = x.astype(jnp.float32)
    x1, x2 = xf[..., :MLA_ROPE // 2], xf[..., MLA_ROPE // 2:]
    return jnp.concatenate([x1 * cos - x2 * sin, x1 * sin + x2 * cos], axis=-1).astype(x.dtype)


def dense_attention(q, k, v):
    B, H, S, Dq = q.shape
    Dv = v.shape[-1]
    nb = S // Q_BLOCK
    qb = jnp.transpose(q.reshape(B, H, nb, Q_BLOCK, Dq), (2, 0, 1, 3, 4))

    def one_block(qi):
        s = jnp.einsum('bhqd,bhkd->bhqk', qi, k, preferred_element_type=jnp.float32) * MLA_SCALE
        p = jax.nn.softmax(s, axis=-1)
        return jnp.einsum('bhqk,bhkd->bhqd', p.astype(v.dtype), v)

    o = lax.map(one_block, qb)
    return jnp.transpose(o, (1, 0, 3, 2, 4)).reshape(B, S, H, Dv)


def mla_branch(c_q, c_kv, k_pe, q_norm_g, kv_norm_g, w_uq, w_ukv, cos, sin):
    B, S, _ = c_q.shape
    q = jnp.einsum('bsr,re->bse', rmsnorm(c_q, q_norm_g), w_uq).reshape(B, S, MLA_HEADS, MLA_NOPE + MLA_ROPE)
    kv = jnp.einsum('bsr,re->bse', rmsnorm(c_kv, kv_norm_g), w_ukv).reshape(B, S, MLA_HEADS, MLA_NOPE + MLA_DV)
    q_nope, q_pe = q[..., :MLA_NOPE], q[..., MLA_NOPE:]
    k_nope, v = kv[..., :MLA_NOPE], kv[..., MLA_NOPE:]
    q_pe = apply_rope(q_pe, cos[:, None, :], sin[:, None, :])
    k_pe = apply_rope(k_pe, cos, sin)
    k_pe = jnp.broadcast_to(k_pe[:, :, None, :], (B, S, MLA_HEADS, MLA_ROPE))
    qh = jnp.transpose(jnp.concatenate([q_nope, q_pe], axis=-1), (0, 2, 1, 3))
    kh = jnp.transpose(jnp.concatenate([k_nope, k_pe], axis=-1), (0, 2, 1, 3))
    vh = jnp.transpose(v, (0, 2, 1, 3))
    return dense_attention(qh, kh, vh).reshape(B, S, MLA_WIDTH)


def hybrid_layer(x, pre_g, post_g, w_in, conv_w, a_log, dt_bias, gdn_norm_g,
                 q_norm_g, kv_norm_g, w_uq, w_ukv, w_out, cos, sin):
    h = rmsnorm(x, pre_g)
    proj = jnp.einsum('bsd,de->bse', h, w_in)
    qkv_a, z_a, b_a, a_a, c_q, c_kv, k_pe, z_b = jnp.split(proj, SPLIT_IDX, axis=-1)
    qkv_a = jax.nn.silu(centred_depthwise_conv(qkv_a, conv_w))
    o_a = gdn_branch(qkv_a, b_a, a_a, a_log, dt_bias, gdn_norm_g).astype(x.dtype)
    o_b = mla_branch(c_q, c_kv, k_pe, q_norm_g, kv_norm_g, w_uq, w_ukv, cos, sin)
    mix = jnp.concatenate([o_a * jax.nn.silu(z_a), o_b * jax.nn.silu(z_b)], axis=-1)
    y = jnp.einsum('bse,ed->bsd', mix, w_out)
    return x + rmsnorm(y, post_g)


def run_trunk(x, pre_norm_g, post_norm_g, w_in, conv_w, gdn_a_log, gdn_dt_bias, gdn_norm_g,
              mla_q_norm_g, mla_kv_norm_g, mla_w_uq, mla_w_ukv, w_out):
    cos, sin = rope_tables(x.shape[1])
    for l in range(DEPTH):
        x = hybrid_layer(x, pre_norm_g[l], post_norm_g[l], w_in[l], conv_w[l], gdn_a_log[l],
                         gdn_dt_bias[l], gdn_norm_g[l], mla_q_norm_g[l], mla_kv_norm_g[l],
                         mla_w_uq[l], mla_w_ukv[l], w_out[l], cos, sin)
    return x


def setup_inputs(seed: int = 0) -> dict:
    key = jax.random.key(seed)
    ks = jax.random.split(key, 16)
    f32 = jnp.float32
    nrm = lambda k, shape, scale: jax.random.normal(k, shape, f32) * scale
    gain = lambda k, shape: 1.0 + 0.02 * jax.random.normal(k, shape, f32)
    x_prompt = jax.random.normal(ks[0], (BATCH, SEQ, D_MODEL), f32)
    x_sample = jax.random.normal(ks[1], (DEC_BATCH, DEC_SEQ, D_MODEL), f32)
    pre_norm_g = gain(ks[2], (DEPTH, D_MODEL))
    post_norm_g = gain(ks[3], (DEPTH, D_MODEL))
    w_in = nrm(ks[4], (DEPTH, D_MODEL, D_IN), D_MODEL ** -0.5)
    conv_w = nrm(ks[5], (DEPTH, CONV_K, GDN_QKV), CONV_K ** -0.5)
    gdn_a_log = jnp.log(jax.random.uniform(ks[6], (DEPTH, 2, GDN_HEADS), f32, 1.0, 16.0))
    dt = jnp.exp(jax.random.uniform(ks[7], (DEPTH, 2, GDN_HEADS), f32, np.log(1e-3), np.log(1e-1)))
    gdn_dt_bias = dt + jnp.log(-jnp.expm1(-dt))
    gdn_norm_g = gain(ks[8], (DEPTH, GDN_DV))
    mla_q_norm_g = gain(ks[9], (DEPTH, Q_LORA))
    mla_kv_norm_g = gain(ks[10], (DEPTH, KV_LORA))
    mla_w_uq = nrm(ks[11], (DEPTH, Q_LORA, MLA_HEADS * (MLA_NOPE + MLA_ROPE)), Q_LORA ** -0.5)
    mla_w_ukv = nrm(ks[12], (DEPTH, KV_LORA, MLA_HEADS * (MLA_NOPE + MLA_DV)), KV_LORA ** -0.5)
    w_out = nrm(ks[13], (DEPTH, D_MIX, D_MODEL), D_MIX ** -0.5)
    return {"x_prompt": x_prompt, "x_sample": x_sample, "pre_norm_g": pre_norm_g,
            "post_norm_g": post_norm_g, "w_in": w_in, "conv_w": conv_w, "gdn_a_log": gdn_a_log,
            "gdn_dt_bias": gdn_dt_bias, "gdn_norm_g": gdn_norm_g, "mla_q_norm_g": mla_q_norm_g,
            "mla_kv_norm_g": mla_kv_norm_g, "mla_w_uq": mla_w_uq, "mla_w_ukv": mla_w_ukv,
            "w_out": w_out}


def reference(x_prompt, x_sample, pre_norm_g, post_norm_g, w_in, conv_w, gdn_a_log, gdn_dt_bias,
              gdn_norm_g, mla_q_norm_g, mla_kv_norm_g, mla_w_uq, mla_w_ukv, w_out):
    y_prompt = run_trunk(x_prompt, pre_norm_g, post_norm_g, w_in, conv_w, gdn_a_log, gdn_dt_bias,
                         gdn_norm_g, mla_q_norm_g, mla_kv_norm_g, mla_w_uq, mla_w_ukv, w_out)
    y_sample = run_trunk(x_sample, pre_norm_g, post_norm_g, w_in, conv_w, gdn_a_log, gdn_dt_bias,
                         gdn_norm_g, mla_q_norm_g, mla_kv_norm_g, mla_w_uq, mla_w_ukv, w_out)
    return (y_prompt, y_sample)
```

```python
import numpy as np
import ml_dtypes
from contextlib import ExitStack
import concourse.bass as bass
import concourse.mybir as mybir
from concourse.bass_utils import run_bass_kernel_spmd

F32 = mybir.dt.float32
BF16 = mybir.dt.bfloat16
AF = mybir.ActivationFunctionType
ALU = mybir.AluOpType

D = 2048
NH = 8
EPS = 1e-6
NROW = 6016
R_QKV, R_ZA, R_CQ, R_CKV, R_KPE, R_ZB = 0, 3072, 4096, 4608, 4864, 4992
MLA_SCALE = 192 ** -0.5
GDN_SCALE = 128 ** -0.5
NEG = -30000.0
PHASES = (1, 2, 3, 4, 5, 6, 7)
DBG3 = (3, 8)


class Prog:
    ENG = ('pe', 'act', 'dve', 'pool', 'sp')
    NL = 8

    def __init__(self, nc, stack):
        self.nc = nc
        self.q = {e: [] for e in self.ENG}
        self.sem = {e: stack.enter_context(nc.semaphore('s_' + e)) for e in self.ENG}
        self.cnt = {e: 0 for e in self.ENG}
        self.dq = ('sp', 'pool')
        self.dsem = {q: [stack.enter_context(nc.semaphore('d_%s%d' % (q, i))) for i in range(self.NL)]
                     for q in self.dq}
        self.dn = {q: 0 for q in self.dq}
        self.seen = {e: {} for e in self.ENG}
        self.lastw = {}
        self.readers = {}

    def _semh(self, key):
        return self.sem[key[1]] if key[0] == 'e' else self.dsem[key[1]][key[2]]

    def _collect(self, eng, reads, writes, is_dma):
        deps = {}

        def add(d, raw):
            if d is None:
                return
            key, val, src = d
            if src == eng and key[0] == 'e' and not is_dma:
                if eng == 'pe' or not raw:
                    return
            if self.seen[eng].get(key, 0) >= val:
                return
            if deps.get(key, 0) < val:
                deps[key] = val
        for t in reads:
            add(self.lastw.get(t), True)
        for t in writes:
            add(self.lastw.get(t), False)
            for d in self.readers.get(t, {}).values():
                add(d, False)
        for key, val in deps.items():
            self.seen[eng][key] = val
        return [(self._semh(k), v) for k, v in deps.items()]

    def _register(self, dep, reads, writes):
        for t in writes:
            self.lastw[t] = dep
            self.readers[t] = {}
        for t in reads:
            self.readers.setdefault(t, {})[dep[0]] = dep

    PSUM_NAMES = {'pt', 'pm', 'pba', 'pc', 'pss', 'ptr', 'pcol', 'pv', 'pp', 'psT', 'po', 'pl', 'py', 'psmall', 'pf'}

    def _isps(self, t):
        return (t[0] if isinstance(t, tuple) else t) in self.PSUM_NAMES

    def op(self, eng, fn, reads=(), writes=(), inc=True):
        writes = list(writes) + [t for t in reads if self._isps(t)]
        reads = [t for t in reads if not self._isps(t)]
        waits = self._collect(eng, reads, writes, False)
        if inc:
            self.cnt[eng] += 1
            dep = (('e', eng), self.cnt[eng], eng)
            self.q[eng].append((waits, fn, self.sem[eng], 1))
        else:
            dep = (('e', eng), self.cnt[eng] + 1, eng)
            self.q[eng].append((waits, fn, None, 0))
        self._register(dep, reads, writes)

    def dma(self, q, out, in_, reads=(), writes=()):
        n = self.dn[q]
        lane = n % self.NL
        self.dn[q] += 1
        key = ('d', q, lane)
        val = 16 * (n // self.NL + 1)
        waits = self._collect(q, reads, writes, True)
        prev = val - 16
        if prev > 0 and self.seen[q].get(key, 0) < prev:
            waits.append((self.dsem[q][lane], prev))
            self.seen[q][key] = prev
        self.q[q].append((waits, lambda e: e.dma_start(out=out, in_=in_), self.dsem[q][lane], 16))
        self._register((key, val, q), reads, writes)

    def coll(self, kind, ins, outs, reads=(), writes=(), ncores=8):
        q = 'pool'
        n = self.dn[q]
        lane = n % self.NL
        self.dn[q] += 1
        key = ('d', q, lane)
        val = 16 * (n // self.NL + 1)
        waits = self._collect(q, reads, writes, True)
        prev = val - 16
        if prev > 0 and self.seen[q].get(key, 0) < prev:
            waits.append((self.dsem[q][lane], prev))
            self.seen[q][key] = prev
        rg = [list(range(ncores))]
        self.q[q].append((waits, lambda e: e.collective_compute(kind, ALU.bypass, replica_groups=rg, ins=[a for a in ins], outs=[a for a in outs]), self.dsem[q][lane], 16))
        self._register((key, val, q), reads, writes)

    def barrier(self):
        for e in self.ENG:
            waits = []
            for e2 in self.ENG:
                key = ('e', e2)
                if self.cnt[e2] > self.seen[e].get(key, 0):
                    waits.append((self.sem[e2], self.cnt[e2]))
                    self.seen[e][key] = self.cnt[e2]
            for q in self.dq:
                for lane in range(self.NL):
                    n = self.dn[q]
                    k = (n - lane + self.NL - 1) // self.NL if n > lane else 0
                    val = 16 * k
                    key = ('d', q, lane)
                    if val > self.seen[e].get(key, 0):
                        waits.append((self.dsem[q][lane], val))
                        self.seen[e][key] = val
            self.q[e].append((waits, None, None, 0))
        self.lastw.clear()
        self.readers.clear()

    def emit(self):
        nc = self.nc
        with nc.Block() as block:
            decos = {'pe': block.tensor, 'act': block.scalar, 'dve': block.vector,
                     'pool': block.gpsimd, 'sp': block.sync}
            for name in self.ENG:
                def body(e, name=name):
                    for waits, fn, sem, inc in self.q[name]:
                        for s, v in waits:
                            e.wait_ge(s, v)
                        if fn is not None:
                            r = fn(e)
                            if inc:
                                r.then_inc(sem, inc)
                decos[name](body)


class Ctx:
    uid = 0

    def newuid(self):
        self.uid += 1
        return "_%d" % self.uid


def sl(i, n):
    return slice(i * n, (i + 1) * n)


def run_rr(gens):
    gens = list(gens)
    while gens:
        for g in list(gens):
            try:
                next(g)
            except StopIteration:
                gens.remove(g)


def phase_norm(C, P, x_d, S, l):
    nc = C.nc
    with ExitStack() as st:
        u_ = C.newuid()
        sb = lambda n, s, d: st.enter_context(nc.sbuf_tensor(n + u_, s, d))
        psm = lambda n, s, d: st.enter_context(nc.psum_tensor(n + u_, s, d))
        gb = sb("n_gb", [128, D], F32)
        xt = [sb("n_xt%d" % i, [128, D], F32) for i in range(3)]
        hs = [sb("n_hs%d" % i, [128, D], BF16) for i in range(2)]
        junk = sb("n_junk", [128, D], BF16)
        ss = sb("n_ss", [128, 8], F32)
        hT = [sb("n_hT%d" % i, [128, 16, 512], BF16) for i in range(2)]
        pt = [psm("n_pt%d" % i, [128, 1024], BF16) for i in range(4)]
        P.dma('sp', gb[:], C.pre_g[l:l + 1, :].partition_broadcast(128), writes=['gb'])
        nt = S // 128
        for t in range(nt):
            xb = xt[t % 3]
            xtk = ('xt', t % 3)
            P.dma('sp', xb[:], x_d[sl(t, 128), :], writes=[xtk])
            c0 = (t % 2) * 4
            sst, rst = ('ss', t % 2), ('rs', t % 2)
            P.op('act', lambda e, xb=xb, c0=c0: e.activation(out=junk[:], in_=xb[:], func=AF.Square, accum_out=ss[:, c0:c0 + 1]),
                 reads=[xtk], writes=['junk', sst])
            P.op('act', lambda e, c0=c0: e.activation(out=ss[:, c0 + 1:c0 + 2], in_=ss[:, c0:c0 + 1], func=AF.Sqrt, bias=EPS, scale=1.0 / D),
                 reads=[sst], writes=[rst])
            P.op('dve', lambda e, c0=c0: e.reciprocal(out=ss[:, c0 + 2:c0 + 3], in_=ss[:, c0 + 1:c0 + 2]),
                 reads=[rst], writes=[rst])
            hb = hs[t % 2]
            hk = ('hs', t % 2)
            P.op('dve', lambda e, xb=xb, hb=hb, c0=c0: e.scalar_tensor_tensor(out=hb[:], in0=xb[:], scalar=ss[:, c0 + 2:c0 + 3], in1=gb[:], op0=ALU.mult, op1=ALU.mult),
                 reads=[xtk, rst, 'gb'], writes=[hk])
            tt, j = t // 4, t % 4
            hTb = hT[tt % 2]
            hTk = ('hT', tt % 2)
            for half in range(2):
                pi = (t % 2) * 2 + half
                ptk = ('pt', pi)
                for c in range(8):
                    cc = half * 8 + c
                    P.op('pe', lambda e, hb=hb, cc=cc, pi=pi, c=c: e.transpose(out=pt[pi][:, sl(c, 128)], in_=hb[:, sl(cc, 128)], identity=C.idb[:]),
                         reads=[hk, 'idb'], writes=[ptk], inc=(c == 7))
                if half == 0:
                    P.op('act', lambda e, hTb=hTb, j=j, pi=pi: e.activation(out=hTb[:, 0:8, sl(j, 128)], in_=pt[pi][:].rearrange("p (c t) -> p c t", c=8), func=AF.Copy),
                         reads=[ptk], writes=[hTk])
                else:
                    P.op('pool' if False else 'dve', lambda e, hTb=hTb, j=j, pi=pi: e.tensor_copy(out=hTb[:, 8:16, sl(j, 128)], in_=pt[pi][:].rearrange("p (c t) -> p c t", c=8)),
                         reads=[ptk], writes=[hTk])
            if j == 3:
                P.dma('pool', C.hT_d.rearrange("c p t -> p c t")[:, :, sl(tt, 512)], hTb[:], reads=[hTk], writes=[('hT_d', tt)])
    P.barrier()


def phase_inproj(C, P, S, l):
    nc = C.nc
    with ExitStack() as st:
        u_ = C.newuid()
        sb = lambda n, s, d: st.enter_context(nc.sbuf_tensor(n + u_, s, d))
        psm = lambda n, s, d: st.enter_context(nc.psum_tensor(n + u_, s, d))
        wb = [sb("p_wb%d" % i, [128, 16, 1024], BF16) for i in range(2)]
        wba = sb("p_wba", [128, 16, 32], BF16)
        hT = [sb("p_hT%d" % i, [128, 16, 512], BF16) for i in range(2)]
        yo = [sb("p_yo%d" % i, [128, 8, 512], BF16) for i in range(2)]
        ba = sb("p_ba", [128, 4, 32], F32)
        cst = sb("p_cst", [128, 48], F32)
        tmp = sb("p_tmp", [128, 4, 16], F32)
        gbt = [sb("p_gbt%d" % i, [128, 4, 32], F32) for i in range(2)]
        pm = [psm("p_pm%d" % i, [128, 512], F32) for i in range(6)]
        pba_full = psm("p_pba", [128, 512], F32)
        pba = pba_full[:, 0:128].rearrange("p (j n) -> p j n", j=4)
        ntt = S // 512
        groups = [(g * 8, min(8, 47 - g * 8)) for g in range(6)]
        win = C.w_in_r[l]
        P.dma('pool', wba[:], win[:, NROW:NROW + 32].rearrange("(c p) n -> p c n", p=128), writes=['wba'])
        P.dma('sp', cst[:, 0:16], C.a_log[l:l + 1, :].partition_broadcast(128), writes=['cst'])
        P.dma('sp', cst[:, 16:32], C.dt_bias[l:l + 1, :].partition_broadcast(128), writes=['cst'])
        P.op('act', lambda e: e.activation(out=cst[:, 32:48], in_=cst[:, 0:16], func=AF.Exp), reads=['cst'], writes=['cstA'])
        it = 0
        for gi, (m0, nm) in enumerate(groups):
            wbb = wb[gi % 2]
            wk = ('wb', gi % 2)
            P.dma('pool', wbb[:, :, 0:nm * 128], win[:, m0 * 128:(m0 + nm) * 128].rearrange("(c p) n -> p c n", p=128), writes=[wk])
            for tt in range(ntt):
                hTb = hT[it % 2]
                hk = ('hT', it % 2)
                P.dma('sp', hTb[:], C.hT_d.rearrange("c p t -> p c t")[:, :, sl(tt, 512)], writes=[hk])
                yob = yo[it % 2]
                yk = ('yo', it % 2)
                for m in range(nm):
                    pmi = (it * 8 + m) % 6
                    pk = ('pm', pmi)
                    for c in range(16):
                        P.op('pe', lambda e, wbb=wbb, hTb=hTb, m=m, c=c, pmi=pmi: e.matmul(pm[pmi][:], wbb[:, c, sl(m, 128)], hTb[:, c, :], start=(c == 0), stop=(c == 15)),
                             reads=[wk, hk], writes=[pk], inc=(c == 15))
                    if m % 2 == 0:
                        P.op('act', lambda e, yob=yob, m=m, pmi=pmi: e.activation(out=yob[:, m, :], in_=pm[pmi][:], func=AF.Copy),
                             reads=[pk], writes=[yk])
                    else:
                        P.op('dve', lambda e, yob=yob, m=m, pmi=pmi: e.tensor_copy(out=yob[:, m, :], in_=pm[pmi][:]),
                             reads=[pk], writes=[yk])
                P.dma('pool', C.projT_d[m0 * 128:(m0 + nm) * 128, sl(tt, 512)].rearrange("(m p) t -> p m t", p=128), yob[:, 0:nm, :],
                      reads=[yk], writes=[('projT_d', gi, tt)])
                if gi == 0:
                    for j in range(4):
                        for c in range(16):
                            P.op('pe', lambda e, hTb=hTb, j=j, c=c: e.matmul(pba[:, j, :], hTb[:, c, sl(j, 128)], wba[:, c, :], start=(c == 0), stop=(c == 15)),
                                 reads=[hk, 'wba'], writes=['pba'], inc=(c == 15))
                    gbb = gbt[tt % 2]
                    gk = ('gbt', tt % 2)
                    P.op('dve', lambda e: e.tensor_copy(out=ba[:], in_=pba[:]), reads=['pba'], writes=['ba'])
                    P.op('act', lambda e, gbb=gbb: e.activation(out=gbb[:, :, 16:32], in_=ba[:, :, 0:16], func=AF.Sigmoid), reads=['ba'], writes=[gk])
                    for j in range(4):
                        P.op('dve', lambda e, j=j: e.tensor_tensor(out=tmp[:, j, :], in0=ba[:, j, 16:32], in1=cst[:, 16:32], op=ALU.add),
                             reads=['ba', 'cst'], writes=['tmp'])
                    P.op('act', lambda e: e.activation(out=tmp[:], in_=tmp[:], func=AF.Exp), reads=['tmp'], writes=['tmp'])
                    P.op('act', lambda e: e.activation(out=tmp[:], in_=tmp[:], func=AF.Ln, bias=1.0), reads=['tmp'], writes=['tmp'])
                    for j in range(4):
                        P.op('dve', lambda e, j=j, gbb=gbb: e.scalar_tensor_tensor(out=gbb[:, j, 0:16], in0=tmp[:, j, :], scalar=-1.0, in1=cst[:, 32:48], op0=ALU.mult, op1=ALU.mult),
                             reads=['tmp', 'cstA'], writes=[gk])
                    P.dma('pool', C.gb_d[sl(tt, 512), :].rearrange("(j p) n -> p j n", p=128), gbb[:], reads=[gk], writes=[('gb_d', tt)])
                it += 1
    P.barrier()


def phase_gdn_prep(C, P, S, l):
    nc = C.nc
    with ExitStack() as st:
        u_ = C.newuid()
        sb = lambda n, s, d: st.enter_context(nc.sbuf_tensor(n + u_, s, d))
        psm = lambda n, s, d: st.enter_context(nc.psum_tensor(n + u_, s, d))
        cw = sb("c_cw", [128, 120], F32)
        dg = sb("c_dg", [128, 120, 128], BF16)
        raw = [sb("c_raw%d" % i, [128, S + 4], BF16) for i in range(2)]
        NI = 3
        act = [sb("c_act%d" % i, [128, 512], F32) for i in range(NI)]
        sq = [sb("c_sq%d" % i, [128, 512], BF16) for i in range(NI)]
        rr = [sb("c_rr%d" % i, [128, 512], F32) for i in range(NI)]
        ofm = [sb("c_ofm%d" % i, [128, 512], BF16) for i in range(NI)]
        otm = [sb("c_otm%d" % i, [128, 4, 128], BF16) for i in range(NI)]
        pc = [psm("c_pc%d" % i, [128, 512], F32) for i in range(NI)]
        pss = [psm("c_pss%d" % i, [128, 512], F32) for i in range(NI)]
        P.dma('sp', cw[:], C.conv_r[l], writes=['cw'])
        for i in range(120):
            P.op('dve' if i % 2 else 'act', (lambda e, i=i: e.tensor_scalar(out=dg[:, i, :], in0=C.idf[:], scalar1=cw[:, i:i + 1], scalar2=None, op0=ALU.mult)) if i % 2 else
                 (lambda e, i=i: e.activation(out=dg[:, i, :], in_=C.idf[:], func=AF.Copy, scale=cw[:, i:i + 1])),
                 reads=['cw', 'idf'], writes=[('dg', i)])
        ntt = S // 512

        def tile(which, h, ct, rb, rk, tt, b2):
            pk = ('pc', b2)
            for j in range(5):
                P.op('pe', lambda e, j=j: e.matmul(pc[b2][:], dg[:, j * 24 + ct, :], rb[:, tt * 512 + j:tt * 512 + j + 512], start=(j == 0), stop=(j == 4)),
                     reads=[rk, ('dg', j * 24 + ct)], writes=[pk], inc=(j == 4))
            yield
            ab, ak = act[b2], ('act', b2)
            P.op('act', lambda e: e.activation(out=ab[:], in_=pc[b2][:], func=AF.Silu), reads=[pk], writes=[ak])
            ob, ok = ofm[b2], ('ofm', b2)
            if which < 2:
                sqb, sk = sq[b2], ('sq', b2)
                P.op('pool', lambda e: e.tensor_tensor(out=sqb[:], in0=ab[:], in1=ab[:], op=ALU.mult), reads=[ak], writes=[sk])
                yield
                psk = ('pss', b2)
                P.op('pe', lambda e: e.matmul(pss[b2][:], C.onesb[:], sqb[:], start=True, stop=True), reads=[sk, 'onesb'], writes=[psk])
                yield
                rb_, rrk = rr[b2], ('rr', b2)
                P.op('act', lambda e: e.activation(out=rb_[:], in_=pss[b2][:], func=AF.Sqrt, bias=EPS, scale=1.0), reads=[psk], writes=[rrk])
                yield
                P.op('dve', lambda e: e.reciprocal(out=rb_[:], in_=rb_[:]), reads=[rrk], writes=[rrk])
                scl = GDN_SCALE if which == 0 else 1.0
                P.op('dve', lambda e: e.scalar_tensor_tensor(out=ob[:], in0=ab[:], scalar=scl, in1=rb_[:], op0=ALU.mult, op1=ALU.mult),
                     reads=[ak, rrk], writes=[ok])
                dst = C.gq_d if which == 0 else C.gk_d
                P.dma('pool', dst[h, :, sl(tt, 512)], ob[:], reads=[ok], writes=[('gfm', which, h, tt)])
            else:
                P.op('dve', lambda e: e.tensor_copy(out=ob[:], in_=ab[:]), reads=[ak], writes=[ok])
            yield
            if which >= 1:
                tk = pk
                ptv = pc[b2][:, 0:256].bitcast(BF16).rearrange("p (j t) -> p j t", j=4)
                for j in range(4):
                    P.op('pe', lambda e, j=j: e.transpose(out=ptv[:, j, :], in_=ob[:, sl(j, 128)], identity=C.idb[:]),
                         reads=[ok, 'idb'], writes=[tk], inc=(j == 3))
                yield
                otb, otk = otm[b2], ('otm', b2)
                P.op('act', lambda e: e.activation(out=otb[:], in_=ptv, func=AF.Copy), reads=[tk], writes=[otk])
                dst = C.gkT_d if which == 1 else C.gv_d
                P.dma('pool', dst[sl(tt, 512), sl(h, 128)].rearrange("(j p) n -> p j n", p=128), otb[:], reads=[otk], writes=[('gtm', which, h, tt)])

        for which in range(DBG3[0]):
            for h in range(DBG3[1]):
                ct = which * 8 + h
                rb = raw[ct % 2]
                rk = ('raw', ct % 2)
                P.op('pool', lambda e, rb=rb: e.memset(rb[:, 0:2], 0.0), writes=[rk])
                P.op('pool', lambda e, rb=rb: e.memset(rb[:, S + 2:S + 4], 0.0), writes=[rk])
                P.dma('sp', rb[:, 2:S + 2], C.projT_d[ct * 128:(ct + 1) * 128, :], writes=[rk])
                for g0 in range(0, ntt, NI):
                    run_rr([tile(which, h, ct, rb, rk, tt, k) for k, tt in enumerate(range(g0, min(g0 + NI, ntt)))])
    P.barrier()


def phase_gdn(C, P, S, l):
    nc = C.nc
    N = S // 128
    with ExitStack() as st:
        u_ = C.newuid()
        sb = lambda n, s, d: st.enter_context(nc.sbuf_tensor(n + u_, s, d))
        psm = lambda n, s, d: st.enter_context(nc.psum_tensor(n + u_, s, d))
        NB = 2
        kF = [[sb("g_kF%d%d" % (b, d), [128, NH, 128], BF16) for d in range(2)] for b in range(NB)]
        qF = [[sb("g_qF%d%d" % (b, d), [128, NH, 128], BF16) for d in range(2)] for b in range(NB)]
        kT = [[sb("g_kT%d%d" % (b, d), [128, NH, 128], BF16) for d in range(2)] for b in range(NB)]
        vT = [[sb("g_vT%d%d" % (b, d), [128, NH, 128], BF16) for d in range(2)] for b in range(NB)]
        gbt = [[sb("g_gb%d%d" % (b, d), [128, 32], F32) for d in range(2)] for b in range(NB)]
        stt = [[sb("g_st%d%d" % (b, d), [128, 48], F32) for d in range(2)] for b in range(NB)]
        NS = 4
        TG = [sb("g_TG%d" % i, [128, 128], F32) for i in range(NS)]
        Dm = [sb("g_D%d" % i, [128, 128], F32) for i in range(NS)]
        tm_ = [sb("g_t%d" % i, [128, 128], F32) for i in range(NS)]
        fA = [sb("g_fA%d" % i, [128, 128], BF16) for i in range(NS)]
        fB = [sb("g_fB%d" % i, [128, 128], BF16) for i in range(NS)]
        fX1 = [sb("g_fX1%d" % i, [128, 128], BF16) for i in range(NS)]
        fY1 = [sb("g_fY1%d" % i, [128, 128], BF16) for i in range(NS)]
        fXY = [[sb("g_fXY%d_%d" % (i, k), [128, 256], BF16) for k in range(3)] for i in range(NS)]
        fP = [[sb("g_fP%d_%d" % (i, k), [128, 128], BF16) for k in range(2)] for i in range(NS)]
        bT = [[sb("g_bT%d_%d" % (i, k), [128, 128], BF16) for k in range(2)] for i in range(NS)]
        bTT = [[sb("g_bTT%d_%d" % (i, k), [128, 128], BF16) for k in range(2)] for i in range(NS)]
        bC = [[sb("g_bC%d_%d" % (i, k), [128, 128], BF16) for k in range(2)] for i in range(NS)]
        bNR = [sb("g_bNR%d" % i, [128, 256], BF16) for i in range(NS)]
        fTTb = [sb("g_fTTb%d" % i, [128, 128], BF16) for i in range(NS)]
        ktl = [sb("g_ktl%d" % i, [128, 128], BF16) for i in range(NS)]
        aT = [sb("g_aT%d" % b, [128, 16, 128], BF16) for b in range(NB)]
        wT = [sb("g_wT%d" % b, [128, 16, 128], BF16) for b in range(NB)]
        uu = [sb("g_uu%d" % b, [128, 16, 128], F32) for b in range(NB)]
        kd = [sb("g_kd%d" % b, [128, 16, 128], BF16) for b in range(NB)]
        Sf = sb("g_S", [128, 16, 128], F32)
        Sb_ = sb("g_Sb", [128, 16, 128], BF16)
        vn = [sb("g_vn%d" % i, [128, 128], BF16) for i in range(4)]
        t2 = [sb("g_t2%d" % i, [128, 128], F32) for i in range(4)]
        ob = [[sb("g_ob%d%d" % (b, d), [128, NH, 128], F32) for d in range(2)] for b in range(NB)]
        pf = [psm("g_pf%d" % i, [128, 512], F32) for i in range(7)]
        psmall = psm("g_psm", [128, 512], F32)

        P.op('pool', lambda e: e.memset(Sf[:], 0.0), writes=[('Sf', ch) for ch in range(16)])
        P.op('pool', lambda e: e.memset(Sb_[:], 0.0), writes=[('Sb', ch) for ch in range(16)])

        pctr = [0]
        for n in range(N):
            b = n % NB
            for d in range(2):
                c = n if d == 0 else N - 1 - n
                P.dma('sp', kF[b][d][:], C.gk_d[:, :, sl(c, 128)].rearrange("h p t -> p h t"), writes=[('kF', b, d)])
                P.dma('sp', qF[b][d][:], C.gq_d[:, :, sl(c, 128)].rearrange("h p t -> p h t"), writes=[('qF', b, d)])
                P.dma('sp', kT[b][d][:], C.gkT_d[sl(c, 128), :].rearrange("p (h f) -> p h f", h=NH), writes=[('kT', b, d)])
                P.dma('sp', vT[b][d][:], C.gv_d[sl(c, 128), :].rearrange("p (h f) -> p h f", h=NH), writes=[('vT', b, d)])
                P.dma('sp', gbt[b][d][:], C.gb_d[sl(c, 128), :], writes=[('gbt', b, d)])
            for d in range(2):
                tri = C.tri[d]
                negI = C.negI[d]
                g_ = gbt[b][d][:, d * 8:d * 8 + 8]
                be_ = gbt[b][d][:, 16 + d * 8:16 + d * 8 + 8]
                s_ = stt[b][d]
                sk = ('stt', b, d)
                gk_ = ('gbt', b, d)
                P.op('pe', lambda e, tri=tri, g_=g_: e.matmul(psmall[:, 0:8], tri[:], g_, start=True, stop=True), reads=[gk_, 'consts'], writes=['psmall'])
                P.op('pe', lambda e, g_=g_: e.matmul(psmall[:, 8:16], C.onesf[:], g_, start=True, stop=True), reads=[gk_, 'consts'], writes=['psmall'])
                P.op('dve', lambda e, s_=s_: e.tensor_copy(out=s_[:, 0:8], in_=psmall[:, 0:8]), reads=['psmall'], writes=[sk])
                P.op('dve', lambda e, s_=s_: e.tensor_scalar(out=s_[:, 8:16], in0=psmall[:, 0:8], scalar1=-1.0, scalar2=None, op0=ALU.mult), reads=['psmall'], writes=[sk])
                P.op('act', lambda e, s_=s_: e.activation(out=s_[:, 16:24], in_=psmall[:, 0:8], func=AF.Exp), reads=['psmall'], writes=[sk])
                P.op('act', lambda e, s_=s_: e.activation(out=s_[:, 32:40], in_=psmall[:, 8:16], func=AF.Exp), reads=['psmall'], writes=[sk])
                P.op('dve', lambda e, s_=s_: e.tensor_tensor(out=s_[:, 24:32], in0=psmall[:, 8:16], in1=s_[:, 0:8], op=ALU.subtract), reads=['psmall', sk], writes=[sk])
                P.op('act', lambda e, s_=s_: e.activation(out=s_[:, 24:32], in_=s_[:, 24:32], func=AF.Exp), reads=[sk], writes=[sk])
                P.op('dve', lambda e, s_=s_, be_=be_: e.tensor_scalar(out=s_[:, 40:48], in0=be_, scalar1=-1.0, scalar2=None, op0=ALU.mult), reads=[gk_], writes=[sk])
                def prob(h, i, d=d, tri=tri, g_=g_, be_=be_, s_=s_, sk=sk, gk_=gk_):
                    ch = d * 8 + h
                    kFh = kF[b][d][:, h, :]
                    qFh = qF[b][d][:, h, :]
                    kTh = kT[b][d][:, h, :]
                    vTh = vT[b][d][:, h, :]
                    bk = pf[i]
                    bkk = ('pf', i)
                    pkk, pkk_k = bk[:, 0:256], bkk
                    P.op('pe', lambda e, pkk=pkk, kFh=kFh: e.matmul(pkk[:, 0:128], kFh, kFh, start=True, stop=True), reads=[('kF', b, d)], writes=[pkk_k], inc=False)
                    P.op('pe', lambda e, pkk=pkk, kFh=kFh, qFh=qFh: e.matmul(pkk[:, 128:256], kFh, qFh, start=True, stop=True), reads=[('kF', b, d), ('qF', b, d)], writes=[pkk_k])
                    P.op('act', lambda e, i=i, tri=tri, g_=g_, h=h: e.activation(out=TG[i][:], in_=tri[:], func=AF.Copy, scale=g_[:, h:h + 1]),
                         reads=[gk_, 'consts'], writes=[('TG', i)])
                    pb, pb_k = bk[:, 256:384], bkk
                    P.op('pe', lambda e, pb=pb, i=i: e.matmul(pb[:, 0:128], C.onesf[:], TG[i][:], start=True, stop=False), reads=[('TG', i), 'consts'], writes=[pb_k], inc=False)
                    P.op('pe', lambda e, pb=pb, d=d: e.matmul(pb[:, 0:128], C.idb[:], C.negIb[d][:], start=False, stop=True), reads=['consts', 'idb'], writes=[pb_k])
                    yield
                    P.op('act', lambda e, i=i, pb=pb, s_=s_, h=h: e.activation(out=Dm[i][:], in_=pb[:, 0:128], func=AF.Exp, bias=s_[:, 8 + h:9 + h], scale=1.0),
                         reads=[pb_k, sk], writes=[('D', i)])
                    P.op('dve', lambda e, i=i, pkk=pkk, b=b, ch=ch: e.tensor_tensor(out=aT[b][:, ch, :], in0=pkk[:, 128:256], in1=Dm[i][:], op=ALU.mult),
                         reads=[pkk_k, ('D', i)], writes=[('aT', b, ch)])
                    P.op('dve', lambda e, i=i, pkk=pkk: e.tensor_tensor(out=tm_[i][:], in0=pkk[:, 0:128], in1=Dm[i][:], op=ALU.mult),
                         reads=[pkk_k, ('D', i)], writes=[('tm', i)])
                    A_, Bm_ = fA[i], fB[i]
                    Ak, Bk = ('fA', i), ('fB', i)
                    P.op('dve', lambda e, i=i, A_=A_, be_=be_, h=h: e.scalar_tensor_tensor(out=A_[:], in0=tm_[i][:], scalar=be_[:, h:h + 1], in1=C.offd[:], op0=ALU.mult, op1=ALU.mult),
                         reads=[('tm', i), gk_, 'consts'], writes=[Ak])
                    P.op('pe', lambda e, bk=bk, A_=A_: e.transpose(out=bk[:, 384:448].bitcast(BF16), in_=A_[:], identity=C.idb[:]), reads=[Ak, 'idb'], writes=[bkk])
                    yield
                    P.op('act', lambda e, bk=bk, Bm_=Bm_: e.activation(out=Bm_[:], in_=bk[:, 384:448].bitcast(BF16), func=AF.Copy), reads=[bkk], writes=[Bk])
                    X1, Y1 = fX1[i], fY1[i]
                    P.op('pool', lambda e, X1=X1, A_=A_: e.tensor_tensor(out=X1[:], in0=A_[:], in1=C.bd16[:], op=ALU.mult), reads=[Ak, 'consts'], writes=[('fX1', i)])
                    P.op('pool', lambda e, Y1=Y1, Bm_=Bm_: e.tensor_tensor(out=Y1[:], in0=Bm_[:], in1=C.bd16[:], op=ALU.mult), reads=[Bk, 'consts'], writes=[('fY1', i)])
                    Pp = fP[i]
                    P.op('pool', lambda e, Pp=Pp, X1=X1: e.tensor_tensor(out=Pp[0][:], in0=C.idb[:], in1=X1[:], op=ALU.subtract), reads=[('fX1', i), 'idb'], writes=[('fP', i, 0)])
                    yield
                    XY = fXY[i]
                    Yc, Xc, yk_ = Y1[:], X1[:], [('fX1', i), ('fY1', i)]
                    for lv in range(1, 4):
                        last = (lv == 3)
                        P.op('pe', lambda e, bk=bk, Yc=Yc, Xc=Xc: e.matmul(bk[:, 0:128], Xc, Yc, start=True, stop=True), reads=yk_, writes=[bkk], inc=last)
                        if not last:
                            P.op('pe', lambda e, bk=bk, Yc=Yc, Xc=Xc: e.matmul(bk[:, 128:256], Yc, Xc, start=True, stop=True), reads=yk_, writes=[bkk])
                        yield
                        dst = XY[lv - 1]
                        wc = 128 if last else 256
                        P.op('act', lambda e, bk=bk, dst=dst, wc=wc: e.activation(out=dst[:, 0:wc], in_=bk[:, 0:wc], func=AF.Copy), reads=[bkk], writes=[('fXY', i, lv)])
                        Yc, Xc, yk_ = dst[:, 0:128], dst[:, 128:256], [('fXY', i, lv)]
                        src, dstp = Pp[(lv - 1) % 2], Pp[lv % 2]
                        P.op('pe', lambda e, bk=bk, src=src, Yc=Yc: e.matmul(bk[:, 256:384], Yc, src[:], start=True, stop=True), reads=[('fP', i, (lv - 1) % 2), ('fXY', i, lv)], writes=[bkk])
                        yield
                        if not last:
                            P.op('dve', lambda e, bk=bk, dstp=dstp, src=src: e.tensor_tensor(out=dstp[:], in0=src[:], in1=bk[:, 256:384], op=ALU.add), reads=[bkk, ('fP', i, (lv - 1) % 2)], writes=[('fP', i, lv % 2)])
                        else:
                            P.op('dve', lambda e, bk=bk, i=i, src=src: e.tensor_tensor(out=bTT[i][0][:], in0=src[:], in1=bk[:, 256:384], op=ALU.add), reads=[bkk, ('fP', i, (lv - 1) % 2)], writes=[('bTT', i, 0)])
                        yield
                    TTk, TTkk = bTT[i][0], ('bTT', i, 0)
                    Tk, Tkk = bT[i][0], ('bT', i, 0)
                    tpb = bk[:, 384:448].bitcast(BF16)
                    P.op('pe', lambda e, tpb=tpb, TTk=TTk: e.transpose(out=tpb, in_=TTk[:], identity=C.idb[:]), reads=[TTkk, 'idb'], writes=[bkk])
                    P.op('act', lambda e, tpb=tpb, Tk=Tk: e.activation(out=Tk[:], in_=tpb, func=AF.Copy), reads=[bkk], writes=[Tkk])
                    yield
                    TT = fTTb[i]
                    TTk_ = ('fTTb', i)
                    for ci, mk in enumerate(C.mlev):
                        lastc = (ci == 2)
                        Ck, CTk = bC[i][0], bC[i][1]
                        P.op('pool', lambda e, Ck=Ck, Bm_=Bm_, mk=mk: e.tensor_tensor(out=Ck[:], in0=Bm_[:], in1=mk[:], op=ALU.mult), reads=[Bk, 'consts'], writes=[('bC', i, 0)])
                        P.op('pe', lambda e, bk=bk, Ck=Ck, TTk=TTk: e.matmul(bk[:, 0:128], Ck[:], TTk[:], start=True, stop=True), reads=[('bC', i, 0), TTkk], writes=[bkk], inc=lastc)
                        if not lastc:
                            P.op('pool', lambda e, CTk=CTk, A_=A_, mk=mk: e.tensor_tensor(out=CTk[:], in0=A_[:], in1=mk[:], op=ALU.mult), reads=[Ak, 'consts'], writes=[('bC', i, 1)])
                            P.op('pe', lambda e, bk=bk, CTk=CTk, Tk=Tk: e.matmul(bk[:, 128:256], CTk[:], Tk[:], start=True, stop=True), reads=[('bC', i, 1), Tkk], writes=[bkk])
                            yield
                        NR = bNR[i]
                        wc = 128 if lastc else 256
                        P.op('act', lambda e, bk=bk, NR=NR, wc=wc: e.activation(out=NR[:, 0:wc], in_=bk[:, 0:wc], func=AF.Copy), reads=[bkk], writes=[('bNR', i)])
                        P.op('pe', lambda e, bk=bk, NR=NR, Tk=Tk: e.matmul(bk[:, 256:384], Tk[:], NR[:, 0:128], start=True, stop=True), reads=[('bNR', i), Tkk], writes=[bkk], inc=lastc)
                        if lastc:
                            yield
                        if not lastc:
                            P.op('pe', lambda e, bk=bk, NR=NR, TTk=TTk: e.matmul(bk[:, 384:512], TTk[:], NR[:, 128:256], start=True, stop=True), reads=[('bNR', i), TTkk], writes=[bkk])
                            yield
                            nTT, nT = bTT[i][(ci + 1) % 2], bT[i][(ci + 1) % 2]
                            nTTk, nTk = ('bTT', i, (ci + 1) % 2), ('bT', i, (ci + 1) % 2)
                            P.op('dve', lambda e, bk=bk, nTT=nTT, TTk=TTk: e.tensor_tensor(out=nTT[:], in0=TTk[:], in1=bk[:, 256:384], op=ALU.subtract), reads=[TTkk, bkk], writes=[nTTk])
                            P.op('dve', lambda e, bk=bk, nT=nT, Tk=Tk: e.tensor_tensor(out=nT[:], in0=Tk[:], in1=bk[:, 384:512], op=ALU.subtract), reads=[Tkk, bkk], writes=[nTk])
                            yield
                            TTk, TTkk, Tk, Tkk = nTT, nTTk, nT, nTk
                        else:
                            P.op('dve', lambda e, bk=bk, TT=TT, TTk=TTk: e.tensor_tensor(out=TT[:], in0=TTk[:], in1=bk[:, 256:384], op=ALU.subtract), reads=[TTkk, bkk], writes=[TTk_])
                    TTk = TTk_
                    P.op('act', lambda e, i=i, kTh=kTh, s_=s_, h=h: e.activation(out=ktl[i][:], in_=kTh, func=AF.Copy, scale=s_[:, 16 + h:17 + h]),
                         reads=[('kT', b, d), sk], writes=[('ktl', i)])
                    P.op('act', lambda e, kTh=kTh, s_=s_, h=h, b=b, ch=ch: e.activation(out=kd[b][:, ch, :], in_=kTh, func=AF.Copy, scale=s_[:, 24 + h:25 + h]),
                         reads=[('kT', b, d), sk], writes=[('kd', b, ch)])
                    pu, pu_k = bk[:, 0:256], bkk
                    P.op('pe', lambda e, pu=pu, TT=TT, vTh=vTh: e.matmul(pu[:, 0:128], TT[:], vTh, start=True, stop=True), reads=[TTk, ('vT', b, d)], writes=[pu_k], inc=False)
                    P.op('pe', lambda e, pu=pu, TT=TT, i=i: e.matmul(pu[:, 128:256], ktl[i][:], TT[:], start=True, stop=True), reads=[TTk, ('ktl', i)], writes=[pu_k])
                    yield
                    P.op('act', lambda e, pu=pu, be_=be_, h=h, b=b, ch=ch: e.activation(out=uu[b][:, ch, :], in_=pu[:, 0:128], func=AF.Copy, scale=be_[:, h:h + 1]),
                         reads=[pu_k, gk_], writes=[('uu', b, ch)])
                    P.op('dve', lambda e, pu=pu, b=b, ch=ch: e.tensor_copy(out=wT[b][:, ch, :], in_=pu[:, 128:256]), reads=[pu_k], writes=[('wT', b, ch)])
                for g0 in range(0, NH, NS):
                    run_rr([prob(g0 + k, k) for k in range(NS)])
            for d in range(2):
                s_ = stt[b][d]
                sk = ('stt', b, d)
                c = n if d == 0 else N - 1 - n
                def chain(h, vi, d=d, s_=s_, sk=sk):
                    ch = d * 8 + h
                    qFh = qF[b][d][:, h, :]
                    sbk = 3 + vi
                    p1, p1_k = pf[sbk][:, 0:256], ('pf', sbk)
                    P.op('pe', lambda e, p1=p1, b=b, ch=ch: e.matmul(p1[:, 0:128], wT[b][:, ch, :], Sb_[:, ch, :], start=True, stop=True),
                         reads=[('wT', b, ch), ('Sb', ch)], writes=[p1_k], inc=False)
                    P.op('pe', lambda e, p1=p1, qFh=qFh, ch=ch: e.matmul(p1[:, 128:256], qFh, Sb_[:, ch, :], start=True, stop=True),
                         reads=[('qF', b, d), ('Sb', ch)], writes=[p1_k])
                    yield
                    P.op('dve', lambda e, p1=p1, s_=s_, h=h, b=b, ch=ch, vi=vi: e.scalar_tensor_tensor(out=vn[vi][:], in0=p1[:, 0:128], scalar=s_[:, 40 + h:41 + h], in1=uu[b][:, ch, :], op0=ALU.mult, op1=ALU.add),
                         reads=[p1_k, sk, ('uu', b, ch)], writes=[('vn', vi)])
                    yield
                    p2, p2_k = pf[sbk][:, 256:512], ('pf', sbk)
                    P.op('pe', lambda e, p2=p2, b=b, ch=ch, vi=vi: e.matmul(p2[:, 0:128], aT[b][:, ch, :], vn[vi][:], start=True, stop=True),
                         reads=[('aT', b, ch), ('vn', vi)], writes=[p2_k], inc=False)
                    P.op('pe', lambda e, p2=p2, b=b, ch=ch, vi=vi: e.matmul(p2[:, 128:256], kd[b][:, ch, :], vn[vi][:], start=True, stop=True),
                         reads=[('kd', b, ch), ('vn', vi)], writes=[p2_k])
                    yield
                    P.op('act', lambda e, p2=p2, vi=vi: e.activation(out=t2[vi][:], in_=p2[:, 0:128], func=AF.Copy), reads=[p2_k], writes=[('t2', vi)])
                    P.op('dve', lambda e, p1=p1, s_=s_, h=h, b=b, d=d, vi=vi: e.scalar_tensor_tensor(out=ob[b][d][:, h, :], in0=p1[:, 128:256], scalar=s_[:, 16 + h:17 + h], in1=t2[vi][:], op0=ALU.mult, op1=ALU.add),
                         reads=[p1_k, sk, ('t2', vi)], writes=[('ob', b, d)])
                    P.op('dve', lambda e, p2=p2, s_=s_, h=h, ch=ch: e.scalar_tensor_tensor(out=Sf[:, ch, :], in0=Sf[:, ch, :], scalar=s_[:, 32 + h:33 + h], in1=p2[:, 128:256], op0=ALU.mult, op1=ALU.add),
                         reads=[p2_k, sk, ('Sf', ch)], writes=[('Sf', ch)])
                    P.op('pool', lambda e, ch=ch: e.tensor_copy(out=Sb_[:, ch, :], in_=Sf[:, ch, :]), reads=[('Sf', ch)], writes=[('Sb', ch)])
                for g0 in range(0, NH, 4):
                    run_rr([chain(g0 + k, k) for k in range(4)])
                P.dma('pool', C.od_d[d, sl(c, 128), :].rearrange("p (h f) -> p h f", h=NH), ob[b][d][:], reads=[('ob', b, d)], writes=[('od_d', d, c)])
    P.barrier()


def phase_mla_prep(C, P, S, l, pos0=0):
    nc = C.nc
    with ExitStack() as st:
        u_ = C.newuid()
        sb = lambda n, s, d: st.enter_context(nc.sbuf_tensor(n + u_, s, d))
        psm = lambda n, s, d: st.enter_context(nc.psum_tensor(n + u_, s, d))
        wst = sb("m_wst", [128, 4, 2048], F32)
        wq = sb("m_wq", [128, 4, 2048], BF16)
        wkv = sb("m_wkv", [128, 2, 2048], BF16)
        gq = sb("m_gq", [128, 4], F32)
        gkv = sb("m_gkv", [128, 2], F32)
        cq = [sb("m_cq%d" % i, [128, 4, 512], BF16) for i in range(2)]
        ckv = [sb("m_ckv%d" % i, [128, 2, 512], BF16) for i in range(2)]
        kpe = [sb("m_kpe%d" % i, [64, 2, 512], BF16) for i in range(2)]
        sqq = sb("m_sqq", [128, 4, 512], BF16)
        sqk = sb("m_sqk", [128, 2, 512], BF16)
        rq = sb("m_rq", [128, 512], F32)
        rkv = sb("m_rkv", [128, 512], F32)
        rkc = sb("m_rkc", [128, 4], F32)
        cs = [sb("m_cs%d" % i, [64, 2, 512], F32) for i in range(2)]
        csr = sb("m_csr", [64, 2, 512], F32)
        t1 = [sb("m_t1%d" % i, [64, 512], F32) for i in range(2)]
        t2 = [sb("m_t2%d" % i, [64, 512], F32) for i in range(2)]
        oq = [sb("m_oq%d" % i, [128, 512], BF16) for i in range(3)]
        ope = [sb("m_ope%d" % i, [64, 512], BF16) for i in range(3)]
        ov = [sb("m_ov%d" % i, [128, 4, 1024], BF16) for i in range(2)]
        pm = [psm("m_pm%d" % i, [128, 512], F32) for i in range(4)]
        pp = [psm("m_pp%d" % i, [64, 512], F32) for i in range(2)]
        pcol = psm("m_pcol", [128, 512], F32)[:, 0:4]
        pv = psm("m_pv", [128, 512], F32)
        P.dma('sp', gq[:], C.gq_r[l], writes=['gq'])
        P.dma('sp', gkv[:], C.gkv_r[l], writes=['gkv'])
        P.dma('sp', wst[:], C.wuq_r[l].rearrange("(c p) n -> p c n", p=128), writes=['wst'])
        for c in range(4):
            P.op('dve' if c % 2 else 'pool', lambda e, c=c: e.tensor_scalar(out=wq[:, c, :], in0=wst[:, c, :], scalar1=gq[:, c:c + 1], scalar2=None, op0=ALU.mult),
                 reads=['wst', 'gq'], writes=['wq'])
        P.dma('sp', wst[:, 0:2, :], C.wukv_r[l].rearrange("(c p) n -> p c n", p=128), reads=[], writes=['wst'])
        for c in range(2):
            P.op('dve' if c % 2 else 'pool', lambda e, c=c: e.tensor_scalar(out=wkv[:, c, :], in0=wst[:, c, :], scalar1=gkv[:, c:c + 1], scalar2=None, op0=ALU.mult),
                 reads=['wst', 'gkv'], writes=['wkv'])
        ntt = S // 512
        oc = 0
        for tt in range(ntt):
            b2 = tt % 2
            P.dma('sp', cq[b2][:], C.projT_d[R_CQ:R_CQ + 512, sl(tt, 512)].rearrange("(c p) t -> p c t", p=128), writes=[('cq', b2)])
            P.dma('sp', ckv[b2][:], C.projT_d[R_CKV:R_CKV + 256, sl(tt, 512)].rearrange("(c p) t -> p c t", p=128), writes=[('ckv', b2)])
            P.dma('sp', kpe[b2][:], C.projT_d[R_KPE:R_KPE + 128, sl(tt, 512)].rearrange("(c p) t -> p c t", p=64), writes=[('kpe', b2)])
            P.dma('sp', cs[b2][:], C.rope_d[:, :, pos0 + tt * 512:pos0 + (tt + 1) * 512], writes=[('cs', b2)])
            P.op('pool', lambda e, b2=b2: e.tensor_tensor(out=sqq[:], in0=cq[b2][:], in1=cq[b2][:], op=ALU.mult), reads=[('cq', b2)], writes=['sqq'])
            P.op('pool', lambda e, b2=b2: e.tensor_tensor(out=sqk[:], in0=ckv[b2][:], in1=ckv[b2][:], op=ALU.mult), reads=[('ckv', b2)], writes=['sqk'])
            for c in range(4):
                P.op('pe', lambda e, c=c: e.matmul(pm[0][:], C.onesb[:], sqq[:, c, :], start=(c == 0), stop=(c == 3)), reads=['sqq', 'onesb'], writes=[('pm', 0)], inc=(c == 3))
            P.op('act', lambda e: e.activation(out=rq[:], in_=pm[0][:], func=AF.Sqrt, bias=EPS, scale=1.0 / 512), reads=[('pm', 0)], writes=['rq'])
            P.op('dve', lambda e: e.reciprocal(out=rq[:], in_=rq[:]), reads=['rq'], writes=['rq'])
            P.op('dve', lambda e: e.tensor_scalar(out=rq[:], in0=rq[:], scalar1=MLA_SCALE, scalar2=None, op0=ALU.mult), reads=['rq'], writes=['rq'])
            for c in range(2):
                P.op('pe', lambda e, c=c: e.matmul(pm[1][:], C.onesb[:], sqk[:, c, :], start=(c == 0), stop=(c == 1)), reads=['sqk', 'onesb'], writes=[('pm', 1)], inc=(c == 1))
            P.op('act', lambda e: e.activation(out=rkv[:], in_=pm[1][:], func=AF.Sqrt, bias=EPS, scale=1.0 / 256), reads=[('pm', 1)], writes=['rkv'])
            P.op('dve', lambda e: e.reciprocal(out=rkv[:], in_=rkv[:]), reads=['rkv'], writes=['rkv'])
            for j in range(4):
                for c in range(2):
                    P.op('pe', lambda e, j=j, c=c: e.matmul(pcol[:, j:j + 1], sqk[:, c, sl(j, 128)], C.onesb[:, 0:1], start=(c == 0), stop=(c == 1)),
                         reads=['sqk', 'onesb'], writes=['pcol'], inc=(c == 1))
            P.op('act', lambda e: e.activation(out=rkc[:], in_=pcol[:], func=AF.Sqrt, bias=EPS, scale=1.0 / 256), reads=['pcol'], writes=['rkc'])
            P.op('dve', lambda e: e.reciprocal(out=rkc[:], in_=rkc[:]), reads=['rkc'], writes=['rkc'])
            for w_ in range(2):
                P.op('pool', lambda e, w_=w_, b2=b2: e.tensor_tensor(out=csr[:, w_, :], in0=cs[b2][:, w_, :], in1=rq[0:64, :], op=ALU.mult), reads=[('cs', b2), 'rq'], writes=['csr'])
            P.op('dve', lambda e, b2=b2: e.tensor_tensor(out=t1[0][:], in0=kpe[b2][:, 0, :], in1=cs[b2][:, 0, :], op=ALU.mult), reads=[('kpe', b2), ('cs', b2)], writes=[('t1', 0)])
            P.op('pool', lambda e, b2=b2: e.tensor_tensor(out=t2[0][:], in0=kpe[b2][:, 1, :], in1=cs[b2][:, 1, :], op=ALU.mult), reads=[('kpe', b2), ('cs', b2)], writes=[('t2', 0)])
            o_ = ope[oc % 3]
            ok = ('ope', oc % 3)
            P.op('dve', lambda e, o_=o_: e.tensor_tensor(out=o_[:], in0=t1[0][:], in1=t2[0][:], op=ALU.add), reads=[('t1', 0), ('t2', 0)], writes=[ok])
            P.dma('pool', C.akpe_d[:, sl(tt, 512)], o_[:], reads=[ok], writes=[('akpe_d', tt)])
            oc += 1
            for h in range(NH):
                pi = (h * 2) % 4
                for c in range(4):
                    P.op('pe', lambda e, c=c, h=h, b2=b2, pi=pi: e.matmul(pm[pi][:], wq[:, c, h * 256:h * 256 + 128], cq[b2][:, c, :], start=(c == 0), stop=(c == 3)),
                         reads=['wq', ('cq', b2)], writes=[('pm', pi)], inc=(c == 3))
                o_ = oq[oc % 3]
                ok = ('oq', oc % 3)
                P.op('dve', lambda e, o_=o_, pi=pi: e.tensor_tensor(out=o_[:], in0=pm[pi][:], in1=rq[:], op=ALU.mult), reads=[('pm', pi), 'rq'], writes=[ok])
                P.dma('pool', C.aq_d[h, 0, :, sl(tt, 512)], o_[:], reads=[ok], writes=[('aq_d', h, 0, tt)])
                for w_ in range(2):
                    for c in range(4):
                        P.op('pe', lambda e, c=c, h=h, b2=b2, w_=w_: e.matmul(pp[w_][:], wq[:, c, h * 256 + 128 + w_ * 64:h * 256 + 192 + w_ * 64], cq[b2][:, c, :], start=(c == 0), stop=(c == 3)),
                             reads=['wq', ('cq', b2)], writes=[('pp', w_)], inc=(c == 3))
                P.op('dve', lambda e: e.tensor_tensor(out=t1[1][:], in0=pp[0][:], in1=csr[:, 0, :], op=ALU.mult), reads=[('pp', 0), 'csr'], writes=[('t1', 1)])
                P.op('dve', lambda e: e.tensor_tensor(out=t2[1][:], in0=pp[1][:], in1=csr[:, 1, :], op=ALU.mult), reads=[('pp', 1), 'csr'], writes=[('t2', 1)])
                o2 = ope[oc % 3]
                ok2 = ('ope', oc % 3)
                P.op('pool', lambda e, o2=o2: e.tensor_tensor(out=o2[:], in0=t1[1][:], in1=t2[1][:], op=ALU.add), reads=[('t1', 1), ('t2', 1)], writes=[ok2])
                P.dma('pool', C.aq_d[h, 1, 0:64, sl(tt, 512)], o2[:], reads=[ok2], writes=[('aq_d', h, 1, tt)])
                oc += 1
                pi = (h * 2 + 1) % 4
                for c in range(2):
                    P.op('pe', lambda e, c=c, h=h, b2=b2, pi=pi: e.matmul(pm[pi][:], wkv[:, c, h * 256:h * 256 + 128], ckv[b2][:, c, :], start=(c == 0), stop=(c == 1)),
                         reads=['wkv', ('ckv', b2)], writes=[('pm', pi)], inc=(c == 1))
                o_ = oq[oc % 3]
                ok = ('oq', oc % 3)
                P.op('dve', lambda e, o_=o_, pi=pi: e.tensor_tensor(out=o_[:], in0=pm[pi][:], in1=rkv[:], op=ALU.mult), reads=[('pm', pi), 'rkv'], writes=[ok])
                P.dma('pool', C.ak_d[h, :, sl(tt, 512)], o_[:], reads=[ok], writes=[('ak_d', h, tt)])
                oc += 1
            ovb = ov[b2]
            for j in range(4):
                for gp in range(2):
                    for c in range(2):
                        rhs = wkv[:, c, :].rearrange("p (h w) -> p h w", h=NH)[:, gp * 4:gp * 4 + 4, 128:256]
                        P.op('pe', lambda e, j=j, c=c, b2=b2, rhs=rhs: e.matmul(pv[:].rearrange("p (h w) -> p h w", h=4), ckv[b2][:, c, sl(j, 128)], rhs, start=(c == 0), stop=(c == 1)),
                             reads=['wkv', ('ckv', b2)], writes=['pv'], inc=(c == 1))
                    P.op('act', lambda e, j=j, gp=gp, ovb=ovb: e.activation(out=ovb[:, j, gp * 512:(gp + 1) * 512], in_=pv[:], func=AF.Copy, scale=rkc[:, j:j + 1]),
                         reads=['pv', 'rkc'], writes=[('ov', b2)])
            P.dma('pool', C.av_d[sl(tt, 512), :].rearrange("(j p) n -> p j n", p=128), ovb[:], reads=[('ov', b2)], writes=[('av_d', tt)])
    P.barrier()


def phase_attn(C, P, S, l):
    nc = C.nc
    with ExitStack() as st:
        u_ = C.newuid()
        sb = lambda n, s, d: st.enter_context(nc.sbuf_tensor(n + u_, s, d))
        psm = lambda n, s, d: st.enter_context(nc.psum_tensor(n + u_, s, d))
        NK = S // 128
        kpe = sb("a_kpe", [128, S], BF16)
        kn = sb("a_kn", [128, S], BF16)
        vv = sb("a_vv", [128, NK, 128], BF16)
        qn = [sb("a_qn%d" % i, [128, 512], BF16) for i in range(2)]
        qp = [sb("a_qp%d" % i, [128, 512], BF16) for i in range(2)]
        pT = [sb("a_pT%d" % i, [128, 512], BF16) for i in range(4)]
        acc = [[sb("a_acc%d%d" % (i, k), [128, 512], F32) for k in range(2)] for i in range(2)]
        rs = sb("a_rs", [128, 512], F32)
        oo = [sb("a_oo%d" % i, [128, 512], BF16) for i in range(2)]
        psT = [psm("a_ps%d" % i, [128, 512], F32) for i in range(4)]
        po = [psm("a_po%d" % i, [128, 512], F32) for i in range(2)]
        pl = [psm("a_pl%d" % i, [128, 512], F32) for i in range(2)]
        P.op('pool', lambda e: e.memset(kpe[64:128, :], 0.0), writes=['kpe'])
        for i in range(2):
            P.op('pool', lambda e, i=i: e.memset(qp[i][64:128, :], 0.0), writes=[('qp', i)])
        P.dma('sp', kpe[0:64, :], C.akpe_d[:, :], writes=['kpe'])
        it = 0
        ic = 0
        for h in range(NH):
            P.dma('sp', kn[:], C.ak_d[h], writes=['kn'])
            for v0 in range(0, NK, 8):
                v1 = min(v0 + 8, NK)
                P.dma('sp', vv[:, v0:v1, :], C.av_d[v0 * 128:v1 * 128, sl(h, 128)].rearrange("(t p) f -> p t f", p=128), writes=['vv'])
            for qt in range(S // 512):
                b2 = it % 2
                P.dma('sp', qn[b2][:], C.aq_d[h, 0, :, sl(qt, 512)], writes=[('qn', b2)])
                P.dma('sp', qp[b2][0:64, :], C.aq_d[h, 1, 0:64, sl(qt, 512)], writes=[('qp', b2)])
                def qk_exp(kt, si, b2=b2):
                    P.op('pe', lambda e: e.matmul(psT[si][:], kn[:, sl(kt, 128)], qn[b2][:], start=True, stop=False),
                         reads=['kn', ('qn', b2)], writes=[('psT', si)], inc=False)
                    P.op('pe', lambda e: e.matmul(psT[si][:], kpe[:, sl(kt, 128)], qp[b2][:], start=False, stop=True),
                         reads=['kpe', ('qp', b2)], writes=[('psT', si)])
                    P.op('act', lambda e: e.activation(out=pT[si][:], in_=psT[si][:], func=AF.Exp), reads=[('psT', si)], writes=[('pT', si)])

                def pv_acc(kt, si, b2=b2):
                    P.op('pe', lambda e: e.matmul(po[b2][:], vv[:, kt, :], pT[si][:], start=(kt == 0), stop=(kt == NK - 1)),
                         reads=['vv', ('pT', si)], writes=[('po', b2)])
                    ae = 'dve' if kt % 2 == 0 else 'pool'
                    ab = acc[b2][kt % 2]
                    ak_ = ('acc', b2, kt % 2)
                    if kt < 2:
                        P.op(ae, lambda e: e.tensor_copy(out=ab[:], in_=pT[si][:]), reads=[('pT', si)], writes=[ak_])
                    else:
                        P.op(ae, lambda e: e.tensor_tensor(out=ab[:], in0=ab[:], in1=pT[si][:], op=ALU.add), reads=[('pT', si), ak_], writes=[ak_])
                prev = None
                for kt in range(NK):
                    si = ic % 4
                    ic += 1
                    qk_exp(kt, si)
                    if prev is not None:
                        pv_acc(*prev)
                    prev = (kt, si)
                pv_acc(*prev)
                P.op('pe', lambda e, b2=b2: e.matmul(pl[b2][:], C.onesf[:], acc[b2][0][:], start=True, stop=False), reads=['consts', ('acc', b2, 0)], writes=[('pl', b2)], inc=False)
                P.op('pe', lambda e, b2=b2: e.matmul(pl[b2][:], C.onesf[:], acc[b2][1][:], start=False, stop=True), reads=['consts', ('acc', b2, 1)], writes=[('pl', b2)])
                P.op('dve', lambda e, b2=b2: e.reciprocal(out=rs[:], in_=pl[b2][:]), reads=[('pl', b2)], writes=['rs'])
                P.op('dve', lambda e, b2=b2: e.tensor_tensor(out=oo[b2][:], in0=po[b2][:], in1=rs[:], op=ALU.mult), reads=[('po', b2), 'rs'], writes=[('oo', b2)])
                P.dma('pool', C.ao_d[h, :, sl(qt, 512)], oo[b2][:], reads=[('oo', b2)], writes=[('ao_d', h, qt)])
                it += 1
    P.barrier()


def phase_out(C, P, S, l, x_d, xo_d):
    nc = C.nc
    with ExitStack() as st:
        u_ = C.newuid()
        sb = lambda n, s, d: st.enter_context(nc.sbuf_tensor(n + u_, s, d))
        psm = lambda n, s, d: st.enter_context(nc.psum_tensor(n + u_, s, d))
        wo = sb("o_wo", [128, 16, D], BF16)
        gpb = sb("o_gpb", [128, D], F32)
        gng = sb("o_gng", [128, 1], F32)
        za = [sb("o_za%d" % i, [128, 16, 128], BF16) for i in range(2)]
        sz = [sb("o_sz%d" % i, [128, 16, 128], F32) for i in range(2)]
        of_ = [sb("o_of%d" % i, [128, D // 2], F32) for i in range(2)]
        ob_ = [sb("o_ob%d" % i, [128, D // 2], F32) for i in range(2)]
        junk = sb("o_junk", [128, 512], F32)
        st8 = [sb("o_st%d" % i, [128, 32], F32) for i in range(2)]
        on = [sb("o_on%d" % i, [128, NH, 128], BF16) for i in range(2)]
        ao = [sb("o_ao%d" % i, [128, NH, 128], BF16) for i in range(2)]
        mix = [sb("o_mix%d" % i, [128, 16, 128], BF16) for i in range(2)]
        xt = [sb("o_xt%d" % i, [128, D], F32) for i in range(2)]
        yt = [sb("o_yt%d" % i, [128, D], F32) for i in range(2)]
        ptr = psm("o_ptr", [128, NH, 128], BF16)
        py = [psm("o_py%d" % i, [128, 512], F32) for i in range(4)]
        P.dma('pool', wo[:], C.w_out[l].rearrange("(c p) n -> p c n", p=128), writes=['wo'])
        P.dma('sp', gpb[:], C.post_g[l:l + 1, :].partition_broadcast(128), writes=['gpb'])
        P.dma('sp', gng[:], C.gng_r[l], writes=['gng'])
        def stageA(t):
            b2 = t % 2
            s8 = st8[b2]
            P.dma('sp', za[b2][:, 0:8, :], C.projT_d[R_ZA:R_ZA + 1024, sl(t, 128)].rearrange("(c p) t -> p c t", p=128), writes=[('za', b2)])
            P.dma('sp', za[b2][:, 8:16, :], C.projT_d[R_ZB:R_ZB + 1024, sl(t, 128)].rearrange("(c p) t -> p c t", p=128), writes=[('za', b2)])
            P.dma('sp', of_[b2][:], C.od_d[0, sl(t, 128), :], writes=[('of', b2)])
            P.dma('sp', ob_[b2][:], C.od_d[1, sl(t, 128), :], writes=[('ob', b2)])
            P.dma('sp', ao[b2][:], C.ao_d[:, :, sl(t, 128)].rearrange("h p t -> p h t"), writes=[('ao', b2)])
            P.dma('sp', xt[b2][:], x_d[sl(t, 128), :], writes=[('xt', b2)])
            P.op('act', lambda e, b2=b2: e.activation(out=sz[b2][:], in_=za[b2][:], func=AF.Silu), reads=[('za', b2)], writes=[('sz', b2)])
            P.op('pool', lambda e, b2=b2: e.tensor_tensor(out=of_[b2][:], in0=of_[b2][:], in1=ob_[b2][:], op=ALU.add), reads=[('of', b2), ('ob', b2)], writes=[('of', b2)])
            s8 = st8[b2]
            for h in range(NH):
                P.op('act', lambda e, b2=b2, h=h, s8=s8: e.activation(out=junk[:, 0:128], in_=of_[b2][:, sl(h, 128)], func=AF.Square, accum_out=s8[:, h:h + 1]),
                     reads=[('of', b2)], writes=['junk', ('s8', b2, h)])
            P.op('act', lambda e, s8=s8: e.activation(out=s8[:, 8:16], in_=s8[:, 0:8], func=AF.Sqrt, bias=EPS, scale=1.0 / 128), reads=[('s8', b2, h) for h in range(NH)], writes=[('s8r', b2)])
            P.op('dve', lambda e, s8=s8: e.reciprocal(out=s8[:, 16:24], in_=s8[:, 8:16]), reads=[('s8r', b2)], writes=[('s8r', b2)])
            for h in range(NH):
                P.op('dve' if h % 2 else 'pool', lambda e, b2=b2, h=h, s8=s8: e.tensor_scalar(out=on[b2][:, h, :], in0=of_[b2][:, sl(h, 128)], scalar1=s8[:, 16 + h:17 + h], scalar2=None, op0=ALU.mult),
                     reads=[('of', b2), ('s8r', b2)], writes=[('on', b2)])
            for h in range(NH):
                P.op('pe', lambda e, b2=b2, h=h: e.transpose(out=ptr[:, h, :], in_=on[b2][:, h, :], identity=C.idb[:]), reads=[('on', b2), 'idb'], writes=['ptr'], inc=(h == NH - 1))
            P.op('dve', lambda e, b2=b2: e.scalar_tensor_tensor(out=mix[b2][:, 0:8, :], in0=ptr[:], scalar=gng[:, 0:1], in1=sz[b2][:, 0:8, :], op0=ALU.mult, op1=ALU.mult),
                 reads=['ptr', 'gng', ('sz', b2)], writes=[('mix', b2)])
            P.op('pool', lambda e, b2=b2: e.tensor_tensor(out=mix[b2][:, 8:16, :], in0=ao[b2][:], in1=sz[b2][:, 8:16, :], op=ALU.mult),
                 reads=[('ao', b2), ('sz', b2)], writes=[('mix', b2)])

        def stageB(t):
            b2 = t % 2
            s8 = st8[b2]
            for nb in range(4):
                for c in range(16):
                    P.op('pe', lambda e, b2=b2, nb=nb, c=c: e.matmul(py[nb][:], mix[b2][:, c, :], wo[:, c, sl(nb, 512)], start=(c == 0), stop=(c == 15)),
                         reads=[('mix', b2), 'wo'], writes=[('py', nb)], inc=(c == 15))
            for nb in range(4):
                P.op('act', lambda e, nb=nb, s8=s8: e.activation(out=junk[:], in_=py[nb][:], func=AF.Square, accum_out=s8[:, 24 + nb:25 + nb]),
                     reads=[('py', nb)], writes=['junk', ('s8y', b2, nb)])
            P.op('dve', lambda e, s8=s8: e.tensor_tensor(out=s8[:, 28:30], in0=s8[:, 24:26], in1=s8[:, 26:28], op=ALU.add), reads=[('s8y', b2, nb) for nb in range(4)], writes=[('s8z', b2)])
            P.op('dve', lambda e, s8=s8: e.tensor_tensor(out=s8[:, 30:31], in0=s8[:, 28:29], in1=s8[:, 29:30], op=ALU.add), reads=[('s8z', b2)], writes=[('s8z', b2)])
            P.op('act', lambda e, s8=s8: e.activation(out=s8[:, 31:32], in_=s8[:, 30:31], func=AF.Sqrt, bias=EPS, scale=1.0 / D), reads=[('s8z', b2)], writes=[('s8w', b2)])
            P.op('dve', lambda e, s8=s8: e.reciprocal(out=s8[:, 31:32], in_=s8[:, 31:32]), reads=[('s8w', b2)], writes=[('s8w', b2)])
            for nb in range(4):
                P.op('dve', lambda e, b2=b2, nb=nb, s8=s8: e.scalar_tensor_tensor(out=yt[b2][:, sl(nb, 512)], in0=py[nb][:], scalar=s8[:, 31:32], in1=gpb[:, sl(nb, 512)], op0=ALU.mult, op1=ALU.mult),
                     reads=[('py', nb), ('s8w', b2), 'gpb'], writes=[('yt', b2)])
            P.op('pool', lambda e, b2=b2: e.tensor_tensor(out=yt[b2][:], in0=yt[b2][:], in1=xt[b2][:], op=ALU.add), reads=[('yt', b2), ('xt', b2)], writes=[('yt', b2)])
            P.dma('pool', xo_d[sl(t, 128), :], yt[b2][:], reads=[('yt', b2)], writes=[('xo', t)])

        NT7 = S // 128
        stageA(0)
        for t in range(NT7):
            if t + 1 < NT7:
                stageA(t + 1)
            stageB(t)
    P.barrier()


def build(seqs, depth, debug=False):
    nc = bass.Bass("TRN2", target_bir_lowering=False)
    C = Ctx()
    C.nc = nc
    dt = lambda n, s, d, k="ExternalInput": nc.dram_tensor(n, s, d, kind=k).ap()
    C.pre_g = dt("pre_g", [depth, D], F32)
    C.post_g = dt("post_g", [depth, D], F32)
    C.w_in_r = dt("w_in_r", [depth, D, NROW + 32], F32)
    C.conv_r = dt("conv_r", [depth, 128, 120], F32)
    C.a_log = dt("a_log", [depth, 16], F32)
    C.dt_bias = dt("dt_bias", [depth, 16], F32)
    C.gng_r = dt("gng_r", [depth, 128, 1], F32)
    C.gq_r = dt("gq_r", [depth, 128, 4], F32)
    C.gkv_r = dt("gkv_r", [depth, 128, 2], F32)
    C.wuq_r = dt("wuq_r", [depth, 512, 2048], F32)
    C.wukv_r = dt("wukv_r", [depth, 256, 2048], F32)
    C.w_out = dt("w_out", [depth, D, D], F32)
    Smax = max(s for _, s in seqs)
    C.rope_d = dt("rope", [64, 2, Smax], F32)
    cst_d = dt("cmats", [128, 11, 128], F32)
    xs, ys = {}, {}
    for name, S in seqs:
        xs[name] = dt("x_" + name, [S, D], F32)
        ys[name] = dt("y_" + name, [S, D], F32, "ExternalOutput")
    kind_s = "ExternalOutput" if debug else "Internal"
    scr = {}
    for name, S in seqs:
        d = {}
        d['hT_d'] = dt("hT_" + name, [16, 128, S], BF16, kind_s)
        d['projT_d'] = dt("projT_" + name, [NROW, S], BF16, kind_s)
        d['gb_d'] = dt("gb_" + name, [S, 32], F32, kind_s)
        d['gq_d'] = dt("gq_" + name, [NH, 128, S], BF16, kind_s)
        d['gk_d'] = dt("gk_" + name, [NH, 128, S], BF16, kind_s)
        d['gkT_d'] = dt("gkT_" + name, [S, 1024], BF16, kind_s)
        d['gv_d'] = dt("gv_" + name, [S, 1024], BF16, kind_s)
        d['od_d'] = dt("od_" + name, [2, S, 1024], F32, kind_s)
        d['aq_d'] = dt("aq_" + name, [NH, 2, 128, S], BF16, kind_s)
        d['ak_d'] = dt("ak_" + name, [NH, 128, S], BF16, kind_s)
        d['akpe_d'] = dt("akpe_" + name, [64, S], BF16, kind_s)
        d['av_d'] = dt("av_" + name, [S, 1024], BF16, kind_s)
        d['ao_d'] = dt("ao_" + name, [NH, 128, S], BF16, kind_s)
        d['xmid'] = [dt("xm%d_%s" % (i, name), [S, D], F32, kind_s) for i in range(depth - 1)]
        scr[name] = d
    with ExitStack() as st:
        P = Prog(nc, st)
        sb = lambda n, s, d: st.enter_context(nc.sbuf_tensor(n, s, d))
        cm = sb("k_cm", [128, 11, 128], F32)
        C.idb = sb("k_idb", [128, 128], BF16)
        C.onesb = sb("k_onesb", [128, 128], BF16)
        P.dma('sp', cm[:], cst_d, writes=['cm'])
        P.op('dve', lambda e: e.tensor_copy(out=C.idb[:], in_=cm[:, 0, :]), reads=['cm'], writes=['idb'])
        P.op('dve', lambda e: e.tensor_copy(out=C.onesb[:], in_=cm[:, 6, :]), reads=['cm'], writes=['onesb'])
        C.negIb = [sb('k_negIb%d' % d_, [128, 128], BF16) for d_ in range(2)]
        for d_ in range(2):
            P.op('dve', lambda e, d_=d_: e.tensor_copy(out=C.negIb[d_][:], in_=cm[:, 3 + d_, :]), reads=['cm'], writes=['negIb'])
        C.idf = cm[:, 0, :]
        C.tri = [cm[:, 1, :], cm[:, 2, :]]
        C.negI = [cm[:, 3, :], cm[:, 4, :]]
        C.offd = cm[:, 5, :]
        C.onesf = cm[:, 6, :]
        C.bd16 = cm[:, 7, :]
        C.mlev = [cm[:, 8, :], cm[:, 9, :], cm[:, 10, :]]
        P.barrier()
        for l in range(depth):
            for name, S in seqs:
                for k, v in scr[name].items():
                    setattr(C, k, v)
                x_in = xs[name] if l == 0 else scr[name]['xmid'][l - 1]
                x_out = ys[name] if l == depth - 1 else scr[name]['xmid'][l]
                if 1 in PHASES: phase_norm(C, P, x_in, S, l)
                if 2 in PHASES: phase_inproj(C, P, S, l)
                if 3 in PHASES: phase_gdn_prep(C, P, S, l)
                if 4 in PHASES: phase_gdn(C, P, S, l)
                if 5 in PHASES: phase_mla_prep(C, P, S, l)
                if 6 in PHASES: phase_attn(C, P, S, l)
                if 7 in PHASES: phase_out(C, P, S, l, x_in, x_out)
        P.emit()
    return nc


def const_mats():
    i = np.arange(128)
    ident = np.eye(128, dtype=np.float32)
    tri_f = (i[:, None] <= i[None, :]).astype(np.float32)
    tri_b = (i[:, None] >= i[None, :]).astype(np.float32)
    negI_f = np.where(i[:, None] <= i[None, :], 0.0, NEG).astype(np.float32)
    negI_b = np.where(i[:, None] >= i[None, :], 0.0, NEG).astype(np.float32)
    offd = (1.0 - ident).astype(np.float32)
    ones = np.ones((128, 128), np.float32)
    bd = lambda n: (i[:, None] // n == i[None, :] // n).astype(np.float32)
    bd16, bd32, bd64 = bd(16), bd(32), bd(64)
    return np.ascontiguousarray(np.stack([ident, tri_f, tri_b, negI_f, negI_b, offd, ones,
                                          bd16, bd32 - bd16, bd64 - bd32, ones - bd64], axis=1))


def rope_table(S):
    pos = np.arange(S, dtype=np.float32)
    inv = (np.float32(10000.0) ** (-np.arange(0, 64, 2, dtype=np.float32) / np.float32(64))).astype(np.float32)
    ang = (pos[None, :] * inv[:, None]).astype(np.float32)
    c, s = np.cos(ang).astype(np.float32), np.sin(ang).astype(np.float32)
    cosf = np.concatenate([c, c], 0)
    sins = np.concatenate([-s, s], 0)
    return np.ascontiguousarray(np.stack([cosf, sins], axis=1))


def layout_weights(pre_norm_g, post_norm_g, w_in, conv_w, gdn_a_log, gdn_dt_bias, gdn_norm_g,
                   mla_q_norm_g, mla_kv_norm_g, mla_w_uq, mla_w_ukv, w_out):
    depth = w_in.shape[0]
    f = lambda a: np.ascontiguousarray(np.asarray(a, dtype=np.float32))
    w_in = f(w_in)
    o_qkv, o_za, o_b, o_a, o_cq, o_ckv, o_kpe, o_zb = 0, 3072, 4096, 4112, 4128, 4640, 4896, 4960
    kpe_idx = np.arange(o_kpe, o_kpe + 64)
    kpe_sw = np.concatenate([kpe_idx[32:], kpe_idx[:32]])
    cols = np.concatenate([np.arange(o_qkv, o_qkv + 3072), np.arange(o_za, o_za + 1024),
                           np.arange(o_cq, o_cq + 512), np.arange(o_ckv, o_ckv + 256),
                           kpe_idx, kpe_sw, np.arange(o_zb, o_zb + 1024),
                           np.arange(o_b, o_b + 16), np.arange(o_a, o_a + 16)])
    w_in_r = np.ascontiguousarray(w_in[:, :, cols])
    conv_r = np.ascontiguousarray(f(conv_w).reshape(depth, 5, 24, 128).transpose(0, 3, 1, 2).reshape(depth, 128, 120))
    wuq = f(mla_w_uq).reshape(depth, 512, NH, 192)
    wuq_r = np.ascontiguousarray(np.concatenate([wuq[..., :128], wuq[..., 128:192], wuq[..., 160:192], wuq[..., 128:160]], axis=-1).reshape(depth, 512, NH * 256))
    wukv_r = f(mla_w_ukv)
    return {
        "pre_g": f(pre_norm_g), "post_g": f(post_norm_g), "w_in_r": w_in_r, "conv_r": conv_r,
        "a_log": f(gdn_a_log).reshape(depth, 16), "dt_bias": f(gdn_dt_bias).reshape(depth, 16),
        "gng_r": f(gdn_norm_g).reshape(depth, 128, 1),
        "gq_r": np.ascontiguousarray(f(mla_q_norm_g).reshape(depth, 4, 128).transpose(0, 2, 1)),
        "gkv_r": np.ascontiguousarray(f(mla_kv_norm_g).reshape(depth, 2, 128).transpose(0, 2, 1)),
        "wuq_r": wuq_r, "wukv_r": wukv_r, "w_out": f(w_out),
    }


def kernel(x_prompt, x_sample, pre_norm_g, post_norm_g, w_in, conv_w, gdn_a_log, gdn_dt_bias, gdn_norm_g,
           mla_q_norm_g, mla_kv_norm_g, mla_w_uq, mla_w_ukv, w_out):
    x_prompt = np.asarray(x_prompt, dtype=np.float32)
    x_sample = np.asarray(x_sample, dtype=np.float32)
    depth = np.asarray(w_in).shape[0]
    Sp, Ss = x_prompt.shape[1], x_sample.shape[1]
    seqs = [("s", Ss), ("p", Sp)]
    nc = build(seqs, depth)
    base = layout_weights(pre_norm_g, post_norm_g, w_in, conv_w, gdn_a_log, gdn_dt_bias, gdn_norm_g,
                          mla_q_norm_g, mla_kv_norm_g, mla_w_uq, mla_w_ukv, w_out)
    base["rope"] = rope_table(max(Sp, Ss))
    base["cmats"] = const_mats()
    ncores = x_sample.shape[0]
    in_maps = []
    for c in range(ncores):
        m = dict(base)
        m["x_s"] = np.ascontiguousarray(x_sample[c])
        m["x_p"] = np.ascontiguousarray(x_prompt[0])
        in_maps.append(m)
    res = run_bass_kernel_spmd(nc, in_maps, core_ids=list(range(ncores)))
    y_sample = np.stack([np.asarray(res.results[c]["y_s"], dtype=np.float32) for c in range(ncores)], axis=0)
    y_prompt = np.asarray(res.results[0]["y_p"], dtype=np.float32)[None]
    return (y_prompt, y_sample)
```

```python
import numpy as np
import ml_dtypes
from contextlib import ExitStack
import concourse.bass as bass
import concourse.mybir as mybir
from concourse.bass_utils import run_bass_kernel_spmd

F32 = mybir.dt.float32
BF16 = mybir.dt.bfloat16
AF = mybir.ActivationFunctionType
ALU = mybir.AluOpType

D = 2048
NH = 8
EPS = 1e-6
NROW = 6016
R_QKV, R_ZA, R_CQ, R_CKV, R_KPE, R_ZB = 0, 3072, 4096, 4608, 4864, 4992
MLA_SCALE = 192 ** -0.5
GDN_SCALE = 128 ** -0.5
NEG = -30000.0
PHASES = (1, 2, 3, 4, 5, 6, 7)
DBG3 = (3, 8)


class Prog:
    ENG = ('pe', 'act', 'dve', 'pool', 'sp')
    NL = 8

    def __init__(self, nc, stack):
        self.nc = nc
        self.q = {e: [] for e in self.ENG}
        self.sem = {e: stack.enter_context(nc.semaphore('s_' + e)) for e in self.ENG}
        self.cnt = {e: 0 for e in self.ENG}
        self.dq = ('sp', 'pool')
        self.dsem = {q: [stack.enter_context(nc.semaphore('d_%s%d' % (q, i))) for i in range(self.NL)]
                     for q in self.dq}
        self.dn = {q: 0 for q in self.dq}
        self.seen = {e: {} for e in self.ENG}
        self.lastw = {}
        self.readers = {}

    def _semh(self, key):
        return self.sem[key[1]] if key[0] == 'e' else self.dsem[key[1]][key[2]]

    def _collect(self, eng, reads, writes, is_dma):
        deps = {}

        def add(d, raw):
            if d is None:
                return
            key, val, src = d
            if src == eng and key[0] == 'e' and not is_dma:
                if eng == 'pe' or not raw:
                    return
            if self.seen[eng].get(key, 0) >= val:
                return
            if deps.get(key, 0) < val:
                deps[key] = val
        for t in reads:
            add(self.lastw.get(t), True)
        for t in writes:
            add(self.lastw.get(t), False)
            for d in self.readers.get(t, {}).values():
                add(d, False)
        for key, val in deps.items():
            self.seen[eng][key] = val
        return [(self._semh(k), v) for k, v in deps.items()]

    def _register(self, dep, reads, writes):
        for t in writes:
            self.lastw[t] = dep
            self.readers[t] = {}
        for t in reads:
            self.readers.setdefault(t, {})[dep[0]] = dep

    PSUM_NAMES = {'pt', 'pm', 'pba', 'pc', 'pss', 'ptr', 'pcol', 'pv', 'pp', 'psT', 'po', 'pl', 'py', 'psmall', 'pf'}

    def _isps(self, t):
        return (t[0] if isinstance(t, tuple) else t) in self.PSUM_NAMES

    def op(self, eng, fn, reads=(), writes=(), inc=True):
        writes = list(writes) + [t for t in reads if self._isps(t)]
        reads = [t for t in reads if not self._isps(t)]
        waits = self._collect(eng, reads, writes, False)
        if inc:
            self.cnt[eng] += 1
            dep = (('e', eng), self.cnt[eng], eng)
            self.q[eng].append((waits, fn, self.sem[eng], 1))
        else:
            dep = (('e', eng), self.cnt[eng] + 1, eng)
            self.q[eng].append((waits, fn, None, 0))
        self._register(dep, reads, writes)

    def dma(self, q, out, in_, reads=(), writes=()):
        n = self.dn[q]
        lane = n % self.NL
        self.dn[q] += 1
        key = ('d', q, lane)
        val = 16 * (n // self.NL + 1)
        waits = self._collect(q, reads, writes, True)
        prev = val - 16
        if prev > 0 and self.seen[q].get(key, 0) < prev:
            waits.append((self.dsem[q][lane], prev))
            self.seen[q][key] = prev
        self.q[q].append((waits, lambda e: e.dma_start(out=out, in_=in_), self.dsem[q][lane], 16))
        self._register((key, val, q), reads, writes)

    def coll(self, kind, ins, outs, reads=(), writes=(), ncores=8):
        q = 'pool'
        n = self.dn[q]
        lane = n % self.NL
        self.dn[q] += 1
        key = ('d', q, lane)
        val = 16 * (n // self.NL + 1)
        waits = self._collect(q, reads, writes, True)
        prev = val - 16
        if prev > 0 and self.seen[q].get(key, 0) < prev:
            waits.append((self.dsem[q][lane], prev))
            self.seen[q][key] = prev
        rg = [list(range(ncores))]
        self.q[q].append((waits, lambda e: e.collective_compute(kind, ALU.bypass, replica_groups=rg, ins=[a for a in ins], outs=[a for a in outs]), self.dsem[q][lane], 16))
        self._register((key, val, q), reads, writes)

    def barrier(self):
        for e in self.ENG:
            waits = []
            for e2 in self.ENG:
                key = ('e', e2)
                if self.cnt[e2] > self.seen[e].get(key, 0):
                    waits.append((self.sem[e2], self.cnt[e2]))
                    self.seen[e][key] = self.cnt[e2]
            for q in self.dq:
                for lane in range(self.NL):
                    n = self.dn[q]
                    k = (n - lane + self.NL - 1) // self.NL if n > lane else 0
                    val = 16 * k
                    key = ('d', q, lane)
                    if val > self.seen[e].get(key, 0):
                        waits.append((self.dsem[q][lane], val))
                        self.seen[e][key] = val
            self.q[e].append((waits, None, None, 0))
        self.lastw.clear()
        self.readers.clear()

    def emit(self):
        nc = self.nc
        with nc.Block() as block:
            decos = {'pe': block.tensor, 'act': block.scalar, 'dve': block.vector,
                     'pool': block.gpsimd, 'sp': block.sync}
            for name in self.ENG:
                def body(e, name=name):
                    for waits, fn, sem, inc in self.q[name]:
                        for s, v in waits:
                            e.wait_ge(s, v)
                        if fn is not None:
                            r = fn(e)
                            if inc:
                                r.then_inc(sem, inc)
                decos[name](body)


class Ctx:
    uid = 0

    def newuid(self):
        self.uid += 1
        return "_%d" % self.uid


def sl(i, n):
    return slice(i * n, (i + 1) * n)


def run_rr(gens):
    gens = list(gens)
    while gens:
        for g in list(gens):
            try:
                next(g)
            except StopIteration:
                gens.remove(g)


def phase_norm(C, P, x_d, S, l):
    nc = C.nc
    with ExitStack() as st:
        u_ = C.newuid()
        sb = lambda n, s, d: st.enter_context(nc.sbuf_tensor(n + u_, s, d))
        psm = lambda n, s, d: st.enter_context(nc.psum_tensor(n + u_, s, d))
        gb = sb("n_gb", [128, D], F32)
        xt = [sb("n_xt%d" % i, [128, D], F32) for i in range(3)]
        hs = [sb("n_hs%d" % i, [128, D], BF16) for i in range(2)]
        junk = sb("n_junk", [128, D], BF16)
        ss = sb("n_ss", [128, 8], F32)
        hT = [sb("n_hT%d" % i, [128, 16, 512], BF16) for i in range(2)]
        pt = [psm("n_pt%d" % i, [128, 1024], BF16) for i in range(4)]
        P.dma('sp', gb[:], C.pre_g[l:l + 1, :].partition_broadcast(128), writes=['gb'])
        nt = S // 128
        for t in range(nt):
            xb = xt[t % 3]
            xtk = ('xt', t % 3)
            P.dma('sp', xb[:], x_d[sl(t, 128), :], writes=[xtk])
            c0 = (t % 2) * 4
            sst, rst = ('ss', t % 2), ('rs', t % 2)
            P.op('act', lambda e, xb=xb, c0=c0: e.activation(out=junk[:], in_=xb[:], func=AF.Square, accum_out=ss[:, c0:c0 + 1]),
                 reads=[xtk], writes=['junk', sst])
            P.op('act', lambda e, c0=c0: e.activation(out=ss[:, c0 + 1:c0 + 2], in_=ss[:, c0:c0 + 1], func=AF.Sqrt, bias=EPS, scale=1.0 / D),
                 reads=[sst], writes=[rst])
            P.op('dve', lambda e, c0=c0: e.reciprocal(out=ss[:, c0 + 2:c0 + 3], in_=ss[:, c0 + 1:c0 + 2]),
                 reads=[rst], writes=[rst])
            hb = hs[t % 2]
            hk = ('hs', t % 2)
            P.op('dve', lambda e, xb=xb, hb=hb, c0=c0: e.scalar_tensor_tensor(out=hb[:], in0=xb[:], scalar=ss[:, c0 + 2:c0 + 3], in1=gb[:], op0=ALU.mult, op1=ALU.mult),
                 reads=[xtk, rst, 'gb'], writes=[hk])
            tt, j = t // 4, t % 4
            hTb = hT[tt % 2]
            hTk = ('hT', tt % 2)
            for half in range(2):
                pi = (t % 2) * 2 + half
                ptk = ('pt', pi)
                for c in range(8):
                    cc = half * 8 + c
                    P.op('pe', lambda e, hb=hb, cc=cc, pi=pi, c=c: e.transpose(out=pt[pi][:, sl(c, 128)], in_=hb[:, sl(cc, 128)], identity=C.idb[:]),
                         reads=[hk, 'idb'], writes=[ptk], inc=(c == 7))
                if half == 0:
                    P.op('act', lambda e, hTb=hTb, j=j, pi=pi: e.activation(out=hTb[:, 0:8, sl(j, 128)], in_=pt[pi][:].rearrange("p (c t) -> p c t", c=8), func=AF.Copy),
                         reads=[ptk], writes=[hTk])
                else:
                    P.op('pool' if False else 'dve', lambda e, hTb=hTb, j=j, pi=pi: e.tensor_copy(out=hTb[:, 8:16, sl(j, 128)], in_=pt[pi][:].rearrange("p (c t) -> p c t", c=8)),
                         reads=[ptk], writes=[hTk])
            if j == 3:
                P.dma('pool', C.hT_d.rearrange("c p t -> p c t")[:, :, sl(tt, 512)], hTb[:], reads=[hTk], writes=[('hT_d', tt)])
    P.barrier()


def phase_inproj(C, P, S, l):
    nc = C.nc
    with ExitStack() as st:
        u_ = C.newuid()
        sb = lambda n, s, d: st.enter_context(nc.sbuf_tensor(n + u_, s, d))
        psm = lambda n, s, d: st.enter_context(nc.psum_tensor(n + u_, s, d))
        wb = [sb("p_wb%d" % i, [128, 16, 1024], BF16) for i in range(2)]
        wba = sb("p_wba", [128, 16, 32], BF16)
        hT = [sb("p_hT%d" % i, [128, 16, 512], BF16) for i in range(2)]
        yo = [sb("p_yo%d" % i, [128, 8, 512], BF16) for i in range(2)]
        ba = sb("p_ba", [128, 4, 32], F32)
        cst = sb("p_cst", [128, 48], F32)
        tmp = sb("p_tmp", [128, 4, 16], F32)
        gbt = [sb("p_gbt%d" % i, [128, 4, 32], F32) for i in range(2)]
        pm = [psm("p_pm%d" % i, [128, 512], F32) for i in range(6)]
        pba_full = psm("p_pba", [128, 512], F32)
        pba = pba_full[:, 0:128].rearrange("p (j n) -> p j n", j=4)
        ntt = S // 512
        groups = [(g * 8, min(8, 47 - g * 8)) for g in range(6)]
        win = C.w_in_r[l]
        P.dma('pool', wba[:], win[:, NROW:NROW + 32].rearrange("(c p) n -> p c n", p=128), writes=['wba'])
        P.dma('sp', cst[:, 0:16], C.a_log[l:l + 1, :].partition_broadcast(128), writes=['cst'])
        P.dma('sp', cst[:, 16:32], C.dt_bias[l:l + 1, :].partition_broadcast(128), writes=['cst'])
        P.op('act', lambda e: e.activation(out=cst[:, 32:48], in_=cst[:, 0:16], func=AF.Exp), reads=['cst'], writes=['cstA'])
        it = 0
        for gi, (m0, nm) in enumerate(groups):
            wbb = wb[gi % 2]
            wk = ('wb', gi % 2)
            P.dma('pool', wbb[:, :, 0:nm * 128], win[:, m0 * 128:(m0 + nm) * 128].rearrange("(c p) n -> p c n", p=128), writes=[wk])
            for tt in range(ntt):
                hTb = hT[it % 2]
                hk = ('hT', it % 2)
                P.dma('sp', hTb[:], C.hT_d.rearrange("c p t -> p c t")[:, :, sl(tt, 512)], writes=[hk])
                yob = yo[it % 2]
                yk = ('yo', it % 2)
                for m in range(nm):
                    pmi = (it * 8 + m) % 6
                    pk = ('pm', pmi)
                    for c in range(16):
                        P.op('pe', lambda e, wbb=wbb, hTb=hTb, m=m, c=c, pmi=pmi: e.matmul(pm[pmi][:], wbb[:, c, sl(m, 128)], hTb[:, c, :], start=(c == 0), stop=(c == 15)),
                             reads=[wk, hk], writes=[pk], inc=(c == 15))
                    if m % 2 == 0:
                        P.op('act', lambda e, yob=yob, m=m, pmi=pmi: e.activation(out=yob[:, m, :], in_=pm[pmi][:], func=AF.Copy),
                             reads=[pk], writes=[yk])
                    else:
                        P.op('dve', lambda e, yob=yob, m=m, pmi=pmi: e.tensor_copy(out=yob[:, m, :], in_=pm[pmi][:]),
                             reads=[pk], writes=[yk])
                P.dma('pool', C.projT_d[m0 * 128:(m0 + nm) * 128, sl(tt, 512)].rearrange("(m p) t -> p m t", p=128), yob[:, 0:nm, :],
                      reads=[yk], writes=[('projT_d', gi, tt)])
                if gi == 0:
                    for j in range(4):
                        for c in range(16):
                            P.op('pe', lambda e, hTb=hTb, j=j, c=c: e.matmul(pba[:, j, :], hTb[:, c, sl(j, 128)], wba[:, c, :], start=(c == 0), stop=(c == 15)),
                                 reads=[hk, 'wba'], writes=['pba'], inc=(c == 15))
                    gbb = gbt[tt % 2]
                    gk = ('gbt', tt % 2)
                    P.op('dve', lambda e: e.tensor_copy(out=ba[:], in_=pba[:]), reads=['pba'], writes=['ba'])
                    P.op('act', lambda e, gbb=gbb: e.activation(out=gbb[:, :, 16:32], in_=ba[:, :, 0:16], func=AF.Sigmoid), reads=['ba'], writes=[gk])
                    for j in range(4):
                        P.op('dve', lambda e, j=j: e.tensor_tensor(out=tmp[:, j, :], in0=ba[:, j, 16:32], in1=cst[:, 16:32], op=ALU.add),
                             reads=['ba', 'cst'], writes=['tmp'])
                    P.op('act', lambda e: e.activation(out=tmp[:], in_=tmp[:], func=AF.Exp), reads=['tmp'], writes=['tmp'])
                    P.op('act', lambda e: e.activation(out=tmp[:], in_=tmp[:], func=AF.Ln, bias=1.0), reads=['tmp'], writes=['tmp'])
                    for j in range(4):
                        P.op('dve', lambda e, j=j, gbb=gbb: e.scalar_tensor_tensor(out=gbb[:, j, 0:16], in0=tmp[:, j, :], scalar=-1.0, in1=cst[:, 32:48], op0=ALU.mult, op1=ALU.mult),
                             reads=['tmp', 'cstA'], writes=[gk])
                    P.dma('pool', C.gb_d[sl(tt, 512), :].rearrange("(j p) n -> p j n", p=128), gbb[:], reads=[gk], writes=[('gb_d', tt)])
                it += 1
    P.barrier()


def phase_gdn_prep(C, P, S, l):
    nc = C.nc
    with ExitStack() as st:
        u_ = C.newuid()
        sb = lambda n, s, d: st.enter_context(nc.sbuf_tensor(n + u_, s, d))
        psm = lambda n, s, d: st.enter_context(nc.psum_tensor(n + u_, s, d))
        cw = sb("c_cw", [128, 120], F32)
        dg = sb("c_dg", [128, 120, 128], BF16)
        raw = [sb("c_raw%d" % i, [128, S + 4], BF16) for i in range(2)]
        NI = 3
        act = [sb("c_act%d" % i, [128, 512], F32) for i in range(NI)]
        sq = [sb("c_sq%d" % i, [128, 512], BF16) for i in range(NI)]
        rr = [sb("c_rr%d" % i, [128, 512], F32) for i in range(NI)]
        ofm = [sb("c_ofm%d" % i, [128, 512], BF16) for i in range(NI)]
        otm = [sb("c_otm%d" % i, [128, 4, 128], BF16) for i in range(NI)]
        pc = [psm("c_pc%d" % i, [128, 512], F32) for i in range(NI)]
        pss = [psm("c_pss%d" % i, [128, 512], F32) for i in range(NI)]
        P.dma('sp', cw[:], C.conv_r[l], writes=['cw'])
        for i in range(120):
            P.op('dve' if i % 2 else 'act', (lambda e, i=i: e.tensor_scalar(out=dg[:, i, :], in0=C.idf[:], scalar1=cw[:, i:i + 1], scalar2=None, op0=ALU.mult)) if i % 2 else
                 (lambda e, i=i: e.activation(out=dg[:, i, :], in_=C.idf[:], func=AF.Copy, scale=cw[:, i:i + 1])),
                 reads=['cw', 'idf'], writes=[('dg', i)])
        ntt = S // 512

        def tile(which, h, ct, rb, rk, tt, b2):
            pk = ('pc', b2)
            for j in range(5):
                P.op('pe', lambda e, j=j: e.matmul(pc[b2][:], dg[:, j * 24 + ct, :], rb[:, tt * 512 + j:tt * 512 + j + 512], start=(j == 0), stop=(j == 4)),
                     reads=[rk, ('dg', j * 24 + ct)], writes=[pk], inc=(j == 4))
            yield
            ab, ak = act[b2], ('act', b2)
            P.op('act', lambda e: e.activation(out=ab[:], in_=pc[b2][:], func=AF.Silu), reads=[pk], writes=[ak])
            ob, ok = ofm[b2], ('ofm', b2)
            if which < 2:
                sqb, sk = sq[b2], ('sq', b2)
                P.op('pool', lambda e: e.tensor_tensor(out=sqb[:], in0=ab[:], in1=ab[:], op=ALU.mult), reads=[ak], writes=[sk])
                yield
                psk = ('pss', b2)
                P.op('pe', lambda e: e.matmul(pss[b2][:], C.onesb[:], sqb[:], start=True, stop=True), reads=[sk, 'onesb'], writes=[psk])
                yield
                rb_, rrk = rr[b2], ('rr', b2)
                P.op('act', lambda e: e.activation(out=rb_[:], in_=pss[b2][:], func=AF.Sqrt, bias=EPS, scale=1.0), reads=[psk], writes=[rrk])
                yield
                P.op('dve', lambda e: e.reciprocal(out=rb_[:], in_=rb_[:]), reads=[rrk], writes=[rrk])
                scl = GDN_SCALE if which == 0 else 1.0
                P.op('dve', lambda e: e.scalar_tensor_tensor(out=ob[:], in0=ab[:], scalar=scl, in1=rb_[:], op0=ALU.mult, op1=ALU.mult),
                     reads=[ak, rrk], writes=[ok])
                dst = C.gq_d if which == 0 else C.gk_d
                P.dma('pool', dst[h, :, sl(tt, 512)], ob[:], reads=[ok], writes=[('gfm', which, h, tt)])
            else:
                P.op('dve', lambda e: e.tensor_copy(out=ob[:], in_=ab[:]), reads=[ak], writes=[ok])
            yield
            if which >= 1:
                tk = pk
                ptv = pc[b2][:, 0:256].bitcast(BF16).rearrange("p (j t) -> p j t", j=4)
                for j in range(4):
                    P.op('pe', lambda e, j=j: e.transpose(out=ptv[:, j, :], in_=ob[:, sl(j, 128)], identity=C.idb[:]),
                         reads=[ok, 'idb'], writes=[tk], inc=(j == 3))
                yield
                otb, otk = otm[b2], ('otm', b2)
                P.op('act', lambda e: e.activation(out=otb[:], in_=ptv, func=AF.Copy), reads=[tk], writes=[otk])
                dst = C.gkT_d if which == 1 else C.gv_d
                P.dma('pool', dst[sl(tt, 512), sl(h, 128)].rearrange("(j p) n -> p j n", p=128), otb[:], reads=[otk], writes=[('gtm', which, h, tt)])

        for which in range(DBG3[0]):
            for h in range(DBG3[1]):
                ct = which * 8 + h
                rb = raw[ct % 2]
                rk = ('raw', ct % 2)
                P.op('pool', lambda e, rb=rb: e.memset(rb[:, 0:2], 0.0), writes=[rk])
                P.op('pool', lambda e, rb=rb: e.memset(rb[:, S + 2:S + 4], 0.0), writes=[rk])
                P.dma('sp', rb[:, 2:S + 2], C.projT_d[ct * 128:(ct + 1) * 128, :], writes=[rk])
                for g0 in range(0, ntt, NI):
                    run_rr([tile(which, h, ct, rb, rk, tt, k) for k, tt in enumerate(range(g0, min(g0 + NI, ntt)))])
    P.barrier()


def phase_gdn(C, P, S, l):
    nc = C.nc
    N = S // 128
    with ExitStack() as st:
        u_ = C.newuid()
        sb = lambda n, s, d: st.enter_context(nc.sbuf_tensor(n + u_, s, d))
        psm = lambda n, s, d: st.enter_context(nc.psum_tensor(n + u_, s, d))
        NB = 2
        kF = [[sb("g_kF%d%d" % (b, d), [128, NH, 128], BF16) for d in range(2)] for b in range(NB)]
        qF = [[sb("g_qF%d%d" % (b, d), [128, NH, 128], BF16) for d in range(2)] for b in range(NB)]
        kT = [[sb("g_kT%d%d" % (b, d), [128, NH, 128], BF16) for d in range(2)] for b in range(NB)]
        vT = [[sb("g_vT%d%d" % (b, d), [128, NH, 128], BF16) for d in range(2)] for b in range(NB)]
        gbt = [[sb("g_gb%d%d" % (b, d), [128, 32], F32) for d in range(2)] for b in range(NB)]
        stt = [[sb("g_st%d%d" % (b, d), [128, 48], F32) for d in range(2)] for b in range(NB)]
        NS = 4
        TG = [sb("g_TG%d" % i, [128, 128], F32) for i in range(NS)]
        Dm = [sb("g_D%d" % i, [128, 128], F32) for i in range(NS)]
        tm_ = [sb("g_t%d" % i, [128, 128], F32) for i in range(NS)]
        fA = [sb("g_fA%d" % i, [128, 128], BF16) for i in range(NS)]
        fB = [sb("g_fB%d" % i, [128, 128], BF16) for i in range(NS)]
        fX1 = [sb("g_fX1%d" % i, [128, 128], BF16) for i in range(NS)]
        fY1 = [sb("g_fY1%d" % i, [128, 128], BF16) for i in range(NS)]
        fXY = [[sb("g_fXY%d_%d" % (i, k), [128, 256], BF16) for k in range(3)] for i in range(NS)]
        fP = [[sb("g_fP%d_%d" % (i, k), [128, 128], BF16) for k in range(2)] for i in range(NS)]
        bT = [[sb("g_bT%d_%d" % (i, k), [128, 128], BF16) for k in range(2)] for i in range(NS)]
        bTT = [[sb("g_bTT%d_%d" % (i, k), [128, 128], BF16) for k in range(2)] for i in range(NS)]
        bC = [[sb("g_bC%d_%d" % (i, k), [128, 128], BF16) for k in range(2)] for i in range(NS)]
        bNR = [sb("g_bNR%d" % i, [128, 256], BF16) for i in range(NS)]
        fTTb = [sb("g_fTTb%d" % i, [128, 128], BF16) for i in range(NS)]
        ktl = [sb("g_ktl%d" % i, [128, 128], BF16) for i in range(NS)]
        aT = [sb("g_aT%d" % b, [128, 16, 128], BF16) for b in range(NB)]
        wT = [sb("g_wT%d" % b, [128, 16, 128], BF16) for b in range(NB)]
        uu = [sb("g_uu%d" % b, [128, 16, 128], F32) for b in range(NB)]
        kd = [sb("g_kd%d" % b, [128, 16, 128], BF16) for b in range(NB)]
        Sf = sb("g_S", [128, 16, 128], F32)
        Sb_ = sb("g_Sb", [128, 16, 128], BF16)
        vn = [sb("g_vn%d" % i, [128, 128], BF16) for i in range(4)]
        t2 = [sb("g_t2%d" % i, [128, 128], F32) for i in range(4)]
        ob = [[sb("g_ob%d%d" % (b, d), [128, NH, 128], F32) for d in range(2)] for b in range(NB)]
        pf = [psm("g_pf%d" % i, [128, 512], F32) for i in range(7)]
        psmall = psm("g_psm", [128, 512], F32)

        P.op('pool', lambda e: e.memset(Sf[:], 0.0), writes=[('Sf', ch) for ch in range(16)])
        P.op('pool', lambda e: e.memset(Sb_[:], 0.0), writes=[('Sb', ch) for ch in range(16)])

        pctr = [0]
        for n in range(N):
            b = n % NB
            for d in range(2):
                c = n if d == 0 else N - 1 - n
                P.dma('sp', kF[b][d][:], C.gk_d[:, :, sl(c, 128)].rearrange("h p t -> p h t"), writes=[('kF', b, d)])
                P.dma('sp', qF[b][d][:], C.gq_d[:, :, sl(c, 128)].rearrange("h p t -> p h t"), writes=[('qF', b, d)])
                P.dma('sp', kT[b][d][:], C.gkT_d[sl(c, 128), :].rearrange("p (h f) -> p h f", h=NH), writes=[('kT', b, d)])
                P.dma('sp', vT[b][d][:], C.gv_d[sl(c, 128), :].rearrange("p (h f) -> p h f", h=NH), writes=[('vT', b, d)])
                P.dma('sp', gbt[b][d][:], C.gb_d[sl(c, 128), :], writes=[('gbt', b, d)])
            for d in range(2):
                tri = C.tri[d]
                negI = C.negI[d]
                g_ = gbt[b][d][:, d * 8:d * 8 + 8]
                be_ = gbt[b][d][:, 16 + d * 8:16 + d * 8 + 8]
                s_ = stt[b][d]
                sk = ('stt', b, d)
                gk_ = ('gbt', b, d)
                P.op('pe', lambda e, tri=tri, g_=g_: e.matmul(psmall[:, 0:8], tri[:], g_, start=True, stop=True), reads=[gk_, 'consts'], writes=['psmall'])
                P.op('pe', lambda e, g_=g_: e.matmul(psmall[:, 8:16], C.onesf[:], g_, start=True, stop=True), reads=[gk_, 'consts'], writes=['psmall'])
                P.op('dve', lambda e, s_=s_: e.tensor_copy(out=s_[:, 0:8], in_=psmall[:, 0:8]), reads=['psmall'], writes=[sk])
                P.op('dve', lambda e, s_=s_: e.tensor_scalar(out=s_[:, 8:16], in0=psmall[:, 0:8], scalar1=-1.0, scalar2=None, op0=ALU.mult), reads=['psmall'], writes=[sk])
                P.op('act', lambda e, s_=s_: e.activation(out=s_[:, 16:24], in_=psmall[:, 0:8], func=AF.Exp), reads=['psmall'], writes=[sk])
                P.op('act', lambda e, s_=s_: e.activation(out=s_[:, 32:40], in_=psmall[:, 8:16], func=AF.Exp), reads=['psmall'], writes=[sk])
                P.op('dve', lambda e, s_=s_: e.tensor_tensor(out=s_[:, 24:32], in0=psmall[:, 8:16], in1=s_[:, 0:8], op=ALU.subtract), reads=['psmall', sk], writes=[sk])
                P.op('act', lambda e, s_=s_: e.activation(out=s_[:, 24:32], in_=s_[:, 24:32], func=AF.Exp), reads=[sk], writes=[sk])
                P.op('dve', lambda e, s_=s_, be_=be_: e.tensor_scalar(out=s_[:, 40:48], in0=be_, scalar1=-1.0, scalar2=None, op0=ALU.mult), reads=[gk_], writes=[sk])
                def prob(h, i, d=d, tri=tri, g_=g_, be_=be_, s_=s_, sk=sk, gk_=gk_):
                    ch = d * 8 + h
                    kFh = kF[b][d][:, h, :]
                    qFh = qF[b][d][:, h, :]
                    kTh = kT[b][d][:, h, :]
                    vTh = vT[b][d][:, h, :]
                    bk = pf[i]
                    bkk = ('pf', i)
                    pkk, pkk_k = bk[:, 0:256], bkk
                    P.op('pe', lambda e, pkk=pkk, kFh=kFh: e.matmul(pkk[:, 0:128], kFh, kFh, start=True, stop=True), reads=[('kF', b, d)], writes=[pkk_k], inc=False)
                    P.op('pe', lambda e, pkk=pkk, kFh=kFh, qFh=qFh: e.matmul(pkk[:, 128:256], kFh, qFh, start=True, stop=True), reads=[('kF', b, d), ('qF', b, d)], writes=[pkk_k])
                    P.op('act', lambda e, i=i, tri=tri, g_=g_, h=h: e.activation(out=TG[i][:], in_=tri[:], func=AF.Copy, scale=g_[:, h:h + 1]),
                         reads=[gk_, 'consts'], writes=[('TG', i)])
                    pb, pb_k = bk[:, 256:384], bkk
                    P.op('pe', lambda e, pb=pb, i=i: e.matmul(pb[:, 0:128], C.onesf[:], TG[i][:], start=True, stop=False), reads=[('TG', i), 'consts'], writes=[pb_k], inc=False)
                    P.op('pe', lambda e, pb=pb, d=d: e.matmul(pb[:, 0:128], C.idb[:], C.negIb[d][:], start=False, stop=True), reads=['consts', 'idb'], writes=[pb_k])
                    yield
                    P.op('act', lambda e, i=i, pb=pb, s_=s_, h=h: e.activation(out=Dm[i][:], in_=pb[:, 0:128], func=AF.Exp, bias=s_[:, 8 + h:9 + h], scale=1.0),
                         reads=[pb_k, sk], writes=[('D', i)])
                    P.op('dve', lambda e, i=i, pkk=pkk, b=b, ch=ch: e.tensor_tensor(out=aT[b][:, ch, :], in0=pkk[:, 128:256], in1=Dm[i][:], op=ALU.mult),
                         reads=[pkk_k, ('D', i)], writes=[('aT', b, ch)])
                    P.op('dve', lambda e, i=i, pkk=pkk: e.tensor_tensor(out=tm_[i][:], in0=pkk[:, 0:128], in1=Dm[i][:], op=ALU.mult),
                         reads=[pkk_k, ('D', i)], writes=[('tm', i)])
                    A_, Bm_ = fA[i], fB[i]
                    Ak, Bk = ('fA', i), ('fB', i)
                    P.op('dve', lambda e, i=i, A_=A_, be_=be_, h=h: e.scalar_tensor_tensor(out=A_[:], in0=tm_[i][:], scalar=be_[:, h:h + 1], in1=C.offd[:], op0=ALU.mult, op1=ALU.mult),
                         reads=[('tm', i), gk_, 'consts'], writes=[Ak])
                    P.op('pe', lambda e, bk=bk, A_=A_: e.transpose(out=bk[:, 384:448].bitcast(BF16), in_=A_[:], identity=C.idb[:]), reads=[Ak, 'idb'], writes=[bkk])
                    yield
                    P.op('act', lambda e, bk=bk, Bm_=Bm_: e.activation(out=Bm_[:], in_=bk[:, 384:448].bitcast(BF16), func=AF.Copy), reads=[bkk], writes=[Bk])
                    X1, Y1 = fX1[i], fY1[i]
                    P.op('pool', lambda e, X1=X1, A_=A_: e.tensor_tensor(out=X1[:], in0=A_[:], in1=C.bd16[:], op=ALU.mult), reads=[Ak, 'consts'], writes=[('fX1', i)])
                    P.op('pool', lambda e, Y1=Y1, Bm_=Bm_: e.tensor_tensor(out=Y1[:], in0=Bm_[:], in1=C.bd16[:], op=ALU.mult), reads=[Bk, 'consts'], writes=[('fY1', i)])
                    Pp = fP[i]
                    P.op('pool', lambda e, Pp=Pp, X1=X1: e.tensor_tensor(out=Pp[0][:], in0=C.idb[:], in1=X1[:], op=ALU.subtract), reads=[('fX1', i), 'idb'], writes=[('fP', i, 0)])
                    yield
                    XY = fXY[i]
                    Yc, Xc, yk_ = Y1[:], X1[:], [('fX1', i), ('fY1', i)]
                    for lv in range(1, 4):
                        last = (lv == 3)
                        P.op('pe', lambda e, bk=bk, Yc=Yc, Xc=Xc: e.matmul(bk[:, 0:128], Xc, Yc, start=True, stop=True), reads=yk_, writes=[bkk], inc=last)
                        if not last:
                            P.op('pe', lambda e, bk=bk, Yc=Yc, Xc=Xc: e.matmul(bk[:, 128:256], Yc, Xc, start=True, stop=True), reads=yk_, writes=[bkk])
                        yield
                        dst = XY[lv - 1]
                        wc = 128 if last else 256
                        P.op('act', lambda e, bk=bk, dst=dst, wc=wc: e.activation(out=dst[:, 0:wc], in_=bk[:, 0:wc], func=AF.Copy), reads=[bkk], writes=[('fXY', i, lv)])
                        Yc, Xc, yk_ = dst[:, 0:128], dst[:, 128:256], [('fXY', i, lv)]
                        src, dstp = Pp[(lv - 1) % 2], Pp[lv % 2]
                        P.op('pe', lambda e, bk=bk, src=src, Yc=Yc: e.matmul(bk[:, 256:384], Yc, src[:], start=True, stop=True), reads=[('fP', i, (lv - 1) % 2), ('fXY', i, lv)], writes=[bkk])
                        yield
                        if not last:
                            P.op('dve', lambda e, bk=bk, dstp=dstp, src=src: e.tensor_tensor(out=dstp[:], in0=src[:], in1=bk[:, 256:384], op=ALU.add), reads=[bkk, ('fP', i, (lv - 1) % 2)], writes=[('fP', i, lv % 2)])
                        else:
                            P.op('dve', lambda e, bk=bk, i=i, src=src: e.tensor_tensor(out=bTT[i][0][:], in0=src[:], in1=bk[:, 256:384], op=ALU.add), reads=[bkk, ('fP', i, (lv - 1) % 2)], writes=[('bTT', i, 0)])
                        yield
                    TTk, TTkk = bTT[i][0], ('bTT', i, 0)
                    Tk, Tkk = bT[i][0], ('bT', i, 0)
                    tpb = bk[:, 384:448].bitcast(BF16)
                    P.op('pe', lambda e, tpb=tpb, TTk=TTk: e.transpose(out=tpb, in_=TTk[:], identity=C.idb[:]), reads=[TTkk, 'idb'], writes=[bkk])
                    P.op('act', lambda e, tpb=tpb, Tk=Tk: e.activation(out=Tk[:], in_=tpb, func=AF.Copy), reads=[bkk], writes=[Tkk])
                    yield
                    TT = fTTb[i]
                    TTk_ = ('fTTb', i)
                    for ci, mk in enumerate(C.mlev):
                        lastc = (ci == 2)
                        Ck, CTk = bC[i][0], bC[i][1]
                        P.op('pool', lambda e, Ck=Ck, Bm_=Bm_, mk=mk: e.tensor_tensor(out=Ck[:], in0=Bm_[:], in1=mk[:], op=ALU.mult), reads=[Bk, 'consts'], writes=[('bC', i, 0)])
                        P.op('pe', lambda e, bk=bk, Ck=Ck, TTk=TTk: e.matmul(bk[:, 0:128], Ck[:], TTk[:], start=True, stop=True), reads=[('bC', i, 0), TTkk], writes=[bkk], inc=lastc)
                        if not lastc:
                            P.op('pool', lambda e, CTk=CTk, A_=A_, mk=mk: e.tensor_tensor(out=CTk[:], in0=A_[:], in1=mk[:], op=ALU.mult), reads=[Ak, 'consts'], writes=[('bC', i, 1)])
                            P.op('pe', lambda e, bk=bk, CTk=CTk, Tk=Tk: e.matmul(bk[:, 128:256], CTk[:], Tk[:], start=True, stop=True), reads=[('bC', i, 1), Tkk], writes=[bkk])
                            yield
                        NR = bNR[i]
                        wc = 128 if lastc else 256
                        P.op('act', lambda e, bk=bk, NR=NR, wc=wc: e.activation(out=NR[:, 0:wc], in_=bk[:, 0:wc], func=AF.Copy), reads=[bkk], writes=[('bNR', i)])
                        P.op('pe', lambda e, bk=bk, NR=NR, Tk=Tk: e.matmul(bk[:, 256:384], Tk[:], NR[:, 0:128], start=True, stop=True), reads=[('bNR', i), Tkk], writes=[bkk], inc=lastc)
                        if lastc:
                            yield
                        if not lastc:
                            P.op('pe', lambda e, bk=bk, NR=NR, TTk=TTk: e.matmul(bk[:, 384:512], TTk[:], NR[:, 128:256], start=True, stop=True), reads=[('bNR', i), TTkk], writes=[bkk])
                            yield
                            nTT, nT = bTT[i][(ci + 1) % 2], bT[i][(ci + 1) % 2]
                            nTTk, nTk = ('bTT', i, (ci + 1) % 2), ('bT', i, (ci + 1) % 2)
                            P.op('dve', lambda e, bk=bk, nTT=nTT, TTk=TTk: e.tensor_tensor(out=nTT[:], in0=TTk[:], in1=bk[:, 256:384], op=ALU.subtract), reads=[TTkk, bkk], writes=[nTTk])
                            P.op('dve', lambda e, bk=bk, nT=nT, Tk=Tk: e.tensor_tensor(out=nT[:], in0=Tk[:], in1=bk[:, 384:512], op=ALU.subtract), reads=[Tkk, bkk], writes=[nTk])
                            yield
                            TTk, TTkk, Tk, Tkk = nTT, nTTk, nT, nTk
                        else:
                            P.op('dve', lambda e, bk=bk, TT=TT, TTk=TTk: e.tensor_tensor(out=TT[:], in0=TTk[:], in1=bk[:, 256:384], op=ALU.subtract), reads=[TTkk, bkk], writes=[TTk_])
                    TTk = TTk_
                    P.op('act', lambda e, i=i, kTh=kTh, s_=s_, h=h: e.activation(out=ktl[i][:], in_=kTh, func=AF.Copy, scale=s_[:, 16 + h:17 + h]),
                         reads=[('kT', b, d), sk], writes=[('ktl', i)])
                    P.op('act', lambda e, kTh=kTh, s_=s_, h=h, b=b, ch=ch: e.activation(out=kd[b][:, ch, :], in_=kTh, func=AF.Copy, scale=s_[:, 24 + h:25 + h]),
                         reads=[('kT', b, d), sk], writes=[('kd', b, ch)])
                    pu, pu_k = bk[:, 0:256], bkk
                    P.op('pe', lambda e, pu=pu, TT=TT, vTh=vTh: e.matmul(pu[:, 0:128], TT[:], vTh, start=True, stop=True), reads=[TTk, ('vT', b, d)], writes=[pu_k], inc=False)
                    P.op('pe', lambda e, pu=pu, TT=TT, i=i: e.matmul(pu[:, 128:256], ktl[i][:], TT[:], start=True, stop=True), reads=[TTk, ('ktl', i)], writes=[pu_k])
                    yield
                    P.op('act', lambda e, pu=pu, be_=be_, h=h, b=b, ch=ch: e.activation(out=uu[b][:, ch, :], in_=pu[:, 0:128], func=AF.Copy, scale=be_[:, h:h + 1]),
                         reads=[pu_k, gk_], writes=[('uu', b, ch)])
                    P.op('dve', lambda e, pu=pu, b=b, ch=ch: e.tensor_copy(out=wT[b][:, ch, :], in_=pu[:, 128:256]), reads=[pu_k], writes=[('wT', b, ch)])
                for g0 in range(0, NH, NS):
                    run_rr([prob(g0 + k, k) for k in range(NS)])
            for d in range(2):
                s_ = stt[b][d]
                sk = ('stt', b, d)
                c = n if d == 0 else N - 1 - n
                def chain(h, vi, d=d, s_=s_, sk=sk):
                    ch = d * 8 + h
                    qFh = qF[b][d][:, h, :]
                    sbk = 3 + vi
                    p1, p1_k = pf[sbk][:, 0:256], ('pf', sbk)
                    P.op('pe', lambda e, p1=p1, b=b, ch=ch: e.matmul(p1[:, 0:128], wT[b][:, ch, :], Sb_[:, ch, :], start=True, stop=True),
                         reads=[('wT', b, ch), ('Sb', ch)], writes=[p1_k], inc=False)
                    P.op('pe', lambda e, p1=p1, qFh=qFh, ch=ch: e.matmul(p1[:, 128:256], qFh, Sb_[:, ch, :], start=True, stop=True),
                         reads=[('qF', b, d), ('Sb', ch)], writes=[p1_k])
                    yield
                    P.op('dve', lambda e, p1=p1, s_=s_, h=h, b=b, ch=ch, vi=vi: e.scalar_tensor_tensor(out=vn[vi][:], in0=p1[:, 0:128], scalar=s_[:, 40 + h:41 + h], in1=uu[b][:, ch, :], op0=ALU.mult, op1=ALU.add),
                         reads=[p1_k, sk, ('uu', b, ch)], writes=[('vn', vi)])
                    yield
                    p2, p2_k = pf[sbk][:, 256:512], ('pf', sbk)
                    P.op('pe', lambda e, p2=p2, b=b, ch=ch, vi=vi: e.matmul(p2[:, 0:128], aT[b][:, ch, :], vn[vi][:], start=True, stop=True),
                         reads=[('aT', b, ch), ('vn', vi)], writes=[p2_k], inc=False)
                    P.op('pe', lambda e, p2=p2, b=b, ch=ch, vi=vi: e.matmul(p2[:, 128:256], kd[b][:, ch, :], vn[vi][:], start=True, stop=True),
                         reads=[('kd', b, ch), ('vn', vi)], writes=[p2_k])
                    yield
                    P.op('act', lambda e, p2=p2, vi=vi: e.activation(out=t2[vi][:], in_=p2[:, 0:128], func=AF.Copy), reads=[p2_k], writes=[('t2', vi)])
                    P.op('dve', lambda e, p1=p1, s_=s_, h=h, b=b, d=d, vi=vi: e.scalar_tensor_tensor(out=ob[b][d][:, h, :], in0=p1[:, 128:256], scalar=s_[:, 16 + h:17 + h], in1=t2[vi][:], op0=ALU.mult, op1=ALU.add),
                         reads=[p1_k, sk, ('t2', vi)], writes=[('ob', b, d)])
                    P.op('dve', lambda e, p2=p2, s_=s_, h=h, ch=ch: e.scalar_tensor_tensor(out=Sf[:, ch, :], in0=Sf[:, ch, :], scalar=s_[:, 32 + h:33 + h], in1=p2[:, 128:256], op0=ALU.mult, op1=ALU.add),
                         reads=[p2_k, sk, ('Sf', ch)], writes=[('Sf', ch)])
                    P.op('pool', lambda e, ch=ch: e.tensor_copy(out=Sb_[:, ch, :], in_=Sf[:, ch, :]), reads=[('Sf', ch)], writes=[('Sb', ch)])
                for g0 in range(0, NH, 4):
                    run_rr([chain(g0 + k, k) for k in range(4)])
                P.dma('pool', C.od_d[d, sl(c, 128), :].rearrange("p (h f) -> p h f", h=NH), ob[b][d][:], reads=[('ob', b, d)], writes=[('od_d', d, c)])
    P.barrier()


def phase_mla_prep(C, P, S, l, pos0=0):
    nc = C.nc
    with ExitStack() as st:
        u_ = C.newuid()
        sb = lambda n, s, d: st.enter_context(nc.sbuf_tensor(n + u_, s, d))
        psm = lambda n, s, d: st.enter_context(nc.psum_tensor(n + u_, s, d))
        wst = sb("m_wst", [128, 4, 2048], F32)
        wq = sb("m_wq", [128, 4, 2048], BF16)
        wkv = sb("m_wkv", [128, 2, 2048], BF16)
        gq = sb("m_gq", [128, 4], F32)
        gkv = sb("m_gkv", [128, 2], F32)
        cq = [sb("m_cq%d" % i, [128, 4, 512], BF16) for i in range(2)]
        ckv = [sb("m_ckv%d" % i, [128, 2, 512], BF16) for i in range(2)]
        kpe = [sb("m_kpe%d" % i, [64, 2, 512], BF16) for i in range(2)]
        sqq = sb("m_sqq", [128, 4, 512], BF16)
        sqk = sb("m_sqk", [128, 2, 512], BF16)
        rq = sb("m_rq", [128, 512], F32)
        rkv = sb("m_rkv", [128, 512], F32)
        rkc = sb("m_rkc", [128, 4], F32)
        cs = [sb("m_cs%d" % i, [64, 2, 512], F32) for i in range(2)]
        csr = sb("m_csr", [64, 2, 512], F32)
        t1 = [sb("m_t1%d" % i, [64, 512], F32) for i in range(2)]
        t2 = [sb("m_t2%d" % i, [64, 512], F32) for i in range(2)]
        oq = [sb("m_oq%d" % i, [128, 512], BF16) for i in range(3)]
        ope = [sb("m_ope%d" % i, [64, 512], BF16) for i in range(3)]
        ov = [sb("m_ov%d" % i, [128, 4, 1024], BF16) for i in range(2)]
        pm = [psm("m_pm%d" % i, [128, 512], F32) for i in range(4)]
        pp = [psm("m_pp%d" % i, [64, 512], F32) for i in range(2)]
        pcol = psm("m_pcol", [128, 512], F32)[:, 0:4]
        pv = psm("m_pv", [128, 512], F32)
        P.dma('sp', gq[:], C.gq_r[l], writes=['gq'])
        P.dma('sp', gkv[:], C.gkv_r[l], writes=['gkv'])
        P.dma('sp', wst[:], C.wuq_r[l].rearrange("(c p) n -> p c n", p=128), writes=['wst'])
        for c in range(4):
            P.op('dve' if c % 2 else 'pool', lambda e, c=c: e.tensor_scalar(out=wq[:, c, :], in0=wst[:, c, :], scalar1=gq[:, c:c + 1], scalar2=None, op0=ALU.mult),
                 reads=['wst', 'gq'], writes=['wq'])
        P.dma('sp', wst[:, 0:2, :], C.wukv_r[l].rearrange("(c p) n -> p c n", p=128), reads=[], writes=['wst'])
        for c in range(2):
            P.op('dve' if c % 2 else 'pool', lambda e, c=c: e.tensor_scalar(out=wkv[:, c, :], in0=wst[:, c, :], scalar1=gkv[:, c:c + 1], scalar2=None, op0=ALU.mult),
                 reads=['wst', 'gkv'], writes=['wkv'])
        ntt = S // 512
        oc = 0
        for tt in range(ntt):
            b2 = tt % 2
            P.dma('sp', cq[b2][:], C.projT_d[R_CQ:R_CQ + 512, sl(tt, 512)].rearrange("(c p) t -> p c t", p=128), writes=[('cq', b2)])
            P.dma('sp', ckv[b2][:], C.projT_d[R_CKV:R_CKV + 256, sl(tt, 512)].rearrange("(c p) t -> p c t", p=128), writes=[('ckv', b2)])
            P.dma('sp', kpe[b2][:], C.projT_d[R_KPE:R_KPE + 128, sl(tt, 512)].rearrange("(c p) t -> p c t", p=64), writes=[('kpe', b2)])
            P.dma('sp', cs[b2][:], C.rope_d[:, :, pos0 + tt * 512:pos0 + (tt + 1) * 512], writes=[('cs', b2)])
            P.op('pool', lambda e, b2=b2: e.tensor_tensor(out=sqq[:], in0=cq[b2][:], in1=cq[b2][:], op=ALU.mult), reads=[('cq', b2)], writes=['sqq'])
            P.op('pool', lambda e, b2=b2: e.tensor_tensor(out=sqk[:], in0=ckv[b2][:], in1=ckv[b2][:], op=ALU.mult), reads=[('ckv', b2)], writes=['sqk'])
            for c in range(4):
                P.op('pe', lambda e, c=c: e.matmul(pm[0][:], C.onesb[:], sqq[:, c, :], start=(c == 0), stop=(c == 3)), reads=['sqq', 'onesb'], writes=[('pm', 0)], inc=(c == 3))
            P.op('act', lambda e: e.activation(out=rq[:], in_=pm[0][:], func=AF.Sqrt, bias=EPS, scale=1.0 / 512), reads=[('pm', 0)], writes=['rq'])
            P.op('dve', lambda e: e.reciprocal(out=rq[:], in_=rq[:]), reads=['rq'], writes=['rq'])
            P.op('dve', lambda e: e.tensor_scalar(out=rq[:], in0=rq[:], scalar1=MLA_SCALE, scalar2=None, op0=ALU.mult), reads=['rq'], writes=['rq'])
            for c in range(2):
                P.op('pe', lambda e, c=c: e.matmul(pm[1][:], C.onesb[:], sqk[:, c, :], start=(c == 0), stop=(c == 1)), reads=['sqk', 'onesb'], writes=[('pm', 1)], inc=(c == 1))
            P.op('act', lambda e: e.activation(out=rkv[:], in_=pm[1][:], func=AF.Sqrt, bias=EPS, scale=1.0 / 256), reads=[('pm', 1)], writes=['rkv'])
            P.op('dve', lambda e: e.reciprocal(out=rkv[:], in_=rkv[:]), reads=['rkv'], writes=['rkv'])
            for j in range(4):
                for c in range(2):
                    P.op('pe', lambda e, j=j, c=c: e.matmul(pcol[:, j:j + 1], sqk[:, c, sl(j, 128)], C.onesb[:, 0:1], start=(c == 0), stop=(c == 1)),
                         reads=['sqk', 'onesb'], writes=['pcol'], inc=(c == 1))
            P.op('act', lambda e: e.activation(out=rkc[:], in_=pcol[:], func=AF.Sqrt, bias=EPS, scale=1.0 / 256), reads=['pcol'], writes=['rkc'])
            P.op('dve', lambda e: e.reciprocal(out=rkc[:], in_=rkc[:]), reads=['rkc'], writes=['rkc'])
            for w_ in range(2):
                P.op('pool', lambda e, w_=w_, b2=b2: e.tensor_tensor(out=csr[:, w_, :], in0=cs[b2][:, w_, :], in1=rq[0:64, :], op=ALU.mult), reads=[('cs', b2), 'rq'], writes=['csr'])
            P.op('dve', lambda e, b2=b2: e.tensor_tensor(out=t1[0][:], in0=kpe[b2][:, 0, :], in1=cs[b2][:, 0, :], op=ALU.mult), reads=[('kpe', b2), ('cs', b2)], writes=[('t1', 0)])
            P.op('pool', lambda e, b2=b2: e.tensor_tensor(out=t2[0][:], in0=kpe[b2][:, 1, :], in1=cs[b2][:, 1, :], op=ALU.mult), reads=[('kpe', b2), ('cs', b2)], writes=[('t2', 0)])
            o_ = ope[oc % 3]
            ok = ('ope', oc % 3)
            P.op('dve', lambda e, o_=o_: e.tensor_tensor(out=o_[:], in0=t1[0][:], in1=t2[0][:], op=ALU.add), reads=[('t1', 0), ('t2', 0)], writes=[ok])
            P.dma('pool', C.akpe_d[:, sl(tt, 512)], o_[:], reads=[ok], writes=[('akpe_d', tt)])
            oc += 1
            for h in range(NH):
                pi = (h * 2) % 4
                for c in range(4):
                    P.op('pe', lambda e, c=c, h=h, b2=b2, pi=pi: e.matmul(pm[pi][:], wq[:, c, h * 256:h * 256 + 128], cq[b2][:, c, :], start=(c == 0), stop=(c == 3)),
                         reads=['wq', ('cq', b2)], writes=[('pm', pi)], inc=(c == 3))
                o_ = oq[oc % 3]
                ok = ('oq', oc % 3)
                P.op('dve', lambda e, o_=o_, pi=pi: e.tensor_tensor(out=o_[:], in0=pm[pi][:], in1=rq[:], op=ALU.mult), reads=[('pm', pi), 'rq'], writes=[ok])
                P.dma('pool', C.aq_d[h, 0, :, sl(tt, 512)], o_[:], reads=[ok], writes=[('aq_d', h, 0, tt)])
                for w_ in range(2):
                    for c in range(4):
                        P.op('pe', lambda e, c=c, h=h, b2=b2, w_=w_: e.matmul(pp[w_][:], wq[:, c, h * 256 + 128 + w_ * 64:h * 256 + 192 + w_ * 64], cq[b2][:, c, :], start=(c == 0), stop=(c == 3)),
                             reads=['wq', ('cq', b2)], writes=[('pp', w_)], inc=(c == 3))
                P.op('dve', lambda e: e.tensor_tensor(out=t1[1][:], in0=pp[0][:], in1=csr[:, 0, :], op=ALU.mult), reads=[('pp', 0), 'csr'], writes=[('t1', 1)])
                P.op('dve', lambda e: e.tensor_tensor(out=t2[1][:], in0=pp[1][:], in1=csr[:, 1, :], op=ALU.mult), reads=[('pp', 1), 'csr'], writes=[('t2', 1)])
                o2 = ope[oc % 3]
                ok2 = ('ope', oc % 3)
                P.op('pool', lambda e, o2=o2: e.tensor_tensor(out=o2[:], in0=t1[1][:], in1=t2[1][:], op=ALU.add), reads=[('t1', 1), ('t2', 1)], writes=[ok2])
                P.dma('pool', C.aq_d[h, 1, 0:64, sl(tt, 512)], o2[:], reads=[ok2], writes=[('aq_d', h, 1, tt)])
                oc += 1
                pi = (h * 2 + 1) % 4
                for c in range(2):
                    P.op('pe', lambda e, c=c, h=h, b2=b2, pi=pi: e.matmul(pm[pi][:], wkv[:, c, h * 256:h * 256 + 128], ckv[b2][:, c, :], start=(c == 0), stop=(c == 1)),
                         reads=['wkv', ('ckv', b2)], writes=[('pm', pi)], inc=(c == 1))
                o_ = oq[oc % 3]
                ok = ('oq', oc % 3)
                P.op('dve', lambda e, o_=o_, pi=pi: e.tensor_tensor(out=o_[:], in0=pm[pi][:], in1=rkv[:], op=ALU.mult), reads=[('pm', pi), 'rkv'], writes=[ok])
                P.dma('pool', C.ak_d[h, :, sl(tt, 512)], o_[:], reads=[ok], writes=[('ak_d', h, tt)])
                oc += 1
            ovb = ov[b2]
            for j in range(4):
                for gp in range(2):
                    for c in range(2):
                        rhs = wkv[:, c, :].rearrange("p (h w) -> p h w", h=NH)[:, gp * 4:gp * 4 + 4, 128:256]
                        P.op('pe', lambda e, j=j, c=c, b2=b2, rhs=rhs: e.matmul(pv[:].rearrange("p (h w) -> p h w", h=4), ckv[b2][:, c, sl(j, 128)], rhs, start=(c == 0), stop=(c == 1)),
                             reads=['wkv', ('ckv', b2)], writes=['pv'], inc=(c == 1))
                    P.op('act', lambda e, j=j, gp=gp, ovb=ovb: e.activation(out=ovb[:, j, gp * 512:(gp + 1) * 512], in_=pv[:], func=AF.Copy, scale=rkc[:, j:j + 1]),
                         reads=['pv', 'rkc'], writes=[('ov', b2)])
            P.dma('pool', C.av_d[sl(tt, 512), :].rearrange("(j p) n -> p j n", p=128), ovb[:], reads=[('ov', b2)], writes=[('av_d', tt)])
    P.barrier()


def phase_attn(C, P, S, l):
    nc = C.nc
    with ExitStack() as st:
        u_ = C.newuid()
        sb = lambda n, s, d: st.enter_context(nc.sbuf_tensor(n + u_, s, d))
        psm = lambda n, s, d: st.enter_context(nc.psum_tensor(n + u_, s, d))
        NK = S // 128
        NQ = S // 512
        kpe = sb("a_kpe", [128, S], BF16)
        kn = sb("a_kn", [128, S], BF16)
        vv = sb("a_vv", [128, NK, 128], BF16)
        qn = [sb("a_qn%d" % i, [128, 512], BF16) for i in range(4)]
        qp = [sb("a_qp%d" % i, [128, 512], BF16) for i in range(4)]
        pT = [sb("a_pT%d" % i, [128, 512], BF16) for i in range(4)]
        acc = [[sb("a_acc%d%d" % (i, k), [128, 512], F32) for k in range(2)] for i in range(2)]
        rs = sb("a_rs", [128, 512], F32)
        oo = [sb("a_oo%d" % i, [128, 512], BF16) for i in range(2)]
        psT = [psm("a_ps%d" % i, [128, 512], F32) for i in range(4)]
        po = [psm("a_po%d" % i, [128, 512], F32) for i in range(2)]
        pl = [psm("a_pl%d" % i, [128, 512], F32) for i in range(2)]
        P.op('pool', lambda e: e.memset(kpe[64:128, :], 0.0), writes=['kpe'])
        for i in range(4):
            P.op('pool', lambda e, i=i: e.memset(qp[i][64:128, :], 0.0), writes=[('qp', i)])
        P.dma('sp', kpe[0:64, :], C.akpe_d[:, :], writes=['kpe'])
        ipr = 0
        for h in range(NH):
            P.dma('sp', kn[:], C.ak_d[h], writes=['kn'])
            for v0 in range(0, NK, 8):
                v1 = min(v0 + 8, NK)
                P.dma('sp', vv[:, v0:v1, :], C.av_d[v0 * 128:v1 * 128, sl(h, 128)].rearrange("(t p) f -> p t f", p=128), writes=['vv'])
            for q0 in range(0, NQ, 2):
                tiles = list(range(q0, min(q0 + 2, NQ)))
                npt = len(tiles)
                qi = [(ipr % 2) * 2 + ab for ab in range(npt)]
                ipr += 1
                for ab, qt in enumerate(tiles):
                    P.dma('sp', qn[qi[ab]][:], C.aq_d[h, 0, :, sl(qt, 512)], writes=[('qn', qi[ab])])
                    P.dma('sp', qp[qi[ab]][0:64, :], C.aq_d[h, 1, 0:64, sl(qt, 512)], writes=[('qp', qi[ab])])

                def qk_exp(kt, s2, qi=qi, npt=npt):
                    for ab in range(npt):
                        P.op('pe', lambda e, ab=ab: e.matmul(psT[s2 + ab][:], kn[:, sl(kt, 128)], qn[qi[ab]][:], start=True, stop=False),
                             reads=['kn', ('qn', qi[ab])], writes=[('psT', s2 + ab)], inc=False)
                    for ab in range(npt):
                        P.op('pe', lambda e, ab=ab: e.matmul(psT[s2 + ab][:], kpe[:, sl(kt, 128)], qp[qi[ab]][:], start=False, stop=True),
                             reads=['kpe', ('qp', qi[ab])], writes=[('psT', s2 + ab)], inc=(ab == npt - 1))
                    for ab in range(npt):
                        P.op('act', lambda e, ab=ab: e.activation(out=pT[s2 + ab][:], in_=psT[s2 + ab][:], func=AF.Exp), reads=[('psT', s2 + ab)], writes=[('pT', s2 + ab)])

                def pv_acc(kt, s2, npt=npt):
                    for ab in range(npt):
                        P.op('pe', lambda e, ab=ab: e.matmul(po[ab][:], vv[:, kt, :], pT[s2 + ab][:], start=(kt == 0), stop=(kt == NK - 1)),
                             reads=['vv', ('pT', s2 + ab)], writes=[('po', ab)], inc=(ab == npt - 1))
                    for ab in range(npt):
                        ae = 'dve' if (kt + ab) % 2 == 0 else 'pool'
                        ab_ = acc[ab][kt % 2]
                        ak_ = ('acc', ab, kt % 2)
                        if kt < 2:
                            P.op(ae, lambda e, ab_=ab_, ab=ab: e.tensor_copy(out=ab_[:], in_=pT[s2 + ab][:]), reads=[('pT', s2 + ab)], writes=[ak_])
                        else:
                            P.op(ae, lambda e, ab_=ab_, ab=ab: e.tensor_tensor(out=ab_[:], in0=ab_[:], in1=pT[s2 + ab][:], op=ALU.add), reads=[('pT', s2 + ab), ak_], writes=[ak_])
                prev = None
                for kt in range(NK):
                    s2 = (kt % 2) * 2
                    qk_exp(kt, s2)
                    if prev is not None:
                        pv_acc(*prev)
                    prev = (kt, s2)
                pv_acc(*prev)
                for ab, qt in enumerate(tiles):
                    P.op('pe', lambda e, ab=ab: e.matmul(pl[ab][:], C.onesf[:], acc[ab][0][:], start=True, stop=(NK < 2)), reads=['consts', ('acc', ab, 0)], writes=[('pl', ab)], inc=(NK < 2))
                    if NK >= 2:
                        P.op('pe', lambda e, ab=ab: e.matmul(pl[ab][:], C.onesf[:], acc[ab][1][:], start=False, stop=True), reads=['consts', ('acc', ab, 1)], writes=[('pl', ab)])
                    P.op('dve', lambda e, ab=ab: e.reciprocal(out=rs[:], in_=pl[ab][:]), reads=[('pl', ab)], writes=['rs'])
                    P.op('dve', lambda e, ab=ab: e.tensor_tensor(out=oo[ab][:], in0=po[ab][:], in1=rs[:], op=ALU.mult), reads=[('po', ab), 'rs'], writes=[('oo', ab)])
                    P.dma('pool', C.ao_d[h, :, sl(qt, 512)], oo[ab][:], reads=[('oo', ab)], writes=[('ao_d', h, qt)])
    P.barrier()


def phase_out(C, P, S, l, x_d, xo_d):
    nc = C.nc
    with ExitStack() as st:
        u_ = C.newuid()
        sb = lambda n, s, d: st.enter_context(nc.sbuf_tensor(n + u_, s, d))
        psm = lambda n, s, d: st.enter_context(nc.psum_tensor(n + u_, s, d))
        wo = sb("o_wo", [128, 16, D], BF16)
        gpb = sb("o_gpb", [128, D], F32)
        gng = sb("o_gng", [128, 1], F32)
        za = [sb("o_za%d" % i, [128, 16, 128], BF16) for i in range(2)]
        sz = [sb("o_sz%d" % i, [128, 16, 128], F32) for i in range(2)]
        of_ = [sb("o_of%d" % i, [128, D // 2], F32) for i in range(2)]
        ob_ = [sb("o_ob%d" % i, [128, D // 2], F32) for i in range(2)]
        junk = sb("o_junk", [128, 512], F32)
        st8 = [sb("o_st%d" % i, [128, 32], F32) for i in range(2)]
        on = [sb("o_on%d" % i, [128, NH, 128], BF16) for i in range(2)]
        ao = [sb("o_ao%d" % i, [128, NH, 128], BF16) for i in range(2)]
        mix = [sb("o_mix%d" % i, [128, 16, 128], BF16) for i in range(2)]
        xt = [sb("o_xt%d" % i, [128, D], F32) for i in range(2)]
        yt = [sb("o_yt%d" % i, [128, D], F32) for i in range(2)]
        ptr = psm("o_ptr", [128, NH, 128], BF16)
        py = [psm("o_py%d" % i, [128, 512], F32) for i in range(4)]
        P.dma('pool', wo[:], C.w_out[l].rearrange("(c p) n -> p c n", p=128), writes=['wo'])
        P.dma('sp', gpb[:], C.post_g[l:l + 1, :].partition_broadcast(128), writes=['gpb'])
        P.dma('sp', gng[:], C.gng_r[l], writes=['gng'])
        def stageA(t):
            b2 = t % 2
            s8 = st8[b2]
            P.dma('sp', za[b2][:, 0:8, :], C.projT_d[R_ZA:R_ZA + 1024, sl(t, 128)].rearrange("(c p) t -> p c t", p=128), writes=[('za', b2)])
            P.dma('sp', za[b2][:, 8:16, :], C.projT_d[R_ZB:R_ZB + 1024, sl(t, 128)].rearrange("(c p) t -> p c t", p=128), writes=[('za', b2)])
            P.dma('sp', of_[b2][:], C.od_d[0, sl(t, 128), :], writes=[('of', b2)])
            P.dma('sp', ob_[b2][:], C.od_d[1, sl(t, 128), :], writes=[('ob', b2)])
            P.dma('sp', ao[b2][:], C.ao_d[:, :, sl(t, 128)].rearrange("h p t -> p h t"), writes=[('ao', b2)])
            P.dma('sp', xt[b2][:], x_d[sl(t, 128), :], writes=[('xt', b2)])
            P.op('act', lambda e, b2=b2: e.activation(out=sz[b2][:], in_=za[b2][:], func=AF.Silu), reads=[('za', b2)], writes=[('sz', b2)])
            P.op('pool', lambda e, b2=b2: e.tensor_tensor(out=of_[b2][:], in0=of_[b2][:], in1=ob_[b2][:], op=ALU.add), reads=[('of', b2), ('ob', b2)], writes=[('of', b2)])
            s8 = st8[b2]
            for h in range(NH):
                P.op('act', lambda e, b2=b2, h=h, s8=s8: e.activation(out=junk[:, 0:128], in_=of_[b2][:, sl(h, 128)], func=AF.Square, accum_out=s8[:, h:h + 1]),
                     reads=[('of', b2)], writes=['junk', ('s8', b2, h)])
            P.op('act', lambda e, s8=s8: e.activation(out=s8[:, 8:16], in_=s8[:, 0:8], func=AF.Sqrt, bias=EPS, scale=1.0 / 128), reads=[('s8', b2, h) for h in range(NH)], writes=[('s8r', b2)])
            P.op('dve', lambda e, s8=s8: e.reciprocal(out=s8[:, 16:24], in_=s8[:, 8:16]), reads=[('s8r', b2)], writes=[('s8r', b2)])
            for h in range(NH):
                P.op('dve' if h % 2 else 'pool', lambda e, b2=b2, h=h, s8=s8: e.tensor_scalar(out=on[b2][:, h, :], in0=of_[b2][:, sl(h, 128)], scalar1=s8[:, 16 + h:17 + h], scalar2=None, op0=ALU.mult),
                     reads=[('of', b2), ('s8r', b2)], writes=[('on', b2)])

        def stageA2(t):
            b2 = t % 2
            for h in range(NH):
                P.op('pe', lambda e, b2=b2, h=h: e.transpose(out=ptr[:, h, :], in_=on[b2][:, h, :], identity=C.idb[:]), reads=[('on', b2), 'idb'], writes=['ptr'], inc=(h == NH - 1))
            P.op('dve', lambda e, b2=b2: e.scalar_tensor_tensor(out=mix[b2][:, 0:8, :], in0=ptr[:], scalar=gng[:, 0:1], in1=sz[b2][:, 0:8, :], op0=ALU.mult, op1=ALU.mult),
                 reads=['ptr', 'gng', ('sz', b2)], writes=[('mix', b2)])
            P.op('pool', lambda e, b2=b2: e.tensor_tensor(out=mix[b2][:, 8:16, :], in0=ao[b2][:], in1=sz[b2][:, 8:16, :], op=ALU.mult),
                 reads=[('ao', b2), ('sz', b2)], writes=[('mix', b2)])

        def stageB(t):
            b2 = t % 2
            s8 = st8[b2]
            for nb in range(4):
                for c in range(16):
                    P.op('pe', lambda e, b2=b2, nb=nb, c=c: e.matmul(py[nb][:], mix[b2][:, c, :], wo[:, c, sl(nb, 512)], start=(c == 0), stop=(c == 15)),
                         reads=[('mix', b2), 'wo'], writes=[('py', nb)], inc=(c == 15))

        def stageB2(t):
            b2 = t % 2
            s8 = st8[b2]
            for nb in range(4):
                P.op('act', lambda e, nb=nb, s8=s8: e.activation(out=junk[:], in_=py[nb][:], func=AF.Square, accum_out=s8[:, 24 + nb:25 + nb]),
                     reads=[('py', nb)], writes=['junk', ('s8y', b2, nb)])
            P.op('dve', lambda e, s8=s8: e.tensor_tensor(out=s8[:, 28:30], in0=s8[:, 24:26], in1=s8[:, 26:28], op=ALU.add), reads=[('s8y', b2, nb) for nb in range(4)], writes=[('s8z', b2)])
            P.op('dve', lambda e, s8=s8: e.tensor_tensor(out=s8[:, 30:31], in0=s8[:, 28:29], in1=s8[:, 29:30], op=ALU.add), reads=[('s8z', b2)], writes=[('s8z', b2)])
            P.op('act', lambda e, s8=s8: e.activation(out=s8[:, 31:32], in_=s8[:, 30:31], func=AF.Sqrt, bias=EPS, scale=1.0 / D), reads=[('s8z', b2)], writes=[('s8w', b2)])
            P.op('dve', lambda e, s8=s8: e.reciprocal(out=s8[:, 31:32], in_=s8[:, 31:32]), reads=[('s8w', b2)], writes=[('s8w', b2)])
            for nb in range(4):
                P.op('dve', lambda e, b2=b2, nb=nb, s8=s8: e.scalar_tensor_tensor(out=yt[b2][:, sl(nb, 512)], in0=py[nb][:], scalar=s8[:, 31:32], in1=gpb[:, sl(nb, 512)], op0=ALU.mult, op1=ALU.mult),
                     reads=[('py', nb), ('s8w', b2), 'gpb'], writes=[('yt', b2)])
            P.op('pool', lambda e, b2=b2: e.tensor_tensor(out=yt[b2][:], in0=yt[b2][:], in1=xt[b2][:], op=ALU.add), reads=[('yt', b2), ('xt', b2)], writes=[('yt', b2)])
            P.dma('pool', xo_d[sl(t, 128), :], yt[b2][:], reads=[('yt', b2)], writes=[('xo', t)])

        NT7 = S // 128
        stageA(0)
        stageA2(0)
        for t in range(NT7):
            if t + 1 < NT7:
                stageA(t + 1)
            stageB(t)
            if t + 1 < NT7:
                stageA2(t + 1)
            stageB2(t)
    P.barrier()


def build(seqs, depth, debug=False):
    nc = bass.Bass("TRN2", target_bir_lowering=False)
    C = Ctx()
    C.nc = nc
    dt = lambda n, s, d, k="ExternalInput": nc.dram_tensor(n, s, d, kind=k).ap()
    C.pre_g = dt("pre_g", [depth, D], F32)
    C.post_g = dt("post_g", [depth, D], F32)
    C.w_in_r = dt("w_in_r", [depth, D, NROW + 32], F32)
    C.conv_r = dt("conv_r", [depth, 128, 120], F32)
    C.a_log = dt("a_log", [depth, 16], F32)
    C.dt_bias = dt("dt_bias", [depth, 16], F32)
    C.gng_r = dt("gng_r", [depth, 128, 1], F32)
    C.gq_r = dt("gq_r", [depth, 128, 4], F32)
    C.gkv_r = dt("gkv_r", [depth, 128, 2], F32)
    C.wuq_r = dt("wuq_r", [depth, 512, 2048], F32)
    C.wukv_r = dt("wukv_r", [depth, 256, 2048], F32)
    C.w_out = dt("w_out", [depth, D, D], F32)
    Smax = max(s for _, s in seqs)
    C.rope_d = dt("rope", [64, 2, Smax], F32)
    cst_d = dt("cmats", [128, 11, 128], F32)
    xs, ys = {}, {}
    for name, S in seqs:
        xs[name] = dt("x_" + name, [S, D], F32)
        ys[name] = dt("y_" + name, [S, D], F32, "ExternalOutput")
    kind_s = "ExternalOutput" if debug else "Internal"
    scr = {}
    for name, S in seqs:
        d = {}
        d['hT_d'] = dt("hT_" + name, [16, 128, S], BF16, kind_s)
        d['projT_d'] = dt("projT_" + name, [NROW, S], BF16, kind_s)
        d['gb_d'] = dt("gb_" + name, [S, 32], F32, kind_s)
        d['gq_d'] = dt("gq_" + name, [NH, 128, S], BF16, kind_s)
        d['gk_d'] = dt("gk_" + name, [NH, 128, S], BF16, kind_s)
        d['gkT_d'] = dt("gkT_" + name, [S, 1024], BF16, kind_s)
        d['gv_d'] = dt("gv_" + name, [S, 1024], BF16, kind_s)
        d['od_d'] = dt("od_" + name, [2, S, 1024], F32, kind_s)
        d['aq_d'] = dt("aq_" + name, [NH, 2, 128, S], BF16, kind_s)
        d['ak_d'] = dt("ak_" + name, [NH, 128, S], BF16, kind_s)
        d['akpe_d'] = dt("akpe_" + name, [64, S], BF16, kind_s)
        d['av_d'] = dt("av_" + name, [S, 1024], BF16, kind_s)
        d['ao_d'] = dt("ao_" + name, [NH, 128, S], BF16, kind_s)
        d['xmid'] = [dt("xm%d_%s" % (i, name), [S, D], F32, kind_s) for i in range(depth - 1)]
        scr[name] = d
    with ExitStack() as st:
        P = Prog(nc, st)
        sb = lambda n, s, d: st.enter_context(nc.sbuf_tensor(n, s, d))
        cm = sb("k_cm", [128, 11, 128], F32)
        C.idb = sb("k_idb", [128, 128], BF16)
        C.onesb = sb("k_onesb", [128, 128], BF16)
        P.dma('sp', cm[:], cst_d, writes=['cm'])
        P.op('dve', lambda e: e.tensor_copy(out=C.idb[:], in_=cm[:, 0, :]), reads=['cm'], writes=['idb'])
        P.op('dve', lambda e: e.tensor_copy(out=C.onesb[:], in_=cm[:, 6, :]), reads=['cm'], writes=['onesb'])
        C.negIb = [sb('k_negIb%d' % d_, [128, 128], BF16) for d_ in range(2)]
        for d_ in range(2):
            P.op('dve', lambda e, d_=d_: e.tensor_copy(out=C.negIb[d_][:], in_=cm[:, 3 + d_, :]), reads=['cm'], writes=['negIb'])
        C.idf = cm[:, 0, :]
        C.tri = [cm[:, 1, :], cm[:, 2, :]]
        C.negI = [cm[:, 3, :], cm[:, 4, :]]
        C.offd = cm[:, 5, :]
        C.onesf = cm[:, 6, :]
        C.bd16 = cm[:, 7, :]
        C.mlev = [cm[:, 8, :], cm[:, 9, :], cm[:, 10, :]]
        P.barrier()
        for l in range(depth):
            for name, S in seqs:
                for k, v in scr[name].items():
                    setattr(C, k, v)
                x_in = xs[name] if l == 0 else scr[name]['xmid'][l - 1]
                x_out = ys[name] if l == depth - 1 else scr[name]['xmid'][l]
                if 1 in PHASES: phase_norm(C, P, x_in, S, l)
                if 2 in PHASES: phase_inproj(C, P, S, l)
                if 3 in PHASES: phase_gdn_prep(C, P, S, l)
                if 4 in PHASES: phase_gdn(C, P, S, l)
                if 5 in PHASES: phase_mla_prep(C, P, S, l)
                if 6 in PHASES: phase_attn(C, P, S, l)
                if 7 in PHASES: phase_out(C, P, S, l, x_in, x_out)
        P.emit()
    return nc


def const_mats():
    i = np.arange(128)
    ident = np.eye(128, dtype=np.float32)
    tri_f = (i[:, None] <= i[None, :]).astype(np.float32)
    tri_b = (i[:, None] >= i[None, :]).astype(np.float32)
    negI_f = np.where(i[:, None] <= i[None, :], 0.0, NEG).astype(np.float32)
    negI_b = np.where(i[:, None] >= i[None, :], 0.0, NEG).astype(np.float32)
    offd = (1.0 - ident).astype(np.float32)
    ones = np.ones((128, 128), np.float32)
    bd = lambda n: (i[:, None] // n == i[None, :] // n).astype(np.float32)
    bd16, bd32, bd64 = bd(16), bd(32), bd(64)
    return np.ascontiguousarray(np.stack([ident, tri_f, tri_b, negI_f, negI_b, offd, ones,
                                          bd16, bd32 - bd16, bd64 - bd32, ones - bd64], axis=1))


def rope_table(S):
    pos = np.arange(S, dtype=np.float32)
    inv = (np.float32(10000.0) ** (-np.arange(0, 64, 2, dtype=np.float32) / np.float32(64))).astype(np.float32)
    ang = (pos[None, :] * inv[:, None]).astype(np.float32)
    c, s = np.cos(ang).astype(np.float32), np.sin(ang).astype(np.float32)
    cosf = np.concatenate([c, c], 0)
    sins = np.concatenate([-s, s], 0)
    return np.ascontiguousarray(np.stack([cosf, sins], axis=1))


def layout_weights(pre_norm_g, post_norm_g, w_in, conv_w, gdn_a_log, gdn_dt_bias, gdn_norm_g,
                   mla_q_norm_g, mla_kv_norm_g, mla_w_uq, mla_w_ukv, w_out):
    depth = w_in.shape[0]
    f = lambda a: np.ascontiguousarray(np.asarray(a, dtype=np.float32))
    w_in = f(w_in)
    o_qkv, o_za, o_b, o_a, o_cq, o_ckv, o_kpe, o_zb = 0, 3072, 4096, 4112, 4128, 4640, 4896, 4960
    kpe_idx = np.arange(o_kpe, o_kpe + 64)
    kpe_sw = np.concatenate([kpe_idx[32:], kpe_idx[:32]])
    cols = np.concatenate([np.arange(o_qkv, o_qkv + 3072), np.arange(o_za, o_za + 1024),
                           np.arange(o_cq, o_cq + 512), np.arange(o_ckv, o_ckv + 256),
                           kpe_idx, kpe_sw, np.arange(o_zb, o_zb + 1024),
                           np.arange(o_b, o_b + 16), np.arange(o_a, o_a + 16)])
    w_in_r = np.ascontiguousarray(w_in[:, :, cols])
    conv_r = np.ascontiguousarray(f(conv_w).reshape(depth, 5, 24, 128).transpose(0, 3, 1, 2).reshape(depth, 128, 120))
    wuq = f(mla_w_uq).reshape(depth, 512, NH, 192)
    wuq_r = np.ascontiguousarray(np.concatenate([wuq[..., :128], wuq[..., 128:192], wuq[..., 160:192], wuq[..., 128:160]], axis=-1).reshape(depth, 512, NH * 256))
    wukv_r = f(mla_w_ukv)
    return {
        "pre_g": f(pre_norm_g), "post_g": f(post_norm_g), "w_in_r": w_in_r, "conv_r": conv_r,
        "a_log": f(gdn_a_log).reshape(depth, 16), "dt_bias": f(gdn_dt_bias).reshape(depth, 16),
        "gng_r": f(gdn_norm_g).reshape(depth, 128, 1),
        "gq_r": np.ascontiguousarray(f(mla_q_norm_g).reshape(depth, 4, 128).transpose(0, 2, 1)),
        "gkv_r": np.ascontiguousarray(f(mla_kv_norm_g).reshape(depth, 2, 128).transpose(0, 2, 1)),
        "wuq_r": wuq_r, "wukv_r": wukv_r, "w_out": f(w_out),
    }


def kernel(x_prompt, x_sample, pre_norm_g, post_norm_g, w_in, conv_w, gdn_a_log, gdn_dt_bias, gdn_norm_g,
           mla_q_norm_g, mla_kv_norm_g, mla_w_uq, mla_w_ukv, w_out):
    x_prompt = np.asarray(x_prompt, dtype=np.float32)
    x_sample = np.asarray(x_sample, dtype=np.float32)
    depth = np.asarray(w_in).shape[0]
    Sp, Ss = x_prompt.shape[1], x_sample.shape[1]
    seqs = [("s", Ss), ("p", Sp)]
    nc = build(seqs, depth)
    base = layout_weights(pre_norm_g, post_norm_g, w_in, conv_w, gdn_a_log, gdn_dt_bias, gdn_norm_g,
                          mla_q_norm_g, mla_kv_norm_g, mla_w_uq, mla_w_ukv, w_out)
    base["rope"] = rope_table(max(Sp, Ss))
    base["cmats"] = const_mats()
    ncores = x_sample.shape[0]
    in_maps = []
    for c in range(ncores):
        m = dict(base)
        m["x_s"] = np.ascontiguousarray(x_sample[c])
        m["x_p"] = np.ascontiguousarray(x_prompt[0])
        in_maps.append(m)
    res = run_bass_kernel_spmd(nc, in_maps, core_ids=list(range(ncores)))
    y_sample = np.stack([np.asarray(res.results[c]["y_s"], dtype=np.float32) for c in range(ncores)], axis=0)
    y_prompt = np.asarray(res.results[0]["y_p"], dtype=np.float32)[None]
    return (y_prompt, y_sample)
```

```python
import numpy as np
import ml_dtypes
from contextlib import ExitStack
import concourse.bass as bass
import concourse.mybir as mybir
from concourse.bass_utils import run_bass_kernel_spmd

F32 = mybir.dt.float32
BF16 = mybir.dt.bfloat16
AF = mybir.ActivationFunctionType
ALU = mybir.AluOpType

D = 2048
NH = 8
EPS = 1e-6
NROW = 6016
R_QKV, R_ZA, R_CQ, R_CKV, R_KPE, R_ZB = 0, 3072, 4096, 4608, 4864, 4992
MLA_SCALE = 192 ** -0.5
GDN_SCALE = 128 ** -0.5
NEG = -30000.0
PHASES = (1, 2, 3, 4, 5, 6, 7)
DBG3 = (3, 8)


class Prog:
    ENG = ('pe', 'act', 'dve', 'pool', 'sp')
    NL = 8

    def __init__(self, nc, stack):
        self.nc = nc
        self.q = {e: [] for e in self.ENG}
        self.sem = {e: stack.enter_context(nc.semaphore('s_' + e)) for e in self.ENG}
        self.cnt = {e: 0 for e in self.ENG}
        self.dq = ('sp', 'pool')
        self.dsem = {q: [stack.enter_context(nc.semaphore('d_%s%d' % (q, i))) for i in range(self.NL)]
                     for q in self.dq}
        self.dn = {q: 0 for q in self.dq}
        self.seen = {e: {} for e in self.ENG}
        self.lastw = {}
        self.readers = {}

    def _semh(self, key):
        return self.sem[key[1]] if key[0] == 'e' else self.dsem[key[1]][key[2]]

    def _collect(self, eng, reads, writes, is_dma):
        deps = {}

        def add(d, raw):
            if d is None:
                return
            key, val, src = d
            if src == eng and key[0] == 'e' and not is_dma:
                if eng == 'pe' or not raw:
                    return
            if self.seen[eng].get(key, 0) >= val:
                return
            if deps.get(key, 0) < val:
                deps[key] = val
        for t in reads:
            add(self.lastw.get(t), True)
        for t in writes:
            add(self.lastw.get(t), False)
            for d in self.readers.get(t, {}).values():
                add(d, False)
        for key, val in deps.items():
            self.seen[eng][key] = val
        return [(self._semh(k), v) for k, v in deps.items()]

    def _register(self, dep, reads, writes):
        for t in writes:
            self.lastw[t] = dep
            self.readers[t] = {}
        for t in reads:
            self.readers.setdefault(t, {})[dep[0]] = dep

    PSUM_NAMES = {'pt', 'pm', 'pba', 'pc', 'pss', 'ptr', 'pcol', 'pv', 'pp', 'psT', 'po', 'pl', 'py', 'psmall', 'pf'}

    def _isps(self, t):
        return (t[0] if isinstance(t, tuple) else t) in self.PSUM_NAMES

    def op(self, eng, fn, reads=(), writes=(), inc=True):
        writes = list(writes) + [t for t in reads if self._isps(t)]
        reads = [t for t in reads if not self._isps(t)]
        waits = self._collect(eng, reads, writes, False)
        if inc:
            self.cnt[eng] += 1
            dep = (('e', eng), self.cnt[eng], eng)
            self.q[eng].append((waits, fn, self.sem[eng], 1))
        else:
            dep = (('e', eng), self.cnt[eng] + 1, eng)
            self.q[eng].append((waits, fn, None, 0))
        self._register(dep, reads, writes)

    def dma(self, q, out, in_, reads=(), writes=()):
        n = self.dn[q]
        lane = n % self.NL
        self.dn[q] += 1
        key = ('d', q, lane)
        val = 16 * (n // self.NL + 1)
        waits = self._collect(q, reads, writes, True)
        prev = val - 16
        if prev > 0 and self.seen[q].get(key, 0) < prev:
            waits.append((self.dsem[q][lane], prev))
            self.seen[q][key] = prev
        self.q[q].append((waits, lambda e: e.dma_start(out=out, in_=in_), self.dsem[q][lane], 16))
        self._register((key, val, q), reads, writes)

    def coll(self, kind, ins, outs, reads=(), writes=(), ncores=8):
        q = 'pool'
        n = self.dn[q]
        lane = n % self.NL
        self.dn[q] += 1
        key = ('d', q, lane)
        val = 16 * (n // self.NL + 1)
        waits = self._collect(q, reads, writes, True)
        prev = val - 16
        if prev > 0 and self.seen[q].get(key, 0) < prev:
            waits.append((self.dsem[q][lane], prev))
            self.seen[q][key] = prev
        rg = [list(range(ncores))]
        self.q[q].append((waits, lambda e: e.collective_compute(kind, ALU.bypass, replica_groups=rg, ins=[a for a in ins], outs=[a for a in outs]), self.dsem[q][lane], 16))
        self._register((key, val, q), reads, writes)

    def barrier(self):
        for e in self.ENG:
            waits = []
            for e2 in self.ENG:
                key = ('e', e2)
                if self.cnt[e2] > self.seen[e].get(key, 0):
                    waits.append((self.sem[e2], self.cnt[e2]))
                    self.seen[e][key] = self.cnt[e2]
            for q in self.dq:
                for lane in range(self.NL):
                    n = self.dn[q]
                    k = (n - lane + self.NL - 1) // self.NL if n > lane else 0
                    val = 16 * k
                    key = ('d', q, lane)
                    if val > self.seen[e].get(key, 0):
                        waits.append((self.dsem[q][lane], val))
                        self.seen[e][key] = val
            self.q[e].append((waits, None, None, 0))
        self.lastw.clear()
        self.readers.clear()

    def emit(self):
        nc = self.nc
        with nc.Block() as block:
            decos = {'pe': block.tensor, 'act': block.scalar, 'dve': block.vector,
                     'pool': block.gpsimd, 'sp': block.sync}
            for name in self.ENG:
                def body(e, name=name):
                    for waits, fn, sem, inc in self.q[name]:
                        for s, v in waits:
                            e.wait_ge(s, v)
                        if fn is not None:
                            r = fn(e)
                            if inc:
                                r.then_inc(sem, inc)
                decos[name](body)


class Ctx:
    uid = 0

    def newuid(self):
        self.uid += 1
        return "_%d" % self.uid


def sl(i, n):
    return slice(i * n, (i + 1) * n)


def run_rr(gens):
    gens = list(gens)
    while gens:
        for g in list(gens):
            try:
                next(g)
            except StopIteration:
                gens.remove(g)


def phase_norm(C, P, x_d, S, l):
    nc = C.nc
    with ExitStack() as st:
        u_ = C.newuid()
        sb = lambda n, s, d: st.enter_context(nc.sbuf_tensor(n + u_, s, d))
        psm = lambda n, s, d: st.enter_context(nc.psum_tensor(n + u_, s, d))
        gb = sb("n_gb", [128, D], F32)
        xt = [sb("n_xt%d" % i, [128, D], F32) for i in range(3)]
        hs = [sb("n_hs%d" % i, [128, D], BF16) for i in range(2)]
        junk = sb("n_junk", [128, D], BF16)
        ss = sb("n_ss", [128, 8], F32)
        hT = [sb("n_hT%d" % i, [128, 16, 512], BF16) for i in range(2)]
        pt = [psm("n_pt%d" % i, [128, 1024], BF16) for i in range(4)]
        P.dma('sp', gb[:], C.pre_g[l:l + 1, :].partition_broadcast(128), writes=['gb'])
        nt = S // 128
        for t in range(nt):
            xb = xt[t % 3]
            xtk = ('xt', t % 3)
            P.dma('sp', xb[:], x_d[sl(t, 128), :], writes=[xtk])
            c0 = (t % 2) * 4
            sst, rst = ('ss', t % 2), ('rs', t % 2)
            P.op('act', lambda e, xb=xb, c0=c0: e.activation(out=junk[:], in_=xb[:], func=AF.Square, accum_out=ss[:, c0:c0 + 1]),
                 reads=[xtk], writes=['junk', sst])
            P.op('act', lambda e, c0=c0: e.activation(out=ss[:, c0 + 1:c0 + 2], in_=ss[:, c0:c0 + 1], func=AF.Sqrt, bias=EPS, scale=1.0 / D),
                 reads=[sst], writes=[rst])
            P.op('dve', lambda e, c0=c0: e.reciprocal(out=ss[:, c0 + 2:c0 + 3], in_=ss[:, c0 + 1:c0 + 2]),
                 reads=[rst], writes=[rst])
            hb = hs[t % 2]
            hk = ('hs', t % 2)
            P.op('dve', lambda e, xb=xb, hb=hb, c0=c0: e.scalar_tensor_tensor(out=hb[:], in0=xb[:], scalar=ss[:, c0 + 2:c0 + 3], in1=gb[:], op0=ALU.mult, op1=ALU.mult),
                 reads=[xtk, rst, 'gb'], writes=[hk])
            tt, j = t // 4, t % 4
            hTb = hT[tt % 2]
            hTk = ('hT', tt % 2)
            for half in range(2):
                pi = (t % 2) * 2 + half
                ptk = ('pt', pi)
                for c in range(8):
                    cc = half * 8 + c
                    P.op('pe', lambda e, hb=hb, cc=cc, pi=pi, c=c: e.transpose(out=pt[pi][:, sl(c, 128)], in_=hb[:, sl(cc, 128)], identity=C.idb[:]),
                         reads=[hk, 'idb'], writes=[ptk], inc=(c == 7))
                if half == 0:
                    P.op('act', lambda e, hTb=hTb, j=j, pi=pi: e.activation(out=hTb[:, 0:8, sl(j, 128)], in_=pt[pi][:].rearrange("p (c t) -> p c t", c=8), func=AF.Copy),
                         reads=[ptk], writes=[hTk])
                else:
                    P.op('pool' if False else 'dve', lambda e, hTb=hTb, j=j, pi=pi: e.tensor_copy(out=hTb[:, 8:16, sl(j, 128)], in_=pt[pi][:].rearrange("p (c t) -> p c t", c=8)),
                         reads=[ptk], writes=[hTk])
            if j == 3:
                P.dma('pool', C.hT_d.rearrange("c p t -> p c t")[:, :, sl(tt, 512)], hTb[:], reads=[hTk], writes=[('hT_d', tt)])
    P.barrier()


def phase_inproj(C, P, S, l):
    nc = C.nc
    with ExitStack() as st:
        u_ = C.newuid()
        sb = lambda n, s, d: st.enter_context(nc.sbuf_tensor(n + u_, s, d))
        psm = lambda n, s, d: st.enter_context(nc.psum_tensor(n + u_, s, d))
        wb = [sb("p_wb%d" % i, [128, 16, 1024], BF16) for i in range(2)]
        wba = sb("p_wba", [128, 16, 32], BF16)
        hT = [sb("p_hT%d" % i, [128, 16, 512], BF16) for i in range(2)]
        yo = [sb("p_yo%d" % i, [128, 8, 512], BF16) for i in range(2)]
        ba = sb("p_ba", [128, 4, 32], F32)
        cst = sb("p_cst", [128, 48], F32)
        tmp = sb("p_tmp", [128, 4, 16], F32)
        gbt = [sb("p_gbt%d" % i, [128, 4, 32], F32) for i in range(2)]
        pm = [psm("p_pm%d" % i, [128, 512], F32) for i in range(6)]
        pba_full = psm("p_pba", [128, 512], F32)
        pba = pba_full[:, 0:128].rearrange("p (j n) -> p j n", j=4)
        ntt = S // 512
        groups = [(g * 8, min(8, 47 - g * 8)) for g in range(6)]
        win = C.w_in_r[l]
        P.dma('pool', wba[:], win[:, NROW:NROW + 32].rearrange("(c p) n -> p c n", p=128), writes=['wba'])
        P.dma('sp', cst[:, 0:16], C.a_log[l:l + 1, :].partition_broadcast(128), writes=['cst'])
        P.dma('sp', cst[:, 16:32], C.dt_bias[l:l + 1, :].partition_broadcast(128), writes=['cst'])
        P.op('act', lambda e: e.activation(out=cst[:, 32:48], in_=cst[:, 0:16], func=AF.Exp), reads=['cst'], writes=['cstA'])
        it = 0
        for gi, (m0, nm) in enumerate(groups):
            wbb = wb[gi % 2]
            wk = ('wb', gi % 2)
            P.dma('pool', wbb[:, :, 0:nm * 128], win[:, m0 * 128:(m0 + nm) * 128].rearrange("(c p) n -> p c n", p=128), writes=[wk])
            for tt in range(ntt):
                hTb = hT[it % 2]
                hk = ('hT', it % 2)
                P.dma('sp', hTb[:], C.hT_d.rearrange("c p t -> p c t")[:, :, sl(tt, 512)], writes=[hk])
                yob = yo[it % 2]
                yk = ('yo', it % 2)
                for m in range(nm):
                    pmi = (it * 8 + m) % 6
                    pk = ('pm', pmi)
                    for c in range(16):
                        P.op('pe', lambda e, wbb=wbb, hTb=hTb, m=m, c=c, pmi=pmi: e.matmul(pm[pmi][:], wbb[:, c, sl(m, 128)], hTb[:, c, :], start=(c == 0), stop=(c == 15)),
                             reads=[wk, hk], writes=[pk], inc=(c == 15))
                    if m % 2 == 0:
                        P.op('act', lambda e, yob=yob, m=m, pmi=pmi: e.activation(out=yob[:, m, :], in_=pm[pmi][:], func=AF.Copy),
                             reads=[pk], writes=[yk])
                    else:
                        P.op('dve', lambda e, yob=yob, m=m, pmi=pmi: e.tensor_copy(out=yob[:, m, :], in_=pm[pmi][:]),
                             reads=[pk], writes=[yk])
                P.dma('pool', C.projT_d[m0 * 128:(m0 + nm) * 128, sl(tt, 512)].rearrange("(m p) t -> p m t", p=128), yob[:, 0:nm, :],
                      reads=[yk], writes=[('projT_d', gi, tt)])
                if gi == 0:
                    for j in range(4):
                        for c in range(16):
                            P.op('pe', lambda e, hTb=hTb, j=j, c=c: e.matmul(pba[:, j, :], hTb[:, c, sl(j, 128)], wba[:, c, :], start=(c == 0), stop=(c == 15)),
                                 reads=[hk, 'wba'], writes=['pba'], inc=(c == 15))
                    gbb = gbt[tt % 2]
                    gk = ('gbt', tt % 2)
                    P.op('dve', lambda e: e.tensor_copy(out=ba[:], in_=pba[:]), reads=['pba'], writes=['ba'])
                    P.op('act', lambda e, gbb=gbb: e.activation(out=gbb[:, :, 16:32], in_=ba[:, :, 0:16], func=AF.Sigmoid), reads=['ba'], writes=[gk])
                    for j in range(4):
                        P.op('dve', lambda e, j=j: e.tensor_tensor(out=tmp[:, j, :], in0=ba[:, j, 16:32], in1=cst[:, 16:32], op=ALU.add),
                             reads=['ba', 'cst'], writes=['tmp'])
                    P.op('act', lambda e: e.activation(out=tmp[:], in_=tmp[:], func=AF.Exp), reads=['tmp'], writes=['tmp'])
                    P.op('act', lambda e: e.activation(out=tmp[:], in_=tmp[:], func=AF.Ln, bias=1.0), reads=['tmp'], writes=['tmp'])
                    for j in range(4):
                        P.op('dve', lambda e, j=j, gbb=gbb: e.scalar_tensor_tensor(out=gbb[:, j, 0:16], in0=tmp[:, j, :], scalar=-1.0, in1=cst[:, 32:48], op0=ALU.mult, op1=ALU.mult),
                             reads=['tmp', 'cstA'], writes=[gk])
                    P.dma('pool', C.gb_d[sl(tt, 512), :].rearrange("(j p) n -> p j n", p=128), gbb[:], reads=[gk], writes=[('gb_d', tt)])
                it += 1
    P.barrier()


def phase_gdn_prep(C, P, S, l):
    nc = C.nc
    with ExitStack() as st:
        u_ = C.newuid()
        sb = lambda n, s, d: st.enter_context(nc.sbuf_tensor(n + u_, s, d))
        psm = lambda n, s, d: st.enter_context(nc.psum_tensor(n + u_, s, d))
        cw = sb("c_cw", [128, 120], F32)
        dg = sb("c_dg", [128, 120, 128], BF16)
        raw = [sb("c_raw%d" % i, [128, S + 4], BF16) for i in range(2)]
        NI = 3
        act = [sb("c_act%d" % i, [128, 512], F32) for i in range(NI)]
        sq = [sb("c_sq%d" % i, [128, 512], BF16) for i in range(NI)]
        rr = [sb("c_rr%d" % i, [128, 512], F32) for i in range(NI)]
        ofm = [sb("c_ofm%d" % i, [128, 512], BF16) for i in range(NI)]
        otm = [sb("c_otm%d" % i, [128, 4, 128], BF16) for i in range(NI)]
        pc = [psm("c_pc%d" % i, [128, 512], F32) for i in range(NI)]
        pss = [psm("c_pss%d" % i, [128, 512], F32) for i in range(NI)]
        P.dma('sp', cw[:], C.conv_r[l], writes=['cw'])
        for i in range(120):
            P.op('dve' if i % 2 else 'act', (lambda e, i=i: e.tensor_scalar(out=dg[:, i, :], in0=C.idf[:], scalar1=cw[:, i:i + 1], scalar2=None, op0=ALU.mult)) if i % 2 else
                 (lambda e, i=i: e.activation(out=dg[:, i, :], in_=C.idf[:], func=AF.Copy, scale=cw[:, i:i + 1])),
                 reads=['cw', 'idf'], writes=[('dg', i)])
        ntt = S // 512

        def tile(which, h, ct, rb, rk, tt, b2):
            pk = ('pc', b2)
            for j in range(5):
                P.op('pe', lambda e, j=j: e.matmul(pc[b2][:], dg[:, j * 24 + ct, :], rb[:, tt * 512 + j:tt * 512 + j + 512], start=(j == 0), stop=(j == 4)),
                     reads=[rk, ('dg', j * 24 + ct)], writes=[pk], inc=(j == 4))
            yield
            ab, ak = act[b2], ('act', b2)
            P.op('act', lambda e: e.activation(out=ab[:], in_=pc[b2][:], func=AF.Silu), reads=[pk], writes=[ak])
            ob, ok = ofm[b2], ('ofm', b2)
            if which < 2:
                sqb, sk = sq[b2], ('sq', b2)
                P.op('pool', lambda e: e.tensor_tensor(out=sqb[:], in0=ab[:], in1=ab[:], op=ALU.mult), reads=[ak], writes=[sk])
                yield
                psk = ('pss', b2)
                P.op('pe', lambda e: e.matmul(pss[b2][:], C.onesb[:], sqb[:], start=True, stop=True), reads=[sk, 'onesb'], writes=[psk])
                yield
                rb_, rrk = rr[b2], ('rr', b2)
                P.op('act', lambda e: e.activation(out=rb_[:], in_=pss[b2][:], func=AF.Sqrt, bias=EPS, scale=1.0), reads=[psk], writes=[rrk])
                yield
                P.op('dve', lambda e: e.reciprocal(out=rb_[:], in_=rb_[:]), reads=[rrk], writes=[rrk])
                scl = GDN_SCALE if which == 0 else 1.0
                P.op('dve', lambda e: e.scalar_tensor_tensor(out=ob[:], in0=ab[:], scalar=scl, in1=rb_[:], op0=ALU.mult, op1=ALU.mult),
                     reads=[ak, rrk], writes=[ok])
                dst = C.gq_d if which == 0 else C.gk_d
                P.dma('pool', dst[h, :, sl(tt, 512)], ob[:], reads=[ok], writes=[('gfm', which, h, tt)])
            else:
                P.op('dve', lambda e: e.tensor_copy(out=ob[:], in_=ab[:]), reads=[ak], writes=[ok])
            yield
            if which >= 1:
                tk = pk
                ptv = pc[b2][:, 0:256].bitcast(BF16).rearrange("p (j t) -> p j t", j=4)
                for j in range(4):
                    P.op('pe', lambda e, j=j: e.transpose(out=ptv[:, j, :], in_=ob[:, sl(j, 128)], identity=C.idb[:]),
                         reads=[ok, 'idb'], writes=[tk], inc=(j == 3))
                yield
                otb, otk = otm[b2], ('otm', b2)
                P.op('act', lambda e: e.activation(out=otb[:], in_=ptv, func=AF.Copy), reads=[tk], writes=[otk])
                dst = C.gkT_d if which == 1 else C.gv_d
                P.dma('pool', dst[sl(tt, 512), sl(h, 128)].rearrange("(j p) n -> p j n", p=128), otb[:], reads=[otk], writes=[('gtm', which, h, tt)])

        for which in range(DBG3[0]):
            for h in range(DBG3[1]):
                ct = which * 8 + h
                rb = raw[ct % 2]
                rk = ('raw', ct % 2)
                P.op('pool', lambda e, rb=rb: e.memset(rb[:, 0:2], 0.0), writes=[rk])
                P.op('pool', lambda e, rb=rb: e.memset(rb[:, S + 2:S + 4], 0.0), writes=[rk])
                P.dma('sp', rb[:, 2:S + 2], C.projT_d[ct * 128:(ct + 1) * 128, :], writes=[rk])
                for g0 in range(0, ntt, NI):
                    run_rr([tile(which, h, ct, rb, rk, tt, k) for k, tt in enumerate(range(g0, min(g0 + NI, ntt)))])
    P.barrier()


def phase_gdn(C, P, S, l):
    nc = C.nc
    N = S // 128
    with ExitStack() as st:
        u_ = C.newuid()
        sb = lambda n, s, d: st.enter_context(nc.sbuf_tensor(n + u_, s, d))
        psm = lambda n, s, d: st.enter_context(nc.psum_tensor(n + u_, s, d))
        NB = 2
        kq = [[sb("g_kq%d%d" % (b, d), [128, NH, 256], BF16) for d in range(2)] for b in range(NB)]
        kT = [[sb("g_kT%d%d" % (b, d), [128, NH, 128], BF16) for d in range(2)] for b in range(NB)]
        vT = [[sb("g_vT%d%d" % (b, d), [128, NH, 128], BF16) for d in range(2)] for b in range(NB)]
        gbt = [[sb("g_gb%d%d" % (b, d), [128, 32], F32) for d in range(2)] for b in range(NB)]
        stt = [[sb("g_st%d%d" % (b, d), [128, 48], F32) for d in range(2)] for b in range(NB)]
        NS = 4
        TG = [sb("g_TG%d" % i, [128, 128], F32) for i in range(NS)]
        Dm = [sb("g_D%d" % i, [128, 128], F32) for i in range(NS)]
        tm_ = [sb("g_t%d" % i, [128, 128], F32) for i in range(NS)]
        fA = [sb("g_fA%d" % i, [128, 128], BF16) for i in range(NS)]
        fB = [sb("g_fB%d" % i, [128, 128], BF16) for i in range(NS)]
        fX1 = [sb("g_fX1%d" % i, [128, 128], BF16) for i in range(NS)]
        fY1 = [sb("g_fY1%d" % i, [128, 128], BF16) for i in range(NS)]
        fXY = [[sb("g_fXY%d_%d" % (i, k), [128, 256], BF16) for k in range(3)] for i in range(NS)]
        fP = [[sb("g_fP%d_%d" % (i, k), [128, 128], BF16) for k in range(2)] for i in range(NS)]
        bT = [[sb("g_bT%d_%d" % (i, k), [128, 128], BF16) for k in range(2)] for i in range(NS)]
        bTT = [[sb("g_bTT%d_%d" % (i, k), [128, 128], BF16) for k in range(2)] for i in range(NS)]
        bC = [[sb("g_bC%d_%d" % (i, k), [128, 128], BF16) for k in range(2)] for i in range(NS)]
        bNR = [sb("g_bNR%d" % i, [128, 256], BF16) for i in range(NS)]
        fTTb = [sb("g_fTTb%d" % i, [128, 128], BF16) for i in range(NS)]
        ktl = [sb("g_ktl%d" % i, [128, 128], BF16) for i in range(NS)]
        aT = [sb("g_aT%d" % b, [128, 16, 128], BF16) for b in range(NB)]
        wT = [sb("g_wT%d" % b, [128, 16, 128], BF16) for b in range(NB)]
        uu = [sb("g_uu%d" % b, [128, 16, 128], F32) for b in range(NB)]
        kd = [sb("g_kd%d" % b, [128, 16, 128], BF16) for b in range(NB)]
        Sf = sb("g_S", [128, 16, 128], F32)
        Sb_ = sb("g_Sb", [128, 16, 128], BF16)
        vn = [sb("g_vn%d" % i, [128, 128], BF16) for i in range(4)]
        t2 = [sb("g_t2%d" % i, [128, 128], F32) for i in range(4)]
        ob = [[sb("g_ob%d%d" % (b, d), [128, NH, 128], F32) for d in range(2)] for b in range(NB)]
        pf = [psm("g_pf%d" % i, [128, 512], F32) for i in range(7)]
        psmall = psm("g_psm", [128, 512], F32)

        P.op('pool', lambda e: e.memset(Sf[:], 0.0), writes=[('Sf', ch) for ch in range(16)])
        P.op('pool', lambda e: e.memset(Sb_[:], 0.0), writes=[('Sb', ch) for ch in range(16)])

        pctr = [0]
        for n in range(N):
            b = n % NB
            for d in range(2):
                c = n if d == 0 else N - 1 - n
                P.dma('sp', kq[b][d][:, :, 0:128], C.gk_d[:, :, sl(c, 128)].rearrange("h p t -> p h t"), writes=[('kF', b, d)])
                P.dma('sp', kq[b][d][:, :, 128:256], C.gq_d[:, :, sl(c, 128)].rearrange("h p t -> p h t"), writes=[('qF', b, d)])
                P.dma('sp', kT[b][d][:], C.gkT_d[sl(c, 128), :].rearrange("p (h f) -> p h f", h=NH), writes=[('kT', b, d)])
                P.dma('sp', vT[b][d][:], C.gv_d[sl(c, 128), :].rearrange("p (h f) -> p h f", h=NH), writes=[('vT', b, d)])
                P.dma('sp', gbt[b][d][:], C.gb_d[sl(c, 128), :], writes=[('gbt', b, d)])
            for d in range(2):
                tri = C.tri[d]
                negI = C.negI[d]
                g_ = gbt[b][d][:, d * 8:d * 8 + 8]
                be_ = gbt[b][d][:, 16 + d * 8:16 + d * 8 + 8]
                s_ = stt[b][d]
                sk = ('stt', b, d)
                gk_ = ('gbt', b, d)
                P.op('pe', lambda e, tri=tri, g_=g_: e.matmul(psmall[:, 0:8], tri[:], g_, start=True, stop=True), reads=[gk_, 'consts'], writes=['psmall'])
                P.op('pe', lambda e, g_=g_: e.matmul(psmall[:, 8:16], C.onesf[:], g_, start=True, stop=True), reads=[gk_, 'consts'], writes=['psmall'])
                P.op('dve', lambda e, s_=s_: e.tensor_copy(out=s_[:, 0:8], in_=psmall[:, 0:8]), reads=['psmall'], writes=[sk])
                P.op('dve', lambda e, s_=s_: e.tensor_scalar(out=s_[:, 8:16], in0=psmall[:, 0:8], scalar1=-1.0, scalar2=None, op0=ALU.mult), reads=['psmall'], writes=[sk])
                P.op('act', lambda e, s_=s_: e.activation(out=s_[:, 16:24], in_=psmall[:, 0:8], func=AF.Exp), reads=['psmall'], writes=[sk])
                P.op('act', lambda e, s_=s_: e.activation(out=s_[:, 32:40], in_=psmall[:, 8:16], func=AF.Exp), reads=['psmall'], writes=[sk])
                P.op('dve', lambda e, s_=s_: e.tensor_tensor(out=s_[:, 24:32], in0=psmall[:, 8:16], in1=s_[:, 0:8], op=ALU.subtract), reads=['psmall', sk], writes=[sk])
                P.op('act', lambda e, s_=s_: e.activation(out=s_[:, 24:32], in_=s_[:, 24:32], func=AF.Exp), reads=[sk], writes=[sk])
                P.op('dve', lambda e, s_=s_, be_=be_: e.tensor_scalar(out=s_[:, 40:48], in0=be_, scalar1=-1.0, scalar2=None, op0=ALU.mult), reads=[gk_], writes=[sk])
                def prob(h, i, d=d, tri=tri, g_=g_, be_=be_, s_=s_, sk=sk, gk_=gk_):
                    ch = d * 8 + h
                    kFh = kq[b][d][:, h, 0:128]
                    kqh = kq[b][d][:, h, :]
                    kTh = kT[b][d][:, h, :]
                    vTh = vT[b][d][:, h, :]
                    bk = pf[i]
                    bkk = ('pf', i)
                    pkk, pkk_k = bk[:, 0:256], bkk
                    P.op('pe', lambda e, pkk=pkk, kFh=kFh, kqh=kqh: e.matmul(pkk[:, 0:256], kFh, kqh, start=True, stop=True), reads=[('kF', b, d), ('qF', b, d)], writes=[pkk_k])
                    P.op('act', lambda e, i=i, tri=tri, g_=g_, h=h: e.activation(out=TG[i][:], in_=tri[:], func=AF.Copy, scale=g_[:, h:h + 1]),
                         reads=[gk_, 'consts'], writes=[('TG', i)])
                    pb, pb_k = bk[:, 256:384], bkk
                    P.op('pe', lambda e, pb=pb, i=i: e.matmul(pb[:, 0:128], C.onesf[:], TG[i][:], start=True, stop=False), reads=[('TG', i), 'consts'], writes=[pb_k], inc=False)
                    P.op('pe', lambda e, pb=pb, d=d: e.matmul(pb[:, 0:128], C.idb[:], C.negIb[d][:], start=False, stop=True), reads=['consts', 'idb'], writes=[pb_k])
                    yield
                    P.op('act', lambda e, i=i, pb=pb, s_=s_, h=h: e.activation(out=Dm[i][:], in_=pb[:, 0:128], func=AF.Exp, bias=s_[:, 8 + h:9 + h], scale=1.0),
                         reads=[pb_k, sk], writes=[('D', i)])
                    P.op('dve', lambda e, i=i, pkk=pkk, b=b, ch=ch: e.tensor_tensor(out=aT[b][:, ch, :], in0=pkk[:, 128:256], in1=Dm[i][:], op=ALU.mult),
                         reads=[pkk_k, ('D', i)], writes=[('aT', b, ch)])
                    P.op('dve', lambda e, i=i, pkk=pkk: e.tensor_tensor(out=tm_[i][:], in0=pkk[:, 0:128], in1=Dm[i][:], op=ALU.mult),
                         reads=[pkk_k, ('D', i)], writes=[('tm', i)])
                    A_, Bm_ = fA[i], fB[i]
                    Ak, Bk = ('fA', i), ('fB', i)
                    P.op('dve', lambda e, i=i, A_=A_, be_=be_, h=h: e.scalar_tensor_tensor(out=A_[:], in0=tm_[i][:], scalar=be_[:, h:h + 1], in1=C.offd[:], op0=ALU.mult, op1=ALU.mult),
                         reads=[('tm', i), gk_, 'consts'], writes=[Ak])
                    P.op('pe', lambda e, bk=bk, A_=A_: e.transpose(out=bk[:, 384:448].bitcast(BF16), in_=A_[:], identity=C.idb[:]), reads=[Ak, 'idb'], writes=[bkk])
                    yield
                    P.op('act', lambda e, bk=bk, Bm_=Bm_: e.activation(out=Bm_[:], in_=bk[:, 384:448].bitcast(BF16), func=AF.Copy), reads=[bkk], writes=[Bk])
                    X1, Y1 = fX1[i], fY1[i]
                    P.op('pool', lambda e, X1=X1, A_=A_: e.tensor_tensor(out=X1[:], in0=A_[:], in1=C.bd16[:], op=ALU.mult), reads=[Ak, 'consts'], writes=[('fX1', i)])
                    P.op('pool', lambda e, Y1=Y1, Bm_=Bm_: e.tensor_tensor(out=Y1[:], in0=Bm_[:], in1=C.bd16[:], op=ALU.mult), reads=[Bk, 'consts'], writes=[('fY1', i)])
                    Pp = fP[i]
                    P.op('pool', lambda e, Pp=Pp, X1=X1: e.tensor_tensor(out=Pp[0][:], in0=C.idb[:], in1=X1[:], op=ALU.subtract), reads=[('fX1', i), 'idb'], writes=[('fP', i, 0)])
                    yield
                    XY = fXY[i]
                    Yc, Xc, yk_ = Y1[:], X1[:], [('fX1', i), ('fY1', i)]
                    for lv in range(1, 4):
                        last = (lv == 3)
                        P.op('pe', lambda e, bk=bk, Yc=Yc, Xc=Xc: e.matmul(bk[:, 0:128], Xc, Yc, start=True, stop=True), reads=yk_, writes=[bkk], inc=last)
                        if not last:
                            P.op('pe', lambda e, bk=bk, Yc=Yc, Xc=Xc: e.matmul(bk[:, 128:256], Yc, Xc, start=True, stop=True), reads=yk_, writes=[bkk])
                        yield
                        dst = XY[lv - 1]
                        wc = 128 if last else 256
                        P.op('act', lambda e, bk=bk, dst=dst, wc=wc: e.activation(out=dst[:, 0:wc], in_=bk[:, 0:wc], func=AF.Copy), reads=[bkk], writes=[('fXY', i, lv)])
                        Yc, Xc, yk_ = dst[:, 0:128], dst[:, 128:256], [('fXY', i, lv)]
                        src, dstp = Pp[(lv - 1) % 2], Pp[lv % 2]
                        P.op('pe', lambda e, bk=bk, src=src, Yc=Yc: e.matmul(bk[:, 256:384], Yc, src[:], start=True, stop=True), reads=[('fP', i, (lv - 1) % 2), ('fXY', i, lv)], writes=[bkk])
                        yield
                        if not last:
                            P.op('dve', lambda e, bk=bk, dstp=dstp, src=src: e.tensor_tensor(out=dstp[:], in0=src[:], in1=bk[:, 256:384], op=ALU.add), reads=[bkk, ('fP', i, (lv - 1) % 2)], writes=[('fP', i, lv % 2)])
                        else:
                            P.op('dve', lambda e, bk=bk, i=i, src=src: e.tensor_tensor(out=bTT[i][0][:], in0=src[:], in1=bk[:, 256:384], op=ALU.add), reads=[bkk, ('fP', i, (lv - 1) % 2)], writes=[('bTT', i, 0)])
                        yield
                    TTk, TTkk = bTT[i][0], ('bTT', i, 0)
                    Tk, Tkk = bT[i][0], ('bT', i, 0)
                    tpb = bk[:, 384:448].bitcast(BF16)
                    P.op('pe', lambda e, tpb=tpb, TTk=TTk: e.transpose(out=tpb, in_=TTk[:], identity=C.idb[:]), reads=[TTkk, 'idb'], writes=[bkk])
                    P.op('act', lambda e, tpb=tpb, Tk=Tk: e.activation(out=Tk[:], in_=tpb, func=AF.Copy), reads=[bkk], writes=[Tkk])
                    yield
                    TT = fTTb[i]
                    TTk_ = ('fTTb', i)
                    for ci, mk in enumerate(C.mlev):
                        lastc = (ci == 2)
                        Ck = bC[i][0]
                        P.op('pool', lambda e, Ck=Ck, Bm_=Bm_, mk=mk: e.tensor_tensor(out=Ck[:], in0=Bm_[:], in1=mk[:], op=ALU.mult), reads=[Bk, 'consts'], writes=[('bC', i, 0)])
                        P.op('pe', lambda e, bk=bk, Ck=Ck, TTk=TTk: e.matmul(bk[:, 0:128], Ck[:], TTk[:], start=True, stop=True), reads=[('bC', i, 0), TTkk], writes=[bkk])
                        yield
                        NR = bNR[i]
                        P.op('act', lambda e, bk=bk, NR=NR: e.activation(out=NR[:, 0:128], in_=bk[:, 0:128], func=AF.Copy), reads=[bkk], writes=[('bNR', i)])
                        P.op('pe', lambda e, bk=bk, NR=NR, Tk=Tk: e.matmul(bk[:, 256:384], Tk[:], NR[:, 0:128], start=True, stop=True), reads=[('bNR', i), Tkk], writes=[bkk])
                        yield
                        if not lastc:
                            nTT, nT = bTT[i][(ci + 1) % 2], bT[i][(ci + 1) % 2]
                            nTTk, nTk = ('bTT', i, (ci + 1) % 2), ('bT', i, (ci + 1) % 2)
                            P.op('dve', lambda e, bk=bk, nTT=nTT, TTk=TTk: e.tensor_tensor(out=nTT[:], in0=TTk[:], in1=bk[:, 256:384], op=ALU.subtract), reads=[TTkk, bkk], writes=[nTTk])
                            P.op('pe', lambda e, tpb=tpb, nTT=nTT: e.transpose(out=tpb, in_=nTT[:], identity=C.idb[:]), reads=[nTTk, 'idb'], writes=[bkk])
                            yield
                            P.op('act', lambda e, tpb=tpb, nT=nT: e.activation(out=nT[:], in_=tpb, func=AF.Copy), reads=[bkk], writes=[nTk])
                            TTk, TTkk, Tk, Tkk = nTT, nTTk, nT, nTk
                        else:
                            P.op('dve', lambda e, bk=bk, TT=TT, TTk=TTk: e.tensor_tensor(out=TT[:], in0=TTk[:], in1=bk[:, 256:384], op=ALU.subtract), reads=[TTkk, bkk], writes=[TTk_])
                    TTk = TTk_
                    P.op('act', lambda e, i=i, kTh=kTh, s_=s_, h=h: e.activation(out=ktl[i][:], in_=kTh, func=AF.Copy, scale=s_[:, 16 + h:17 + h]),
                         reads=[('kT', b, d), sk], writes=[('ktl', i)])
                    P.op('act', lambda e, kTh=kTh, s_=s_, h=h, b=b, ch=ch: e.activation(out=kd[b][:, ch, :], in_=kTh, func=AF.Copy, scale=s_[:, 24 + h:25 + h]),
                         reads=[('kT', b, d), sk], writes=[('kd', b, ch)])
                    pu, pu_k = bk[:, 0:256], bkk
                    P.op('pe', lambda e, pu=pu, TT=TT, vTh=vTh: e.matmul(pu[:, 0:128], TT[:], vTh, start=True, stop=True), reads=[TTk, ('vT', b, d)], writes=[pu_k], inc=False)
                    P.op('pe', lambda e, pu=pu, TT=TT, i=i: e.matmul(pu[:, 128:256], ktl[i][:], TT[:], start=True, stop=True), reads=[TTk, ('ktl', i)], writes=[pu_k])
                    yield
                    P.op('act', lambda e, pu=pu, be_=be_, h=h, b=b, ch=ch: e.activation(out=uu[b][:, ch, :], in_=pu[:, 0:128], func=AF.Copy, scale=be_[:, h:h + 1]),
                         reads=[pu_k, gk_], writes=[('uu', b, ch)])
                    P.op('dve', lambda e, pu=pu, b=b, ch=ch: e.tensor_copy(out=wT[b][:, ch, :], in_=pu[:, 128:256]), reads=[pu_k], writes=[('wT', b, ch)])
                for g0 in range(0, NH, NS):
                    run_rr([prob(g0 + k, k) for k in range(NS)])
            for d in range(2):
                s_ = stt[b][d]
                sk = ('stt', b, d)
                c = n if d == 0 else N - 1 - n
                def chain(h, vi, d=d, s_=s_, sk=sk):
                    ch = d * 8 + h
                    qFh = kq[b][d][:, h, 128:256]
                    sbk = 3 + vi
                    p1, p1_k = pf[sbk][:, 0:256], ('pf', sbk)
                    P.op('pe', lambda e, p1=p1, b=b, ch=ch: e.matmul(p1[:, 0:128], wT[b][:, ch, :], Sb_[:, ch, :], start=True, stop=True),
                         reads=[('wT', b, ch), ('Sb', ch)], writes=[p1_k], inc=False)
                    P.op('pe', lambda e, p1=p1, qFh=qFh, ch=ch: e.matmul(p1[:, 128:256], qFh, Sb_[:, ch, :], start=True, stop=True),
                         reads=[('qF', b, d), ('Sb', ch)], writes=[p1_k])
                    yield
                    P.op('dve', lambda e, p1=p1, s_=s_, h=h, b=b, ch=ch, vi=vi: e.scalar_tensor_tensor(out=vn[vi][:], in0=p1[:, 0:128], scalar=s_[:, 40 + h:41 + h], in1=uu[b][:, ch, :], op0=ALU.mult, op1=ALU.add),
                         reads=[p1_k, sk, ('uu', b, ch)], writes=[('vn', vi)])
                    yield
                    p2, p2_k = pf[sbk][:, 256:512], ('pf', sbk)
                    P.op('pe', lambda e, p2=p2, b=b, ch=ch, vi=vi: e.matmul(p2[:, 0:128], aT[b][:, ch, :], vn[vi][:], start=True, stop=True),
                         reads=[('aT', b, ch), ('vn', vi)], writes=[p2_k], inc=False)
                    P.op('pe', lambda e, p2=p2, b=b, ch=ch, vi=vi: e.matmul(p2[:, 128:256], kd[b][:, ch, :], vn[vi][:], start=True, stop=True),
                         reads=[('kd', b, ch), ('vn', vi)], writes=[p2_k])
                    yield
                    P.op('act', lambda e, p2=p2, vi=vi: e.activation(out=t2[vi][:], in_=p2[:, 0:128], func=AF.Copy), reads=[p2_k], writes=[('t2', vi)])
                    P.op('dve', lambda e, p1=p1, s_=s_, h=h, b=b, d=d, vi=vi: e.scalar_tensor_tensor(out=ob[b][d][:, h, :], in0=p1[:, 128:256], scalar=s_[:, 16 + h:17 + h], in1=t2[vi][:], op0=ALU.mult, op1=ALU.add),
                         reads=[p1_k, sk, ('t2', vi)], writes=[('ob', b, d)])
                    P.op('dve', lambda e, p2=p2, s_=s_, h=h, ch=ch: e.scalar_tensor_tensor(out=Sf[:, ch, :], in0=Sf[:, ch, :], scalar=s_[:, 32 + h:33 + h], in1=p2[:, 128:256], op0=ALU.mult, op1=ALU.add),
                         reads=[p2_k, sk, ('Sf', ch)], writes=[('Sf', ch)])
                    P.op('pool', lambda e, ch=ch: e.tensor_copy(out=Sb_[:, ch, :], in_=Sf[:, ch, :]), reads=[('Sf', ch)], writes=[('Sb', ch)])
                for g0 in range(0, NH, 4):
                    run_rr([chain(g0 + k, k) for k in range(4)])
                P.dma('pool', C.od_d[d, sl(c, 128), :].rearrange("p (h f) -> p h f", h=NH), ob[b][d][:], reads=[('ob', b, d)], writes=[('od_d', d, c)])
    P.barrier()


def phase_mla_prep(C, P, S, l, pos0=0):
    nc = C.nc
    with ExitStack() as st:
        u_ = C.newuid()
        sb = lambda n, s, d: st.enter_context(nc.sbuf_tensor(n + u_, s, d))
        psm = lambda n, s, d: st.enter_context(nc.psum_tensor(n + u_, s, d))
        wst = sb("m_wst", [128, 4, 2048], F32)
        wq = sb("m_wq", [128, 4, 2048], BF16)
        wkv = sb("m_wkv", [128, 2, 2048], BF16)
        gq = sb("m_gq", [128, 4], F32)
        gkv = sb("m_gkv", [128, 2], F32)
        cq = [sb("m_cq%d" % i, [128, 4, 512], BF16) for i in range(2)]
        ckv = [sb("m_ckv%d" % i, [128, 2, 512], BF16) for i in range(2)]
        kpe = [sb("m_kpe%d" % i, [64, 2, 512], BF16) for i in range(2)]
        sqq = sb("m_sqq", [128, 4, 512], BF16)
        sqk = sb("m_sqk", [128, 2, 512], BF16)
        rq = sb("m_rq", [128, 512], F32)
        rkv = sb("m_rkv", [128, 512], F32)
        rkc = sb("m_rkc", [128, 4], F32)
        cs = [sb("m_cs%d" % i, [64, 2, 512], F32) for i in range(2)]
        csr = sb("m_csr", [64, 2, 512], F32)
        t1 = [sb("m_t1%d" % i, [64, 512], F32) for i in range(2)]
        t2 = [sb("m_t2%d" % i, [64, 512], F32) for i in range(2)]
        oq = [sb("m_oq%d" % i, [128, 512], BF16) for i in range(3)]
        ope = [sb("m_ope%d" % i, [64, 512], BF16) for i in range(3)]
        ov = [sb("m_ov%d" % i, [128, 4, 1024], BF16) for i in range(2)]
        pm = [psm("m_pm%d" % i, [128, 512], F32) for i in range(4)]
        pp = [psm("m_pp%d" % i, [64, 512], F32) for i in range(2)]
        pcol = psm("m_pcol", [128, 512], F32)[:, 0:4]
        pv = psm("m_pv", [128, 512], F32)
        P.dma('sp', gq[:], C.gq_r[l], writes=['gq'])
        P.dma('sp', gkv[:], C.gkv_r[l], writes=['gkv'])
        P.dma('sp', wst[:], C.wuq_r[l].rearrange("(c p) n -> p c n", p=128), writes=['wst'])
        for c in range(4):
            P.op('dve' if c % 2 else 'pool', lambda e, c=c: e.tensor_scalar(out=wq[:, c, :], in0=wst[:, c, :], scalar1=gq[:, c:c + 1], scalar2=None, op0=ALU.mult),
                 reads=['wst', 'gq'], writes=['wq'])
        P.dma('sp', wst[:, 0:2, :], C.wukv_r[l].rearrange("(c p) n -> p c n", p=128), reads=[], writes=['wst'])
        for c in range(2):
            P.op('dve' if c % 2 else 'pool', lambda e, c=c: e.tensor_scalar(out=wkv[:, c, :], in0=wst[:, c, :], scalar1=gkv[:, c:c + 1], scalar2=None, op0=ALU.mult),
                 reads=['wst', 'gkv'], writes=['wkv'])
        ntt = S // 512
        oc = 0
        for tt in range(ntt):
            b2 = tt % 2
            P.dma('sp', cq[b2][:], C.projT_d[R_CQ:R_CQ + 512, sl(tt, 512)].rearrange("(c p) t -> p c t", p=128), writes=[('cq', b2)])
            P.dma('sp', ckv[b2][:], C.projT_d[R_CKV:R_CKV + 256, sl(tt, 512)].rearrange("(c p) t -> p c t", p=128), writes=[('ckv', b2)])
            P.dma('sp', kpe[b2][:], C.projT_d[R_KPE:R_KPE + 128, sl(tt, 512)].rearrange("(c p) t -> p c t", p=64), writes=[('kpe', b2)])
            P.dma('sp', cs[b2][:], C.rope_d[:, :, pos0 + tt * 512:pos0 + (tt + 1) * 512], writes=[('cs', b2)])
            P.op('pool', lambda e, b2=b2: e.tensor_tensor(out=sqq[:], in0=cq[b2][:], in1=cq[b2][:], op=ALU.mult), reads=[('cq', b2)], writes=['sqq'])
            P.op('pool', lambda e, b2=b2: e.tensor_tensor(out=sqk[:], in0=ckv[b2][:], in1=ckv[b2][:], op=ALU.mult), reads=[('ckv', b2)], writes=['sqk'])
            for c in range(4):
                P.op('pe', lambda e, c=c: e.matmul(pm[0][:], C.onesb[:], sqq[:, c, :], start=(c == 0), stop=(c == 3)), reads=['sqq', 'onesb'], writes=[('pm', 0)], inc=(c == 3))
            P.op('act', lambda e: e.activation(out=rq[:], in_=pm[0][:], func=AF.Sqrt, bias=EPS, scale=1.0 / 512), reads=[('pm', 0)], writes=['rq'])
            P.op('dve', lambda e: e.reciprocal(out=rq[:], in_=rq[:]), reads=['rq'], writes=['rq'])
            P.op('dve', lambda e: e.tensor_scalar(out=rq[:], in0=rq[:], scalar1=MLA_SCALE, scalar2=None, op0=ALU.mult), reads=['rq'], writes=['rq'])
            for c in range(2):
                P.op('pe', lambda e, c=c: e.matmul(pm[1][:], C.onesb[:], sqk[:, c, :], start=(c == 0), stop=(c == 1)), reads=['sqk', 'onesb'], writes=[('pm', 1)], inc=(c == 1))
            P.op('act', lambda e: e.activation(out=rkv[:], in_=pm[1][:], func=AF.Sqrt, bias=EPS, scale=1.0 / 256), reads=[('pm', 1)], writes=['rkv'])
            P.op('dve', lambda e: e.reciprocal(out=rkv[:], in_=rkv[:]), reads=['rkv'], writes=['rkv'])
            for j in range(4):
                for c in range(2):
                    P.op('pe', lambda e, j=j, c=c: e.matmul(pcol[:, j:j + 1], sqk[:, c, sl(j, 128)], C.onesb[:, 0:1], start=(c == 0), stop=(c == 1)),
                         reads=['sqk', 'onesb'], writes=['pcol'], inc=(c == 1))
            P.op('act', lambda e: e.activation(out=rkc[:], in_=pcol[:], func=AF.Sqrt, bias=EPS, scale=1.0 / 256), reads=['pcol'], writes=['rkc'])
            P.op('dve', lambda e: e.reciprocal(out=rkc[:], in_=rkc[:]), reads=['rkc'], writes=['rkc'])
            for w_ in range(2):
                P.op('pool', lambda e, w_=w_, b2=b2: e.tensor_tensor(out=csr[:, w_, :], in0=cs[b2][:, w_, :], in1=rq[0:64, :], op=ALU.mult), reads=[('cs', b2), 'rq'], writes=['csr'])
            P.op('dve', lambda e, b2=b2: e.tensor_tensor(out=t1[0][:], in0=kpe[b2][:, 0, :], in1=cs[b2][:, 0, :], op=ALU.mult), reads=[('kpe', b2), ('cs', b2)], writes=[('t1', 0)])
            P.op('pool', lambda e, b2=b2: e.tensor_tensor(out=t2[0][:], in0=kpe[b2][:, 1, :], in1=cs[b2][:, 1, :], op=ALU.mult), reads=[('kpe', b2), ('cs', b2)], writes=[('t2', 0)])
            o_ = ope[oc % 3]
            ok = ('ope', oc % 3)
            P.op('dve', lambda e, o_=o_: e.tensor_tensor(out=o_[:], in0=t1[0][:], in1=t2[0][:], op=ALU.add), reads=[('t1', 0), ('t2', 0)], writes=[ok])
            P.dma('pool', C.akpe_d[:, sl(tt, 512)], o_[:], reads=[ok], writes=[('akpe_d', tt)])
            oc += 1
            for h in range(NH):
                pi = (h * 2) % 4
                for c in range(4):
                    P.op('pe', lambda e, c=c, h=h, b2=b2, pi=pi: e.matmul(pm[pi][:], wq[:, c, h * 256:h * 256 + 128], cq[b2][:, c, :], start=(c == 0), stop=(c == 3)),
                         reads=['wq', ('cq', b2)], writes=[('pm', pi)], inc=(c == 3))
                o_ = oq[oc % 3]
                ok = ('oq', oc % 3)
                P.op('dve', lambda e, o_=o_, pi=pi: e.tensor_tensor(out=o_[:], in0=pm[pi][:], in1=rq[:], op=ALU.mult), reads=[('pm', pi), 'rq'], writes=[ok])
                P.dma('pool', C.aq_d[h, 0, :, sl(tt, 512)], o_[:], reads=[ok], writes=[('aq_d', h, 0, tt)])
                for w_ in range(2):
                    for c in range(4):
                        P.op('pe', lambda e, c=c, h=h, b2=b2, w_=w_: e.matmul(pp[w_][:], wq[:, c, h * 256 + 128 + w_ * 64:h * 256 + 192 + w_ * 64], cq[b2][:, c, :], start=(c == 0), stop=(c == 3)),
                             reads=['wq', ('cq', b2)], writes=[('pp', w_)], inc=(c == 3))
                P.op('dve', lambda e: e.tensor_tensor(out=t1[1][:], in0=pp[0][:], in1=csr[:, 0, :], op=ALU.mult), reads=[('pp', 0), 'csr'], writes=[('t1', 1)])
                P.op('dve', lambda e: e.tensor_tensor(out=t2[1][:], in0=pp[1][:], in1=csr[:, 1, :], op=ALU.mult), reads=[('pp', 1), 'csr'], writes=[('t2', 1)])
                o2 = ope[oc % 3]
                ok2 = ('ope', oc % 3)
                P.op('pool', lambda e, o2=o2: e.tensor_tensor(out=o2[:], in0=t1[1][:], in1=t2[1][:], op=ALU.add), reads=[('t1', 1), ('t2', 1)], writes=[ok2])
                P.dma('pool', C.aq_d[h, 1, 0:64, sl(tt, 512)], o2[:], reads=[ok2], writes=[('aq_d', h, 1, tt)])
                oc += 1
                pi = (h * 2 + 1) % 4
                for c in range(2):
                    P.op('pe', lambda e, c=c, h=h, b2=b2, pi=pi: e.matmul(pm[pi][:], wkv[:, c, h * 256:h * 256 + 128], ckv[b2][:, c, :], start=(c == 0), stop=(c == 1)),
                         reads=['wkv', ('ckv', b2)], writes=[('pm', pi)], inc=(c == 1))
                o_ = oq[oc % 3]
                ok = ('oq', oc % 3)
                P.op('dve', lambda e, o_=o_, pi=pi: e.tensor_tensor(out=o_[:], in0=pm[pi][:], in1=rkv[:], op=ALU.mult), reads=[('pm', pi), 'rkv'], writes=[ok])
                P.dma('pool', C.ak_d[h, :, sl(tt, 512)], o_[:], reads=[ok], writes=[('ak_d', h, tt)])
                oc += 1
            ovb = ov[b2]
            for j in range(4):
                for gp in range(2):
                    for c in range(2):
                        rhs = wkv[:, c, :].rearrange("p (h w) -> p h w", h=NH)[:, gp * 4:gp * 4 + 4, 128:256]
                        P.op('pe', lambda e, j=j, c=c, b2=b2, rhs=rhs: e.matmul(pv[:].rearrange("p (h w) -> p h w", h=4), ckv[b2][:, c, sl(j, 128)], rhs, start=(c == 0), stop=(c == 1)),
                             reads=['wkv', ('ckv', b2)], writes=['pv'], inc=(c == 1))
                    P.op('act', lambda e, j=j, gp=gp, ovb=ovb: e.activation(out=ovb[:, j, gp * 512:(gp + 1) * 512], in_=pv[:], func=AF.Copy, scale=rkc[:, j:j + 1]),
                         reads=['pv', 'rkc'], writes=[('ov', b2)])
            P.dma('pool', C.av_d[sl(tt, 512), :].rearrange("(j p) n -> p j n", p=128), ovb[:], reads=[('ov', b2)], writes=[('av_d', tt)])
    P.barrier()


def phase_attn(C, P, S, l):
    nc = C.nc
    with ExitStack() as st:
        u_ = C.newuid()
        sb = lambda n, s, d: st.enter_context(nc.sbuf_tensor(n + u_, s, d))
        psm = lambda n, s, d: st.enter_context(nc.psum_tensor(n + u_, s, d))
        NK = S // 128
        NQ = S // 512
        kpe = sb("a_kpe", [128, S], BF16)
        kn = sb("a_kn", [128, S], BF16)
        vv = sb("a_vv", [128, NK, 128], BF16)
        qn = [sb("a_qn%d" % i, [128, 512], BF16) for i in range(4)]
        qp = [sb("a_qp%d" % i, [128, 512], BF16) for i in range(4)]
        pT = [sb("a_pT%d" % i, [128, 512], BF16) for i in range(4)]
        acc = [[sb("a_acc%d%d" % (i, k), [128, 512], F32) for k in range(2)] for i in range(2)]
        rs = sb("a_rs", [128, 512], F32)
        oo = [sb("a_oo%d" % i, [128, 512], BF16) for i in range(2)]
        psT = [psm("a_ps%d" % i, [128, 512], F32) for i in range(4)]
        po = [psm("a_po%d" % i, [128, 512], F32) for i in range(2)]
        pl = [psm("a_pl%d" % i, [128, 512], F32) for i in range(2)]
        P.op('pool', lambda e: e.memset(kpe[64:128, :], 0.0), writes=['kpe'])
        for i in range(4):
            P.op('pool', lambda e, i=i: e.memset(qp[i][64:128, :], 0.0), writes=[('qp', i)])
        P.dma('sp', kpe[0:64, :], C.akpe_d[:, :], writes=['kpe'])
        ipr = 0
        for h in range(NH):
            P.dma('sp', kn[:], C.ak_d[h], writes=['kn'])
            for v0 in range(0, NK, 8):
                v1 = min(v0 + 8, NK)
                P.dma('sp', vv[:, v0:v1, :], C.av_d[v0 * 128:v1 * 128, sl(h, 128)].rearrange("(t p) f -> p t f", p=128), writes=['vv'])
            for q0 in range(0, NQ, 2):
                tiles = list(range(q0, min(q0 + 2, NQ)))
                npt = len(tiles)
                qi = [(ipr % 2) * 2 + ab for ab in range(npt)]
                ipr += 1
                for ab, qt in enumerate(tiles):
                    P.dma('sp', qn[qi[ab]][:], C.aq_d[h, 0, :, sl(qt, 512)], writes=[('qn', qi[ab])])
                    P.dma('sp', qp[qi[ab]][0:64, :], C.aq_d[h, 1, 0:64, sl(qt, 512)], writes=[('qp', qi[ab])])

                def qk_exp(kt, s2, qi=qi, npt=npt):
                    for ab in range(npt):
                        P.op('pe', lambda e, ab=ab: e.matmul(psT[s2 + ab][:], kn[:, sl(kt, 128)], qn[qi[ab]][:], start=True, stop=False),
                             reads=['kn', ('qn', qi[ab])], writes=[('psT', s2 + ab)], inc=False)
                    for ab in range(npt):
                        P.op('pe', lambda e, ab=ab: e.matmul(psT[s2 + ab][:], kpe[:, sl(kt, 128)], qp[qi[ab]][:], start=False, stop=True),
                             reads=['kpe', ('qp', qi[ab])], writes=[('psT', s2 + ab)], inc=(ab == npt - 1))
                    for ab in range(npt):
                        P.op('act', lambda e, ab=ab: e.activation(out=pT[s2 + ab][:], in_=psT[s2 + ab][:], func=AF.Exp), reads=[('psT', s2 + ab)], writes=[('pT', s2 + ab)])

                def pv_acc(kt, s2, npt=npt):
                    for ab in range(npt):
                        P.op('pe', lambda e, ab=ab: e.matmul(po[ab][:], vv[:, kt, :], pT[s2 + ab][:], start=(kt == 0), stop=(kt == NK - 1)),
                             reads=['vv', ('pT', s2 + ab)], writes=[('po', ab)], inc=(ab == npt - 1))
                    for ab in range(npt):
                        ae = 'dve' if (kt + ab) % 2 == 0 else 'pool'
                        ab_ = acc[ab][kt % 2]
                        ak_ = ('acc', ab, kt % 2)
                        if kt < 2:
                            P.op(ae, lambda e, ab_=ab_, ab=ab: e.tensor_copy(out=ab_[:], in_=pT[s2 + ab][:]), reads=[('pT', s2 + ab)], writes=[ak_])
                        else:
                            P.op(ae, lambda e, ab_=ab_, ab=ab: e.tensor_tensor(out=ab_[:], in0=ab_[:], in1=pT[s2 + ab][:], op=ALU.add), reads=[('pT', s2 + ab), ak_], writes=[ak_])
                prev = None
                for kt in range(NK):
                    s2 = (kt % 2) * 2
                    qk_exp(kt, s2)
                    if prev is not None:
                        pv_acc(*prev)
                    prev = (kt, s2)
                pv_acc(*prev)
                for ab, qt in enumerate(tiles):
                    P.op('pe', lambda e, ab=ab: e.matmul(pl[ab][:], C.onesf[:], acc[ab][0][:], start=True, stop=(NK < 2)), reads=['consts', ('acc', ab, 0)], writes=[('pl', ab)], inc=(NK < 2))
                    if NK >= 2:
                        P.op('pe', lambda e, ab=ab: e.matmul(pl[ab][:], C.onesf[:], acc[ab][1][:], start=False, stop=True), reads=['consts', ('acc', ab, 1)], writes=[('pl', ab)])
                    P.op('dve', lambda e, ab=ab: e.reciprocal(out=rs[:], in_=pl[ab][:]), reads=[('pl', ab)], writes=['rs'])
                    P.op('dve', lambda e, ab=ab: e.tensor_tensor(out=oo[ab][:], in0=po[ab][:], in1=rs[:], op=ALU.mult), reads=[('po', ab), 'rs'], writes=[('oo', ab)])
                    P.dma('pool', C.ao_d[h, :, sl(qt, 512)], oo[ab][:], reads=[('oo', ab)], writes=[('ao_d', h, qt)])
    P.barrier()


def phase_out(C, P, S, l, x_d, xo_d):
    nc = C.nc
    with ExitStack() as st:
        u_ = C.newuid()
        sb = lambda n, s, d: st.enter_context(nc.sbuf_tensor(n + u_, s, d))
        psm = lambda n, s, d: st.enter_context(nc.psum_tensor(n + u_, s, d))
        wo = sb("o_wo", [128, 16, D], BF16)
        gpb = sb("o_gpb", [128, D], F32)
        gng = sb("o_gng", [128, 1], F32)
        za = [sb("o_za%d" % i, [128, 16, 128], BF16) for i in range(2)]
        sz = [sb("o_sz%d" % i, [128, 16, 128], F32) for i in range(2)]
        of_ = [sb("o_of%d" % i, [128, D // 2], F32) for i in range(2)]
        ob_ = [sb("o_ob%d" % i, [128, D // 2], F32) for i in range(2)]
        junk = sb("o_junk", [128, 512], F32)
        st8 = [sb("o_st%d" % i, [128, 32], F32) for i in range(2)]
        on = [sb("o_on%d" % i, [128, NH, 128], BF16) for i in range(2)]
        ao = [sb("o_ao%d" % i, [128, NH, 128], BF16) for i in range(2)]
        mix = [sb("o_mix%d" % i, [128, 16, 128], BF16) for i in range(2)]
        xt = [sb("o_xt%d" % i, [128, D], F32) for i in range(2)]
        yt = [sb("o_yt%d" % i, [128, D], F32) for i in range(2)]
        ptr = psm("o_ptr", [128, NH, 128], BF16)
        py = [psm("o_py%d" % i, [128, 512], F32) for i in range(4)]
        P.dma('pool', wo[:], C.w_out[l].rearrange("(c p) n -> p c n", p=128), writes=['wo'])
        P.dma('sp', gpb[:], C.post_g[l:l + 1, :].partition_broadcast(128), writes=['gpb'])
        P.dma('sp', gng[:], C.gng_r[l], writes=['gng'])
        def stageA(t):
            b2 = t % 2
            s8 = st8[b2]
            P.dma('sp', za[b2][:, 0:8, :], C.projT_d[R_ZA:R_ZA + 1024, sl(t, 128)].rearrange("(c p) t -> p c t", p=128), writes=[('za', b2)])
            P.dma('sp', za[b2][:, 8:16, :], C.projT_d[R_ZB:R_ZB + 1024, sl(t, 128)].rearrange("(c p) t -> p c t", p=128), writes=[('za', b2)])
            P.dma('sp', of_[b2][:], C.od_d[0, sl(t, 128), :], writes=[('of', b2)])
            P.dma('sp', ob_[b2][:], C.od_d[1, sl(t, 128), :], writes=[('ob', b2)])
            P.dma('sp', ao[b2][:], C.ao_d[:, :, sl(t, 128)].rearrange("h p t -> p h t"), writes=[('ao', b2)])
            P.dma('sp', xt[b2][:], x_d[sl(t, 128), :], writes=[('xt', b2)])
            P.op('act', lambda e, b2=b2: e.activation(out=sz[b2][:], in_=za[b2][:], func=AF.Silu), reads=[('za', b2)], writes=[('sz', b2)])
            P.op('pool', lambda e, b2=b2: e.tensor_tensor(out=of_[b2][:], in0=of_[b2][:], in1=ob_[b2][:], op=ALU.add), reads=[('of', b2), ('ob', b2)], writes=[('of', b2)])
            s8 = st8[b2]
            for h in range(NH):
                P.op('act', lambda e, b2=b2, h=h, s8=s8: e.activation(out=junk[:, 0:128], in_=of_[b2][:, sl(h, 128)], func=AF.Square, accum_out=s8[:, h:h + 1]),
                     reads=[('of', b2)], writes=['junk', ('s8', b2, h)])
            P.op('act', lambda e, s8=s8: e.activation(out=s8[:, 8:16], in_=s8[:, 0:8], func=AF.Sqrt, bias=EPS, scale=1.0 / 128), reads=[('s8', b2, h) for h in range(NH)], writes=[('s8r', b2)])
            P.op('dve', lambda e, s8=s8: e.reciprocal(out=s8[:, 16:24], in_=s8[:, 8:16]), reads=[('s8r', b2)], writes=[('s8r', b2)])
            for h in range(NH):
                P.op('dve' if h % 2 else 'pool', lambda e, b2=b2, h=h, s8=s8: e.tensor_scalar(out=on[b2][:, h, :], in0=of_[b2][:, sl(h, 128)], scalar1=s8[:, 16 + h:17 + h], scalar2=None, op0=ALU.mult),
                     reads=[('of', b2), ('s8r', b2)], writes=[('on', b2)])

        def stageA2(t):
            b2 = t % 2
            for h in range(NH):
                P.op('pe', lambda e, b2=b2, h=h: e.transpose(out=ptr[:, h, :], in_=on[b2][:, h, :], identity=C.idb[:]), reads=[('on', b2), 'idb'], writes=['ptr'], inc=(h == NH - 1))
            P.op('dve', lambda e, b2=b2: e.scalar_tensor_tensor(out=mix[b2][:, 0:8, :], in0=ptr[:], scalar=gng[:, 0:1], in1=sz[b2][:, 0:8, :], op0=ALU.mult, op1=ALU.mult),
                 reads=['ptr', 'gng', ('sz', b2)], writes=[('mix', b2)])
            P.op('pool', lambda e, b2=b2: e.tensor_tensor(out=mix[b2][:, 8:16, :], in0=ao[b2][:], in1=sz[b2][:, 8:16, :], op=ALU.mult),
                 reads=[('ao', b2), ('sz', b2)], writes=[('mix', b2)])

        def stageB(t):
            b2 = t % 2
            s8 = st8[b2]
            for nb in range(4):
                for c in range(16):
                    P.op('pe', lambda e, b2=b2, nb=nb, c=c: e.matmul(py[nb][:], mix[b2][:, c, :], wo[:, c, sl(nb, 512)], start=(c == 0), stop=(c == 15)),
                         reads=[('mix', b2), 'wo'], writes=[('py', nb)], inc=(c == 15))

        def stageB2(t):
            b2 = t % 2
            s8 = st8[b2]
            for nb in range(4):
                P.op('act', lambda e, nb=nb, s8=s8: e.activation(out=junk[:], in_=py[nb][:], func=AF.Square, accum_out=s8[:, 24 + nb:25 + nb]),
                     reads=[('py', nb)], writes=['junk', ('s8y', b2, nb)])
            P.op('dve', lambda e, s8=s8: e.tensor_tensor(out=s8[:, 28:30], in0=s8[:, 24:26], in1=s8[:, 26:28], op=ALU.add), reads=[('s8y', b2, nb) for nb in range(4)], writes=[('s8z', b2)])
            P.op('dve', lambda e, s8=s8: e.tensor_tensor(out=s8[:, 30:31], in0=s8[:, 28:29], in1=s8[:, 29:30], op=ALU.add), reads=[('s8z', b2)], writes=[('s8z', b2)])
            P.op('act', lambda e, s8=s8: e.activation(out=s8[:, 31:32], in_=s8[:, 30:31], func=AF.Sqrt, bias=EPS, scale=1.0 / D), reads=[('s8z', b2)], writes=[('s8w', b2)])
            P.op('dve', lambda e, s8=s8: e.reciprocal(out=s8[:, 31:32], in_=s8[:, 31:32]), reads=[('s8w', b2)], writes=[('s8w', b2)])
            for nb in range(4):
                P.op('dve', lambda e, b2=b2, nb=nb, s8=s8: e.scalar_tensor_tensor(out=yt[b2][:, sl(nb, 512)], in0=py[nb][:], scalar=s8[:, 31:32], in1=gpb[:, sl(nb, 512)], op0=ALU.mult, op1=ALU.mult),
                     reads=[('py', nb), ('s8w', b2), 'gpb'], writes=[('yt', b2)])
            P.op('pool', lambda e, b2=b2: e.tensor_tensor(out=yt[b2][:], in0=yt[b2][:], in1=xt[b2][:], op=ALU.add), reads=[('yt', b2), ('xt', b2)], writes=[('yt', b2)])
            P.dma('pool', xo_d[sl(t, 128), :], yt[b2][:], reads=[('yt', b2)], writes=[('xo', t)])

        NT7 = S // 128
        stageA(0)
        stageA2(0)
        for t in range(NT7):
            if t + 1 < NT7:
                stageA(t + 1)
            stageB(t)
            if t + 1 < NT7:
                stageA2(t + 1)
            stageB2(t)
    P.barrier()


def build(seqs, depth, debug=False):
    nc = bass.Bass("TRN2", target_bir_lowering=False)
    C = Ctx()
    C.nc = nc
    dt = lambda n, s, d, k="ExternalInput": nc.dram_tensor(n, s, d, kind=k).ap()
    C.pre_g = dt("pre_g", [depth, D], F32)
    C.post_g = dt("post_g", [depth, D], F32)
    C.w_in_r = dt("w_in_r", [depth, D, NROW + 32], F32)
    C.conv_r = dt("conv_r", [depth, 128, 120], F32)
    C.a_log = dt("a_log", [depth, 16], F32)
    C.dt_bias = dt("dt_bias", [depth, 16], F32)
    C.gng_r = dt("gng_r", [depth, 128, 1], F32)
    C.gq_r = dt("gq_r", [depth, 128, 4], F32)
    C.gkv_r = dt("gkv_r", [depth, 128, 2], F32)
    C.wuq_r = dt("wuq_r", [depth, 512, 2048], F32)
    C.wukv_r = dt("wukv_r", [depth, 256, 2048], F32)
    C.w_out = dt("w_out", [depth, D, D], F32)
    Smax = max(s for _, s in seqs)
    C.rope_d = dt("rope", [64, 2, Smax], F32)
    cst_d = dt("cmats", [128, 11, 128], F32)
    xs, ys = {}, {}
    for name, S in seqs:
        xs[name] = dt("x_" + name, [S, D], F32)
        ys[name] = dt("y_" + name, [S, D], F32, "ExternalOutput")
    kind_s = "ExternalOutput" if debug else "Internal"
    scr = {}
    for name, S in seqs:
        d = {}
        d['hT_d'] = dt("hT_" + name, [16, 128, S], BF16, kind_s)
        d['projT_d'] = dt("projT_" + name, [NROW, S], BF16, kind_s)
        d['gb_d'] = dt("gb_" + name, [S, 32], F32, kind_s)
        d['gq_d'] = dt("gq_" + name, [NH, 128, S], BF16, kind_s)
        d['gk_d'] = dt("gk_" + name, [NH, 128, S], BF16, kind_s)
        d['gkT_d'] = dt("gkT_" + name, [S, 1024], BF16, kind_s)
        d['gv_d'] = dt("gv_" + name, [S, 1024], BF16, kind_s)
        d['od_d'] = dt("od_" + name, [2, S, 1024], F32, kind_s)
        d['aq_d'] = dt("aq_" + name, [NH, 2, 128, S], BF16, kind_s)
        d['ak_d'] = dt("ak_" + name, [NH, 128, S], BF16, kind_s)
        d['akpe_d'] = dt("akpe_" + name, [64, S], BF16, kind_s)
        d['av_d'] = dt("av_" + name, [S, 1024], BF16, kind_s)
        d['ao_d'] = dt("ao_" + name, [NH, 128, S], BF16, kind_s)
        d['xmid'] = [dt("xm%d_%s" % (i, name), [S, D], F32, kind_s) for i in range(depth - 1)]
        scr[name] = d
    with ExitStack() as st:
        P = Prog(nc, st)
        sb = lambda n, s, d: st.enter_context(nc.sbuf_tensor(n, s, d))
        cm = sb("k_cm", [128, 11, 128], F32)
        C.idb = sb("k_idb", [128, 128], BF16)
        C.onesb = sb("k_onesb", [128, 128], BF16)
        P.dma('sp', cm[:], cst_d, writes=['cm'])
        P.op('dve', lambda e: e.tensor_copy(out=C.idb[:], in_=cm[:, 0, :]), reads=['cm'], writes=['idb'])
        P.op('dve', lambda e: e.tensor_copy(out=C.onesb[:], in_=cm[:, 6, :]), reads=['cm'], writes=['onesb'])
        C.negIb = [sb('k_negIb%d' % d_, [128, 128], BF16) for d_ in range(2)]
        for d_ in range(2):
            P.op('dve', lambda e, d_=d_: e.tensor_copy(out=C.negIb[d_][:], in_=cm[:, 3 + d_, :]), reads=['cm'], writes=['negIb'])
        C.idf = cm[:, 0, :]
        C.tri = [cm[:, 1, :], cm[:, 2, :]]
        C.negI = [cm[:, 3, :], cm[:, 4, :]]
        C.offd = cm[:, 5, :]
        C.onesf = cm[:, 6, :]
        C.bd16 = cm[:, 7, :]
        C.mlev = [cm[:, 8, :], cm[:, 9, :], cm[:, 10, :]]
        P.barrier()
        for l in range(depth):
            for name, S in seqs:
                for k, v in scr[name].items():
                    setattr(C, k, v)
                x_in = xs[name] if l == 0 else scr[name]['xmid'][l - 1]
                x_out = ys[name] if l == depth - 1 else scr[name]['xmid'][l]
                if 1 in PHASES: phase_norm(C, P, x_in, S, l)
                if 2 in PHASES: phase_inproj(C, P, S, l)
                if 3 in PHASES: phase_gdn_prep(C, P, S, l)
                if 4 in PHASES: phase_gdn(C, P, S, l)
                if 5 in PHASES: phase_mla_prep(C, P, S, l)
                if 6 in PHASES: phase_attn(C, P, S, l)
                if 7 in PHASES: phase_out(C, P, S, l, x_in, x_out)
        P.emit()
    return nc


def const_mats():
    i = np.arange(128)
    ident = np.eye(128, dtype=np.float32)
    tri_f = (i[:, None] <= i[None, :]).astype(np.float32)
    tri_b = (i[:, None] >= i[None, :]).astype(np.float32)
    negI_f = np.where(i[:, None] <= i[None, :], 0.0, NEG).astype(np.float32)
    negI_b = np.where(i[:, None] >= i[None, :], 0.0, NEG).astype(np.float32)
    offd = (1.0 - ident).astype(np.float32)
    ones = np.ones((128, 128), np.float32)
    bd = lambda n: (i[:, None] // n == i[None, :] // n).astype(np.float32)
    bd16, bd32, bd64 = bd(16), bd(32), bd(64)
    return np.ascontiguousarray(np.stack([ident, tri_f, tri_b, negI_f, negI_b, offd, ones,
                                          bd16, bd32 - bd16, bd64 - bd32, ones - bd64], axis=1))


def rope_table(S):
    pos = np.arange(S, dtype=np.float32)
    inv = (np.float32(10000.0) ** (-np.arange(0, 64, 2, dtype=np.float32) / np.float32(64))).astype(np.float32)
    ang = (pos[None, :] * inv[:, None]).astype(np.float32)
    c, s = np.cos(ang).astype(np.float32), np.sin(ang).astype(np.float32)
    cosf = np.concatenate([c, c], 0)
    sins = np.concatenate([-s, s], 0)
    return np.ascontiguousarray(np.stack([cosf, sins], axis=1))


def layout_weights(pre_norm_g, post_norm_g, w_in, conv_w, gdn_a_log, gdn_dt_bias, gdn_norm_g,
                   mla_q_norm_g, mla_kv_norm_g, mla_w_uq, mla_w_ukv, w_out):
    depth = w_in.shape[0]
    f = lambda a: np.ascontiguousarray(np.asarray(a, dtype=np.float32))
    w_in = f(w_in)
    o_qkv, o_za, o_b, o_a, o_cq, o_ckv, o_kpe, o_zb = 0, 3072, 4096, 4112, 4128, 4640, 4896, 4960
    kpe_idx = np.arange(o_kpe, o_kpe + 64)
    kpe_sw = np.concatenate([kpe_idx[32:], kpe_idx[:32]])
    cols = np.concatenate([np.arange(o_qkv, o_qkv + 3072), np.arange(o_za, o_za + 1024),
                           np.arange(o_cq, o_cq + 512), np.arange(o_ckv, o_ckv + 256),
                           kpe_idx, kpe_sw, np.arange(o_zb, o_zb + 1024),
                           np.arange(o_b, o_b + 16), np.arange(o_a, o_a + 16)])
    w_in_r = np.ascontiguousarray(w_in[:, :, cols])
    conv_r = np.ascontiguousarray(f(conv_w).reshape(depth, 5, 24, 128).transpose(0, 3, 1, 2).reshape(depth, 128, 120))
    wuq = f(mla_w_uq).reshape(depth, 512, NH, 192)
    wuq_r = np.ascontiguousarray(np.concatenate([wuq[..., :128], wuq[..., 128:192], wuq[..., 160:192], wuq[..., 128:160]], axis=-1).reshape(depth, 512, NH * 256))
    wukv_r = f(mla_w_ukv)
    return {
        "pre_g": f(pre_norm_g), "post_g": f(post_norm_g), "w_in_r": w_in_r, "conv_r": conv_r,
        "a_log": f(gdn_a_log).reshape(depth, 16), "dt_bias": f(gdn_dt_bias).reshape(depth, 16),
        "gng_r": f(gdn_norm_g).reshape(depth, 128, 1),
        "gq_r": np.ascontiguousarray(f(mla_q_norm_g).reshape(depth, 4, 128).transpose(0, 2, 1)),
        "gkv_r": np.ascontiguousarray(f(mla_kv_norm_g).reshape(depth, 2, 128).transpose(0, 2, 1)),
        "wuq_r": wuq_r, "wukv_r": wukv_r, "w_out": f(w_out),
    }


def kernel(x_prompt, x_sample, pre_norm_g, post_norm_g, w_in, conv_w, gdn_a_log, gdn_dt_bias, gdn_norm_g,
           mla_q_norm_g, mla_kv_norm_g, mla_w_uq, mla_w_ukv, w_out):
    x_prompt = np.asarray(x_prompt, dtype=np.float32)
    x_sample = np.asarray(x_sample, dtype=np.float32)
    depth = np.asarray(w_in).shape[0]
    Sp, Ss = x_prompt.shape[1], x_sample.shape[1]
    seqs = [("s", Ss), ("p", Sp)]
    nc = build(seqs, depth)
    base = layout_weights(pre_norm_g, post_norm_g, w_in, conv_w, gdn_a_log, gdn_dt_bias, gdn_norm_g,
                          mla_q_norm_g, mla_kv_norm_g, mla_w_uq, mla_w_ukv, w_out)
    base["rope"] = rope_table(max(Sp, Ss))
    base["cmats"] = const_mats()
    ncores = x_sample.shape[0]
    in_maps = []
    for c in range(ncores):
        m = dict(base)
        m["x_s"] = np.ascontiguousarray(x_sample[c])
        m["x_p"] = np.ascontiguousarray(x_prompt[0])
        in_maps.append(m)
    res = run_bass_kernel_spmd(nc, in_maps, core_ids=list(range(ncores)))
    y_sample = np.stack([np.asarray(res.results[c]["y_s"], dtype=np.float32) for c in range(ncores)], axis=0)
    y_prompt = np.asarray(res.results[0]["y_p"], dtype=np.float32)[None]
    return (y_prompt, y_sample)
```

```python
import numpy as np
import ml_dtypes
from contextlib import ExitStack
import concourse.bass as bass
import concourse.mybir as mybir
from concourse.bass_utils import run_bass_kernel_spmd

F32 = mybir.dt.float32
BF16 = mybir.dt.bfloat16
AF = mybir.ActivationFunctionType
ALU = mybir.AluOpType

D = 2048
NH = 8
EPS = 1e-6
NROW = 6016
R_QKV, R_ZA, R_CQ, R_CKV, R_KPE, R_ZB = 0, 3072, 4096, 4608, 4864, 4992
MLA_SCALE = 192 ** -0.5
GDN_SCALE = 128 ** -0.5
NEG = -30000.0
PHASES = (1, 2, 3, 4, 5, 6, 7)
DBG3 = (3, 8)


class Prog:
    ENG = ('pe', 'act', 'dve', 'pool', 'sp')
    NL = 8

    def __init__(self, nc, stack):
        self.nc = nc
        self.q = {e: [] for e in self.ENG}
        self.sem = {e: stack.enter_context(nc.semaphore('s_' + e)) for e in self.ENG}
        self.cnt = {e: 0 for e in self.ENG}
        self.dq = ('sp', 'pool')
        self.dsem = {q: [stack.enter_context(nc.semaphore('d_%s%d' % (q, i))) for i in range(self.NL)]
                     for q in self.dq}
        self.dn = {q: 0 for q in self.dq}
        self.seen = {e: {} for e in self.ENG}
        self.lastw = {}
        self.readers = {}

    def _semh(self, key):
        return self.sem[key[1]] if key[0] == 'e' else self.dsem[key[1]][key[2]]

    def _collect(self, eng, reads, writes, is_dma):
        deps = {}

        def add(d, raw):
            if d is None:
                return
            key, val, src = d
            if src == eng and key[0] == 'e' and not is_dma:
                if eng == 'pe' or not raw:
                    return
            if self.seen[eng].get(key, 0) >= val:
                return
            if deps.get(key, 0) < val:
                deps[key] = val
        for t in reads:
            add(self.lastw.get(t), True)
        for t in writes:
            add(self.lastw.get(t), False)
            for d in self.readers.get(t, {}).values():
                add(d, False)
        for key, val in deps.items():
            self.seen[eng][key] = val
        return [(self._semh(k), v) for k, v in deps.items()]

    def _register(self, dep, reads, writes):
        for t in writes:
            self.lastw[t] = dep
            self.readers[t] = {}
        for t in reads:
            self.readers.setdefault(t, {})[dep[0]] = dep

    PSUM_NAMES = {'pt', 'pm', 'pba', 'pc', 'pss', 'ptr', 'pcol', 'pv', 'pp', 'psT', 'po', 'pl', 'py', 'psmall', 'pf'}

    def _isps(self, t):
        return (t[0] if isinstance(t, tuple) else t) in self.PSUM_NAMES

    def op(self, eng, fn, reads=(), writes=(), inc=True):
        writes = list(writes) + [t for t in reads if self._isps(t)]
        reads = [t for t in reads if not self._isps(t)]
        waits = self._collect(eng, reads, writes, False)
        if inc:
            self.cnt[eng] += 1
            dep = (('e', eng), self.cnt[eng], eng)
            self.q[eng].append((waits, fn, self.sem[eng], 1))
        else:
            dep = (('e', eng), self.cnt[eng] + 1, eng)
            self.q[eng].append((waits, fn, None, 0))
        self._register(dep, reads, writes)

    def dma(self, q, out, in_, reads=(), writes=()):
        n = self.dn[q]
        lane = n % self.NL
        self.dn[q] += 1
        key = ('d', q, lane)
        val = 16 * (n // self.NL + 1)
        waits = self._collect(q, reads, writes, True)
        prev = val - 16
        if prev > 0 and self.seen[q].get(key, 0) < prev:
            waits.append((self.dsem[q][lane], prev))
            self.seen[q][key] = prev
        self.q[q].append((waits, lambda e: e.dma_start(out=out, in_=in_), self.dsem[q][lane], 16))
        self._register((key, val, q), reads, writes)

    def coll(self, kind, ins, outs, reads=(), writes=(), ncores=8):
        q = 'pool'
        n = self.dn[q]
        lane = n % self.NL
        self.dn[q] += 1
        key = ('d', q, lane)
        val = 16 * (n // self.NL + 1)
        waits = self._collect(q, reads, writes, True)
        prev = val - 16
        if prev > 0 and self.seen[q].get(key, 0) < prev:
            waits.append((self.dsem[q][lane], prev))
            self.seen[q][key] = prev
        rg = [list(range(ncores))]
        self.q[q].append((waits, lambda e: e.collective_compute(kind, ALU.bypass, replica_groups=rg, ins=[a for a in ins], outs=[a for a in outs]), self.dsem[q][lane], 16))
        self._register((key, val, q), reads, writes)

    def barrier(self):
        for e in self.ENG:
            waits = []
            for e2 in self.ENG:
                key = ('e', e2)
                if self.cnt[e2] > self.seen[e].get(key, 0):
                    waits.append((self.sem[e2], self.cnt[e2]))
                    self.seen[e][key] = self.cnt[e2]
            for q in self.dq:
                for lane in range(self.NL):
                    n = self.dn[q]
                    k = (n - lane + self.NL - 1) // self.NL if n > lane else 0
                    val = 16 * k
                    key = ('d', q, lane)
                    if val > self.seen[e].get(key, 0):
                        waits.append((self.dsem[q][lane], val))
                        self.seen[e][key] = val
            self.q[e].append((waits, None, None, 0))
        self.lastw.clear()
        self.readers.clear()

    def emit(self):
        nc = self.nc
        with nc.Block() as block:
            decos = {'pe': block.tensor, 'act': block.scalar, 'dve': block.vector,
                     'pool': block.gpsimd, 'sp': block.sync}
            for name in self.ENG:
                def body(e, name=name):
                    for waits, fn, sem, inc in self.q[name]:
                        for s, v in waits:
                            e.wait_ge(s, v)
                        if fn is not None:
                            r = fn(e)
                            if inc:
                                r.then_inc(sem, inc)
                decos[name](body)


class Ctx:
    uid = 0

    def newuid(self):
        self.uid += 1
        return "_%d" % self.uid


def sl(i, n):
    return slice(i * n, (i + 1) * n)


def run_rr(gens):
    gens = list(gens)
    while gens:
        for g in list(gens):
            try:
                next(g)
            except StopIteration:
                gens.remove(g)


def phase_norm(C, P, x_d, S, l):
    nc = C.nc
    with ExitStack() as st:
        u_ = C.newuid()
        sb = lambda n, s, d: st.enter_context(nc.sbuf_tensor(n + u_, s, d))
        psm = lambda n, s, d: st.enter_context(nc.psum_tensor(n + u_, s, d))
        gb = sb("n_gb", [128, D], F32)
        xt = [sb("n_xt%d" % i, [128, D], F32) for i in range(3)]
        hs = [sb("n_hs%d" % i, [128, D], BF16) for i in range(2)]
        junk = sb("n_junk", [128, D], BF16)
        ss = sb("n_ss", [128, 8], F32)
        hT = [sb("n_hT%d" % i, [128, 16, 512], BF16) for i in range(2)]
        pt = [psm("n_pt%d" % i, [128, 1024], BF16) for i in range(4)]
        P.dma('sp', gb[:], C.pre_g[l:l + 1, :].partition_broadcast(128), writes=['gb'])
        nt = S // 128
        for t in range(nt):
            xb = xt[t % 3]
            xtk = ('xt', t % 3)
            P.dma('sp', xb[:], x_d[sl(t, 128), :], writes=[xtk])
            c0 = (t % 2) * 4
            sst, rst = ('ss', t % 2), ('rs', t % 2)
            P.op('act', lambda e, xb=xb, c0=c0: e.activation(out=junk[:], in_=xb[:], func=AF.Square, accum_out=ss[:, c0:c0 + 1]),
                 reads=[xtk], writes=['junk', sst])
            P.op('act', lambda e, c0=c0: e.activation(out=ss[:, c0 + 1:c0 + 2], in_=ss[:, c0:c0 + 1], func=AF.Sqrt, bias=EPS, scale=1.0 / D),
                 reads=[sst], writes=[rst])
            P.op('dve', lambda e, c0=c0: e.reciprocal(out=ss[:, c0 + 2:c0 + 3], in_=ss[:, c0 + 1:c0 + 2]),
                 reads=[rst], writes=[rst])
            hb = hs[t % 2]
            hk = ('hs', t % 2)
            P.op('dve', lambda e, xb=xb, hb=hb, c0=c0: e.scalar_tensor_tensor(out=hb[:], in0=xb[:], scalar=ss[:, c0 + 2:c0 + 3], in1=gb[:], op0=ALU.mult, op1=ALU.mult),
                 reads=[xtk, rst, 'gb'], writes=[hk])
            tt, j = t // 4, t % 4
            hTb = hT[tt % 2]
            hTk = ('hT', tt % 2)
            for half in range(2):
                pi = (t % 2) * 2 + half
                ptk = ('pt', pi)
                for c in range(8):
                    cc = half * 8 + c
                    P.op('pe', lambda e, hb=hb, cc=cc, pi=pi, c=c: e.transpose(out=pt[pi][:, sl(c, 128)], in_=hb[:, sl(cc, 128)], identity=C.idb[:]),
                         reads=[hk, 'idb'], writes=[ptk], inc=(c == 7))
                if half == 0:
                    P.op('act', lambda e, hTb=hTb, j=j, pi=pi: e.activation(out=hTb[:, 0:8, sl(j, 128)], in_=pt[pi][:].rearrange("p (c t) -> p c t", c=8), func=AF.Copy),
                         reads=[ptk], writes=[hTk])
                else:
                    P.op('pool' if False else 'dve', lambda e, hTb=hTb, j=j, pi=pi: e.tensor_copy(out=hTb[:, 8:16, sl(j, 128)], in_=pt[pi][:].rearrange("p (c t) -> p c t", c=8)),
                         reads=[ptk], writes=[hTk])
            if j == 3:
                P.dma('pool', C.hT_d.rearrange("c p t -> p c t")[:, :, sl(tt, 512)], hTb[:], reads=[hTk], writes=[('hT_d', tt)])
    P.barrier()


def phase_inproj(C, P, S, l):
    nc = C.nc
    with ExitStack() as st:
        u_ = C.newuid()
        sb = lambda n, s, d: st.enter_context(nc.sbuf_tensor(n + u_, s, d))
        psm = lambda n, s, d: st.enter_context(nc.psum_tensor(n + u_, s, d))
        wb = [sb("p_wb%d" % i, [128, 16, 1024], BF16) for i in range(2)]
        wba = sb("p_wba", [128, 16, 32], BF16)
        hT = [sb("p_hT%d" % i, [128, 16, 512], BF16) for i in range(2)]
        yo = [sb("p_yo%d" % i, [128, 8, 512], BF16) for i in range(2)]
        ba = sb("p_ba", [128, 4, 32], F32)
        cst = sb("p_cst", [128, 48], F32)
        tmp = sb("p_tmp", [128, 4, 16], F32)
        gbt = [sb("p_gbt%d" % i, [128, 4, 32], F32) for i in range(2)]
        pm = [psm("p_pm%d" % i, [128, 512], F32) for i in range(6)]
        pba_full = psm("p_pba", [128, 512], F32)
        pba = pba_full[:, 0:128].rearrange("p (j n) -> p j n", j=4)
        ntt = S // 512
        groups = [(g * 8, min(8, 47 - g * 8)) for g in range(6)]
        win = C.w_in_r[l]
        P.dma('pool', wba[:], win[:, NROW:NROW + 32].rearrange("(c p) n -> p c n", p=128), writes=['wba'])
        P.dma('sp', cst[:, 0:16], C.a_log[l:l + 1, :].partition_broadcast(128), writes=['cst'])
        P.dma('sp', cst[:, 16:32], C.dt_bias[l:l + 1, :].partition_broadcast(128), writes=['cst'])
        P.op('act', lambda e: e.activation(out=cst[:, 32:48], in_=cst[:, 0:16], func=AF.Exp), reads=['cst'], writes=['cstA'])
        it = 0
        for gi, (m0, nm) in enumerate(groups):
            wbb = wb[gi % 2]
            wk = ('wb', gi % 2)
            P.dma('pool', wbb[:, :, 0:nm * 128], win[:, m0 * 128:(m0 + nm) * 128].rearrange("(c p) n -> p c n", p=128), writes=[wk])
            for tt in range(ntt):
                hTb = hT[it % 2]
                hk = ('hT', it % 2)
                P.dma('sp', hTb[:], C.hT_d.rearrange("c p t -> p c t")[:, :, sl(tt, 512)], writes=[hk])
                yob = yo[it % 2]
                yk = ('yo', it % 2)
                for m in range(nm):
                    pmi = (it * 8 + m) % 6
                    pk = ('pm', pmi)
                    for c in range(16):
                        P.op('pe', lambda e, wbb=wbb, hTb=hTb, m=m, c=c, pmi=pmi: e.matmul(pm[pmi][:], wbb[:, c, sl(m, 128)], hTb[:, c, :], start=(c == 0), stop=(c == 15)),
                             reads=[wk, hk], writes=[pk], inc=(c == 15))
                    if m % 2 == 0:
                        P.op('act', lambda e, yob=yob, m=m, pmi=pmi: e.activation(out=yob[:, m, :], in_=pm[pmi][:], func=AF.Copy),
                             reads=[pk], writes=[yk])
                    else:
                        P.op('dve', lambda e, yob=yob, m=m, pmi=pmi: e.tensor_copy(out=yob[:, m, :], in_=pm[pmi][:]),
                             reads=[pk], writes=[yk])
                P.dma('pool', C.projT_d[m0 * 128:(m0 + nm) * 128, sl(tt, 512)].rearrange("(m p) t -> p m t", p=128), yob[:, 0:nm, :],
                      reads=[yk], writes=[('projT_d', gi, tt)])
                if gi == 0:
                    for j in range(4):
                        for c in range(16):
                            P.op('pe', lambda e, hTb=hTb, j=j, c=c: e.matmul(pba[:, j, :], hTb[:, c, sl(j, 128)], wba[:, c, :], start=(c == 0), stop=(c == 15)),
                                 reads=[hk, 'wba'], writes=['pba'], inc=(c == 15))
                    gbb = gbt[tt % 2]
                    gk = ('gbt', tt % 2)
                    P.op('dve', lambda e: e.tensor_copy(out=ba[:], in_=pba[:]), reads=['pba'], writes=['ba'])
                    P.op('act', lambda e, gbb=gbb: e.activation(out=gbb[:, :, 16:32], in_=ba[:, :, 0:16], func=AF.Sigmoid), reads=['ba'], writes=[gk])
                    for j in range(4):
                        P.op('dve', lambda e, j=j: e.tensor_tensor(out=tmp[:, j, :], in0=ba[:, j, 16:32], in1=cst[:, 16:32], op=ALU.add),
                             reads=['ba', 'cst'], writes=['tmp'])
                    P.op('act', lambda e: e.activation(out=tmp[:], in_=tmp[:], func=AF.Exp), reads=['tmp'], writes=['tmp'])
                    P.op('act', lambda e: e.activation(out=tmp[:], in_=tmp[:], func=AF.Ln, bias=1.0), reads=['tmp'], writes=['tmp'])
                    for j in range(4):
                        P.op('dve', lambda e, j=j, gbb=gbb: e.scalar_tensor_tensor(out=gbb[:, j, 0:16], in0=tmp[:, j, :], scalar=-1.0, in1=cst[:, 32:48], op0=ALU.mult, op1=ALU.mult),
                             reads=['tmp', 'cstA'], writes=[gk])
                    P.dma('pool', C.gb_d[sl(tt, 512), :].rearrange("(j p) n -> p j n", p=128), gbb[:], reads=[gk], writes=[('gb_d', tt)])
                it += 1
    P.barrier()


def phase_gdn_prep(C, P, S, l):
    nc = C.nc
    with ExitStack() as st:
        u_ = C.newuid()
        sb = lambda n, s, d: st.enter_context(nc.sbuf_tensor(n + u_, s, d))
        psm = lambda n, s, d: st.enter_context(nc.psum_tensor(n + u_, s, d))
        cw = sb("c_cw", [128, 120], F32)
        dg = sb("c_dg", [128, 120, 128], BF16)
        raw = [sb("c_raw%d" % i, [128, S + 4], BF16) for i in range(2)]
        NI = 3
        act = [sb("c_act%d" % i, [128, 512], F32) for i in range(NI)]
        sq = [sb("c_sq%d" % i, [128, 512], BF16) for i in range(NI)]
        rr = [sb("c_rr%d" % i, [128, 512], F32) for i in range(NI)]
        ofm = [sb("c_ofm%d" % i, [128, 512], BF16) for i in range(NI)]
        otm = [sb("c_otm%d" % i, [128, 4, 128], BF16) for i in range(NI)]
        pc = [psm("c_pc%d" % i, [128, 512], F32) for i in range(NI)]
        pss = [psm("c_pss%d" % i, [128, 512], F32) for i in range(NI)]
        P.dma('sp', cw[:], C.conv_r[l], writes=['cw'])
        for i in range(120):
            P.op('dve' if i % 2 else 'act', (lambda e, i=i: e.tensor_scalar(out=dg[:, i, :], in0=C.idf[:], scalar1=cw[:, i:i + 1], scalar2=None, op0=ALU.mult)) if i % 2 else
                 (lambda e, i=i: e.activation(out=dg[:, i, :], in_=C.idf[:], func=AF.Copy, scale=cw[:, i:i + 1])),
                 reads=['cw', 'idf'], writes=[('dg', i)])
        ntt = S // 512

        def tile(which, h, ct, rb, rk, tt, b2):
            pk = ('pc', b2)
            for j in range(5):
                P.op('pe', lambda e, j=j: e.matmul(pc[b2][:], dg[:, j * 24 + ct, :], rb[:, tt * 512 + j:tt * 512 + j + 512], start=(j == 0), stop=(j == 4)),
                     reads=[rk, ('dg', j * 24 + ct)], writes=[pk], inc=(j == 4))
            yield
            ab, ak = act[b2], ('act', b2)
            P.op('act', lambda e: e.activation(out=ab[:], in_=pc[b2][:], func=AF.Silu), reads=[pk], writes=[ak])
            ob, ok = ofm[b2], ('ofm', b2)
            if which < 2:
                sqb, sk = sq[b2], ('sq', b2)
                P.op('pool', lambda e: e.tensor_tensor(out=sqb[:], in0=ab[:], in1=ab[:], op=ALU.mult), reads=[ak], writes=[sk])
                yield
                psk = ('pss', b2)
                P.op('pe', lambda e: e.matmul(pss[b2][:], C.onesb[:], sqb[:], start=True, stop=True), reads=[sk, 'onesb'], writes=[psk])
                yield
                rb_, rrk = rr[b2], ('rr', b2)
                P.op('act', lambda e: e.activation(out=rb_[:], in_=pss[b2][:], func=AF.Sqrt, bias=EPS, scale=1.0), reads=[psk], writes=[rrk])
                yield
                P.op('dve', lambda e: e.reciprocal(out=rb_[:], in_=rb_[:]), reads=[rrk], writes=[rrk])
                scl = GDN_SCALE if which == 0 else 1.0
                P.op('dve', lambda e: e.scalar_tensor_tensor(out=ob[:], in0=ab[:], scalar=scl, in1=rb_[:], op0=ALU.mult, op1=ALU.mult),
                     reads=[ak, rrk], writes=[ok])
                dst = C.gq_d if which == 0 else C.gk_d
                P.dma('pool', dst[h, :, sl(tt, 512)], ob[:], reads=[ok], writes=[('gfm', which, h, tt)])
            else:
                P.op('dve', lambda e: e.tensor_copy(out=ob[:], in_=ab[:]), reads=[ak], writes=[ok])
            yield
            if which >= 1:
                tk = pk
                ptv = pc[b2][:, 0:256].bitcast(BF16).rearrange("p (j t) -> p j t", j=4)
                for j in range(4):
                    P.op('pe', lambda e, j=j: e.transpose(out=ptv[:, j, :], in_=ob[:, sl(j, 128)], identity=C.idb[:]),
                         reads=[ok, 'idb'], writes=[tk], inc=(j == 3))
                yield
                otb, otk = otm[b2], ('otm', b2)
                P.op('act', lambda e: e.activation(out=otb[:], in_=ptv, func=AF.Copy), reads=[tk], writes=[otk])
                dst = C.gkT_d if which == 1 else C.gv_d
                P.dma('pool', dst[sl(tt, 512), sl(h, 128)].rearrange("(j p) n -> p j n", p=128), otb[:], reads=[otk], writes=[('gtm', which, h, tt)])

        for which in range(DBG3[0]):
            for h in range(DBG3[1]):
                ct = which * 8 + h
                rb = raw[ct % 2]
                rk = ('raw', ct % 2)
                P.op('pool', lambda e, rb=rb: e.memset(rb[:, 0:2], 0.0), writes=[rk])
                P.op('pool', lambda e, rb=rb: e.memset(rb[:, S + 2:S + 4], 0.0), writes=[rk])
                P.dma('sp', rb[:, 2:S + 2], C.projT_d[ct * 128:(ct + 1) * 128, :], writes=[rk])
                for g0 in range(0, ntt, NI):
                    run_rr([tile(which, h, ct, rb, rk, tt, k) for k, tt in enumerate(range(g0, min(g0 + NI, ntt)))])
    P.barrier()


def phase_gdn(C, P, S, l):
    nc = C.nc
    N = S // 128
    with ExitStack() as st:
        u_ = C.newuid()
        sb = lambda n, s, d: st.enter_context(nc.sbuf_tensor(n + u_, s, d))
        psm = lambda n, s, d: st.enter_context(nc.psum_tensor(n + u_, s, d))
        NB = 2
        kq = [[sb("g_kq%d%d" % (b, d), [128, NH, 256], BF16) for d in range(2)] for b in range(NB)]
        kT = [[sb("g_kT%d%d" % (b, d), [128, NH, 128], BF16) for d in range(2)] for b in range(NB)]
        vT = [[sb("g_vT%d%d" % (b, d), [128, NH, 128], BF16) for d in range(2)] for b in range(NB)]
        gbt = [[sb("g_gb%d%d" % (b, d), [128, 32], F32) for d in range(2)] for b in range(NB)]
        stt = [[sb("g_st%d%d" % (b, d), [128, 48], F32) for d in range(2)] for b in range(NB)]
        NS = 4
        TG = [sb("g_TG%d" % i, [128, 128], F32) for i in range(NS)]
        Dm = [sb("g_D%d" % i, [128, 128], F32) for i in range(NS)]
        tm_ = [sb("g_t%d" % i, [128, 128], F32) for i in range(NS)]
        fA = [sb("g_fA%d" % i, [128, 128], BF16) for i in range(NS)]
        fB = [sb("g_fB%d" % i, [128, 128], BF16) for i in range(NS)]
        fX1 = [sb("g_fX1%d" % i, [128, 128], BF16) for i in range(NS)]
        fY1 = [sb("g_fY1%d" % i, [128, 128], BF16) for i in range(NS)]
        fXY = [[sb("g_fXY%d_%d" % (i, k), [128, 256], BF16) for k in range(3)] for i in range(NS)]
        fP = [[sb("g_fP%d_%d" % (i, k), [128, 128], BF16) for k in range(2)] for i in range(NS)]
        bT = [[sb("g_bT%d_%d" % (i, k), [128, 128], BF16) for k in range(2)] for i in range(NS)]
        bTT = [[sb("g_bTT%d_%d" % (i, k), [128, 128], BF16) for k in range(2)] for i in range(NS)]
        bC = [[sb("g_bC%d_%d" % (i, k), [128, 128], BF16) for k in range(2)] for i in range(NS)]
        bNR = [sb("g_bNR%d" % i, [128, 256], BF16) for i in range(NS)]
        fTTb = [sb("g_fTTb%d" % i, [128, 128], BF16) for i in range(NS)]
        ktl = [sb("g_ktl%d" % i, [128, 128], BF16) for i in range(NS)]
        aT = [sb("g_aT%d" % b, [128, 16, 128], BF16) for b in range(NB)]
        wT = [sb("g_wT%d" % b, [128, 16, 128], BF16) for b in range(NB)]
        uu = [sb("g_uu%d" % b, [128, 16, 128], F32) for b in range(NB)]
        kd = [sb("g_kd%d" % b, [128, 16, 128], BF16) for b in range(NB)]
        Sf = sb("g_S", [128, 16, 128], F32)
        Sb_ = sb("g_Sb", [128, 16, 128], BF16)
        vn = [sb("g_vn%d" % i, [128, 128], BF16) for i in range(4)]
        t2 = [sb("g_t2%d" % i, [128, 128], F32) for i in range(4)]
        ob = [[sb("g_ob%d%d" % (b, d), [128, NH, 128], F32) for d in range(2)] for b in range(NB)]
        pf = [psm("g_pf%d" % i, [128, 512], F32) for i in range(7)]
        psmall = psm("g_psm", [128, 512], F32)

        P.op('pool', lambda e: e.memset(Sf[:], 0.0), writes=[('Sf', ch) for ch in range(16)])
        P.op('pool', lambda e: e.memset(Sb_[:], 0.0), writes=[('Sb', ch) for ch in range(16)])

        pctr = [0]
        for n in range(N):
            b = n % NB
            for d in range(2):
                c = n if d == 0 else N - 1 - n
                P.dma('sp', kq[b][d][:, :, 0:128], C.gk_d[:, :, sl(c, 128)].rearrange("h p t -> p h t"), writes=[('kF', b, d)])
                P.dma('sp', kq[b][d][:, :, 128:256], C.gq_d[:, :, sl(c, 128)].rearrange("h p t -> p h t"), writes=[('qF', b, d)])
                P.dma('sp', kT[b][d][:], C.gkT_d[sl(c, 128), :].rearrange("p (h f) -> p h f", h=NH), writes=[('kT', b, d)])
                P.dma('sp', vT[b][d][:], C.gv_d[sl(c, 128), :].rearrange("p (h f) -> p h f", h=NH), writes=[('vT', b, d)])
                P.dma('sp', gbt[b][d][:], C.gb_d[sl(c, 128), :], writes=[('gbt', b, d)])
            for d in range(2):
                tri = C.tri[d]
                negI = C.negI[d]
                g_ = gbt[b][d][:, d * 8:d * 8 + 8]
                be_ = gbt[b][d][:, 16 + d * 8:16 + d * 8 + 8]
                s_ = stt[b][d]
                sk = ('stt', b, d)
                gk_ = ('gbt', b, d)
                P.op('pe', lambda e, tri=tri, g_=g_: e.matmul(psmall[:, 0:8], tri[:], g_, start=True, stop=True), reads=[gk_, 'consts'], writes=['psmall'])
                P.op('pe', lambda e, g_=g_: e.matmul(psmall[:, 8:16], C.onesf[:], g_, start=True, stop=True), reads=[gk_, 'consts'], writes=['psmall'])
                P.op('dve', lambda e, s_=s_: e.tensor_copy(out=s_[:, 0:8], in_=psmall[:, 0:8]), reads=['psmall'], writes=[sk])
                P.op('dve', lambda e, s_=s_: e.tensor_scalar(out=s_[:, 8:16], in0=psmall[:, 0:8], scalar1=-1.0, scalar2=None, op0=ALU.mult), reads=['psmall'], writes=[sk])
                P.op('act', lambda e, s_=s_: e.activation(out=s_[:, 16:24], in_=psmall[:, 0:8], func=AF.Exp), reads=['psmall'], writes=[sk])
                P.op('act', lambda e, s_=s_: e.activation(out=s_[:, 32:40], in_=psmall[:, 8:16], func=AF.Exp), reads=['psmall'], writes=[sk])
                P.op('dve', lambda e, s_=s_: e.tensor_tensor(out=s_[:, 24:32], in0=psmall[:, 8:16], in1=s_[:, 0:8], op=ALU.subtract), reads=['psmall', sk], writes=[sk])
                P.op('act', lambda e, s_=s_: e.activation(out=s_[:, 24:32], in_=s_[:, 24:32], func=AF.Exp), reads=[sk], writes=[sk])
                P.op('dve', lambda e, s_=s_, be_=be_: e.tensor_scalar(out=s_[:, 40:48], in0=be_, scalar1=-1.0, scalar2=None, op0=ALU.mult), reads=[gk_], writes=[sk])
                def prob(h, i, d=d, tri=tri, g_=g_, be_=be_, s_=s_, sk=sk, gk_=gk_):
                    ch = d * 8 + h
                    kFh = kq[b][d][:, h, 0:128]
                    kqh = kq[b][d][:, h, :]
                    kTh = kT[b][d][:, h, :]
                    vTh = vT[b][d][:, h, :]
                    bk = pf[i]
                    bkk = ('pf', i)
                    pkk, pkk_k = bk[:, 0:256], bkk
                    P.op('pe', lambda e, pkk=pkk, kFh=kFh, kqh=kqh: e.matmul(pkk[:, 0:256], kFh, kqh, start=True, stop=True), reads=[('kF', b, d), ('qF', b, d)], writes=[pkk_k])
                    P.op('act', lambda e, i=i, tri=tri, g_=g_, h=h: e.activation(out=TG[i][:], in_=tri[:], func=AF.Copy, scale=g_[:, h:h + 1]),
                         reads=[gk_, 'consts'], writes=[('TG', i)])
                    pb, pb_k = bk[:, 256:384], bkk
                    P.op('pe', lambda e, pb=pb, i=i: e.matmul(pb[:, 0:128], C.onesf[:], TG[i][:], start=True, stop=False), reads=[('TG', i), 'consts'], writes=[pb_k], inc=False)
                    P.op('pe', lambda e, pb=pb, d=d: e.matmul(pb[:, 0:128], C.idb[:], C.negIb[d][:], start=False, stop=True), reads=['consts', 'idb'], writes=[pb_k])
                    yield
                    P.op('act', lambda e, i=i, pb=pb, s_=s_, h=h: e.activation(out=Dm[i][:], in_=pb[:, 0:128], func=AF.Exp, bias=s_[:, 8 + h:9 + h], scale=1.0),
                         reads=[pb_k, sk], writes=[('D', i)])
                    P.op('dve', lambda e, i=i, pkk=pkk, b=b, ch=ch: e.tensor_tensor(out=aT[b][:, ch, :], in0=pkk[:, 128:256], in1=Dm[i][:], op=ALU.mult),
                         reads=[pkk_k, ('D', i)], writes=[('aT', b, ch)])
                    P.op('dve', lambda e, i=i, pkk=pkk: e.tensor_tensor(out=tm_[i][:], in0=pkk[:, 0:128], in1=Dm[i][:], op=ALU.mult),
                         reads=[pkk_k, ('D', i)], writes=[('tm', i)])
                    A_, Bm_ = fA[i], fB[i]
                    Ak, Bk = ('fA', i), ('fB', i)
                    P.op('dve', lambda e, i=i, A_=A_, be_=be_, h=h: e.scalar_tensor_tensor(out=A_[:], in0=tm_[i][:], scalar=be_[:, h:h + 1], in1=C.offd[:], op0=ALU.mult, op1=ALU.mult),
                         reads=[('tm', i), gk_, 'consts'], writes=[Ak])
                    P.op('pe', lambda e, bk=bk, A_=A_: e.transpose(out=bk[:, 384:448].bitcast(BF16), in_=A_[:], identity=C.idb[:]), reads=[Ak, 'idb'], writes=[bkk])
                    yield
                    P.op('act', lambda e, bk=bk, Bm_=Bm_: e.activation(out=Bm_[:], in_=bk[:, 384:448].bitcast(BF16), func=AF.Copy), reads=[bkk], writes=[Bk])
                    X1, Y1 = fX1[i], fY1[i]
                    P.op('pool', lambda e, X1=X1, A_=A_: e.tensor_tensor(out=X1[:], in0=A_[:], in1=C.bd16[:], op=ALU.mult), reads=[Ak, 'consts'], writes=[('fX1', i)])
                    P.op('pool', lambda e, Y1=Y1, Bm_=Bm_: e.tensor_tensor(out=Y1[:], in0=Bm_[:], in1=C.bd16[:], op=ALU.mult), reads=[Bk, 'consts'], writes=[('fY1', i)])
                    Pp = fP[i]
                    P.op('pool', lambda e, Pp=Pp, X1=X1: e.tensor_tensor(out=Pp[0][:], in0=C.idb[:], in1=X1[:], op=ALU.subtract), reads=[('fX1', i), 'idb'], writes=[('fP', i, 0)])
                    yield
                    XY = fXY[i]
                    Yc, Xc, yk_ = Y1[:], X1[:], [('fX1', i), ('fY1', i)]
                    for lv in range(1, 4):
                        last = (lv == 3)
                        P.op('pe', lambda e, bk=bk, Yc=Yc, Xc=Xc: e.matmul(bk[:, 0:128], Xc, Yc, start=True, stop=True), reads=yk_, writes=[bkk], inc=last)
                        if not last:
                            P.op('pe', lambda e, bk=bk, Yc=Yc, Xc=Xc: e.matmul(bk[:, 128:256], Yc, Xc, start=True, stop=True), reads=yk_, writes=[bkk])
                        yield
                        dst = XY[lv - 1]
                        wc = 128 if last else 256
                        P.op('act', lambda e, bk=bk, dst=dst, wc=wc: e.activation(out=dst[:, 0:wc], in_=bk[:, 0:wc], func=AF.Copy), reads=[bkk], writes=[('fXY', i, lv)])
                        Yc, Xc, yk_ = dst[:, 0:128], dst[:, 128:256], [('fXY', i, lv)]
                        src, dstp = Pp[(lv - 1) % 2], Pp[lv % 2]
                        P.op('pe', lambda e, bk=bk, src=src, Yc=Yc: e.matmul(bk[:, 256:384], Yc, src[:], start=True, stop=True), reads=[('fP', i, (lv - 1) % 2), ('fXY', i, lv)], writes=[bkk])
                        yield
                        if not last:
                            P.op('dve', lambda e, bk=bk, dstp=dstp, src=src: e.tensor_tensor(out=dstp[:], in0=src[:], in1=bk[:, 256:384], op=ALU.add), reads=[bkk, ('fP', i, (lv - 1) % 2)], writes=[('fP', i, lv % 2)])
                        else:
                            P.op('dve', lambda e, bk=bk, i=i, src=src: e.tensor_tensor(out=bTT[i][0][:], in0=src[:], in1=bk[:, 256:384], op=ALU.add), reads=[bkk, ('fP', i, (lv - 1) % 2)], writes=[('bTT', i, 0)])
                        yield
                    TTk, TTkk = bTT[i][0], ('bTT', i, 0)
                    Tk, Tkk = bT[i][0], ('bT', i, 0)
                    tpb = bk[:, 384:448].bitcast(BF16)
                    P.op('pe', lambda e, tpb=tpb, TTk=TTk: e.transpose(out=tpb, in_=TTk[:], identity=C.idb[:]), reads=[TTkk, 'idb'], writes=[bkk])
                    P.op('act', lambda e, tpb=tpb, Tk=Tk: e.activation(out=Tk[:], in_=tpb, func=AF.Copy), reads=[bkk], writes=[Tkk])
                    yield
                    TT = fTTb[i]
                    TTk_ = ('fTTb', i)
                    for ci, mk in enumerate(C.mlev):
                        lastc = (ci == 2)
                        Ck = bC[i][0]
                        P.op('pool', lambda e, Ck=Ck, Bm_=Bm_, mk=mk: e.tensor_tensor(out=Ck[:], in0=Bm_[:], in1=mk[:], op=ALU.mult), reads=[Bk, 'consts'], writes=[('bC', i, 0)])
                        P.op('pe', lambda e, bk=bk, Ck=Ck, TTk=TTk: e.matmul(bk[:, 0:128], Ck[:], TTk[:], start=True, stop=True), reads=[('bC', i, 0), TTkk], writes=[bkk])
                        yield
                        NR = bNR[i]
                        P.op('act', lambda e, bk=bk, NR=NR: e.activation(out=NR[:, 0:128], in_=bk[:, 0:128], func=AF.Copy), reads=[bkk], writes=[('bNR', i)])
                        P.op('pe', lambda e, bk=bk, NR=NR, Tk=Tk: e.matmul(bk[:, 256:384], Tk[:], NR[:, 0:128], start=True, stop=True), reads=[('bNR', i), Tkk], writes=[bkk])
                        yield
                        if not lastc:
                            nTT, nT = bTT[i][(ci + 1) % 2], bT[i][(ci + 1) % 2]
                            nTTk, nTk = ('bTT', i, (ci + 1) % 2), ('bT', i, (ci + 1) % 2)
                            P.op('dve', lambda e, bk=bk, nTT=nTT, TTk=TTk: e.tensor_tensor(out=nTT[:], in0=TTk[:], in1=bk[:, 256:384], op=ALU.subtract), reads=[TTkk, bkk], writes=[nTTk])
                            P.op('pe', lambda e, tpb=tpb, nTT=nTT: e.transpose(out=tpb, in_=nTT[:], identity=C.idb[:]), reads=[nTTk, 'idb'], writes=[bkk])
                            yield
                            P.op('act', lambda e, tpb=tpb, nT=nT: e.activation(out=nT[:], in_=tpb, func=AF.Copy), reads=[bkk], writes=[nTk])
                            TTk, TTkk, Tk, Tkk = nTT, nTTk, nT, nTk
                        else:
                            P.op('dve', lambda e, bk=bk, TT=TT, TTk=TTk: e.tensor_tensor(out=TT[:], in0=TTk[:], in1=bk[:, 256:384], op=ALU.subtract), reads=[TTkk, bkk], writes=[TTk_])
                    TTk = TTk_
                    P.op('act', lambda e, i=i, kTh=kTh, s_=s_, h=h: e.activation(out=ktl[i][:], in_=kTh, func=AF.Copy, scale=s_[:, 16 + h:17 + h]),
                         reads=[('kT', b, d), sk], writes=[('ktl', i)])
                    P.op('act', lambda e, kTh=kTh, s_=s_, h=h, b=b, ch=ch: e.activation(out=kd[b][:, ch, :], in_=kTh, func=AF.Copy, scale=s_[:, 24 + h:25 + h]),
                         reads=[('kT', b, d), sk], writes=[('kd', b, ch)])
                    pu, pu_k = bk[:, 0:256], bkk
                    P.op('pe', lambda e, pu=pu, TT=TT, vTh=vTh: e.matmul(pu[:, 0:128], TT[:], vTh, start=True, stop=True), reads=[TTk, ('vT', b, d)], writes=[pu_k], inc=False)
                    P.op('pe', lambda e, pu=pu, TT=TT, i=i: e.matmul(pu[:, 128:256], ktl[i][:], TT[:], start=True, stop=True), reads=[TTk, ('ktl', i)], writes=[pu_k])
                    yield
                    P.op('act', lambda e, pu=pu, be_=be_, h=h, b=b, ch=ch: e.activation(out=uu[b][:, ch, :], in_=pu[:, 0:128], func=AF.Copy, scale=be_[:, h:h + 1]),
                         reads=[pu_k, gk_], writes=[('uu', b, ch)])
                    P.op('dve', lambda e, pu=pu, b=b, ch=ch: e.tensor_copy(out=wT[b][:, ch, :], in_=pu[:, 128:256]), reads=[pu_k], writes=[('wT', b, ch)])
                for g0 in range(0, NH, NS):
                    run_rr([prob(g0 + k, k) for k in range(NS)])
            for d in range(2):
                s_ = stt[b][d]
                sk = ('stt', b, d)
                c = n if d == 0 else N - 1 - n
                def chain(h, vi, d=d, s_=s_, sk=sk):
                    ch = d * 8 + h
                    qFh = kq[b][d][:, h, 128:256]
                    sbk = 3 + vi
                    p1, p1_k = pf[sbk][:, 0:256], ('pf', sbk)
                    P.op('pe', lambda e, p1=p1, b=b, ch=ch: e.matmul(p1[:, 0:128], wT[b][:, ch, :], Sb_[:, ch, :], start=True, stop=True),
                         reads=[('wT', b, ch), ('Sb', ch)], writes=[p1_k], inc=False)
                    P.op('pe', lambda e, p1=p1, qFh=qFh, ch=ch: e.matmul(p1[:, 128:256], qFh, Sb_[:, ch, :], start=True, stop=True),
                         reads=[('qF', b, d), ('Sb', ch)], writes=[p1_k])
                    yield
                    P.op('dve', lambda e, p1=p1, s_=s_, h=h, b=b, ch=ch, vi=vi: e.scalar_tensor_tensor(out=vn[vi][:], in0=p1[:, 0:128], scalar=s_[:, 40 + h:41 + h], in1=uu[b][:, ch, :], op0=ALU.mult, op1=ALU.add),
                         reads=[p1_k, sk, ('uu', b, ch)], writes=[('vn', vi)])
                    yield
                    p2, p2_k = pf[sbk][:, 256:512], ('pf', sbk)
                    P.op('pe', lambda e, p2=p2, b=b, ch=ch, vi=vi: e.matmul(p2[:, 0:128], aT[b][:, ch, :], vn[vi][:], start=True, stop=True),
                         reads=[('aT', b, ch), ('vn', vi)], writes=[p2_k], inc=False)
                    P.op('pe', lambda e, p2=p2, b=b, ch=ch, vi=vi: e.matmul(p2[:, 128:256], kd[b][:, ch, :], vn[vi][:], start=True, stop=True),
                         reads=[('kd', b, ch), ('vn', vi)], writes=[p2_k])
                    yield
                    P.op('act', lambda e, p2=p2, vi=vi: e.activation(out=t2[vi][:], in_=p2[:, 0:128], func=AF.Copy), reads=[p2_k], writes=[('t2', vi)])
                    P.op('dve', lambda e, p1=p1, s_=s_, h=h, b=b, d=d, vi=vi: e.scalar_tensor_tensor(out=ob[b][d][:, h, :], in0=p1[:, 128:256], scalar=s_[:, 16 + h:17 + h], in1=t2[vi][:], op0=ALU.mult, op1=ALU.add),
                         reads=[p1_k, sk, ('t2', vi)], writes=[('ob', b, d)])
                    P.op('dve', lambda e, p2=p2, s_=s_, h=h, ch=ch: e.scalar_tensor_tensor(out=Sf[:, ch, :], in0=Sf[:, ch, :], scalar=s_[:, 32 + h:33 + h], in1=p2[:, 128:256], op0=ALU.mult, op1=ALU.add),
                         reads=[p2_k, sk, ('Sf', ch)], writes=[('Sf', ch)])
                    P.op('pool', lambda e, ch=ch: e.tensor_copy(out=Sb_[:, ch, :], in_=Sf[:, ch, :]), reads=[('Sf', ch)], writes=[('Sb', ch)])
                for g0 in range(0, NH, 4):
                    run_rr([chain(g0 + k, k) for k in range(4)])
                P.dma('pool', C.od_d[d, sl(c, 128), :].rearrange("p (h f) -> p h f", h=NH), ob[b][d][:], reads=[('ob', b, d)], writes=[('od_d', d, c)])
    P.barrier()


def phase_mla_prep(C, P, S, l, pos0=0):
    nc = C.nc
    with ExitStack() as st:
        u_ = C.newuid()
        sb = lambda n, s, d: st.enter_context(nc.sbuf_tensor(n + u_, s, d))
        psm = lambda n, s, d: st.enter_context(nc.psum_tensor(n + u_, s, d))
        wst = sb("m_wst", [128, 4, 2048], F32)
        wq = sb("m_wq", [128, 4, 2048], BF16)
        wkv = sb("m_wkv", [128, 2, 2048], BF16)
        gq = sb("m_gq", [128, 4], F32)
        gkv = sb("m_gkv", [128, 2], F32)
        cq = [sb("m_cq%d" % i, [128, 4, 512], BF16) for i in range(2)]
        ckv = [sb("m_ckv%d" % i, [128, 2, 512], BF16) for i in range(2)]
        kpe = [sb("m_kpe%d" % i, [64, 2, 512], BF16) for i in range(2)]
        sqq = sb("m_sqq", [128, 4, 512], BF16)
        sqk = sb("m_sqk", [128, 2, 512], BF16)
        rq = sb("m_rq", [128, 512], F32)
        rkv = sb("m_rkv", [128, 512], F32)
        rkc = sb("m_rkc", [128, 4], F32)
        cs = [sb("m_cs%d" % i, [64, 2, 512], F32) for i in range(2)]
        csr = sb("m_csr", [64, 2, 512], F32)
        t1 = [sb("m_t1%d" % i, [64, 512], F32) for i in range(2)]
        t2 = [sb("m_t2%d" % i, [64, 512], F32) for i in range(2)]
        oq = [sb("m_oq%d" % i, [128, 512], BF16) for i in range(3)]
        ope = [sb("m_ope%d" % i, [64, 512], BF16) for i in range(3)]
        ov = [sb("m_ov%d" % i, [128, 4, 1024], BF16) for i in range(2)]
        pm = [psm("m_pm%d" % i, [128, 512], F32) for i in range(4)]
        pp = [psm("m_pp%d" % i, [64, 512], F32) for i in range(2)]
        pcol = psm("m_pcol", [128, 512], F32)[:, 0:4]
        pv = psm("m_pv", [128, 512], F32)
        P.dma('sp', gq[:], C.gq_r[l], writes=['gq'])
        P.dma('sp', gkv[:], C.gkv_r[l], writes=['gkv'])
        P.dma('sp', wst[:], C.wuq_r[l].rearrange("(c p) n -> p c n", p=128), writes=['wst'])
        for c in range(4):
            P.op('dve' if c % 2 else 'pool', lambda e, c=c: e.tensor_scalar(out=wq[:, c, :], in0=wst[:, c, :], scalar1=gq[:, c:c + 1], scalar2=None, op0=ALU.mult),
                 reads=['wst', 'gq'], writes=['wq'])
        P.dma('sp', wst[:, 0:2, :], C.wukv_r[l].rearrange("(c p) n -> p c n", p=128), reads=[], writes=['wst'])
        for c in range(2):
            P.op('dve' if c % 2 else 'pool', lambda e, c=c: e.tensor_scalar(out=wkv[:, c, :], in0=wst[:, c, :], scalar1=gkv[:, c:c + 1], scalar2=None, op0=ALU.mult),
                 reads=['wst', 'gkv'], writes=['wkv'])
        ntt = S // 512
        oc = 0
        for tt in range(ntt):
            b2 = tt % 2
            P.dma('sp', cq[b2][:], C.projT_d[R_CQ:R_CQ + 512, sl(tt, 512)].rearrange("(c p) t -> p c t", p=128), writes=[('cq', b2)])
            P.dma('sp', ckv[b2][:], C.projT_d[R_CKV:R_CKV + 256, sl(tt, 512)].rearrange("(c p) t -> p c t", p=128), writes=[('ckv', b2)])
            P.dma('sp', kpe[b2][:], C.projT_d[R_KPE:R_KPE + 128, sl(tt, 512)].rearrange("(c p) t -> p c t", p=64), writes=[('kpe', b2)])
            P.dma('sp', cs[b2][:], C.rope_d[:, :, pos0 + tt * 512:pos0 + (tt + 1) * 512], writes=[('cs', b2)])
            P.op('pool', lambda e, b2=b2: e.tensor_tensor(out=sqq[:], in0=cq[b2][:], in1=cq[b2][:], op=ALU.mult), reads=[('cq', b2)], writes=['sqq'])
            P.op('pool', lambda e, b2=b2: e.tensor_tensor(out=sqk[:], in0=ckv[b2][:], in1=ckv[b2][:], op=ALU.mult), reads=[('ckv', b2)], writes=['sqk'])
            for c in range(4):
                P.op('pe', lambda e, c=c: e.matmul(pm[0][:], C.onesb[:], sqq[:, c, :], start=(c == 0), stop=(c == 3)), reads=['sqq', 'onesb'], writes=[('pm', 0)], inc=(c == 3))
            P.op('act', lambda e: e.activation(out=rq[:], in_=pm[0][:], func=AF.Sqrt, bias=EPS, scale=1.0 / 512), reads=[('pm', 0)], writes=['rq'])
            P.op('dve', lambda e: e.reciprocal(out=rq[:], in_=rq[:]), reads=['rq'], writes=['rq'])
            P.op('dve', lambda e: e.tensor_scalar(out=rq[:], in0=rq[:], scalar1=MLA_SCALE, scalar2=None, op0=ALU.mult), reads=['rq'], writes=['rq'])
            for c in range(2):
                P.op('pe', lambda e, c=c: e.matmul(pm[1][:], C.onesb[:], sqk[:, c, :], start=(c == 0), stop=(c == 1)), reads=['sqk', 'onesb'], writes=[('pm', 1)], inc=(c == 1))
            P.op('act', lambda e: e.activation(out=rkv[:], in_=pm[1][:], func=AF.Sqrt, bias=EPS, scale=1.0 / 256), reads=[('pm', 1)], writes=['rkv'])
            P.op('dve', lambda e: e.reciprocal(out=rkv[:], in_=rkv[:]), reads=['rkv'], writes=['rkv'])
            for j in range(4):
                for c in range(2):
                    P.op('pe', lambda e, j=j, c=c: e.matmul(pcol[:, j:j + 1], sqk[:, c, sl(j, 128)], C.onesb[:, 0:1], start=(c == 0), stop=(c == 1)),
                         reads=['sqk', 'onesb'], writes=['pcol'], inc=(c == 1))
            P.op('act', lambda e: e.activation(out=rkc[:], in_=pcol[:], func=AF.Sqrt, bias=EPS, scale=1.0 / 256), reads=['pcol'], writes=['rkc'])
            P.op('dve', lambda e: e.reciprocal(out=rkc[:], in_=rkc[:]), reads=['rkc'], writes=['rkc'])
            for w_ in range(2):
                P.op('pool', lambda e, w_=w_, b2=b2: e.tensor_tensor(out=csr[:, w_, :], in0=cs[b2][:, w_, :], in1=rq[0:64, :], op=ALU.mult), reads=[('cs', b2), 'rq'], writes=['csr'])
            P.op('dve', lambda e, b2=b2: e.tensor_tensor(out=t1[0][:], in0=kpe[b2][:, 0, :], in1=cs[b2][:, 0, :], op=ALU.mult), reads=[('kpe', b2), ('cs', b2)], writes=[('t1', 0)])
            P.op('pool', lambda e, b2=b2: e.tensor_tensor(out=t2[0][:], in0=kpe[b2][:, 1, :], in1=cs[b2][:, 1, :], op=ALU.mult), reads=[('kpe', b2), ('cs', b2)], writes=[('t2', 0)])
            o_ = ope[oc % 3]
            ok = ('ope', oc % 3)
            P.op('dve', lambda e, o_=o_: e.tensor_tensor(out=o_[:], in0=t1[0][:], in1=t2[0][:], op=ALU.add), reads=[('t1', 0), ('t2', 0)], writes=[ok])
            P.dma('pool', C.akpe_d[:, sl(tt, 512)], o_[:], reads=[ok], writes=[('akpe_d', tt)])
            oc += 1
            for h in range(NH):
                pi = (h * 2) % 4
                for c in range(4):
                    P.op('pe', lambda e, c=c, h=h, b2=b2, pi=pi: e.matmul(pm[pi][:], wq[:, c, h * 256:h * 256 + 128], cq[b2][:, c, :], start=(c == 0), stop=(c == 3)),
                         reads=['wq', ('cq', b2)], writes=[('pm', pi)], inc=(c == 3))
                o_ = oq[oc % 3]
                ok = ('oq', oc % 3)
                P.op('dve', lambda e, o_=o_, pi=pi: e.tensor_tensor(out=o_[:], in0=pm[pi][:], in1=rq[:], op=ALU.mult), reads=[('pm', pi), 'rq'], writes=[ok])
                P.dma('pool', C.aq_d[h, 0, :, sl(tt, 512)], o_[:], reads=[ok], writes=[('aq_d', h, 0, tt)])
                for w_ in range(2):
                    for c in range(4):
                        P.op('pe', lambda e, c=c, h=h, b2=b2, w_=w_: e.matmul(pp[w_][:], wq[:, c, h * 256 + 128 + w_ * 64:h * 256 + 192 + w_ * 64], cq[b2][:, c, :], start=(c == 0), stop=(c == 3)),
                             reads=['wq', ('cq', b2)], writes=[('pp', w_)], inc=(c == 3))
                P.op('dve', lambda e: e.tensor_tensor(out=t1[1][:], in0=pp[0][:], in1=csr[:, 0, :], op=ALU.mult), reads=[('pp', 0), 'csr'], writes=[('t1', 1)])
                P.op('dve', lambda e: e.tensor_tensor(out=t2[1][:], in0=pp[1][:], in1=csr[:, 1, :], op=ALU.mult), reads=[('pp', 1), 'csr'], writes=[('t2', 1)])
                o2 = ope[oc % 3]
                ok2 = ('ope', oc % 3)
                P.op('pool', lambda e, o2=o2: e.tensor_tensor(out=o2[:], in0=t1[1][:], in1=t2[1][:], op=ALU.add), reads=[('t1', 1), ('t2', 1)], writes=[ok2])
                P.dma('pool', C.aq_d[h, 1, 0:64, sl(tt, 512)], o2[:], reads=[ok2], writes=[('aq_d', h, 1, tt)])
                oc += 1
                pi = (h * 2 + 1) % 4
                for c in range(2):
                    P.op('pe', lambda e, c=c, h=h, b2=b2, pi=pi: e.matmul(pm[pi][:], wkv[:, c, h * 256:h * 256 + 128], ckv[b2][:, c, :], start=(c == 0), stop=(c == 1)),
                         reads=['wkv', ('ckv', b2)], writes=[('pm', pi)], inc=(c == 1))
                o_ = oq[oc % 3]
                ok = ('oq', oc % 3)
                P.op('dve', lambda e, o_=o_, pi=pi: e.tensor_tensor(out=o_[:], in0=pm[pi][:], in1=rkv[:], op=ALU.mult), reads=[('pm', pi), 'rkv'], writes=[ok])
                P.dma('pool', C.ak_d[h, :, sl(tt, 512)], o_[:], reads=[ok], writes=[('ak_d', h, tt)])
                oc += 1
            ovb = ov[b2]
            for j in range(4):
                for gp in range(2):
                    for c in range(2):
                        rhs = wkv[:, c, :].rearrange("p (h w) -> p h w", h=NH)[:, gp * 4:gp * 4 + 4, 128:256]
                        P.op('pe', lambda e, j=j, c=c, b2=b2, rhs=rhs: e.matmul(pv[:].rearrange("p (h w) -> p h w", h=4), ckv[b2][:, c, sl(j, 128)], rhs, start=(c == 0), stop=(c == 1)),
                             reads=['wkv', ('ckv', b2)], writes=['pv'], inc=(c == 1))
                    P.op('act', lambda e, j=j, gp=gp, ovb=ovb: e.activation(out=ovb[:, j, gp * 512:(gp + 1) * 512], in_=pv[:], func=AF.Copy, scale=rkc[:, j:j + 1]),
                         reads=['pv', 'rkc'], writes=[('ov', b2)])
            P.dma('pool', C.av_d[sl(tt, 512), :].rearrange("(j p) n -> p j n", p=128), ovb[:], reads=[('ov', b2)], writes=[('av_d', tt)])
    P.barrier()


def phase_attn(C, P, S, l):
    nc = C.nc
    with ExitStack() as st:
        u_ = C.newuid()
        sb = lambda n, s, d: st.enter_context(nc.sbuf_tensor(n + u_, s, d))
        psm = lambda n, s, d: st.enter_context(nc.psum_tensor(n + u_, s, d))
        NK = S // 128
        NQ = S // 512
        kpe = sb("a_kpe", [128, S], BF16)
        kn = sb("a_kn", [128, S], BF16)
        vv = sb("a_vv", [128, NK, 128], BF16)
        qn = [sb("a_qn%d" % i, [128, 512], BF16) for i in range(4)]
        qp = [sb("a_qp%d" % i, [128, 512], BF16) for i in range(4)]
        pT = [sb("a_pT%d" % i, [128, 512], BF16) for i in range(4)]
        acc = [[sb("a_acc%d%d" % (i, k), [128, 512], F32) for k in range(2)] for i in range(2)]
        accb = [[sb("a_accb%d%d" % (i, k), [128, 512], BF16) for k in range(2)] for i in range(2)]
        rs = sb("a_rs", [128, 512], F32)
        oo = [sb("a_oo%d" % i, [128, 512], BF16) for i in range(2)]
        psT = [psm("a_ps%d" % i, [128, 512], F32) for i in range(4)]
        po = [psm("a_po%d" % i, [128, 512], F32) for i in range(2)]
        pl = [psm("a_pl%d" % i, [128, 512], F32) for i in range(2)]
        P.op('pool', lambda e: e.memset(kpe[64:128, :], 0.0), writes=['kpe'])
        if NK < 2:
            for i in range(2):
                P.op('pool', lambda e, i=i: e.memset(acc[i][1][:], 0.0), writes=[('acc', i, 1)])
        for i in range(4):
            P.op('pool', lambda e, i=i: e.memset(qp[i][64:128, :], 0.0), writes=[('qp', i)])
        P.dma('sp', kpe[0:64, :], C.akpe_d[:, :], writes=['kpe'])
        ipr = 0
        for h in range(NH):
            P.dma('sp', kn[:], C.ak_d[h], writes=['kn'])
            for v0 in range(0, NK, 8):
                v1 = min(v0 + 8, NK)
                P.dma('sp', vv[:, v0:v1, :], C.av_d[v0 * 128:v1 * 128, sl(h, 128)].rearrange("(t p) f -> p t f", p=128), writes=['vv'])
            for q0 in range(0, NQ, 2):
                tiles = list(range(q0, min(q0 + 2, NQ)))
                npt = len(tiles)
                qi = [(ipr % 2) * 2 + ab for ab in range(npt)]
                ipr += 1
                for ab, qt in enumerate(tiles):
                    P.dma('sp', qn[qi[ab]][:], C.aq_d[h, 0, :, sl(qt, 512)], writes=[('qn', qi[ab])])
                    P.dma('sp', qp[qi[ab]][0:64, :], C.aq_d[h, 1, 0:64, sl(qt, 512)], writes=[('qp', qi[ab])])

                def qk_exp(kt, s2, qi=qi, npt=npt):
                    for ab in range(npt):
                        P.op('pe', lambda e, ab=ab: e.matmul(psT[s2 + ab][:], kn[:, sl(kt, 128)], qn[qi[ab]][:], start=True, stop=False),
                             reads=['kn', ('qn', qi[ab])], writes=[('psT', s2 + ab)], inc=False)
                    for ab in range(npt):
                        P.op('pe', lambda e, ab=ab: e.matmul(psT[s2 + ab][:], kpe[:, sl(kt, 128)], qp[qi[ab]][:], start=False, stop=True),
                             reads=['kpe', ('qp', qi[ab])], writes=[('psT', s2 + ab)], inc=(ab == npt - 1))
                    for ab in range(npt):
                        P.op('act', lambda e, ab=ab: e.activation(out=pT[s2 + ab][:], in_=psT[s2 + ab][:], func=AF.Exp), reads=[('psT', s2 + ab)], writes=[('pT', s2 + ab)])

                def pv_acc(kt, s2, npt=npt):
                    for ab in range(npt):
                        P.op('pe', lambda e, ab=ab: e.matmul(po[ab][:], vv[:, kt, :], pT[s2 + ab][:], start=(kt == 0), stop=(kt == NK - 1)),
                             reads=['vv', ('pT', s2 + ab)], writes=[('po', ab)], inc=(ab == npt - 1))
                    for ab in range(npt):
                        ae = 'dve' if (kt + ab) % 2 == 0 else 'pool'
                        ab_ = acc[ab][kt % 2]
                        ak_ = ('acc', ab, kt % 2)
                        if kt < 2:
                            P.op(ae, lambda e, ab_=ab_, ab=ab: e.tensor_copy(out=ab_[:], in_=pT[s2 + ab][:]), reads=[('pT', s2 + ab)], writes=[ak_])
                        else:
                            P.op(ae, lambda e, ab_=ab_, ab=ab: e.tensor_tensor(out=ab_[:], in0=ab_[:], in1=pT[s2 + ab][:], op=ALU.add), reads=[('pT', s2 + ab), ak_], writes=[ak_])
                prev = None
                for kt in range(NK):
                    s2 = (kt % 2) * 2
                    qk_exp(kt, s2)
                    if prev is not None:
                        pv_acc(*prev)
                    prev = (kt, s2)
                pv_acc(*prev)
                for ab, qt in enumerate(tiles):
                    for k in range(2):
                        P.op('dve' if k == 0 else 'act',
                             (lambda e, ab=ab, k=k: e.tensor_copy(out=accb[ab][k][:], in_=acc[ab][k][:])) if k == 0 else
                             (lambda e, ab=ab, k=k: e.activation(out=accb[ab][k][:], in_=acc[ab][k][:], func=AF.Copy)),
                             reads=[('acc', ab, k)], writes=[('accb', ab, k)])
                    P.op('pe', lambda e, ab=ab: e.matmul(pl[ab][:], C.onesb[:], accb[ab][0][:], start=True, stop=False), reads=['onesb', ('accb', ab, 0)], writes=[('pl', ab)], inc=False)
                    P.op('pe', lambda e, ab=ab: e.matmul(pl[ab][:], C.onesb[:], accb[ab][1][:], start=False, stop=True), reads=['onesb', ('accb', ab, 1)], writes=[('pl', ab)])
                    P.op('dve', lambda e, ab=ab: e.reciprocal(out=rs[:], in_=pl[ab][:]), reads=[('pl', ab)], writes=['rs'])
                    P.op('dve', lambda e, ab=ab: e.tensor_tensor(out=oo[ab][:], in0=po[ab][:], in1=rs[:], op=ALU.mult), reads=[('po', ab), 'rs'], writes=[('oo', ab)])
                    P.dma('pool', C.ao_d[h, :, sl(qt, 512)], oo[ab][:], reads=[('oo', ab)], writes=[('ao_d', h, qt)])
    P.barrier()


def phase_out(C, P, S, l, x_d, xo_d):
    nc = C.nc
    with ExitStack() as st:
        u_ = C.newuid()
        sb = lambda n, s, d: st.enter_context(nc.sbuf_tensor(n + u_, s, d))
        psm = lambda n, s, d: st.enter_context(nc.psum_tensor(n + u_, s, d))
        wo = sb("o_wo", [128, 16, D], BF16)
        gpb = sb("o_gpb", [128, D], F32)
        gng = sb("o_gng", [128, 1], F32)
        za = [sb("o_za%d" % i, [128, 16, 128], BF16) for i in range(2)]
        sz = [sb("o_sz%d" % i, [128, 16, 128], F32) for i in range(2)]
        of_ = [sb("o_of%d" % i, [128, D // 2], F32) for i in range(2)]
        ob_ = [sb("o_ob%d" % i, [128, D // 2], F32) for i in range(2)]
        junk = sb("o_junk", [128, 512], F32)
        st8 = [sb("o_st%d" % i, [128, 32], F32) for i in range(2)]
        on = [sb("o_on%d" % i, [128, NH, 128], BF16) for i in range(2)]
        ao = [sb("o_ao%d" % i, [128, NH, 128], BF16) for i in range(2)]
        mix = [sb("o_mix%d" % i, [128, 16, 128], BF16) for i in range(2)]
        xt = [sb("o_xt%d" % i, [128, D], F32) for i in range(2)]
        yt = [sb("o_yt%d" % i, [128, D], F32) for i in range(2)]
        ptr = psm("o_ptr", [128, NH, 128], BF16)
        py = [psm("o_py%d" % i, [128, 512], F32) for i in range(4)]
        P.dma('pool', wo[:], C.w_out[l].rearrange("(c p) n -> p c n", p=128), writes=['wo'])
        P.dma('sp', gpb[:], C.post_g[l:l + 1, :].partition_broadcast(128), writes=['gpb'])
        P.dma('sp', gng[:], C.gng_r[l], writes=['gng'])
        def stageA(t):
            b2 = t % 2
            s8 = st8[b2]
            P.dma('sp', za[b2][:, 0:8, :], C.projT_d[R_ZA:R_ZA + 1024, sl(t, 128)].rearrange("(c p) t -> p c t", p=128), writes=[('za', b2)])
            P.dma('sp', za[b2][:, 8:16, :], C.projT_d[R_ZB:R_ZB + 1024, sl(t, 128)].rearrange("(c p) t -> p c t", p=128), writes=[('za', b2)])
            P.dma('sp', of_[b2][:], C.od_d[0, sl(t, 128), :], writes=[('of', b2)])
            P.dma('sp', ob_[b2][:], C.od_d[1, sl(t, 128), :], writes=[('ob', b2)])
            P.dma('sp', ao[b2][:], C.ao_d[:, :, sl(t, 128)].rearrange("h p t -> p h t"), writes=[('ao', b2)])
            P.dma('sp', xt[b2][:], x_d[sl(t, 128), :], writes=[('xt', b2)])
            P.op('act', lambda e, b2=b2: e.activation(out=sz[b2][:], in_=za[b2][:], func=AF.Silu), reads=[('za', b2)], writes=[('sz', b2)])
            P.op('pool', lambda e, b2=b2: e.tensor_tensor(out=of_[b2][:], in0=of_[b2][:], in1=ob_[b2][:], op=ALU.add), reads=[('of', b2), ('ob', b2)], writes=[('of', b2)])
            s8 = st8[b2]
            for h in range(NH):
                P.op('act', lambda e, b2=b2, h=h, s8=s8: e.activation(out=junk[:, 0:128], in_=of_[b2][:, sl(h, 128)], func=AF.Square, accum_out=s8[:, h:h + 1]),
                     reads=[('of', b2)], writes=['junk', ('s8', b2, h)])
            P.op('act', lambda e, s8=s8: e.activation(out=s8[:, 8:16], in_=s8[:, 0:8], func=AF.Sqrt, bias=EPS, scale=1.0 / 128), reads=[('s8', b2, h) for h in range(NH)], writes=[('s8r', b2)])
            P.op('dve', lambda e, s8=s8: e.reciprocal(out=s8[:, 16:24], in_=s8[:, 8:16]), reads=[('s8r', b2)], writes=[('s8r', b2)])
            for h in range(NH):
                P.op('dve' if h % 2 else 'pool', lambda e, b2=b2, h=h, s8=s8: e.tensor_scalar(out=on[b2][:, h, :], in0=of_[b2][:, sl(h, 128)], scalar1=s8[:, 16 + h:17 + h], scalar2=None, op0=ALU.mult),
                     reads=[('of', b2), ('s8r', b2)], writes=[('on', b2)])

        def stageA2(t):
            b2 = t % 2
            for h in range(NH):
                P.op('pe', lambda e, b2=b2, h=h: e.transpose(out=ptr[:, h, :], in_=on[b2][:, h, :], identity=C.idb[:]), reads=[('on', b2), 'idb'], writes=['ptr'], inc=(h == NH - 1))
            P.op('dve', lambda e, b2=b2: e.scalar_tensor_tensor(out=mix[b2][:, 0:8, :], in0=ptr[:], scalar=gng[:, 0:1], in1=sz[b2][:, 0:8, :], op0=ALU.mult, op1=ALU.mult),
                 reads=['ptr', 'gng', ('sz', b2)], writes=[('mix', b2)])
            P.op('pool', lambda e, b2=b2: e.tensor_tensor(out=mix[b2][:, 8:16, :], in0=ao[b2][:], in1=sz[b2][:, 8:16, :], op=ALU.mult),
                 reads=[('ao', b2), ('sz', b2)], writes=[('mix', b2)])

        def stageB(t):
            b2 = t % 2
            s8 = st8[b2]
            for nb in range(4):
                for c in range(16):
                    P.op('pe', lambda e, b2=b2, nb=nb, c=c: e.matmul(py[nb][:], mix[b2][:, c, :], wo[:, c, sl(nb, 512)], start=(c == 0), stop=(c == 15)),
                         reads=[('mix', b2), 'wo'], writes=[('py', nb)], inc=(c == 15))

        def stageB2(t):
            b2 = t % 2
            s8 = st8[b2]
            for nb in range(4):
                P.op('act', lambda e, nb=nb, s8=s8: e.activation(out=junk[:], in_=py[nb][:], func=AF.Square, accum_out=s8[:, 24 + nb:25 + nb]),
                     reads=[('py', nb)], writes=['junk', ('s8y', b2, nb)])
            P.op('dve', lambda e, s8=s8: e.tensor_tensor(out=s8[:, 28:30], in0=s8[:, 24:26], in1=s8[:, 26:28], op=ALU.add), reads=[('s8y', b2, nb) for nb in range(4)], writes=[('s8z', b2)])
            P.op('dve', lambda e, s8=s8: e.tensor_tensor(out=s8[:, 30:31], in0=s8[:, 28:29], in1=s8[:, 29:30], op=ALU.add), reads=[('s8z', b2)], writes=[('s8z', b2)])
            P.op('act', lambda e, s8=s8: e.activation(out=s8[:, 31:32], in_=s8[:, 30:31], func=AF.Sqrt, bias=EPS, scale=1.0 / D), reads=[('s8z', b2)], writes=[('s8w', b2)])
            P.op('dve', lambda e, s8=s8: e.reciprocal(out=s8[:, 31:32], in_=s8[:, 31:32]), reads=[('s8w', b2)], writes=[('s8w', b2)])
            for nb in range(4):
                P.op('dve', lambda e, b2=b2, nb=nb, s8=s8: e.scalar_tensor_tensor(out=yt[b2][:, sl(nb, 512)], in0=py[nb][:], scalar=s8[:, 31:32], in1=gpb[:, sl(nb, 512)], op0=ALU.mult, op1=ALU.mult),
                     reads=[('py', nb), ('s8w', b2), 'gpb'], writes=[('yt', b2)])
            P.op('pool', lambda e, b2=b2: e.tensor_tensor(out=yt[b2][:], in0=yt[b2][:], in1=xt[b2][:], op=ALU.add), reads=[('yt', b2), ('xt', b2)], writes=[('yt', b2)])
            P.dma('pool', xo_d[sl(t, 128), :], yt[b2][:], reads=[('yt', b2)], writes=[('xo', t)])

        NT7 = S // 128
        stageA(0)
        stageA2(0)
        for t in range(NT7):
            if t + 1 < NT7:
                stageA(t + 1)
            stageB(t)
            if t + 1 < NT7:
                stageA2(t + 1)
            stageB2(t)
    P.barrier()


def build(seqs, depth, debug=False):
    nc = bass.Bass("TRN2", target_bir_lowering=False)
    C = Ctx()
    C.nc = nc
    dt = lambda n, s, d, k="ExternalInput": nc.dram_tensor(n, s, d, kind=k).ap()
    C.pre_g = dt("pre_g", [depth, D], F32)
    C.post_g = dt("post_g", [depth, D], F32)
    C.w_in_r = dt("w_in_r", [depth, D, NROW + 32], F32)
    C.conv_r = dt("conv_r", [depth, 128, 120], F32)
    C.a_log = dt("a_log", [depth, 16], F32)
    C.dt_bias = dt("dt_bias", [depth, 16], F32)
    C.gng_r = dt("gng_r", [depth, 128, 1], F32)
    C.gq_r = dt("gq_r", [depth, 128, 4], F32)
    C.gkv_r = dt("gkv_r", [depth, 128, 2], F32)
    C.wuq_r = dt("wuq_r", [depth, 512, 2048], F32)
    C.wukv_r = dt("wukv_r", [depth, 256, 2048], F32)
    C.w_out = dt("w_out", [depth, D, D], F32)
    Smax = max(s for _, s in seqs)
    C.rope_d = dt("rope", [64, 2, Smax], F32)
    cst_d = dt("cmats", [128, 11, 128], F32)
    xs, ys = {}, {}
    for name, S in seqs:
        xs[name] = dt("x_" + name, [S, D], F32)
        ys[name] = dt("y_" + name, [S, D], F32, "ExternalOutput")
    kind_s = "ExternalOutput" if debug else "Internal"
    scr = {}
    for name, S in seqs:
        d = {}
        d['hT_d'] = dt("hT_" + name, [16, 128, S], BF16, kind_s)
        d['projT_d'] = dt("projT_" + name, [NROW, S], BF16, kind_s)
        d['gb_d'] = dt("gb_" + name, [S, 32], F32, kind_s)
        d['gq_d'] = dt("gq_" + name, [NH, 128, S], BF16, kind_s)
        d['gk_d'] = dt("gk_" + name, [NH, 128, S], BF16, kind_s)
        d['gkT_d'] = dt("gkT_" + name, [S, 1024], BF16, kind_s)
        d['gv_d'] = dt("gv_" + name, [S, 1024], BF16, kind_s)
        d['od_d'] = dt("od_" + name, [2, S, 1024], F32, kind_s)
        d['aq_d'] = dt("aq_" + name, [NH, 2, 128, S], BF16, kind_s)
        d['ak_d'] = dt("ak_" + name, [NH, 128, S], BF16, kind_s)
        d['akpe_d'] = dt("akpe_" + name, [64, S], BF16, kind_s)
        d['av_d'] = dt("av_" + name, [S, 1024], BF16, kind_s)
        d['ao_d'] = dt("ao_" + name, [NH, 128, S], BF16, kind_s)
        d['xmid'] = [dt("xm%d_%s" % (i, name), [S, D], F32, kind_s) for i in range(depth - 1)]
        scr[name] = d
    with ExitStack() as st:
        P = Prog(nc, st)
        sb = lambda n, s, d: st.enter_context(nc.sbuf_tensor(n, s, d))
        cm = sb("k_cm", [128, 11, 128], F32)
        C.idb = sb("k_idb", [128, 128], BF16)
        C.onesb = sb("k_onesb", [128, 128], BF16)
        P.dma('sp', cm[:], cst_d, writes=['cm'])
        P.op('dve', lambda e: e.tensor_copy(out=C.idb[:], in_=cm[:, 0, :]), reads=['cm'], writes=['idb'])
        P.op('dve', lambda e: e.tensor_copy(out=C.onesb[:], in_=cm[:, 6, :]), reads=['cm'], writes=['onesb'])
        C.negIb = [sb('k_negIb%d' % d_, [128, 128], BF16) for d_ in range(2)]
        for d_ in range(2):
            P.op('dve', lambda e, d_=d_: e.tensor_copy(out=C.negIb[d_][:], in_=cm[:, 3 + d_, :]), reads=['cm'], writes=['negIb'])
        C.idf = cm[:, 0, :]
        C.tri = [cm[:, 1, :], cm[:, 2, :]]
        C.negI = [cm[:, 3, :], cm[:, 4, :]]
        C.offd = cm[:, 5, :]
        C.onesf = cm[:, 6, :]
        C.bd16 = cm[:, 7, :]
        C.mlev = [cm[:, 8, :], cm[:, 9, :], cm[:, 10, :]]
        P.barrier()
        for l in range(depth):
            for name, S in seqs:
                for k, v in scr[name].items():
                    setattr(C, k, v)
                x_in = xs[name] if l == 0 else scr[name]['xmid'][l - 1]
                x_out = ys[name] if l == depth - 1 else scr[name]['xmid'][l]
                if 1 in PHASES: phase_norm(C, P, x_in, S, l)
                if 2 in PHASES: phase_inproj(C, P, S, l)
                if 3 in PHASES: phase_gdn_prep(C, P, S, l)
                if 4 in PHASES: phase_gdn(C, P, S, l)
                if 5 in PHASES: phase_mla_prep(C, P, S, l)
                if 6 in PHASES: phase_attn(C, P, S, l)
                if 7 in PHASES: phase_out(C, P, S, l, x_in, x_out)
        P.emit()
    return nc


def const_mats():
    i = np.arange(128)
    ident = np.eye(128, dtype=np.float32)
    tri_f = (i[:, None] <= i[None, :]).astype(np.float32)
    tri_b = (i[:, None] >= i[None, :]).astype(np.float32)
    negI_f = np.where(i[:, None] <= i[None, :], 0.0, NEG).astype(np.float32)
    negI_b = np.where(i[:, None] >= i[None, :], 0.0, NEG).astype(np.float32)
    offd = (1.0 - ident).astype(np.float32)
    ones = np.ones((128, 128), np.float32)
    bd = lambda n: (i[:, None] // n == i[None, :] // n).astype(np.float32)
    bd16, bd32, bd64 = bd(16), bd(32), bd(64)
    return np.ascontiguousarray(np.stack([ident, tri_f, tri_b, negI_f, negI_b, offd, ones,
                                          bd16, bd32 - bd16, bd64 - bd32, ones - bd64], axis=1))


def rope_table(S):
    pos = np.arange(S, dtype=np.float32)
    inv = (np.float32(10000.0) ** (-np.arange(0, 64, 2, dtype=np.float32) / np.float32(64))).astype(np.float32)
    ang = (pos[None, :] * inv[:, None]).astype(np.float32)
    c, s = np.cos(ang).astype(np.float32), np.sin(ang).astype(np.float32)
    cosf = np.concatenate([c, c], 0)
    sins = np.concatenate([-s, s], 0)
    return np.ascontiguousarray(np.stack([cosf, sins], axis=1))


def layout_weights(pre_norm_g, post_norm_g, w_in, conv_w, gdn_a_log, gdn_dt_bias, gdn_norm_g,
                   mla_q_norm_g, mla_kv_norm_g, mla_w_uq, mla_w_ukv, w_out):
    depth = w_in.shape[0]
    f = lambda a: np.ascontiguousarray(np.asarray(a, dtype=np.float32))
    w_in = f(w_in)
    o_qkv, o_za, o_b, o_a, o_cq, o_ckv, o_kpe, o_zb = 0, 3072, 4096, 4112, 4128, 4640, 4896, 4960
    kpe_idx = np.arange(o_kpe, o_kpe + 64)
    kpe_sw = np.concatenate([kpe_idx[32:], kpe_idx[:32]])
    cols = np.concatenate([np.arange(o_qkv, o_qkv + 3072), np.arange(o_za, o_za + 1024),
                           np.arange(o_cq, o_cq + 512), np.arange(o_ckv, o_ckv + 256),
                           kpe_idx, kpe_sw, np.arange(o_zb, o_zb + 1024),
                           np.arange(o_b, o_b + 16), np.arange(o_a, o_a + 16)])
    w_in_r = np.ascontiguousarray(w_in[:, :, cols])
    conv_r = np.ascontiguousarray(f(conv_w).reshape(depth, 5, 24, 128).transpose(0, 3, 1, 2).reshape(depth, 128, 120))
    wuq = f(mla_w_uq).reshape(depth, 512, NH, 192)
    wuq_r = np.ascontiguousarray(np.concatenate([wuq[..., :128], wuq[..., 128:192], wuq[..., 160:192], wuq[..., 128:160]], axis=-1).reshape(depth, 512, NH * 256))
    wukv_r = f(mla_w_ukv)
    return {
        "pre_g": f(pre_norm_g), "post_g": f(post_norm_g), "w_in_r": w_in_r, "conv_r": conv_r,
        "a_log": f(gdn_a_log).reshape(depth, 16), "dt_bias": f(gdn_dt_bias).reshape(depth, 16),
        "gng_r": f(gdn_norm_g).reshape(depth, 128, 1),
        "gq_r": np.ascontiguousarray(f(mla_q_norm_g).reshape(depth, 4, 128).transpose(0, 2, 1)),
        "gkv_r": np.ascontiguousarray(f(mla_kv_norm_g).reshape(depth, 2, 128).transpose(0, 2, 1)),
        "wuq_r": wuq_r, "wukv_r": wukv_r, "w_out": f(w_out),
    }


def kernel(x_prompt, x_sample, pre_norm_g, post_norm_g, w_in, conv_w, gdn_a_log, gdn_dt_bias, gdn_norm_g,
           mla_q_norm_g, mla_kv_norm_g, mla_w_uq, mla_w_ukv, w_out):
    x_prompt = np.asarray(x_prompt, dtype=np.float32)
    x_sample = np.asarray(x_sample, dtype=np.float32)
    depth = np.asarray(w_in).shape[0]
    Sp, Ss = x_prompt.shape[1], x_sample.shape[1]
    seqs = [("s", Ss), ("p", Sp)]
    nc = build(seqs, depth)
    base = layout_weights(pre_norm_g, post_norm_g, w_in, conv_w, gdn_a_log, gdn_dt_bias, gdn_norm_g,
                          mla_q_norm_g, mla_kv_norm_g, mla_w_uq, mla_w_ukv, w_out)
    base["rope"] = rope_table(max(Sp, Ss))
    base["cmats"] = const_mats()
    ncores = x_sample.shape[0]
    in_maps = []
    for c in range(ncores):
        m = dict(base)
        m["x_s"] = np.ascontiguousarray(x_sample[c])
        m["x_p"] = np.ascontiguousarray(x_prompt[0])
        in_maps.append(m)
    res = run_bass_kernel_spmd(nc, in_maps, core_ids=list(range(ncores)))
    y_sample = np.stack([np.asarray(res.results[c]["y_s"], dtype=np.float32) for c in range(ncores)], axis=0)
    y_prompt = np.asarray(res.results[0]["y_p"], dtype=np.float32)[None]
    return (y_prompt, y_sample)
```

```python
import numpy as np
import ml_dtypes
from contextlib import ExitStack
import concourse.bass as bass
import concourse.mybir as mybir
from concourse.bass_utils import run_bass_kernel_spmd

F32 = mybir.dt.float32
BF16 = mybir.dt.bfloat16
AF = mybir.ActivationFunctionType
ALU = mybir.AluOpType

D = 2048
NH = 8
EPS = 1e-6
NROW = 6016
R_QKV, R_ZA, R_CQ, R_CKV, R_KPE, R_ZB = 0, 3072, 4096, 4608, 4864, 4992
MLA_SCALE = 192 ** -0.5
GDN_SCALE = 128 ** -0.5
NEG = -30000.0
PHASES = (1, 2, 3, 4, 5, 6, 7)
DBG3 = (3, 8)


class Prog:
    ENG = ('pe', 'act', 'dve', 'pool', 'sp')
    NL = 8

    def __init__(self, nc, stack):
        self.nc = nc
        self.q = {e: [] for e in self.ENG}
        self.sem = {e: stack.enter_context(nc.semaphore('s_' + e)) for e in self.ENG}
        self.cnt = {e: 0 for e in self.ENG}
        self.dq = ('sp', 'pool')
        self.dsem = {q: [stack.enter_context(nc.semaphore('d_%s%d' % (q, i))) for i in range(self.NL)]
                     for q in self.dq}
        self.dn = {q: 0 for q in self.dq}
        self.seen = {e: {} for e in self.ENG}
        self.lastw = {}
        self.readers = {}

    def _semh(self, key):
        return self.sem[key[1]] if key[0] == 'e' else self.dsem[key[1]][key[2]]

    def _collect(self, eng, reads, writes, is_dma):
        deps = {}

        def add(d, raw):
            if d is None:
                return
            key, val, src = d
            if src == eng and key[0] == 'e' and not is_dma:
                if eng == 'pe' or not raw:
                    return
            if self.seen[eng].get(key, 0) >= val:
                return
            if deps.get(key, 0) < val:
                deps[key] = val
        for t in reads:
            add(self.lastw.get(t), True)
        for t in writes:
            add(self.lastw.get(t), False)
            for d in self.readers.get(t, {}).values():
                add(d, False)
        for key, val in deps.items():
            self.seen[eng][key] = val
        return [(self._semh(k), v) for k, v in deps.items()]

    def _register(self, dep, reads, writes):
        for t in writes:
            self.lastw[t] = dep
            self.readers[t] = {}
        for t in reads:
            self.readers.setdefault(t, {})[dep[0]] = dep

    PSUM_NAMES = {'pt', 'pm', 'pba', 'pc', 'pss', 'ptr', 'pcol', 'pv', 'pp', 'psT', 'po', 'pl', 'py', 'psmall', 'pf'}

    def _isps(self, t):
        return (t[0] if isinstance(t, tuple) else t) in self.PSUM_NAMES

    def op(self, eng, fn, reads=(), writes=(), inc=True):
        writes = list(writes) + [t for t in reads if self._isps(t)]
        reads = [t for t in reads if not self._isps(t)]
        waits = self._collect(eng, reads, writes, False)
        if inc:
            self.cnt[eng] += 1
            dep = (('e', eng), self.cnt[eng], eng)
            self.q[eng].append((waits, fn, self.sem[eng], 1))
        else:
            dep = (('e', eng), self.cnt[eng] + 1, eng)
            self.q[eng].append((waits, fn, None, 0))
        self._register(dep, reads, writes)

    def dma(self, q, out, in_, reads=(), writes=()):
        n = self.dn[q]
        lane = n % self.NL
        self.dn[q] += 1
        key = ('d', q, lane)
        val = 16 * (n // self.NL + 1)
        waits = self._collect(q, reads, writes, True)
        prev = val - 16
        if prev > 0 and self.seen[q].get(key, 0) < prev:
            waits.append((self.dsem[q][lane], prev))
            self.seen[q][key] = prev
        self.q[q].append((waits, lambda e: e.dma_start(out=out, in_=in_), self.dsem[q][lane], 16))
        self._register((key, val, q), reads, writes)

    def coll(self, kind, ins, outs, reads=(), writes=(), ncores=8):
        q = 'pool'
        n = self.dn[q]
        lane = n % self.NL
        self.dn[q] += 1
        key = ('d', q, lane)
        val = 16 * (n // self.NL + 1)
        waits = self._collect(q, reads, writes, True)
        prev = val - 16
        if prev > 0 and self.seen[q].get(key, 0) < prev:
            waits.append((self.dsem[q][lane], prev))
            self.seen[q][key] = prev
        rg = [list(range(ncores))]
        self.q[q].append((waits, lambda e: e.collective_compute(kind, ALU.bypass, replica_groups=rg, ins=[a for a in ins], outs=[a for a in outs]), self.dsem[q][lane], 16))
        self._register((key, val, q), reads, writes)

    def barrier(self):
        for e in self.ENG:
            waits = []
            for e2 in self.ENG:
                key = ('e', e2)
                if self.cnt[e2] > self.seen[e].get(key, 0):
                    waits.append((self.sem[e2], self.cnt[e2]))
                    self.seen[e][key] = self.cnt[e2]
            for q in self.dq:
                for lane in range(self.NL):
                    n = self.dn[q]
                    k = (n - lane + self.NL - 1) // self.NL if n > lane else 0
                    val = 16 * k
                    key = ('d', q, lane)
                    if val > self.seen[e].get(key, 0):
                        waits.append((self.dsem[q][lane], val))
                        self.seen[e][key] = val
            self.q[e].append((waits, None, None, 0))
        self.lastw.clear()
        self.readers.clear()

    def emit(self):
        nc = self.nc
        with nc.Block() as block:
            decos = {'pe': block.tensor, 'act': block.scalar, 'dve': block.vector,
                     'pool': block.gpsimd, 'sp': block.sync}
            for name in self.ENG:
                def body(e, name=name):
                    for waits, fn, sem, inc in self.q[name]:
                        for s, v in waits:
                            e.wait_ge(s, v)
                        if fn is not None:
                            r = fn(e)
                            if inc:
                                r.then_inc(sem, inc)
                decos[name](body)


class Ctx:
    uid = 0

    def newuid(self):
        self.uid += 1
        return "_%d" % self.uid


def sl(i, n):
    return slice(i * n, (i + 1) * n)


def run_rr(gens):
    gens = list(gens)
    while gens:
        for g in list(gens):
            try:
                next(g)
            except StopIteration:
                gens.remove(g)


def phase_norm(C, P, x_d, S, l):
    nc = C.nc
    with ExitStack() as st:
        u_ = C.newuid()
        sb = lambda n, s, d: st.enter_context(nc.sbuf_tensor(n + u_, s, d))
        psm = lambda n, s, d: st.enter_context(nc.psum_tensor(n + u_, s, d))
        gb = sb("n_gb", [128, D], F32)
        xt = [sb("n_xt%d" % i, [128, D], F32) for i in range(3)]
        hs = [sb("n_hs%d" % i, [128, D], BF16) for i in range(2)]
        junk = sb("n_junk", [128, D], BF16)
        ss = sb("n_ss", [128, 8], F32)
        hT = [sb("n_hT%d" % i, [128, 16, 512], BF16) for i in range(2)]
        pt = [psm("n_pt%d" % i, [128, 1024], BF16) for i in range(4)]
        P.dma('sp', gb[:], C.pre_g[l:l + 1, :].partition_broadcast(128), writes=['gb'])
        nt = S // 128
        for t in range(nt):
            xb = xt[t % 3]
            xtk = ('xt', t % 3)
            P.dma('sp', xb[:], x_d[sl(t, 128), :], writes=[xtk])
            c0 = (t % 2) * 4
            sst, rst = ('ss', t % 2), ('rs', t % 2)
            P.op('act', lambda e, xb=xb, c0=c0: e.activation(out=junk[:], in_=xb[:], func=AF.Square, accum_out=ss[:, c0:c0 + 1]),
                 reads=[xtk], writes=['junk', sst])
            P.op('act', lambda e, c0=c0: e.activation(out=ss[:, c0 + 1:c0 + 2], in_=ss[:, c0:c0 + 1], func=AF.Sqrt, bias=EPS, scale=1.0 / D),
                 reads=[sst], writes=[rst])
            P.op('dve', lambda e, c0=c0: e.reciprocal(out=ss[:, c0 + 2:c0 + 3], in_=ss[:, c0 + 1:c0 + 2]),
                 reads=[rst], writes=[rst])
            hb = hs[t % 2]
            hk = ('hs', t % 2)
            P.op('dve', lambda e, xb=xb, hb=hb, c0=c0: e.scalar_tensor_tensor(out=hb[:], in0=xb[:], scalar=ss[:, c0 + 2:c0 + 3], in1=gb[:], op0=ALU.mult, op1=ALU.mult),
                 reads=[xtk, rst, 'gb'], writes=[hk])
            tt, j = t // 4, t % 4
            hTb = hT[tt % 2]
            hTk = ('hT', tt % 2)
            for half in range(2):
                pi = (t % 2) * 2 + half
                ptk = ('pt', pi)
                for c in range(8):
                    cc = half * 8 + c
                    P.op('pe', lambda e, hb=hb, cc=cc, pi=pi, c=c: e.transpose(out=pt[pi][:, sl(c, 128)], in_=hb[:, sl(cc, 128)], identity=C.idb[:]),
                         reads=[hk, 'idb'], writes=[ptk], inc=(c == 7))
                if half == 0:
                    P.op('act', lambda e, hTb=hTb, j=j, pi=pi: e.activation(out=hTb[:, 0:8, sl(j, 128)], in_=pt[pi][:].rearrange("p (c t) -> p c t", c=8), func=AF.Copy),
                         reads=[ptk], writes=[hTk])
                else:
                    P.op('pool' if False else 'dve', lambda e, hTb=hTb, j=j, pi=pi: e.tensor_copy(out=hTb[:, 8:16, sl(j, 128)], in_=pt[pi][:].rearrange("p (c t) -> p c t", c=8)),
                         reads=[ptk], writes=[hTk])
            if j == 3:
                P.dma('pool', C.hT_d.rearrange("c p t -> p c t")[:, :, sl(tt, 512)], hTb[:], reads=[hTk], writes=[('hT_d', tt)])
    P.barrier()


def phase_inproj(C, P, S, l):
    nc = C.nc
    with ExitStack() as st:
        u_ = C.newuid()
        sb = lambda n, s, d: st.enter_context(nc.sbuf_tensor(n + u_, s, d))
        psm = lambda n, s, d: st.enter_context(nc.psum_tensor(n + u_, s, d))
        wb = [sb("p_wb%d" % i, [128, 16, 1024], BF16) for i in range(2)]
        wba = sb("p_wba", [128, 16, 32], BF16)
        hT = [sb("p_hT%d" % i, [128, 16, 512], BF16) for i in range(2)]
        yo = [sb("p_yo%d" % i, [128, 8, 512], BF16) for i in range(2)]
        ba = sb("p_ba", [128, 4, 32], F32)
        cst = sb("p_cst", [128, 48], F32)
        tmp = sb("p_tmp", [128, 4, 16], F32)
        gbt = [sb("p_gbt%d" % i, [128, 4, 32], F32) for i in range(2)]
        pm = [psm("p_pm%d" % i, [128, 512], F32) for i in range(6)]
        pba_full = psm("p_pba", [128, 512], F32)
        pba = pba_full[:, 0:128].rearrange("p (j n) -> p j n", j=4)
        ntt = S // 512
        groups = [(g * 8, min(8, 47 - g * 8)) for g in range(6)]
        win = C.w_in_r[l]
        P.dma('pool', wba[:], win[:, NROW:NROW + 32].rearrange("(c p) n -> p c n", p=128), writes=['wba'])
        P.dma('sp', cst[:, 0:16], C.a_log[l:l + 1, :].partition_broadcast(128), writes=['cst'])
        P.dma('sp', cst[:, 16:32], C.dt_bias[l:l + 1, :].partition_broadcast(128), writes=['cst'])
        P.op('act', lambda e: e.activation(out=cst[:, 32:48], in_=cst[:, 0:16], func=AF.Exp), reads=['cst'], writes=['cstA'])
        it = 0
        def load_w(gi):
            m0, nm = groups[gi]
            P.dma('pool', wb[gi % 2][:, :, 0:nm * 128], win[:, m0 * 128:(m0 + nm) * 128].rearrange("(c p) n -> p c n", p=128), writes=[('wb', gi % 2)])
        load_w(0)
        for gi, (m0, nm) in enumerate(groups):
            wbb = wb[gi % 2]
            wk = ('wb', gi % 2)
            if gi + 1 < len(groups):
                load_w(gi + 1)
            for tt in range(ntt):
                hTb = hT[it % 2]
                hk = ('hT', it % 2)
                P.dma('sp', hTb[:], C.hT_d.rearrange("c p t -> p c t")[:, :, sl(tt, 512)], writes=[hk])
                yob = yo[it % 2]
                yk = ('yo', it % 2)
                for m in range(nm):
                    pmi = (it * 8 + m) % 6
                    pk = ('pm', pmi)
                    for c in range(16):
                        P.op('pe', lambda e, wbb=wbb, hTb=hTb, m=m, c=c, pmi=pmi: e.matmul(pm[pmi][:], wbb[:, c, sl(m, 128)], hTb[:, c, :], start=(c == 0), stop=(c == 15)),
                             reads=[wk, hk], writes=[pk], inc=(c == 15))
                    if m % 2 == 0:
                        P.op('act', lambda e, yob=yob, m=m, pmi=pmi: e.activation(out=yob[:, m, :], in_=pm[pmi][:], func=AF.Copy),
                             reads=[pk], writes=[yk])
                    else:
                        P.op('dve', lambda e, yob=yob, m=m, pmi=pmi: e.tensor_copy(out=yob[:, m, :], in_=pm[pmi][:]),
                             reads=[pk], writes=[yk])
                P.dma('pool', C.projT_d[m0 * 128:(m0 + nm) * 128, sl(tt, 512)].rearrange("(m p) t -> p m t", p=128), yob[:, 0:nm, :],
                      reads=[yk], writes=[('projT_d', gi, tt)])
                if gi == 0:
                    for j in range(4):
                        for c in range(16):
                            P.op('pe', lambda e, hTb=hTb, j=j, c=c: e.matmul(pba[:, j, :], hTb[:, c, sl(j, 128)], wba[:, c, :], start=(c == 0), stop=(c == 15)),
                                 reads=[hk, 'wba'], writes=['pba'], inc=(c == 15))
                    gbb = gbt[tt % 2]
                    gk = ('gbt', tt % 2)
                    P.op('dve', lambda e: e.tensor_copy(out=ba[:], in_=pba[:]), reads=['pba'], writes=['ba'])
                    P.op('act', lambda e, gbb=gbb: e.activation(out=gbb[:, :, 16:32], in_=ba[:, :, 0:16], func=AF.Sigmoid), reads=['ba'], writes=[gk])
                    for j in range(4):
                        P.op('dve', lambda e, j=j: e.tensor_tensor(out=tmp[:, j, :], in0=ba[:, j, 16:32], in1=cst[:, 16:32], op=ALU.add),
                             reads=['ba', 'cst'], writes=['tmp'])
                    P.op('act', lambda e: e.activation(out=tmp[:], in_=tmp[:], func=AF.Exp), reads=['tmp'], writes=['tmp'])
                    P.op('act', lambda e: e.activation(out=tmp[:], in_=tmp[:], func=AF.Ln, bias=1.0), reads=['tmp'], writes=['tmp'])
                    for j in range(4):
                        P.op('dve', lambda e, j=j, gbb=gbb: e.scalar_tensor_tensor(out=gbb[:, j, 0:16], in0=tmp[:, j, :], scalar=-1.0, in1=cst[:, 32:48], op0=ALU.mult, op1=ALU.mult),
                             reads=['tmp', 'cstA'], writes=[gk])
                    P.dma('pool', C.gb_d[sl(tt, 512), :].rearrange("(j p) n -> p j n", p=128), gbb[:], reads=[gk], writes=[('gb_d', tt)])
                it += 1
    P.barrier()


def phase_gdn_prep(C, P, S, l):
    nc = C.nc
    with ExitStack() as st:
        u_ = C.newuid()
        sb = lambda n, s, d: st.enter_context(nc.sbuf_tensor(n + u_, s, d))
        psm = lambda n, s, d: st.enter_context(nc.psum_tensor(n + u_, s, d))
        cw = sb("c_cw", [128, 120], F32)
        dg = sb("c_dg", [128, 120, 128], BF16)
        raw = [sb("c_raw%d" % i, [128, S + 4], BF16) for i in range(2)]
        NI = 3
        act = [sb("c_act%d" % i, [128, 512], F32) for i in range(NI)]
        sq = [sb("c_sq%d" % i, [128, 512], BF16) for i in range(NI)]
        rr = [sb("c_rr%d" % i, [128, 512], F32) for i in range(NI)]
        ofm = [sb("c_ofm%d" % i, [128, 512], BF16) for i in range(NI)]
        otm = [sb("c_otm%d" % i, [128, 4, 128], BF16) for i in range(NI)]
        pc = [psm("c_pc%d" % i, [128, 512], F32) for i in range(NI)]
        pss = [psm("c_pss%d" % i, [128, 512], F32) for i in range(NI)]
        P.dma('sp', cw[:], C.conv_r[l], writes=['cw'])
        for i in range(120):
            P.op('dve' if i % 2 else 'act', (lambda e, i=i: e.tensor_scalar(out=dg[:, i, :], in0=C.idf[:], scalar1=cw[:, i:i + 1], scalar2=None, op0=ALU.mult)) if i % 2 else
                 (lambda e, i=i: e.activation(out=dg[:, i, :], in_=C.idf[:], func=AF.Copy, scale=cw[:, i:i + 1])),
                 reads=['cw', 'idf'], writes=[('dg', i)])
        ntt = S // 512

        def tile(which, h, ct, rb, rk, tt, b2):
            pk = ('pc', b2)
            for j in range(5):
                P.op('pe', lambda e, j=j: e.matmul(pc[b2][:], dg[:, j * 24 + ct, :], rb[:, tt * 512 + j:tt * 512 + j + 512], start=(j == 0), stop=(j == 4)),
                     reads=[rk, ('dg', j * 24 + ct)], writes=[pk], inc=(j == 4))
            yield
            ab, ak = act[b2], ('act', b2)
            P.op('act', lambda e: e.activation(out=ab[:], in_=pc[b2][:], func=AF.Silu), reads=[pk], writes=[ak])
            ob, ok = ofm[b2], ('ofm', b2)
            if which < 2:
                sqb, sk = sq[b2], ('sq', b2)
                P.op('pool', lambda e: e.tensor_tensor(out=sqb[:], in0=ab[:], in1=ab[:], op=ALU.mult), reads=[ak], writes=[sk])
                yield
                psk = ('pss', b2)
                P.op('pe', lambda e: e.matmul(pss[b2][:], C.onesb[:], sqb[:], start=True, stop=True), reads=[sk, 'onesb'], writes=[psk])
                yield
                rb_, rrk = rr[b2], ('rr', b2)
                P.op('act', lambda e: e.activation(out=rb_[:], in_=pss[b2][:], func=AF.Sqrt, bias=EPS, scale=1.0), reads=[psk], writes=[rrk])
                yield
                P.op('dve', lambda e: e.reciprocal(out=rb_[:], in_=rb_[:]), reads=[rrk], writes=[rrk])
                scl = GDN_SCALE if which == 0 else 1.0
                P.op('dve', lambda e: e.scalar_tensor_tensor(out=ob[:], in0=ab[:], scalar=scl, in1=rb_[:], op0=ALU.mult, op1=ALU.mult),
                     reads=[ak, rrk], writes=[ok])
                dst = C.gq_d if which == 0 else C.gk_d
                P.dma('pool', dst[h, :, sl(tt, 512)], ob[:], reads=[ok], writes=[('gfm', which, h, tt)])
            else:
                P.op('dve', lambda e: e.tensor_copy(out=ob[:], in_=ab[:]), reads=[ak], writes=[ok])
            yield
            if which >= 1:
                tk = pk
                ptv = pc[b2][:, 0:256].bitcast(BF16).rearrange("p (j t) -> p j t", j=4)
                for j in range(4):
                    P.op('pe', lambda e, j=j: e.transpose(out=ptv[:, j, :], in_=ob[:, sl(j, 128)], identity=C.idb[:]),
                         reads=[ok, 'idb'], writes=[tk], inc=(j == 3))
                yield
                otb, otk = otm[b2], ('otm', b2)
                P.op('act', lambda e: e.activation(out=otb[:], in_=ptv, func=AF.Copy), reads=[tk], writes=[otk])
                dst = C.gkT_d if which == 1 else C.gv_d
                P.dma('pool', dst[sl(tt, 512), sl(h, 128)].rearrange("(j p) n -> p j n", p=128), otb[:], reads=[otk], writes=[('gtm', which, h, tt)])

        for which in range(DBG3[0]):
            for h in range(DBG3[1]):
                ct = which * 8 + h
                rb = raw[ct % 2]
                rk = ('raw', ct % 2)
                P.op('pool', lambda e, rb=rb: e.memset(rb[:, 0:2], 0.0), writes=[rk])
                P.op('pool', lambda e, rb=rb: e.memset(rb[:, S + 2:S + 4], 0.0), writes=[rk])
                P.dma('sp', rb[:, 2:S + 2], C.projT_d[ct * 128:(ct + 1) * 128, :], writes=[rk])
                for g0 in range(0, ntt, NI):
                    run_rr([tile(which, h, ct, rb, rk, tt, k) for k, tt in enumerate(range(g0, min(g0 + NI, ntt)))])
    P.barrier()


def phase_gdn(C, P, S, l):
    nc = C.nc
    N = S // 128
    with ExitStack() as st:
        u_ = C.newuid()
        sb = lambda n, s, d: st.enter_context(nc.sbuf_tensor(n + u_, s, d))
        psm = lambda n, s, d: st.enter_context(nc.psum_tensor(n + u_, s, d))
        NB = 2
        kq = [[sb("g_kq%d%d" % (b, d), [128, NH, 256], BF16) for d in range(2)] for b in range(NB)]
        kT = [[sb("g_kT%d%d" % (b, d), [128, NH, 128], BF16) for d in range(2)] for b in range(NB)]
        vT = [[sb("g_vT%d%d" % (b, d), [128, NH, 128], BF16) for d in range(2)] for b in range(NB)]
        gbt = [[sb("g_gb%d%d" % (b, d), [128, 32], F32) for d in range(2)] for b in range(NB)]
        stt = [[sb("g_st%d%d" % (b, d), [128, 48], F32) for d in range(2)] for b in range(NB)]
        NS = 4
        TG = [sb("g_TG%d" % i, [128, 128], F32) for i in range(NS)]
        Dm = [sb("g_D%d" % i, [128, 128], F32) for i in range(NS)]
        tm_ = [sb("g_t%d" % i, [128, 128], F32) for i in range(NS)]
        fA = [sb("g_fA%d" % i, [128, 128], BF16) for i in range(NS)]
        fB = [sb("g_fB%d" % i, [128, 128], BF16) for i in range(NS)]
        fX1 = [sb("g_fX1%d" % i, [128, 128], BF16) for i in range(NS)]
        fY1 = [sb("g_fY1%d" % i, [128, 128], BF16) for i in range(NS)]
        fXY = [[sb("g_fXY%d_%d" % (i, k), [128, 256], BF16) for k in range(3)] for i in range(NS)]
        fP = [[sb("g_fP%d_%d" % (i, k), [128, 128], BF16) for k in range(2)] for i in range(NS)]
        bT = [[sb("g_bT%d_%d" % (i, k), [128, 128], BF16) for k in range(2)] for i in range(NS)]
        bTT = [[sb("g_bTT%d_%d" % (i, k), [128, 128], BF16) for k in range(2)] for i in range(NS)]
        bC = [[sb("g_bC%d_%d" % (i, k), [128, 128], BF16) for k in range(2)] for i in range(NS)]
        bNR = [sb("g_bNR%d" % i, [128, 256], BF16) for i in range(NS)]
        fTTb = [sb("g_fTTb%d" % i, [128, 128], BF16) for i in range(NS)]
        ktl = [sb("g_ktl%d" % i, [128, 128], BF16) for i in range(NS)]
        aT = [sb("g_aT%d" % b, [128, 16, 128], BF16) for b in range(NB)]
        wT = [sb("g_wT%d" % b, [128, 16, 128], BF16) for b in range(NB)]
        uu = [sb("g_uu%d" % b, [128, 16, 128], F32) for b in range(NB)]
        kd = [sb("g_kd%d" % b, [128, 16, 128], BF16) for b in range(NB)]
        Sf = sb("g_S", [128, 16, 128], F32)
        Sb_ = sb("g_Sb", [128, 16, 128], BF16)
        vn = [sb("g_vn%d" % i, [128, 128], BF16) for i in range(4)]
        t2 = [sb("g_t2%d" % i, [128, 128], F32) for i in range(4)]
        ob = [[sb("g_ob%d%d" % (b, d), [128, NH, 128], F32) for d in range(2)] for b in range(NB)]
        pf = [psm("g_pf%d" % i, [128, 512], F32) for i in range(7)]
        psmall = psm("g_psm", [128, 512], F32)

        P.op('pool', lambda e: e.memset(Sf[:], 0.0), writes=[('Sf', ch) for ch in range(16)])
        P.op('pool', lambda e: e.memset(Sb_[:], 0.0), writes=[('Sb', ch) for ch in range(16)])

        pctr = [0]
        for n in range(N):
            b = n % NB
            for d in range(2):
                c = n if d == 0 else N - 1 - n
                P.dma('sp', kq[b][d][:, :, 0:128], C.gk_d[:, :, sl(c, 128)].rearrange("h p t -> p h t"), writes=[('kF', b, d)])
                P.dma('sp', kq[b][d][:, :, 128:256], C.gq_d[:, :, sl(c, 128)].rearrange("h p t -> p h t"), writes=[('qF', b, d)])
                P.dma('sp', kT[b][d][:], C.gkT_d[sl(c, 128), :].rearrange("p (h f) -> p h f", h=NH), writes=[('kT', b, d)])
                P.dma('sp', vT[b][d][:], C.gv_d[sl(c, 128), :].rearrange("p (h f) -> p h f", h=NH), writes=[('vT', b, d)])
                P.dma('sp', gbt[b][d][:], C.gb_d[sl(c, 128), :], writes=[('gbt', b, d)])
            for d in range(2):
                tri = C.tri[d]
                negI = C.negI[d]
                g_ = gbt[b][d][:, d * 8:d * 8 + 8]
                be_ = gbt[b][d][:, 16 + d * 8:16 + d * 8 + 8]
                s_ = stt[b][d]
                sk = ('stt', b, d)
                gk_ = ('gbt', b, d)
                P.op('pe', lambda e, tri=tri, g_=g_: e.matmul(psmall[:, 0:8], tri[:], g_, start=True, stop=True), reads=[gk_, 'consts'], writes=['psmall'])
                P.op('pe', lambda e, g_=g_: e.matmul(psmall[:, 8:16], C.onesf[:], g_, start=True, stop=True), reads=[gk_, 'consts'], writes=['psmall'])
                P.op('dve', lambda e, s_=s_: e.tensor_copy(out=s_[:, 0:8], in_=psmall[:, 0:8]), reads=['psmall'], writes=[sk])
                P.op('dve', lambda e, s_=s_: e.tensor_scalar(out=s_[:, 8:16], in0=psmall[:, 0:8], scalar1=-1.0, scalar2=None, op0=ALU.mult), reads=['psmall'], writes=[sk])
                P.op('act', lambda e, s_=s_: e.activation(out=s_[:, 16:24], in_=psmall[:, 0:8], func=AF.Exp), reads=['psmall'], writes=[sk])
                P.op('act', lambda e, s_=s_: e.activation(out=s_[:, 32:40], in_=psmall[:, 8:16], func=AF.Exp), reads=['psmall'], writes=[sk])
                P.op('dve', lambda e, s_=s_: e.tensor_tensor(out=s_[:, 24:32], in0=psmall[:, 8:16], in1=s_[:, 0:8], op=ALU.subtract), reads=['psmall', sk], writes=[sk])
                P.op('act', lambda e, s_=s_: e.activation(out=s_[:, 24:32], in_=s_[:, 24:32], func=AF.Exp), reads=[sk], writes=[sk])
                P.op('dve', lambda e, s_=s_, be_=be_: e.tensor_scalar(out=s_[:, 40:48], in0=be_, scalar1=-1.0, scalar2=None, op0=ALU.mult), reads=[gk_], writes=[sk])
                def prob(h, i, d=d, tri=tri, g_=g_, be_=be_, s_=s_, sk=sk, gk_=gk_):
                    ch = d * 8 + h
                    kFh = kq[b][d][:, h, 0:128]
                    kqh = kq[b][d][:, h, :]
                    kTh = kT[b][d][:, h, :]
                    vTh = vT[b][d][:, h, :]
                    bk = pf[i]
                    bkk = ('pf', i)
                    pkk, pkk_k = bk[:, 0:256], bkk
                    P.op('pe', lambda e, pkk=pkk, kFh=kFh, kqh=kqh: e.matmul(pkk[:, 0:256], kFh, kqh, start=True, stop=True), reads=[('kF', b, d), ('qF', b, d)], writes=[pkk_k])
                    P.op('act', lambda e, i=i, tri=tri, g_=g_, h=h: e.activation(out=TG[i][:], in_=tri[:], func=AF.Copy, scale=g_[:, h:h + 1]),
                         reads=[gk_, 'consts'], writes=[('TG', i)])
                    pb, pb_k = bk[:, 256:384], bkk
                    P.op('pe', lambda e, pb=pb, i=i: e.matmul(pb[:, 0:128], C.onesf[:], TG[i][:], start=True, stop=False), reads=[('TG', i), 'consts'], writes=[pb_k], inc=False)
                    P.op('pe', lambda e, pb=pb, d=d: e.matmul(pb[:, 0:128], C.idb[:], C.negIb[d][:], start=False, stop=True), reads=['consts', 'idb'], writes=[pb_k])
                    yield
                    P.op('act', lambda e, i=i, pb=pb, s_=s_, h=h: e.activation(out=Dm[i][:], in_=pb[:, 0:128], func=AF.Exp, bias=s_[:, 8 + h:9 + h], scale=1.0),
                         reads=[pb_k, sk], writes=[('D', i)])
                    P.op('dve', lambda e, i=i, pkk=pkk, b=b, ch=ch: e.tensor_tensor(out=aT[b][:, ch, :], in0=pkk[:, 128:256], in1=Dm[i][:], op=ALU.mult),
                         reads=[pkk_k, ('D', i)], writes=[('aT', b, ch)])
                    P.op('dve', lambda e, i=i, pkk=pkk: e.tensor_tensor(out=tm_[i][:], in0=pkk[:, 0:128], in1=Dm[i][:], op=ALU.mult),
                         reads=[pkk_k, ('D', i)], writes=[('tm', i)])
                    A_, Bm_ = fA[i], fB[i]
                    Ak, Bk = ('fA', i), ('fB', i)
                    P.op('dve', lambda e, i=i, A_=A_, be_=be_, h=h: e.scalar_tensor_tensor(out=A_[:], in0=tm_[i][:], scalar=be_[:, h:h + 1], in1=C.offd[:], op0=ALU.mult, op1=ALU.mult),
                         reads=[('tm', i), gk_, 'consts'], writes=[Ak])
                    P.op('pe', lambda e, bk=bk, A_=A_: e.transpose(out=bk[:, 384:448].bitcast(BF16), in_=A_[:], identity=C.idb[:]), reads=[Ak, 'idb'], writes=[bkk])
                    yield
                    P.op('act', lambda e, bk=bk, Bm_=Bm_: e.activation(out=Bm_[:], in_=bk[:, 384:448].bitcast(BF16), func=AF.Copy), reads=[bkk], writes=[Bk])
                    X1, Y1 = fX1[i], fY1[i]
                    P.op('pool', lambda e, X1=X1, A_=A_: e.tensor_tensor(out=X1[:], in0=A_[:], in1=C.bd16[:], op=ALU.mult), reads=[Ak, 'consts'], writes=[('fX1', i)])
                    P.op('pool', lambda e, Y1=Y1, Bm_=Bm_: e.tensor_tensor(out=Y1[:], in0=Bm_[:], in1=C.bd16[:], op=ALU.mult), reads=[Bk, 'consts'], writes=[('fY1', i)])
                    Pp = fP[i]
                    P.op('pool', lambda e, Pp=Pp, X1=X1: e.tensor_tensor(out=Pp[0][:], in0=C.idb[:], in1=X1[:], op=ALU.subtract), reads=[('fX1', i), 'idb'], writes=[('fP', i, 0)])
                    yield
                    XY = fXY[i]
                    Yc, Xc, yk_ = Y1[:], X1[:], [('fX1', i), ('fY1', i)]
                    for lv in range(1, 4):
                        last = (lv == 3)
                        P.op('pe', lambda e, bk=bk, Yc=Yc, Xc=Xc: e.matmul(bk[:, 0:128], Xc, Yc, start=True, stop=True), reads=yk_, writes=[bkk], inc=last)
                        if not last:
                            P.op('pe', lambda e, bk=bk, Yc=Yc, Xc=Xc: e.matmul(bk[:, 128:256], Yc, Xc, start=True, stop=True), reads=yk_, writes=[bkk])
                        yield
                        dst = XY[lv - 1]
                        wc = 128 if last else 256
                        P.op('act', lambda e, bk=bk, dst=dst, wc=wc: e.activation(out=dst[:, 0:wc], in_=bk[:, 0:wc], func=AF.Copy), reads=[bkk], writes=[('fXY', i, lv)])
                        Yc, Xc, yk_ = dst[:, 0:128], dst[:, 128:256], [('fXY', i, lv)]
                        src, dstp = Pp[(lv - 1) % 2], Pp[lv % 2]
                        P.op('pe', lambda e, bk=bk, src=src, Yc=Yc: e.matmul(bk[:, 256:384], Yc, src[:], start=True, stop=True), reads=[('fP', i, (lv - 1) % 2), ('fXY', i, lv)], writes=[bkk])
                        yield
                        if not last:
                            P.op('dve', lambda e, bk=bk, dstp=dstp, src=src: e.tensor_tensor(out=dstp[:], in0=src[:], in1=bk[:, 256:384], op=ALU.add), reads=[bkk, ('fP', i, (lv - 1) % 2)], writes=[('fP', i, lv % 2)])
                        else:
                            P.op('dve', lambda e, bk=bk, i=i, src=src: e.tensor_tensor(out=bTT[i][0][:], in0=src[:], in1=bk[:, 256:384], op=ALU.add), reads=[bkk, ('fP', i, (lv - 1) % 2)], writes=[('bTT', i, 0)])
                        yield
                    TTk, TTkk = bTT[i][0], ('bTT', i, 0)
                    Tk, Tkk = bT[i][0], ('bT', i, 0)
                    tpb = bk[:, 384:448].bitcast(BF16)
                    P.op('pe', lambda e, tpb=tpb, TTk=TTk: e.transpose(out=tpb, in_=TTk[:], identity=C.idb[:]), reads=[TTkk, 'idb'], writes=[bkk])
                    P.op('act', lambda e, tpb=tpb, Tk=Tk: e.activation(out=Tk[:], in_=tpb, func=AF.Copy), reads=[bkk], writes=[Tkk])
                    yield
                    TT = fTTb[i]
                    TTk_ = ('fTTb', i)
                    for ci, mk in enumerate(C.mlev):
                        lastc = (ci == 2)
                        Ck = bC[i][0]
                        P.op('pool', lambda e, Ck=Ck, Bm_=Bm_, mk=mk: e.tensor_tensor(out=Ck[:], in0=Bm_[:], in1=mk[:], op=ALU.mult), reads=[Bk, 'consts'], writes=[('bC', i, 0)])
                        P.op('pe', lambda e, bk=bk, Ck=Ck, TTk=TTk: e.matmul(bk[:, 0:128], Ck[:], TTk[:], start=True, stop=True), reads=[('bC', i, 0), TTkk], writes=[bkk])
                        yield
                        NR = bNR[i]
                        P.op('act', lambda e, bk=bk, NR=NR: e.activation(out=NR[:, 0:128], in_=bk[:, 0:128], func=AF.Copy), reads=[bkk], writes=[('bNR', i)])
                        P.op('pe', lambda e, bk=bk, NR=NR, Tk=Tk: e.matmul(bk[:, 256:384], Tk[:], NR[:, 0:128], start=True, stop=True), reads=[('bNR', i), Tkk], writes=[bkk])
                        yield
                        if not lastc:
                            nTT, nT = bTT[i][(ci + 1) % 2], bT[i][(ci + 1) % 2]
                            nTTk, nTk = ('bTT', i, (ci + 1) % 2), ('bT', i, (ci + 1) % 2)
                            P.op('dve', lambda e, bk=bk, nTT=nTT, TTk=TTk: e.tensor_tensor(out=nTT[:], in0=TTk[:], in1=bk[:, 256:384], op=ALU.subtract), reads=[TTkk, bkk], writes=[nTTk])
                            P.op('pe', lambda e, tpb=tpb, nTT=nTT: e.transpose(out=tpb, in_=nTT[:], identity=C.idb[:]), reads=[nTTk, 'idb'], writes=[bkk])
                            yield
                            P.op('act', lambda e, tpb=tpb, nT=nT: e.activation(out=nT[:], in_=tpb, func=AF.Copy), reads=[bkk], writes=[nTk])
                            TTk, TTkk, Tk, Tkk = nTT, nTTk, nT, nTk
                        else:
                            P.op('dve', lambda e, bk=bk, TT=TT, TTk=TTk: e.tensor_tensor(out=TT[:], in0=TTk[:], in1=bk[:, 256:384], op=ALU.subtract), reads=[TTkk, bkk], writes=[TTk_])
                    TTk = TTk_
                    P.op('act', lambda e, i=i, kTh=kTh, s_=s_, h=h: e.activation(out=ktl[i][:], in_=kTh, func=AF.Copy, scale=s_[:, 16 + h:17 + h]),
                         reads=[('kT', b, d), sk], writes=[('ktl', i)])
                    P.op('act', lambda e, kTh=kTh, s_=s_, h=h, b=b, ch=ch: e.activation(out=kd[b][:, ch, :], in_=kTh, func=AF.Copy, scale=s_[:, 24 + h:25 + h]),
                         reads=[('kT', b, d), sk], writes=[('kd', b, ch)])
                    pu, pu_k = bk[:, 0:256], bkk
                    P.op('pe', lambda e, pu=pu, TT=TT, vTh=vTh: e.matmul(pu[:, 0:128], TT[:], vTh, start=True, stop=True), reads=[TTk, ('vT', b, d)], writes=[pu_k], inc=False)
                    P.op('pe', lambda e, pu=pu, TT=TT, i=i: e.matmul(pu[:, 128:256], ktl[i][:], TT[:], start=True, stop=True), reads=[TTk, ('ktl', i)], writes=[pu_k])
                    yield
                    P.op('act', lambda e, pu=pu, be_=be_, h=h, b=b, ch=ch: e.activation(out=uu[b][:, ch, :], in_=pu[:, 0:128], func=AF.Copy, scale=be_[:, h:h + 1]),
                         reads=[pu_k, gk_], writes=[('uu', b, ch)])
                    P.op('dve', lambda e, pu=pu, b=b, ch=ch: e.tensor_copy(out=wT[b][:, ch, :], in_=pu[:, 128:256]), reads=[pu_k], writes=[('wT', b, ch)])
                for g0 in range(0, NH, NS):
                    run_rr([prob(g0 + k, k) for k in range(NS)])
            for d in range(2):
                s_ = stt[b][d]
                sk = ('stt', b, d)
                c = n if d == 0 else N - 1 - n
                def chain(h, vi, d=d, s_=s_, sk=sk):
                    ch = d * 8 + h
                    qFh = kq[b][d][:, h, 128:256]
                    sbk = 3 + vi
                    p1, p1_k = pf[sbk][:, 0:256], ('pf', sbk)
                    P.op('pe', lambda e, p1=p1, b=b, ch=ch: e.matmul(p1[:, 0:128], wT[b][:, ch, :], Sb_[:, ch, :], start=True, stop=True),
                         reads=[('wT', b, ch), ('Sb', ch)], writes=[p1_k], inc=False)
                    P.op('pe', lambda e, p1=p1, qFh=qFh, ch=ch: e.matmul(p1[:, 128:256], qFh, Sb_[:, ch, :], start=True, stop=True),
                         reads=[('qF', b, d), ('Sb', ch)], writes=[p1_k])
                    yield
                    P.op('dve', lambda e, p1=p1, s_=s_, h=h, b=b, ch=ch, vi=vi: e.scalar_tensor_tensor(out=vn[vi][:], in0=p1[:, 0:128], scalar=s_[:, 40 + h:41 + h], in1=uu[b][:, ch, :], op0=ALU.mult, op1=ALU.add),
                         reads=[p1_k, sk, ('uu', b, ch)], writes=[('vn', vi)])
                    yield
                    p2, p2_k = pf[sbk][:, 256:512], ('pf', sbk)
                    P.op('pe', lambda e, p2=p2, b=b, ch=ch, vi=vi: e.matmul(p2[:, 0:128], aT[b][:, ch, :], vn[vi][:], start=True, stop=True),
                         reads=[('aT', b, ch), ('vn', vi)], writes=[p2_k], inc=False)
                    P.op('pe', lambda e, p2=p2, b=b, ch=ch, vi=vi: e.matmul(p2[:, 128:256], kd[b][:, ch, :], vn[vi][:], start=True, stop=True),
                         reads=[('kd', b, ch), ('vn', vi)], writes=[p2_k])
                    yield
                    P.op('act', lambda e, p2=p2, vi=vi: e.activation(out=t2[vi][:], in_=p2[:, 0:128], func=AF.Copy), reads=[p2_k], writes=[('t2', vi)])
                    P.op('dve', lambda e, p1=p1, s_=s_, h=h, b=b, d=d, vi=vi: e.scalar_tensor_tensor(out=ob[b][d][:, h, :], in0=p1[:, 128:256], scalar=s_[:, 16 + h:17 + h], in1=t2[vi][:], op0=ALU.mult, op1=ALU.add),
                         reads=[p1_k, sk, ('t2', vi)], writes=[('ob', b, d)])
                    P.op('dve', lambda e, p2=p2, s_=s_, h=h, ch=ch: e.scalar_tensor_tensor(out=Sf[:, ch, :], in0=Sf[:, ch, :], scalar=s_[:, 32 + h:33 + h], in1=p2[:, 128:256], op0=ALU.mult, op1=ALU.add),
                         reads=[p2_k, sk, ('Sf', ch)], writes=[('Sf', ch)])
                    P.op('pool', lambda e, ch=ch: e.tensor_copy(out=Sb_[:, ch, :], in_=Sf[:, ch, :]), reads=[('Sf', ch)], writes=[('Sb', ch)])
                for g0 in range(0, NH, 4):
                    run_rr([chain(g0 + k, k) for k in range(4)])
                P.dma('pool', C.od_d[d, sl(c, 128), :].rearrange("p (h f) -> p h f", h=NH), ob[b][d][:], reads=[('ob', b, d)], writes=[('od_d', d, c)])
    P.barrier()


def phase_mla_prep(C, P, S, l, pos0=0):
    nc = C.nc
    with ExitStack() as st:
        u_ = C.newuid()
        sb = lambda n, s, d: st.enter_context(nc.sbuf_tensor(n + u_, s, d))
        psm = lambda n, s, d: st.enter_context(nc.psum_tensor(n + u_, s, d))
        wst = sb("m_wst", [128, 4, 2048], F32)
        wq = sb("m_wq", [128, 4, 2048], BF16)
        wkv = sb("m_wkv", [128, 2, 2048], BF16)
        gq = sb("m_gq", [128, 4], F32)
        gkv = sb("m_gkv", [128, 2], F32)
        cq = [sb("m_cq%d" % i, [128, 4, 512], BF16) for i in range(2)]
        ckv = [sb("m_ckv%d" % i, [128, 2, 512], BF16) for i in range(2)]
        kpe = [sb("m_kpe%d" % i, [64, 2, 512], BF16) for i in range(2)]
        sqq = sb("m_sqq", [128, 4, 512], BF16)
        sqk = sb("m_sqk", [128, 2, 512], BF16)
        rq = sb("m_rq", [128, 512], F32)
        rkv = sb("m_rkv", [128, 512], F32)
        rkc = sb("m_rkc", [128, 4], F32)
        cs = [sb("m_cs%d" % i, [64, 2, 512], F32) for i in range(2)]
        csr = sb("m_csr", [64, 2, 512], F32)
        t1 = [sb("m_t1%d" % i, [64, 512], F32) for i in range(2)]
        t2 = [sb("m_t2%d" % i, [64, 512], F32) for i in range(2)]
        oq = [sb("m_oq%d" % i, [128, 512], BF16) for i in range(3)]
        ope = [sb("m_ope%d" % i, [64, 512], BF16) for i in range(3)]
        ov = [sb("m_ov%d" % i, [128, 4, 1024], BF16) for i in range(2)]
        pm = [psm("m_pm%d" % i, [128, 512], F32) for i in range(4)]
        pp = [psm("m_pp%d" % i, [64, 512], F32) for i in range(2)]
        pcol = psm("m_pcol", [128, 512], F32)[:, 0:4]
        pv = psm("m_pv", [128, 512], F32)
        P.dma('sp', gq[:], C.gq_r[l], writes=['gq'])
        P.dma('sp', gkv[:], C.gkv_r[l], writes=['gkv'])
        P.dma('sp', wst[:], C.wuq_r[l].rearrange("(c p) n -> p c n", p=128), writes=['wst'])
        for c in range(4):
            P.op('dve' if c % 2 else 'pool', lambda e, c=c: e.tensor_scalar(out=wq[:, c, :], in0=wst[:, c, :], scalar1=gq[:, c:c + 1], scalar2=None, op0=ALU.mult),
                 reads=['wst', 'gq'], writes=['wq'])
        P.dma('sp', wst[:, 0:2, :], C.wukv_r[l].rearrange("(c p) n -> p c n", p=128), reads=[], writes=['wst'])
        for c in range(2):
            P.op('dve' if c % 2 else 'pool', lambda e, c=c: e.tensor_scalar(out=wkv[:, c, :], in0=wst[:, c, :], scalar1=gkv[:, c:c + 1], scalar2=None, op0=ALU.mult),
                 reads=['wst', 'gkv'], writes=['wkv'])
        ntt = S // 512
        oc = 0
        for tt in range(ntt):
            b2 = tt % 2
            P.dma('sp', cq[b2][:], C.projT_d[R_CQ:R_CQ + 512, sl(tt, 512)].rearrange("(c p) t -> p c t", p=128), writes=[('cq', b2)])
            P.dma('sp', ckv[b2][:], C.projT_d[R_CKV:R_CKV + 256, sl(tt, 512)].rearrange("(c p) t -> p c t", p=128), writes=[('ckv', b2)])
            P.dma('sp', kpe[b2][:], C.projT_d[R_KPE:R_KPE + 128, sl(tt, 512)].rearrange("(c p) t -> p c t", p=64), writes=[('kpe', b2)])
            P.dma('sp', cs[b2][:], C.rope_d[:, :, pos0 + tt * 512:pos0 + (tt + 1) * 512], writes=[('cs', b2)])
            P.op('pool', lambda e, b2=b2: e.tensor_tensor(out=sqq[:], in0=cq[b2][:], in1=cq[b2][:], op=ALU.mult), reads=[('cq', b2)], writes=['sqq'])
            P.op('pool', lambda e, b2=b2: e.tensor_tensor(out=sqk[:], in0=ckv[b2][:], in1=ckv[b2][:], op=ALU.mult), reads=[('ckv', b2)], writes=['sqk'])
            for c in range(4):
                P.op('pe', lambda e, c=c: e.matmul(pm[0][:], C.onesb[:], sqq[:, c, :], start=(c == 0), stop=(c == 3)), reads=['sqq', 'onesb'], writes=[('pm', 0)], inc=(c == 3))
            P.op('act', lambda e: e.activation(out=rq[:], in_=pm[0][:], func=AF.Sqrt, bias=EPS, scale=1.0 / 512), reads=[('pm', 0)], writes=['rq'])
            P.op('dve', lambda e: e.reciprocal(out=rq[:], in_=rq[:]), reads=['rq'], writes=['rq'])
            P.op('dve', lambda e: e.tensor_scalar(out=rq[:], in0=rq[:], scalar1=MLA_SCALE, scalar2=None, op0=ALU.mult), reads=['rq'], writes=['rq'])
            for c in range(2):
                P.op('pe', lambda e, c=c: e.matmul(pm[1][:], C.onesb[:], sqk[:, c, :], start=(c == 0), stop=(c == 1)), reads=['sqk', 'onesb'], writes=[('pm', 1)], inc=(c == 1))
            P.op('act', lambda e: e.activation(out=rkv[:], in_=pm[1][:], func=AF.Sqrt, bias=EPS, scale=1.0 / 256), reads=[('pm', 1)], writes=['rkv'])
            P.op('dve', lambda e: e.reciprocal(out=rkv[:], in_=rkv[:]), reads=['rkv'], writes=['rkv'])
            for j in range(4):
                for c in range(2):
                    P.op('pe', lambda e, j=j, c=c: e.matmul(pcol[:, j:j + 1], sqk[:, c, sl(j, 128)], C.onesb[:, 0:1], start=(c == 0), stop=(c == 1)),
                         reads=['sqk', 'onesb'], writes=['pcol'], inc=(c == 1))
            P.op('act', lambda e: e.activation(out=rkc[:], in_=pcol[:], func=AF.Sqrt, bias=EPS, scale=1.0 / 256), reads=['pcol'], writes=['rkc'])
            P.op('dve', lambda e: e.reciprocal(out=rkc[:], in_=rkc[:]), reads=['rkc'], writes=['rkc'])
            for w_ in range(2):
                P.op('pool', lambda e, w_=w_, b2=b2: e.tensor_tensor(out=csr[:, w_, :], in0=cs[b2][:, w_, :], in1=rq[0:64, :], op=ALU.mult), reads=[('cs', b2), 'rq'], writes=['csr'])
            P.op('dve', lambda e, b2=b2: e.tensor_tensor(out=t1[0][:], in0=kpe[b2][:, 0, :], in1=cs[b2][:, 0, :], op=ALU.mult), reads=[('kpe', b2), ('cs', b2)], writes=[('t1', 0)])
            P.op('pool', lambda e, b2=b2: e.tensor_tensor(out=t2[0][:], in0=kpe[b2][:, 1, :], in1=cs[b2][:, 1, :], op=ALU.mult), reads=[('kpe', b2), ('cs', b2)], writes=[('t2', 0)])
            o_ = ope[oc % 3]
            ok = ('ope', oc % 3)
            P.op('dve', lambda e, o_=o_: e.tensor_tensor(out=o_[:], in0=t1[0][:], in1=t2[0][:], op=ALU.add), reads=[('t1', 0), ('t2', 0)], writes=[ok])
            P.dma('pool', C.akpe_d[:, sl(tt, 512)], o_[:], reads=[ok], writes=[('akpe_d', tt)])
            oc += 1
            for h in range(NH):
                pi = (h * 2) % 4
                for c in range(4):
                    P.op('pe', lambda e, c=c, h=h, b2=b2, pi=pi: e.matmul(pm[pi][:], wq[:, c, h * 256:h * 256 + 128], cq[b2][:, c, :], start=(c == 0), stop=(c == 3)),
                         reads=['wq', ('cq', b2)], writes=[('pm', pi)], inc=(c == 3))
                o_ = oq[oc % 3]
                ok = ('oq', oc % 3)
                P.op('dve', lambda e, o_=o_, pi=pi: e.tensor_tensor(out=o_[:], in0=pm[pi][:], in1=rq[:], op=ALU.mult), reads=[('pm', pi), 'rq'], writes=[ok])
                P.dma('pool', C.aq_d[h, 0, :, sl(tt, 512)], o_[:], reads=[ok], writes=[('aq_d', h, 0, tt)])
                for w_ in range(2):
                    for c in range(4):
                        P.op('pe', lambda e, c=c, h=h, b2=b2, w_=w_: e.matmul(pp[w_][:], wq[:, c, h * 256 + 128 + w_ * 64:h * 256 + 192 + w_ * 64], cq[b2][:, c, :], start=(c == 0), stop=(c == 3)),
                             reads=['wq', ('cq', b2)], writes=[('pp', w_)], inc=(c == 3))
                P.op('dve', lambda e: e.tensor_tensor(out=t1[1][:], in0=pp[0][:], in1=csr[:, 0, :], op=ALU.mult), reads=[('pp', 0), 'csr'], writes=[('t1', 1)])
                P.op('dve', lambda e: e.tensor_tensor(out=t2[1][:], in0=pp[1][:], in1=csr[:, 1, :], op=ALU.mult), reads=[('pp', 1), 'csr'], writes=[('t2', 1)])
                o2 = ope[oc % 3]
                ok2 = ('ope', oc % 3)
                P.op('pool', lambda e, o2=o2: e.tensor_tensor(out=o2[:], in0=t1[1][:], in1=t2[1][:], op=ALU.add), reads=[('t1', 1), ('t2', 1)], writes=[ok2])
                P.dma('pool', C.aq_d[h, 1, 0:64, sl(tt, 512)], o2[:], reads=[ok2], writes=[('aq_d', h, 1, tt)])
                oc += 1
                pi = (h * 2 + 1) % 4
                for c in range(2):
                    P.op('pe', lambda e, c=c, h=h, b2=b2, pi=pi: e.matmul(pm[pi][:], wkv[:, c, h * 256:h * 256 + 128], ckv[b2][:, c, :], start=(c == 0), stop=(c == 1)),
                         reads=['wkv', ('ckv', b2)], writes=[('pm', pi)], inc=(c == 1))
                o_ = oq[oc % 3]
                ok = ('oq', oc % 3)
                P.op('dve', lambda e, o_=o_, pi=pi: e.tensor_tensor(out=o_[:], in0=pm[pi][:], in1=rkv[:], op=ALU.mult), reads=[('pm', pi), 'rkv'], writes=[ok])
                P.dma('pool', C.ak_d[h, :, sl(tt, 512)], o_[:], reads=[ok], writes=[('ak_d', h, tt)])
                oc += 1
            ovb = ov[b2]
            for j in range(4):
                for gp in range(2):
                    for c in range(2):
                        rhs = wkv[:, c, :].rearrange("p (h w) -> p h w", h=NH)[:, gp * 4:gp * 4 + 4, 128:256]
                        P.op('pe', lambda e, j=j, c=c, b2=b2, rhs=rhs: e.matmul(pv[:].rearrange("p (h w) -> p h w", h=4), ckv[b2][:, c, sl(j, 128)], rhs, start=(c == 0), stop=(c == 1)),
                             reads=['wkv', ('ckv', b2)], writes=['pv'], inc=(c == 1))
                    P.op('act', lambda e, j=j, gp=gp, ovb=ovb: e.activation(out=ovb[:, j, gp * 512:(gp + 1) * 512], in_=pv[:], func=AF.Copy, scale=rkc[:, j:j + 1]),
                         reads=['pv', 'rkc'], writes=[('ov', b2)])
            P.dma('pool', C.av_d[sl(tt, 512), :].rearrange("(j p) n -> p j n", p=128), ovb[:], reads=[('ov', b2)], writes=[('av_d', tt)])
    P.barrier()


def phase_attn(C, P, S, l):
    nc = C.nc
    with ExitStack() as st:
        u_ = C.newuid()
        sb = lambda n, s, d: st.enter_context(nc.sbuf_tensor(n + u_, s, d))
        psm = lambda n, s, d: st.enter_context(nc.psum_tensor(n + u_, s, d))
        NK = S // 128
        NQ = S // 512
        kpe = sb("a_kpe", [128, S], BF16)
        kn2 = [sb("a_kn%d" % i, [128, S], BF16) for i in range(2)]
        vv2 = [sb("a_vv%d" % i, [128, NK, 128], BF16) for i in range(2)]
        qn = [sb("a_qn%d" % i, [128, 512], BF16) for i in range(4)]
        qp = [sb("a_qp%d" % i, [128, 512], BF16) for i in range(4)]
        pT = [sb("a_pT%d" % i, [128, 512], BF16) for i in range(4)]
        acc = [[sb("a_acc%d%d" % (i, k), [128, 512], F32) for k in range(2)] for i in range(2)]
        rs = sb("a_rs", [128, 512], F32)
        oo = [sb("a_oo%d" % i, [128, 512], BF16) for i in range(2)]
        psT = [psm("a_ps%d" % i, [128, 512], F32) for i in range(4)]
        po = [psm("a_po%d" % i, [128, 512], F32) for i in range(2)]
        pl = [psm("a_pl%d" % i, [128, 512], F32) for i in range(2)]
        P.op('pool', lambda e: e.memset(kpe[64:128, :], 0.0), writes=['kpe'])
        for i in range(4):
            P.op('pool', lambda e, i=i: e.memset(qp[i][64:128, :], 0.0), writes=[('qp', i)])
        P.dma('sp', kpe[0:64, :], C.akpe_d[:, :], writes=['kpe'])
        ipr = 0
        def load_kv(h):
            P.dma('sp', kn2[h % 2][:], C.ak_d[h], writes=[('kn', h % 2)])
            for v0 in range(0, NK, 8):
                v1 = min(v0 + 8, NK)
                P.dma('sp', vv2[h % 2][:, v0:v1, :], C.av_d[v0 * 128:v1 * 128, sl(h, 128)].rearrange("(t p) f -> p t f", p=128), writes=[('vv', h % 2)])
        load_kv(0)
        for h in range(NH):
            kn, vv = kn2[h % 2], vv2[h % 2]
            knk, vvk = ('kn', h % 2), ('vv', h % 2)
            if h + 1 < NH:
                load_kv(h + 1)
            for q0 in range(0, NQ, 2):
                tiles = list(range(q0, min(q0 + 2, NQ)))
                npt = len(tiles)
                qi = [(ipr % 2) * 2 + ab for ab in range(npt)]
                ipr += 1
                for ab, qt in enumerate(tiles):
                    P.dma('sp', qn[qi[ab]][:], C.aq_d[h, 0, :, sl(qt, 512)], writes=[('qn', qi[ab])])
                    P.dma('sp', qp[qi[ab]][0:64, :], C.aq_d[h, 1, 0:64, sl(qt, 512)], writes=[('qp', qi[ab])])

                def qk_exp(kt, s2, qi=qi, npt=npt, kn=kn, knk=knk):
                    for ab in range(npt):
                        P.op('pe', lambda e, ab=ab: e.matmul(psT[s2 + ab][:], kn[:, sl(kt, 128)], qn[qi[ab]][:], start=True, stop=False),
                             reads=[knk, ('qn', qi[ab])], writes=[('psT', s2 + ab)], inc=False)
                    for ab in range(npt):
                        P.op('pe', lambda e, ab=ab: e.matmul(psT[s2 + ab][:], kpe[:, sl(kt, 128)], qp[qi[ab]][:], start=False, stop=True),
                             reads=['kpe', ('qp', qi[ab])], writes=[('psT', s2 + ab)], inc=(ab == npt - 1))
                    for ab in range(npt):
                        P.op('act', lambda e, ab=ab: e.activation(out=pT[s2 + ab][:], in_=psT[s2 + ab][:], func=AF.Exp), reads=[('psT', s2 + ab)], writes=[('pT', s2 + ab)])

                def pv_acc(kt, s2, npt=npt, vv=vv, vvk=vvk):
                    for ab in range(npt):
                        P.op('pe', lambda e, ab=ab: e.matmul(po[ab][:], vv[:, kt, :], pT[s2 + ab][:], start=(kt == 0), stop=(kt == NK - 1)),
                             reads=[vvk, ('pT', s2 + ab)], writes=[('po', ab)], inc=(ab == npt - 1))
                    for ab in range(npt):
                        ae = 'dve' if (kt + ab) % 2 == 0 else 'pool'
                        ab_ = acc[ab][kt % 2]
                        ak_ = ('acc', ab, kt % 2)
                        if kt < 2:
                            P.op(ae, lambda e, ab_=ab_, ab=ab: e.tensor_copy(out=ab_[:], in_=pT[s2 + ab][:]), reads=[('pT', s2 + ab)], writes=[ak_])
                        else:
                            P.op(ae, lambda e, ab_=ab_, ab=ab: e.tensor_tensor(out=ab_[:], in0=ab_[:], in1=pT[s2 + ab][:], op=ALU.add), reads=[('pT', s2 + ab), ak_], writes=[ak_])
                prev = None
                for kt in range(NK):
                    s2 = (kt % 2) * 2
                    qk_exp(kt, s2)
                    if prev is not None:
                        pv_acc(*prev)
                    prev = (kt, s2)
                pv_acc(*prev)
                for ab, qt in enumerate(tiles):
                    P.op('pe', lambda e, ab=ab: e.matmul(pl[ab][:], C.onesf[:], acc[ab][0][:], start=True, stop=(NK < 2)), reads=['consts', ('acc', ab, 0)], writes=[('pl', ab)], inc=(NK < 2))
                    if NK >= 2:
                        P.op('pe', lambda e, ab=ab: e.matmul(pl[ab][:], C.onesf[:], acc[ab][1][:], start=False, stop=True), reads=['consts', ('acc', ab, 1)], writes=[('pl', ab)])
                    P.op('dve', lambda e, ab=ab: e.reciprocal(out=rs[:], in_=pl[ab][:]), reads=[('pl', ab)], writes=['rs'])
                    P.op('dve', lambda e, ab=ab: e.tensor_tensor(out=oo[ab][:], in0=po[ab][:], in1=rs[:], op=ALU.mult), reads=[('po', ab), 'rs'], writes=[('oo', ab)])
                    P.dma('pool', C.ao_d[h, :, sl(qt, 512)], oo[ab][:], reads=[('oo', ab)], writes=[('ao_d', h, qt)])
    P.barrier()


def phase_out(C, P, S, l, x_d, xo_d):
    nc = C.nc
    with ExitStack() as st:
        u_ = C.newuid()
        sb = lambda n, s, d: st.enter_context(nc.sbuf_tensor(n + u_, s, d))
        psm = lambda n, s, d: st.enter_context(nc.psum_tensor(n + u_, s, d))
        wo = sb("o_wo", [128, 16, D], BF16)
        gpb = sb("o_gpb", [128, D], F32)
        gng = sb("o_gng", [128, 1], F32)
        za = [sb("o_za%d" % i, [128, 16, 128], BF16) for i in range(2)]
        sz = [sb("o_sz%d" % i, [128, 16, 128], F32) for i in range(2)]
        of_ = [sb("o_of%d" % i, [128, D // 2], F32) for i in range(2)]
        ob_ = [sb("o_ob%d" % i, [128, D // 2], F32) for i in range(2)]
        junk = sb("o_junk", [128, 512], F32)
        st8 = [sb("o_st%d" % i, [128, 32], F32) for i in range(2)]
        on = [sb("o_on%d" % i, [128, NH, 128], BF16) for i in range(2)]
        ao = [sb("o_ao%d" % i, [128, NH, 128], BF16) for i in range(2)]
        mix = [sb("o_mix%d" % i, [128, 16, 128], BF16) for i in range(2)]
        xt = [sb("o_xt%d" % i, [128, D], F32) for i in range(2)]
        yt = [sb("o_yt%d" % i, [128, D], F32) for i in range(2)]
        ptr = psm("o_ptr", [128, NH, 128], BF16)
        py = [psm("o_py%d" % i, [128, 512], F32) for i in range(4)]
        P.dma('pool', wo[:], C.w_out[l].rearrange("(c p) n -> p c n", p=128), writes=['wo'])
        P.dma('sp', gpb[:], C.post_g[l:l + 1, :].partition_broadcast(128), writes=['gpb'])
        P.dma('sp', gng[:], C.gng_r[l], writes=['gng'])
        def stageA(t):
            b2 = t % 2
            s8 = st8[b2]
            P.dma('sp', za[b2][:, 0:8, :], C.projT_d[R_ZA:R_ZA + 1024, sl(t, 128)].rearrange("(c p) t -> p c t", p=128), writes=[('za', b2)])
            P.dma('sp', za[b2][:, 8:16, :], C.projT_d[R_ZB:R_ZB + 1024, sl(t, 128)].rearrange("(c p) t -> p c t", p=128), writes=[('za', b2)])
            P.dma('sp', of_[b2][:], C.od_d[0, sl(t, 128), :], writes=[('of', b2)])
            P.dma('sp', ob_[b2][:], C.od_d[1, sl(t, 128), :], writes=[('ob', b2)])
            P.dma('sp', ao[b2][:], C.ao_d[:, :, sl(t, 128)].rearrange("h p t -> p h t"), writes=[('ao', b2)])
            P.dma('sp', xt[b2][:], x_d[sl(t, 128), :], writes=[('xt', b2)])
            P.op('act', lambda e, b2=b2: e.activation(out=sz[b2][:], in_=za[b2][:], func=AF.Silu), reads=[('za', b2)], writes=[('sz', b2)])
            P.op('pool', lambda e, b2=b2: e.tensor_tensor(out=of_[b2][:], in0=of_[b2][:], in1=ob_[b2][:], op=ALU.add), reads=[('of', b2), ('ob', b2)], writes=[('of', b2)])
            s8 = st8[b2]
            for h in range(NH):
                P.op('act', lambda e, b2=b2, h=h, s8=s8: e.activation(out=junk[:, 0:128], in_=of_[b2][:, sl(h, 128)], func=AF.Square, accum_out=s8[:, h:h + 1]),
                     reads=[('of', b2)], writes=['junk', ('s8', b2, h)])
            P.op('act', lambda e, s8=s8: e.activation(out=s8[:, 8:16], in_=s8[:, 0:8], func=AF.Sqrt, bias=EPS, scale=1.0 / 128), reads=[('s8', b2, h) for h in range(NH)], writes=[('s8r', b2)])
            P.op('dve', lambda e, s8=s8: e.reciprocal(out=s8[:, 16:24], in_=s8[:, 8:16]), reads=[('s8r', b2)], writes=[('s8r', b2)])
            for h in range(NH):
                P.op('dve' if h % 2 else 'pool', lambda e, b2=b2, h=h, s8=s8: e.tensor_scalar(out=on[b2][:, h, :], in0=of_[b2][:, sl(h, 128)], scalar1=s8[:, 16 + h:17 + h], scalar2=None, op0=ALU.mult),
                     reads=[('of', b2), ('s8r', b2)], writes=[('on', b2)])

        def stageA2(t):
            b2 = t % 2
            for h in range(NH):
                P.op('pe', lambda e, b2=b2, h=h: e.transpose(out=ptr[:, h, :], in_=on[b2][:, h, :], identity=C.idb[:]), reads=[('on', b2), 'idb'], writes=['ptr'], inc=(h == NH - 1))
            P.op('dve', lambda e, b2=b2: e.scalar_tensor_tensor(out=mix[b2][:, 0:8, :], in0=ptr[:], scalar=gng[:, 0:1], in1=sz[b2][:, 0:8, :], op0=ALU.mult, op1=ALU.mult),
                 reads=['ptr', 'gng', ('sz', b2)], writes=[('mix', b2)])
            P.op('pool', lambda e, b2=b2: e.tensor_tensor(out=mix[b2][:, 8:16, :], in0=ao[b2][:], in1=sz[b2][:, 8:16, :], op=ALU.mult),
                 reads=[('ao', b2), ('sz', b2)], writes=[('mix', b2)])

        def stageB(t):
            b2 = t % 2
            s8 = st8[b2]
            for nb in range(4):
                for c in range(16):
                    P.op('pe', lambda e, b2=b2, nb=nb, c=c: e.matmul(py[nb][:], mix[b2][:, c, :], wo[:, c, sl(nb, 512)], start=(c == 0), stop=(c == 15)),
                         reads=[('mix', b2), 'wo'], writes=[('py', nb)], inc=(c == 15))

        def stageB2(t):
            b2 = t % 2
            s8 = st8[b2]
            for nb in range(4):
                P.op('act', lambda e, nb=nb, s8=s8: e.activation(out=junk[:], in_=py[nb][:], func=AF.Square, accum_out=s8[:, 24 + nb:25 + nb]),
                     reads=[('py', nb)], writes=['junk', ('s8y', b2, nb)])
            P.op('dve', lambda e, s8=s8: e.tensor_tensor(out=s8[:, 28:30], in0=s8[:, 24:26], in1=s8[:, 26:28], op=ALU.add), reads=[('s8y', b2, nb) for nb in range(4)], writes=[('s8z', b2)])
            P.op('dve', lambda e, s8=s8: e.tensor_tensor(out=s8[:, 30:31], in0=s8[:, 28:29], in1=s8[:, 29:30], op=ALU.add), reads=[('s8z', b2)], writes=[('s8z', b2)])
            P.op('act', lambda e, s8=s8: e.activation(out=s8[:, 31:32], in_=s8[:, 30:31], func=AF.Sqrt, bias=EPS, scale=1.0 / D), reads=[('s8z', b2)], writes=[('s8w', b2)])
            P.op('dve', lambda e, s8=s8: e.reciprocal(out=s8[:, 31:32], in_=s8[:, 31:32]), reads=[('s8w', b2)], writes=[('s8w', b2)])
            for nb in range(4):
                P.op('dve', lambda e, b2=b2, nb=nb, s8=s8: e.scalar_tensor_tensor(out=yt[b2][:, sl(nb, 512)], in0=py[nb][:], scalar=s8[:, 31:32], in1=gpb[:, sl(nb, 512)], op0=ALU.mult, op1=ALU.mult),
                     reads=[('py', nb), ('s8w', b2), 'gpb'], writes=[('yt', b2)])
            P.op('pool', lambda e, b2=b2: e.tensor_tensor(out=yt[b2][:], in0=yt[b2][:], in1=xt[b2][:], op=ALU.add), reads=[('yt', b2), ('xt', b2)], writes=[('yt', b2)])
            P.dma('pool', xo_d[sl(t, 128), :], yt[b2][:], reads=[('yt', b2)], writes=[('xo', t)])

        NT7 = S // 128
        stageA(0)
        stageA2(0)
        for t in range(NT7):
            if t + 1 < NT7:
                stageA(t + 1)
            stageB(t)
            if t + 1 < NT7:
                stageA2(t + 1)
            stageB2(t)
    P.barrier()


def build(seqs, depth, debug=False):
    nc = bass.Bass("TRN2", target_bir_lowering=False)
    C = Ctx()
    C.nc = nc
    dt = lambda n, s, d, k="ExternalInput": nc.dram_tensor(n, s, d, kind=k).ap()
    C.pre_g = dt("pre_g", [depth, D], F32)
    C.post_g = dt("post_g", [depth, D], F32)
    C.w_in_r = dt("w_in_r", [depth, D, NROW + 32], F32)
    C.conv_r = dt("conv_r", [depth, 128, 120], F32)
    C.a_log = dt("a_log", [depth, 16], F32)
    C.dt_bias = dt("dt_bias", [depth, 16], F32)
    C.gng_r = dt("gng_r", [depth, 128, 1], F32)
    C.gq_r = dt("gq_r", [depth, 128, 4], F32)
    C.gkv_r = dt("gkv_r", [depth, 128, 2], F32)
    C.wuq_r = dt("wuq_r", [depth, 512, 2048], F32)
    C.wukv_r = dt("wukv_r", [depth, 256, 2048], F32)
    C.w_out = dt("w_out", [depth, D, D], F32)
    Smax = max(s for _, s in seqs)
    C.rope_d = dt("rope", [64, 2, Smax], F32)
    cst_d = dt("cmats", [128, 11, 128], F32)
    xs, ys = {}, {}
    for name, S in seqs:
        xs[name] = dt("x_" + name, [S, D], F32)
        ys[name] = dt("y_" + name, [S, D], F32, "ExternalOutput")
    kind_s = "ExternalOutput" if debug else "Internal"
    scr = {}
    for name, S in seqs:
        d = {}
        d['hT_d'] = dt("hT_" + name, [16, 128, S], BF16, kind_s)
        d['projT_d'] = dt("projT_" + name, [NROW, S], BF16, kind_s)
        d['gb_d'] = dt("gb_" + name, [S, 32], F32, kind_s)
        d['gq_d'] = dt("gq_" + name, [NH, 128, S], BF16, kind_s)
        d['gk_d'] = dt("gk_" + name, [NH, 128, S], BF16, kind_s)
        d['gkT_d'] = dt("gkT_" + name, [S, 1024], BF16, kind_s)
        d['gv_d'] = dt("gv_" + name, [S, 1024], BF16, kind_s)
        d['od_d'] = dt("od_" + name, [2, S, 1024], F32, kind_s)
        d['aq_d'] = dt("aq_" + name, [NH, 2, 128, S], BF16, kind_s)
        d['ak_d'] = dt("ak_" + name, [NH, 128, S], BF16, kind_s)
        d['akpe_d'] = dt("akpe_" + name, [64, S], BF16, kind_s)
        d['av_d'] = dt("av_" + name, [S, 1024], BF16, kind_s)
        d['ao_d'] = dt("ao_" + name, [NH, 128, S], BF16, kind_s)
        d['xmid'] = [dt("xm%d_%s" % (i, name), [S, D], F32, kind_s) for i in range(depth - 1)]
        scr[name] = d
    with ExitStack() as st:
        P = Prog(nc, st)
        sb = lambda n, s, d: st.enter_context(nc.sbuf_tensor(n, s, d))
        cm = sb("k_cm", [128, 11, 128], F32)
        C.idb = sb("k_idb", [128, 128], BF16)
        C.onesb = sb("k_onesb", [128, 128], BF16)
        P.dma('sp', cm[:], cst_d, writes=['cm'])
        P.op('dve', lambda e: e.tensor_copy(out=C.idb[:], in_=cm[:, 0, :]), reads=['cm'], writes=['idb'])
        P.op('dve', lambda e: e.tensor_copy(out=C.onesb[:], in_=cm[:, 6, :]), reads=['cm'], writes=['onesb'])
        C.negIb = [sb('k_negIb%d' % d_, [128, 128], BF16) for d_ in range(2)]
        for d_ in range(2):
            P.op('dve', lambda e, d_=d_: e.tensor_copy(out=C.negIb[d_][:], in_=cm[:, 3 + d_, :]), reads=['cm'], writes=['negIb'])
        C.idf = cm[:, 0, :]
        C.tri = [cm[:, 1, :], cm[:, 2, :]]
        C.negI = [cm[:, 3, :], cm[:, 4, :]]
        C.offd = cm[:, 5, :]
        C.onesf = cm[:, 6, :]
        C.bd16 = cm[:, 7, :]
        C.mlev = [cm[:, 8, :], cm[:, 9, :], cm[:, 10, :]]
        P.barrier()
        for l in range(depth):
            for name, S in seqs:
                for k, v in scr[name].items():
                    setattr(C, k, v)
                x_in = xs[name] if l == 0 else scr[name]['xmid'][l - 1]
                x_out = ys[name] if l == depth - 1 else scr[name]['xmid'][l]
                if 1 in PHASES: phase_norm(C, P, x_in, S, l)
                if 2 in PHASES: phase_inproj(C, P, S, l)
                if 3 in PHASES: phase_gdn_prep(C, P, S, l)
                if 4 in PHASES: phase_gdn(C, P, S, l)
                if 5 in PHASES: phase_mla_prep(C, P, S, l)
                if 6 in PHASES: phase_attn(C, P, S, l)
                if 7 in PHASES: phase_out(C, P, S, l, x_in, x_out)
        P.emit()
    return nc


def const_mats():
    i = np.arange(128)
    ident = np.eye(128, dtype=np.float32)
    tri_f = (i[:, None] <= i[None, :]).astype(np.float32)
    tri_b = (i[:, None] >= i[None, :]).astype(np.float32)
    negI_f = np.where(i[:, None] <= i[None, :], 0.0, NEG).astype(np.float32)
    negI_b = np.where(i[:, None] >= i[None, :], 0.0, NEG).astype(np.float32)
    offd = (1.0 - ident).astype(np.float32)
    ones = np.ones((128, 128), np.float32)
    bd = lambda n: (i[:, None] // n == i[None, :] // n).astype(np.float32)
    bd16, bd32, bd64 = bd(16), bd(32), bd(64)
    return np.ascontiguousarray(np.stack([ident, tri_f, tri_b, negI_f, negI_b, offd, ones,
                                          bd16, bd32 - bd16, bd64 - bd32, ones - bd64], axis=1))


def rope_table(S):
    pos = np.arange(S, dtype=np.float32)
    inv = (np.float32(10000.0) ** (-np.arange(0, 64, 2, dtype=np.float32) / np.float32(64))).astype(np.float32)
    ang = (pos[None, :] * inv[:, None]).astype(np.float32)
    c, s = np.cos(ang).astype(np.float32), np.sin(ang).astype(np.float32)
    cosf = np.concatenate([c, c], 0)
    sins = np.concatenate([-s, s], 0)
    return np.ascontiguousarray(np.stack([cosf, sins], axis=1))


def layout_weights(pre_norm_g, post_norm_g, w_in, conv_w, gdn_a_log, gdn_dt_bias, gdn_norm_g,
                   mla_q_norm_g, mla_kv_norm_g, mla_w_uq, mla_w_ukv, w_out):
    depth = w_in.shape[0]
    f = lambda a: np.ascontiguousarray(np.asarray(a, dtype=np.float32))
    w_in = f(w_in)
    o_qkv, o_za, o_b, o_a, o_cq, o_ckv, o_kpe, o_zb = 0, 3072, 4096, 4112, 4128, 4640, 4896, 4960
    kpe_idx = np.arange(o_kpe, o_kpe + 64)
    kpe_sw = np.concatenate([kpe_idx[32:], kpe_idx[:32]])
    cols = np.concatenate([np.arange(o_qkv, o_qkv + 3072), np.arange(o_za, o_za + 1024),
                           np.arange(o_cq, o_cq + 512), np.arange(o_ckv, o_ckv + 256),
                           kpe_idx, kpe_sw, np.arange(o_zb, o_zb + 1024),
                           np.arange(o_b, o_b + 16), np.arange(o_a, o_a + 16)])
    w_in_r = np.ascontiguousarray(w_in[:, :, cols])
    conv_r = np.ascontiguousarray(f(conv_w).reshape(depth, 5, 24, 128).transpose(0, 3, 1, 2).reshape(depth, 128, 120))
    wuq = f(mla_w_uq).reshape(depth, 512, NH, 192)
    wuq_r = np.ascontiguousarray(np.concatenate([wuq[..., :128], wuq[..., 128:192], wuq[..., 160:192], wuq[..., 128:160]], axis=-1).reshape(depth, 512, NH * 256))
    wukv_r = f(mla_w_ukv)
    return {
        "pre_g": f(pre_norm_g), "post_g": f(post_norm_g), "w_in_r": w_in_r, "conv_r": conv_r,
        "a_log": f(gdn_a_log).reshape(depth, 16), "dt_bias": f(gdn_dt_bias).reshape(depth, 16),
        "gng_r": f(gdn_norm_g).reshape(depth, 128, 1),
        "gq_r": np.ascontiguousarray(f(mla_q_norm_g).reshape(depth, 4, 128).transpose(0, 2, 1)),
        "gkv_r": np.ascontiguousarray(f(mla_kv_norm_g).reshape(depth, 2, 128).transpose(0, 2, 1)),
        "wuq_r": wuq_r, "wukv_r": wukv_r, "w_out": f(w_out),
    }


def kernel(x_prompt, x_sample, pre_norm_g, post_norm_g, w_in, conv_w, gdn_a_log, gdn_dt_bias, gdn_norm_g,
           mla_q_norm_g, mla_kv_norm_g, mla_w_uq, mla_w_ukv, w_out):
    x_prompt = np.asarray(x_prompt, dtype=np.float32)
    x_sample = np.asarray(x_sample, dtype=np.float32)
    depth = np.asarray(w_in).shape[0]
    Sp, Ss = x_prompt.shape[1], x_sample.shape[1]
    seqs = [("s", Ss), ("p", Sp)]
    nc = build(seqs, depth)
    base = layout_weights(pre_norm_g, post_norm_g, w_in, conv_w, gdn_a_log, gdn_dt_bias, gdn_norm_g,
                          mla_q_norm_g, mla_kv_norm_g, mla_w_uq, mla_w_ukv, w_out)
    base["rope"] = rope_table(max(Sp, Ss))
    base["cmats"] = const_mats()
    ncores = x_sample.shape[0]
    in_maps = []
    for c in range(ncores):
        m = dict(base)
        m["x_s"] = np.ascontiguousarray(x_sample[c])
        m["x_p"] = np.ascontiguousarray(x_prompt[0])
        in_maps.append(m)
    res = run_bass_kernel_spmd(nc, in_maps, core_ids=list(range(ncores)))
    y_sample = np.stack([np.asarray(res.results[c]["y_s"], dtype=np.float32) for c in range(ncores)], axis=0)
    y_prompt = np.asarray(res.results[0]["y_p"], dtype=np.float32)[None]
    return (y_prompt, y_sample)
```
